# Optimizing a Trainium2 kernel written in Bass

```python
import math
import jax, jax.numpy as jnp
from jax import lax
import numpy as np

D_MODEL = 2048
BATCH = 4
SEQ = 2048
DEPTH = 1

D_MIX = D_MODEL
NSA_HEADS = 8
NSA_KV_HEADS = 2
NSA_GROUP = NSA_HEADS // NSA_KV_HEADS
NSA_HEAD_DIM = 128
NSA_WIDTH = NSA_HEADS * NSA_HEAD_DIM
NSA_KV_WIDTH = NSA_KV_HEADS * NSA_HEAD_DIM
CMP_BLOCK = 32
CMP_STRIDE = 16
SLC_BLOCK = 64
SLC_TOP_N = 16
WINDOW = 512
SLC_QBLK = 64
WIN_QBLK = 128
RWKV_WIDTH = D_MIX - NSA_WIDTH
RWKV_HEAD_DIM = 64
RWKV_HEADS = RWKV_WIDTH // RWKV_HEAD_DIM
DECAY_LORA = 64
AAA_LORA = 64
SHIFT_COLS = 3 * RWKV_WIDTH + DECAY_LORA + AAA_LORA
NUM_BUCKETS = 32
MAX_DISTANCE = 1024
NORM_EPS = 1e-6
RWKV_GN_EPS = 64e-5
COL_Q = NSA_WIDTH
COL_KV = 6 * NSA_KV_WIDTH
COL_GATE = 3 * NSA_HEADS
COL_ZA = NSA_WIDTH
COL_ZB = RWKV_WIDTH
IN_COLS = COL_Q + COL_KV + COL_GATE + COL_ZA + SHIFT_COLS + COL_ZB

kernel_name = 'hybrid_nsa_rwkv7_block'


def rms_norm(x, g):
    xf = x.astype(jnp.float32)
    y = xf * lax.rsqrt(jnp.mean(xf * xf, axis=-1, keepdims=True) + NORM_EPS)
    return (y * g.astype(jnp.float32)).astype(x.dtype)


def rel_bucket(dist):
    n = jnp.maximum(dist, 0)
    max_exact = NUM_BUCKETS // 2
    nf = jnp.maximum(n, 1).astype(jnp.float32)
    large = max_exact + (jnp.log(nf / max_exact) / math.log(MAX_DISTANCE / max_exact)
                         * (NUM_BUCKETS - max_exact)).astype(jnp.int32)
    large = jnp.minimum(large, NUM_BUCKETS - 1)
    return jnp.where(n < max_exact, n, large)


def masked_softmax(logits, mask, axis):
    logits = jnp.where(mask, logits, -jnp.inf)
    m = jnp.max(logits, axis=axis, keepdims=True)
    m = jnp.where(jnp.isfinite(m), m, 0.0)
    e = jnp.where(mask, jnp.exp(logits - m), 0.0)
    s = jnp.sum(e, axis=axis, keepdims=True)
    return e / jnp.maximum(s, 1e-30)


def compress(kv, pos, w1, w2):
    B, S, G, D = kv.shape
    n_cmp = (S - CMP_BLOCK) // CMP_STRIDE + 1
    idx = jnp.arange(n_cmp)[:, None] * CMP_STRIDE + jnp.arange(CMP_BLOCK)[None, :]
    blocks = kv[:, idx] + pos[None, None, :, None, :]
    flat = blocks.transpose(0, 1, 3, 2, 4).reshape(B, n_cmp, G, CMP_BLOCK * D)
    return jax.nn.silu(flat @ w1) @ w2


def nsa_mixer(q, k_cmp, v_cmp, k_slc, v_slc, k_win, v_win, gates, rel_bias_table,
              cmp_pos_k, cmp_pos_v, cmp_k_w1, cmp_k_w2, cmp_v_w1, cmp_v_w2):
    f32 = jnp.float32
    B, S = q.shape[:2]
    G, HG, D = NSA_KV_HEADS, NSA_GROUP, NSA_HEAD_DIM
    qf = q.astype(f32).reshape(B, S, G, HG, D) * (D ** -0.5)
    t = jnp.arange(S)
    table = rel_bias_table.astype(f32)

    kc = compress(k_cmp.astype(f32).reshape(B, S, G, D), cmp_pos_k.astype(f32),
                  cmp_k_w1.astype(f32), cmp_k_w2.astype(f32))
    vc = compress(v_cmp.astype(f32).reshape(B, S, G, D), cmp_pos_v.astype(f32),
                  cmp_v_w1.astype(f32), cmp_v_w2.astype(f32))
    n_cmp = kc.shape[1]
    blk_end = jnp.arange(n_cmp) * CMP_STRIDE + CMP_BLOCK - 1
    dist_c = t[:, None] - blk_end[None, :]
    mask_c = dist_c >= 0
    bias_c = table[rel_bucket(dist_c)].reshape(S, n_cmp, G, HG).transpose(2, 3, 0, 1)
    logits_c = jnp.einsum('bsghd,bigd->bghsi', qf, kc) + bias_c
    p_c = masked_softmax(logits_c, mask_c, -1)
    o_c = jnp.einsum('bghsi,bigd->bsghd', p_c, vc)

    n_blk = S // SLC_BLOCK
    cmp_start = jnp.arange(n_cmp) * CMP_STRIDE
    slc_start = jnp.arange(n_blk) * SLC_BLOCK
    overlap = ((cmp_start[:, None] < slc_start[None, :] + SLC_BLOCK)
               & (cmp_start[:, None] + CMP_BLOCK > slc_start[None, :])).astype(f32)
    imp = jnp.einsum('bghsi,ij->bgsj', p_c, overlap)
    cur = t // SLC_BLOCK
    j = jnp.arange(n_blk)
    forced = (j[None, :] == 0) | (j[None, :] == cur[:, None]) | (j[None, :] == cur[:, None] - 1)
    causal_blk = j[None, :] <= cur[:, None]
    score = jnp.where(forced, jnp.inf, jnp.where(causal_blk, imp, -jnp.inf))
    n_sel = min(SLC_TOP_N, n_blk)
    _, sel_idx = lax.top_k(score, n_sel)

    kb = k_slc.astype(f32).reshape(B, n_blk, SLC_BLOCK, G, D).transpose(0, 3, 1, 2, 4)
    vb = v_slc.astype(f32).reshape(B, n_blk, SLC_BLOCK, G, D).transpose(0, 3, 1, 2, 4)
    nqb = S // SLC_QBLK
    q_blocks = qf.reshape(B, nqb, SLC_QBLK, G, HG, D).transpose(1, 0, 2, 3, 4, 5)
    idx_blocks = sel_idx.reshape(B, G, nqb, SLC_QBLK, n_sel).transpose(2, 0, 1, 3, 4)
    t_blocks = t.reshape(nqb, SLC_QBLK)
    b_ix = jnp.arange(B)[:, None, None, None]
    g_ix = jnp.arange(G)[None, :, None, None]
    g_ix5 = jnp.arange(G)[None, :, None, None, None]
    tab = table.reshape(NUM_BUCKETS, G, HG)

    def slc_block(args):
        qb, ib, tb = args
        kg = kb[b_ix, g_ix, ib]
        vg = vb[b_ix, g_ix, ib]
        s_pos = ib[..., None] * SLC_BLOCK + jnp.arange(SLC_BLOCK)
        dist = tb[None, None, :, None, None] - s_pos
        valid_blk = ib <= (tb // SLC_BLOCK)[None, None, :, None]
        mask = (dist >= 0) & valid_blk[..., None]
        bias = tab[rel_bucket(dist), g_ix5]
        logits = jnp.einsum('bqghd,bgqnkd->bgqnkh', qb, kg) + bias
        m_tok = n_sel * SLC_BLOCK
        logits = logits.reshape(B, G, SLC_QBLK, m_tok, HG)
        p = masked_softmax(logits, mask.reshape(B, G, SLC_QBLK, m_tok)[..., None], -2)
        return jnp.einsum('bgqmh,bgqmd->bqghd', p, vg.reshape(B, G, SLC_QBLK, m_tok, D))

    o_s = lax.map(slc_block, (q_blocks, idx_blocks, t_blocks))
    o_s = o_s.transpose(1, 0, 2, 3, 4, 5).reshape(B, S, G, HG, D)

    nwb = S // WIN_QBLK
    n_pre = WINDOW // WIN_QBLK
    band = n_pre + 1
    m_win = band * WIN_QBLK
    pad = ((0, 0), (WINDOW, 0), (0, 0), (0, 0))
    kw = jnp.pad(k_win.astype(f32).reshape(B, S, G, D), pad).reshape(B, nwb + n_pre, WIN_QBLK, G, D)
    vw = jnp.pad(v_win.astype(f32).reshape(B, S, G, D), pad).reshape(B, nwb + n_pre, WIN_QBLK, G, D)
    kband = jnp.concatenate([kw[:, i:i + nwb] for i in range(band)], axis=2)
    vband = jnp.concatenate([vw[:, i:i + nwb] for i in range(band)], axis=2)
    qw = qf.reshape(B, nwb, WIN_QBLK, G, HG, D)
    tq = t.reshape(nwb, WIN_QBLK)
    s_pos = (jnp.arange(nwb) * WIN_QBLK - WINDOW)[:, None] + jnp.arange(m_win)[None, :]
    dist = tq[:, :, None] - s_pos[:, None, :]
    mask_w = (dist >= 0) & (dist < WINDOW) & (s_pos[:, None, :] >= 0)
    bias_w = table[rel_bucket(dist)].reshape(nwb, WIN_QBLK, m_win, G, HG).transpose(0, 3, 4, 1, 2)
    logits_w = jnp.einsum('bnqghd,bnmgd->bnghqm', qw, kband) + bias_w[None]
    p_w = masked_softmax(logits_w, mask_w[None, :, None, None], -1)
    o_w = jnp.einsum('bnghqm,bnmgd->bnqghd', p_w, vband).reshape(B, S, G, HG, D)

    gt = jax.nn.sigmoid(gates.astype(f32)).reshape(B, S, 3, G, HG, 1)
    o = gt[:, :, 0] * o_c + gt[:, :, 1] * o_s + gt[:, :, 2] * o_w
    return o.reshape(B, S, NSA_WIDTH)


def rwkv7_mixer(feat, mu, w0, w2, a0, a2, k_k, k_a, r_k, ln_w, ln_b):
    f32 = jnp.float32
    B, S, _ = feat.shape
    H, N, W = RWKV_HEADS, RWKV_HEAD_DIM, RWKV_WIDTH
    pf = feat.astype(f32)
    prev = jnp.pad(pf[:, :-1], ((0, 0), (1, 0), (0, 0)))
    pf = pf + mu.astype(f32) * (prev - pf)
    r, k, v, wd, ad = jnp.split(pf, [W, 2 * W, 3 * W, 3 * W + DECAY_LORA], axis=-1)
    w = -jax.nn.softplus(-(w0.astype(f32) + jnp.tanh(wd) @ w2.astype(f32))) - 0.5
    decay = jnp.exp(-jnp.exp(w))
    a = jax.nn.sigmoid(a0.astype(f32) + ad @ a2.astype(f32))
    kk = (k * k_k.astype(f32)).reshape(B, S, H, N)
    kk = kk / jnp.maximum(jnp.sqrt(jnp.sum(kk * kk, axis=-1, keepdims=True)), 1e-12)
    k = k * (1.0 + (a - 1.0) * k_a.astype(f32))
    r4, w4, k4, v4, a4 = (z.reshape(B, S, H, N) for z in (r, decay, k, v, a))
    aa = -kk
    bb = kk * a4

    def step(state, inp):
        r_t, w_t, k_t, v_t, a_t, b_t = inp
        sa = jnp.einsum('bhij,bhj->bhi', state, a_t)
        state = (state * w_t[:, :, None, :] + sa[..., None] * b_t[:, :, None, :]
                 + v_t[..., None] * k_t[:, :, None, :])
        return state, jnp.einsum('bhij,bhj->bhi', state, r_t)

    xs = tuple(z.transpose(1, 0, 2, 3) for z in (r4, w4, k4, v4, aa, bb))
    state0 = jnp.zeros((B, H, N, N), f32)
    _, y = lax.scan(step, state0, xs)
    y = y.transpose(1, 0, 2, 3)
    mean = jnp.mean(y, axis=-1, keepdims=True)
    var = jnp.mean(jnp.square(y - mean), axis=-1, keepdims=True)
    y = ((y - mean) * lax.rsqrt(var + RWKV_GN_EPS)).reshape(B, S, W) * ln_w.astype(f32) + ln_b.astype(f32)
    bonus = jnp.sum(r4 * k4 * r_k.astype(f32), axis=-1, keepdims=True) * v4
    return y + bonus.reshape(B, S, W)


def setup_inputs(seed: int = 0) -> dict:
    key = jax.random.key(seed)
    ks = jax.random.split(key, 24)
    f32 = jnp.float32
    L = DEPTH
    HD = NSA_HEAD_DIM

    def nrm(k, shape, scale):
        return jax.random.normal(k, shape, f32) * scale

    return {
        'x': nrm(ks[0], (BATCH, SEQ, D_MODEL), 1.0),
        'pre_norm_g': 1.0 + nrm(ks[1], (L, D_MODEL), 0.05),
        'w_in': nrm(ks[2], (L, D_MODEL, IN_COLS), D_MODEL ** -0.5),
        'rel_bias_table': nrm(ks[3], (NUM_BUCKETS, NSA_HEADS), 0.5),
        'cmp_pos_k': nrm(ks[4], (L, CMP_BLOCK, HD), 0.1),
        'cmp_pos_v': nrm(ks[5], (L, CMP_BLOCK, HD), 0.1),
        'cmp_k_w1': nrm(ks[6], (L, CMP_BLOCK * HD, HD), (CMP_BLOCK * HD) ** -0.5),
        'cmp_k_w2': nrm(ks[7], (L, HD, HD), HD ** -0.5),
        'cmp_v_w1': nrm(ks[8], (L, CMP_BLOCK * HD, HD), (CMP_BLOCK * HD) ** -0.5),
        'cmp_v_w2': nrm(ks[9], (L, HD, HD), HD ** -0.5),
        'rwkv_mu': jax.random.uniform(ks[10], (L, SHIFT_COLS), f32),
        'rwkv_w0': jax.random.uniform(ks[11], (L, RWKV_WIDTH), f32, -6.0, -1.0),
        'rwkv_w2': nrm(ks[12], (L, DECAY_LORA, RWKV_WIDTH), 0.1),
        'rwkv_a0': nrm(ks[13], (L, RWKV_WIDTH), 0.1),
        'rwkv_a2': nrm(ks[14], (L, AAA_LORA, RWKV_WIDTH), 0.1),
        'rwkv_k_k': 0.85 + nrm(ks[15], (L, RWKV_WIDTH), 0.05),
        'rwkv_k_a': 1.0 + nrm(ks[16], (L, RWKV_WIDTH), 0.05),
        'rwkv_r_k': nrm(ks[17], (L, RWKV_HEADS, RWKV_HEAD_DIM), 0.1),
        'rwkv_ln_w': 1.0 + nrm(ks[18], (L, RWKV_WIDTH), 0.05),
        'rwkv_ln_b': nrm(ks[19], (L, RWKV_WIDTH), 0.02),
        'w_out': nrm(ks[20], (L, D_MIX, D_MODEL), D_MIX ** -0.5),
        'post_norm_g': 1.0 + nrm(ks[21], (L, D_MODEL), 0.05),
    }


def reference(x, pre_norm_g, w_in, rel_bias_table, cmp_pos_k, cmp_pos_v, cmp_k_w1, cmp_k_w2,
              cmp_v_w1, cmp_v_w2, rwkv_mu, rwkv_w0, rwkv_w2, rwkv_a0, rwkv_a2, rwkv_k_k, rwkv_k_a,
              rwkv_r_k, rwkv_ln_w, rwkv_ln_b, w_out, post_norm_g):
    o1 = COL_Q
    o2 = o1 + COL_KV
    o3 = o2 + COL_GATE
    o4 = o3 + COL_ZA
    o5 = o4 + SHIFT_COLS
    h = x
    for l in range(DEPTH):
        hn = rms_norm(h, pre_norm_g[l])
        proj = hn @ w_in[l]
        q, kv6, gates, z_a, feat_b, z_b = jnp.split(proj, [o1, o2, o3, o4, o5], axis=-1)
        k_cmp, v_cmp, k_slc, v_slc, k_win, v_win = jnp.split(kv6, 6, axis=-1)
        o_a = nsa_mixer(q, k_cmp, v_cmp, k_slc, v_slc, k_win, v_win, gates, rel_bias_table,
                        cmp_pos_k[l], cmp_pos_v[l], cmp_k_w1[l], cmp_k_w2[l], cmp_v_w1[l], cmp_v_w2[l])
        o_b = rwkv7_mixer(feat_b, rwkv_mu[l], rwkv_w0[l], rwkv_w2[l], rwkv_a0[l], rwkv_a2[l],
                          rwkv_k_k[l], rwkv_k_a[l], rwkv_r_k[l], rwkv_ln_w[l], rwkv_ln_b[l])
        mix = jnp.concatenate([o_a * jax.nn.silu(z_a.astype(jnp.float32)),
                               o_b * jax.nn.silu(z_b.astype(jnp.float32))], axis=-1).astype(h.dtype)
        y = mix @ w_out[l]
        h = h + rms_norm(y, post_norm_g[l])
    return h
```

```python
import math
from contextlib import ExitStack, contextmanager
import numpy as np
import concourse.bass as bass
import concourse.mybir as mybir
from concourse.bass_utils import run_bass_kernel_spmd

F32 = mybir.dt.float32
BF16 = mybir.dt.bfloat16
ALU = mybir.AluOpType
AF = mybir.ActivationFunctionType
AX = mybir.AxisListType

ENGS = ("pe", "act", "dve", "pool", "sp")
N_DMA_SEMS = 12
S = 2048
D = 2048
NCH = 16
BIG = 1.0e30
C0 = math.exp(-0.5)
USE_F32R = True
RW_STAGGER = 0


class Reg:
    __slots__ = ("lw", "rd")

    def __init__(self):
        self.lw = None
        self.rd = {}


class Prog:
    def __init__(self, nc):
        self.nc = nc
        self.q = {e: [] for e in ENGS}
        self.cnt = {e: 0 for e in ENGS}
        self.pending = {e: False for e in ENGS}
        self.waited = {e: {} for e in ENGS}
        self.dma_k = {e: 0 for e in ENGS}

    def op(self, eng, fn, r=(), w=(), signal=True):
        deps = {}

        def need(key, val, kind):
            if key == eng:
                if eng == "pe":
                    return
            if deps.get(key, 0) < val:
                deps[key] = val

        for reg in r:
            if reg.lw is not None:
                need(reg.lw[0], reg.lw[1], "raw")
        for reg in w:
            if reg.lw is not None:
                need(reg.lw[0], reg.lw[1], "waw")
            for k, v in reg.rd.items():
                need(k, v, "war")
        waits = []
        wd = self.waited[eng]
        for k, v in deps.items():
            if wd.get(k, 0) < v:
                wd[k] = v
                waits.append((k, v))
        n = self.cnt[eng] + 1
        if signal:
            self.cnt[eng] = n
            self.pending[eng] = False
        else:
            self.pending[eng] = True
        self.q[eng].append((waits, fn, (eng, 1) if signal else None))
        for reg in r:
            reg.rd[eng] = n
        for reg in w:
            reg.lw = (eng, n)
            reg.rd = {}

    def dma(self, qe, fn, r=(), w=()):
        deps = {}
        for reg in r:
            if reg.lw is not None:
                k, v = reg.lw
                deps[k] = max(deps.get(k, 0), v)
        for reg in w:
            if reg.lw is not None:
                k, v = reg.lw
                deps[k] = max(deps.get(k, 0), v)
            for k, v in reg.rd.items():
                deps[k] = max(deps.get(k, 0), v)
        kk = self.dma_k[qe]
        self.dma_k[qe] = kk + 1
        slot = kk % N_DMA_SEMS
        key = "dma_%s_%d" % (qe, slot)
        prev = 16 * (kk // N_DMA_SEMS)
        if prev > 0:
            deps[key] = max(deps.get(key, 0), prev)
        waits = []
        wd = self.waited[qe]
        for k, v in deps.items():
            if wd.get(k, 0) < v:
                wd[k] = v
                waits.append((k, v))
        tgt = prev + 16
        self.q[qe].append((waits, fn, (key, 16)))
        for reg in r:
            reg.rd[key] = max(reg.rd.get(key, 0), tgt)
        for reg in w:
            reg.lw = (key, tgt)
            reg.rd = {}

    def barrier(self):
        deps = {e: self.cnt[e] for e in ENGS if self.cnt[e] > 0}
        for qe in ENGS:
            kk = self.dma_k[qe]
            for slot in range(min(N_DMA_SEMS, kk)):
                uses = (kk - slot + N_DMA_SEMS - 1) // N_DMA_SEMS
                deps["dma_%s_%d" % (qe, slot)] = 16 * uses
        for e in ENGS:
            assert not self.pending[e]
            waits = []
            for k, v in deps.items():
                if k != e and self.waited[e].get(k, 0) < v:
                    self.waited[e][k] = v
                    waits.append((k, v))
            if waits:
                self.q[e].append((waits, None, None))

    def final_wait(self, eng, regs):
        deps = {}
        for reg in regs:
            if reg.lw is not None:
                k, v = reg.lw
                deps[k] = max(deps.get(k, 0), v)
        self.q[eng].append((list(deps.items()), None, None))

    def emit(self):
        nc = self.nc
        with ExitStack() as st:
            sems = {}
            for e in ENGS:
                sems[e] = st.enter_context(nc.semaphore("s_" + e))
            for qe in ENGS:
                for i in range(min(N_DMA_SEMS, self.dma_k[qe])):
                    key = "dma_%s_%d" % (qe, i)
                    sems[key] = st.enter_context(nc.semaphore(key))
            block = st.enter_context(nc.Block())
            for e in ENGS:
                assert not self.pending[e], e

            def run(engname):
                def body(eng):
                    for waits, fn, inc in self.q[engname]:
                        for k, v in waits:
                            eng.wait_ge(sems[k], v)
                        if fn is not None:
                            ins = fn(eng)
                            if inc is not None:
                                ins.then_inc(sems[inc[0]], inc[1])
                return body

            block.tensor(run("pe"))
            block.scalar(run("act"))
            block.vector(run("dve"))
            block.gpsimd(run("pool"))
            block.sync(run("sp"))


def _bucket(n):
    n = np.maximum(n, 0)
    nf = np.maximum(n, 1).astype(np.float32)
    large = 16 + (np.log(nf / np.float32(16)) / np.float32(math.log(64)) * np.float32(16)).astype(np.int32)
    large = np.minimum(large, 31)
    return np.where(n < 16, n, large)


def _consts():
    c = {}
    n = np.arange(2048)
    oh = np.zeros((32, 4096), np.float32)
    oh[_bucket(n), n] = 1.0
    c["c_oh"] = oh
    c["c_ident"] = np.eye(128, dtype=np.float32)
    t = np.arange(S)
    cur = t // 64
    j = np.arange(32)
    forced = (j[None, :] == 0) | (j[None, :] == cur[:, None]) | (j[None, :] == cur[:, None] - 1)
    causal = j[None, :] <= cur[:, None]
    m1 = (causal & ~forced).astype(np.float32)
    add = np.where(forced, BIG, np.where(causal, 0.0, -BIG)).astype(np.float32)
    c["c_m1"] = np.ascontiguousarray(m1.reshape(16, 128, 32).transpose(1, 0, 2))
    c["c_add"] = np.ascontiguousarray(add.reshape(16, 128, 32).transpose(1, 0, 2))
    e2 = (np.arange(S)[None, :] // 64 == j[:, None]).astype(np.float32)
    c["c_e2"] = e2
    cs = np.arange(127) * 16
    ss = np.arange(32) * 64
    ov = ((cs[:, None] < ss[None, :] + 64) & (cs[:, None] + 32 > ss[None, :])).astype(np.float32)
    c["c_ov"] = ov
    tri_s = np.triu(np.ones((64, 64), np.float32), 1)
    tri_i = np.triu(np.ones((64, 64), np.float32), 0)
    ts2 = np.triu(np.ones((128, 128), np.float32), 1)
    ti2 = np.triu(np.ones((128, 128), np.float32), 0)
    c["c_mask_sc2"] = np.concatenate([ts2, ti2, ts2, ti2], 1)
    c["c_mask_t8"] = np.ascontiguousarray(np.tile(ts2.T, (1, 4)))
    c["c_ident8"] = np.ascontiguousarray(np.tile(np.eye(128, dtype=np.float32), (1, 4)))
    cm = np.ones((64, S), np.float32)
    cm[:, ::128] = 0.0
    c["c_cmask"] = cm
    return c


W_NSA_FM = 1024
W_NSA_TM = 780


def _layout_inputs(inp):
    L = {}
    w_in = inp["w_in"][0]
    o_kv = 1024
    o_g = 2560
    o_za = 2584
    o_f = 3608
    o_zb = 6808
    for g in range(2):
        cols = list(range(g * 512, g * 512 + 512))
        cols += list(range(o_kv + 0 * 256 + g * 128, o_kv + 0 * 256 + g * 128 + 128))
        cols += list(range(o_kv + 1 * 256 + g * 128, o_kv + 1 * 256 + g * 128 + 128))
        cols += list(range(o_kv + 2 * 256 + g * 128, o_kv + 2 * 256 + g * 128 + 128))
        cols += list(range(o_kv + 4 * 256 + g * 128, o_kv + 4 * 256 + g * 128 + 128))
        cols += list(range(o_kv + 3 * 256 + g * 128, o_kv + 3 * 256 + g * 128 + 128))
        cols += list(range(o_kv + 5 * 256 + g * 128, o_kv + 5 * 256 + g * 128 + 128))
        for br in range(3):
            cols += [o_g + br * 8 + g * 4 + h for h in range(4)]
        cols += list(range(o_za + g * 512, o_za + g * 512 + 512))
        L["w_nsa%d" % g] = np.ascontiguousarray(w_in[:, cols])
    L["w_rkv"] = np.ascontiguousarray(w_in[:, o_f:o_f + 3072])
    L["w_lora"] = np.ascontiguousarray(w_in[:, o_f + 3072:o_f + 3200])
    L["w_zb"] = np.ascontiguousarray(w_in[:, o_zb:o_zb + 1024])
    L["g_pre"] = np.ascontiguousarray(inp["pre_norm_g"][0].reshape(16, 128).T)
    L["g_post"] = np.ascontiguousarray(np.tile(inp["post_norm_g"][0][None, :], (128, 1)))
    L["tab"] = np.ascontiguousarray(inp["rel_bias_table"])
    for nm in ("k", "v"):
        L["pos%sT" % nm] = np.ascontiguousarray(inp["cmp_pos_%s" % nm][0].T)
        L["w1%s" % nm] = np.ascontiguousarray(inp["cmp_%s_w1" % nm][0].reshape(32, 128, 128).transpose(1, 0, 2))
        L["w2%s" % nm] = np.ascontiguousarray(inp["cmp_%s_w2" % nm][0])
    mu = inp["rwkv_mu"][0]
    per = np.zeros((64, 16, 8), np.float32)
    for h in range(16):
        sl = slice(h * 64, h * 64 + 64)
        per[:, h, 0] = mu[0:1024][sl]
        per[:, h, 1] = mu[1024:2048][sl]
        per[:, h, 2] = mu[2048:3072][sl]
        per[:, h, 3] = inp["rwkv_w0"][0][sl]
        per[:, h, 4] = inp["rwkv_a0"][0][sl]
        per[:, h, 5] = inp["rwkv_k_k"][0][sl]
        per[:, h, 6] = inp["rwkv_k_a"][0][sl]
        per[:, h, 7] = inp["rwkv_r_k"][0][h]
    L["rw_per"] = per
    L["mu_lora"] = np.ascontiguousarray(mu[3072:3200].reshape(2, 64).T)
    L["rw_w2"] = np.ascontiguousarray(inp["rwkv_w2"][0])
    L["rw_a2"] = np.ascontiguousarray(inp["rwkv_a2"][0])
    L["ln_w"] = np.ascontiguousarray(np.tile(inp["rwkv_ln_w"][0][None, :], (128, 1)))
    L["ln_b"] = np.ascontiguousarray(np.tile(inp["rwkv_ln_b"][0][None, :], (128, 1)))
    L["w_out"] = np.ascontiguousarray(inp["w_out"][0])
    return L


_IN_SHAPES = None


class B:
    def __init__(self, in_shapes, dbg=()):
        self.dbg = dbg
        nc = self.nc = bass.Bass("TRN2", target_bir_lowering=False)
        self.P = Prog(nc)
        self.din = {}
        for k, shp in in_shapes.items():
            self.din[k] = nc.dram_tensor(k, list(shp), F32, kind="ExternalInput")
        self.out = nc.dram_tensor("out", [S, D], F32, kind="ExternalOutput")
        self.mixd = nc.dram_tensor("mixd", [S, 2048], BF16)
        self.Z = [nc.dram_tensor("Zs%d" % h, [132, 4096], BF16) for h in range(8)]
        self.Zw = [nc.dram_tensor("Zw%d" % h, [132, 4096], BF16) for h in range(8)]
        self.r_Z = [Reg() for _ in range(8)]
        self.r_Zw = [Reg() for _ in range(8)]
        self.r_Z2 = [None] * 8
        self.r_Zw2 = [None] * 8
        self.r_mixd = [Reg() for _ in range(16)]
        self.out_regs = []
        self.rw_store_regs = []
        self.dbg_out = {}
        self.es = ExitStack()

    @contextmanager
    def scope(self):
        with ExitStack() as st:
            yield st
        self.P.barrier()

    def sb(self, st, name, shape, dt=F32):
        self.uid = getattr(self, "uid", 0) + 1
        return st.enter_context(self.nc.sbuf_tensor("s%d_%s" % (self.uid, name), list(shape), dt))

    def ps(self, st, name, shape, dt=F32):
        self.uid = getattr(self, "uid", 0) + 1
        return st.enter_context(self.nc.psum_tensor("p%d_%s" % (self.uid, name), list(shape), dt))

    def dbg_dump(self, name, ap_src, shape, reg, dt=F32):
        if name not in self.dbg:
            return
        t = self.nc.dram_tensor("dbg_" + name, list(shape), dt, kind="ExternalOutput")
        self.dbg_out[name] = t
        ro = Reg()
        self.out_regs.append(ro)
        self.P.dma("sp", lambda e: e.dma_start(out=t.ap(), in_=ap_src), r=[reg] if not isinstance(reg, list) else reg, w=[ro])

    def build(self):
        nc, P = self.nc, self.P
        with self.scope() as st:
            self.ident = self.sb(st, "ident", [128, 128]); self.r_ident = Reg()
            self.identb = self.sb(st, "identb", [128, 128], BF16); self.r_identb = Reg()
            self.ones = self.sb(st, "ones", [128, 128]); self.r_ones = Reg()
            self.xT = self.sb(st, "xT", [128, NCH, S], BF16)
            self.r_xT = [Reg() for _ in range(16)]
            self.rstd_col = self.sb(st, "rstd_col", [128, 16]); self.r_rc = Reg()
            self.rstd_bc = self.sb(st, "rstd_bc", [128, S]); self.r_rb = Reg()
            self.gpre = self.sb(st, "gpre", [128, 16]); self.r_gpre = Reg()
            self.pm = [self.ps(st, "pm%d" % i, [128, 512]) for i in range(6)]
            self.r_pm = [Reg() for _ in range(6)]
            self.ptr = self.ps(st, "ptr", [128, 4, 128], BF16); self.r_ptr = Reg()
            self.px = self.ps(st, "px", [128, 512]); self.r_px = Reg()
            self.pm_i = 0

            P.dma("sp", lambda e: e.dma_start(out=self.ident[:], in_=self.din["c_ident"].ap()), w=[self.r_ident])
            P.op("dve", lambda e: e.tensor_copy(self.identb[:], self.ident[:]), r=[self.r_ident], w=[self.r_identb])
            P.op("pool", lambda e: e.memset(self.ones[:], 1.0), w=[self.r_ones])
            P.dma("sp", lambda e: e.dma_start(out=self.gpre[:], in_=self.din["g_pre"].ap()), w=[self.r_gpre])

            self.phase0(st)
            self.phase_eb()
            for g in range(2):
                with self.scope() as st2:
                    self.phase_nsa(st2, g)
            with self.scope() as st2:
                self.phase_rwkv_proj(st2)
        with self.scope() as st:
            self.pm = [self.ps(st, "rm%d" % i, [128, 512]) for i in range(6)]
            self.r_pm = [Reg() for _ in range(6)]
            self.px = self.ps(st, "rpx", [128, 512]); self.r_px = Reg()
            self.py2 = self.ps(st, "rpy2", [128, 512]); self.r_py2 = Reg()
            self.phase_rwkv(st)
        with self.scope() as st:
            self.pm = [self.ps(st, "qm%d" % i, [128, 512]) for i in range(6)]
            self.r_pm = [Reg() for _ in range(6)]
            self.ptr = self.ps(st, "qtr", [128, 4, 128], BF16); self.r_ptr = Reg()
            self.phase_out(st)
            P.final_wait("sp", self.out_regs)
            P.emit()
        return nc

    def next_pm(self):
        i = self.pm_i
        self.pm_i = (i + 1) % len(self.pm)
        return self.pm[i], self.r_pm[i]

    def phase0(self, st0):
        nc, P = self.nc, self.P
        x = self.din["x"].ap()
        with self.scope() as st:
            xt = [self.sb(st, "xt%d" % i, [128, D]) for i in range(2)]
            r_xt = [Reg(), Reg()]
            xb = [self.sb(st, "xb%d" % i, [128, D], BF16) for i in range(2)]
            r_xb = [Reg(), Reg()]
            junk = self.sb(st, "junk", [128, D]); r_junk = Reg()
            ss = self.sb(st, "ss", [128, 16]); r_ss = Reg()
            dg = self.sb(st, "dg", [128, 128]); r_dg = Reg()
            for tt in range(16):
                b = tt % 2
                P.dma("sp", lambda e, b=b, tt=tt: e.dma_start(out=xt[b][:], in_=x[tt * 128:(tt + 1) * 128, :]), w=[r_xt[b]])
                P.op("act", lambda e, b=b, tt=tt: e.activation(out=junk[:], in_=xt[b][:], func=AF.Square), r=[r_xt[b]], w=[r_junk])
                P.op("dve", lambda e, tt=tt: e.reduce_sum(out=ss[:, tt:tt + 1], in_=junk[:], axis=AX.X), r=[r_junk], w=[r_ss])
                P.op("pool", lambda e, b=b: e.tensor_copy(xb[b][:], xt[b][:]), r=[r_xt[b]], w=[r_xb[b]])
                for gq in range(4):
                    for jq in range(4):
                        c = gq * 4 + jq
                        P.op("pe", lambda e, b=b, c=c, jq=jq: e.transpose(self.ptr[:, jq, :], xb[b][:, c * 128:(c + 1) * 128], self.identb[:]),
                             r=[r_xb[b], self.r_identb], w=[self.r_ptr], signal=(jq == 3))
                    for jq in range(4):
                        c = gq * 4 + jq
                        P.op("dve", lambda e, c=c, jq=jq, tt=tt: e.tensor_scalar(
                            out=self.xT[:, c, tt * 128:(tt + 1) * 128], in0=self.ptr[:, jq, :],
                            scalar1=self.gpre[:, c:c + 1], scalar2=None, op0=ALU.mult),
                            r=[self.r_ptr, self.r_gpre], w=[self.r_xT[tt]])
            P.op("dve", lambda e: e.tensor_scalar(out=ss[:], in0=ss[:], scalar1=1.0 / D, scalar2=1e-6, op0=ALU.mult, op1=ALU.add),
                 r=[r_ss], w=[r_ss])
            P.op("act", lambda e: e.activation(out=ss[:], in_=ss[:], func=AF.Sqrt), r=[r_ss], w=[r_ss])
            P.op("dve", lambda e: e.reciprocal(self.rstd_col[:], ss[:]), r=[r_ss], w=[self.r_rc])
            for tt in range(16):
                P.op("dve", lambda e, tt=tt: e.tensor_scalar(out=dg[:], in0=self.ident[:], scalar1=self.rstd_col[:, tt:tt + 1],
                                                            scalar2=None, op0=ALU.mult), r=[self.r_ident, self.r_rc], w=[r_dg])
                P.op("pe", lambda e: e.matmul(self.px[:, 0:128], lhsT=self.ones[:], rhs=dg[:], start=True, stop=True),
                     r=[self.r_ones, r_dg], w=[self.r_px])
                P.op("act", lambda e, tt=tt: e.activation(out=self.rstd_bc[:, tt * 128:(tt + 1) * 128], in_=self.px[:, 0:128], func=AF.Copy),
                     r=[self.r_px], w=[self.r_rb])

        self.dbg_dump("rstd_col", self.rstd_col[:], [128, 16], self.r_rc)
        self.dbg_dump("rstd_bc", self.rstd_bc[:], [128, S], self.r_rb)
        self.dbg_dump("xT0", self.xT[:, 0, :], [128, S], self.r_xT, BF16)

    def load_w(self, wb, r_wb, src_ap):
        self.P.dma("pool", lambda e: e.dma_start(out=wb, in_=src_ap.rearrange("(c p) n -> p c n", p=128)), w=[r_wb])

    def proj_fm(self, wb, r_wb, j0, ncols, dst_fn, r_dst, scale=None, evac="dve"):
        P = self.P
        for tb in range(4):
            pm, r_pm = self.next_pm()
            for c in range(NCH):
                P.op("pe", lambda e, c=c, tb=tb, pm=pm: e.matmul(pm[0:ncols, :], lhsT=wb[:, c, j0:j0 + ncols],
                                                                 rhs=self.xT[:, c, tb * 512:(tb + 1) * 512],
                                                                 start=(c == 0), stop=(c == NCH - 1)),
                     r=[r_wb] + self.r_xT[tb * 4:tb * 4 + 4], w=[r_pm], signal=(c == NCH - 1))
            if scale is None:
                P.op("dve", lambda e, tb=tb, pm=pm: e.tensor_tensor(out=dst_fn(tb), in0=pm[0:ncols, :],
                                                                   in1=self.rstd_bc[0:ncols, tb * 512:(tb + 1) * 512], op=ALU.mult),
                     r=[r_pm, self.r_rb], w=[r_dst])
            else:
                P.op("dve", lambda e, tb=tb, pm=pm: e.scalar_tensor_tensor(out=dst_fn(tb), in0=pm[0:ncols, :], scalar=scale,
                                                                          in1=self.rstd_bc[0:ncols, tb * 512:(tb + 1) * 512],
                                                                          op0=ALU.mult, op1=ALU.mult),
                     r=[r_pm, self.r_rb], w=[r_dst])

    def proj_tm(self, wb, r_wb, j0, ncols, dst_fn, r_dst, func=AF.Copy):
        P = self.P
        for tt in range(16):
            pm, r_pm = self.next_pm()
            for c in range(NCH):
                P.op("pe", lambda e, c=c, tt=tt, pm=pm: e.matmul(pm[:, 0:ncols], lhsT=self.xT[:, c, tt * 128:(tt + 1) * 128],
                                                                 rhs=wb[:, c, j0:j0 + ncols], start=(c == 0), stop=(c == NCH - 1)),
                     r=[r_wb, self.r_xT[tt]], w=[r_pm], signal=(c == NCH - 1))
            P.op("act", lambda e, tt=tt, pm=pm: e.activation(out=dst_fn(tt), in_=pm[:, 0:ncols], func=func,
                                                            scale=self.rstd_col[:, tt:tt + 1]),
                 r=[r_pm, self.r_rc], w=[r_dst])

    def phase_eb(self):
        nc, P = self.nc, self.P
        with self.scope() as st:
            oh = self.sb(st, "oh", [32, 4096]); r_oh = Reg()
            tab = self.sb(st, "tab", [32, 8]); r_tab = Reg()
            zrow = [self.sb(st, "zrow%d" % i, [128, 4096], BF16) for i in range(2)]
            r_zrow = [Reg(), Reg()]
            P.dma("sp", lambda e: e.dma_start(out=oh[:], in_=self.din["c_oh"].ap()), w=[r_oh])
            P.dma("sp", lambda e: e.dma_start(out=tab[:], in_=self.din["tab"].ap()), w=[r_tab])
            tabrep = self.sb(st, "tabrep", [32, 8, 128]); r_tabrep = Reg()
            for h in range(8):
                P.op("dve", lambda e, h=h: e.tensor_scalar(out=tabrep[:, h, :], in0=self.ones[0:32, :], scalar1=tab[:, h:h + 1], scalar2=None,
                                                          op0=ALU.mult), r=[self.r_ones, r_tab], w=[r_tabrep])
            k = 0
            for h in range(8):
                for win in range(2):
                    zb = zrow[k % 2]; r_zb = r_zrow[k % 2]
                    k += 1
                    nblk = 1 if win else 4
                    if True:
                        P.op("pool", lambda e, zb=zb: e.memset(zb[:, 512 * nblk:], 0.0), w=[r_zb])
                    for blk in range(nblk):
                        pm, r_pm = self.next_pm()
                        P.op("pe", lambda e, h=h, blk=blk, pm=pm: e.matmul(pm[:, :], lhsT=tabrep[:, h, :],
                                                                          rhs=oh[:, blk * 512:(blk + 1) * 512], start=True, stop=True),
                             r=[r_tabrep, r_oh], w=[r_pm])
                        P.op("act", lambda e, blk=blk, pm=pm, zb=zb: e.activation(out=zb[:, blk * 512:(blk + 1) * 512], in_=pm[:, :], func=AF.Exp),
                             r=[r_pm], w=[r_zb])
                    dst = (self.Zw if win else self.Z)[h]
                    r_dst = (self.r_Zw if win else self.r_Z)[h]
                    P.dma("sp", lambda e, dst=dst, zb=zb: e.dma_start(out=dst.ap()[0:128, :], in_=zb[:]), r=[r_zb], w=[r_dst])
                    r_dst2 = Reg()
                    P.dma("sp", lambda e, dst=dst, zb=zb: e.dma_start(out=dst.ap()[128:132, :], in_=zb[0:4, :]), r=[r_zb], w=[r_dst2])
                    (self.r_Zw2 if win else self.r_Z2)[h] = r_dst2

    def toep(self, dst_ap, r_dst, Zt, r_Z, c, pstep, nparts, nfree, r_Z2=None):
        src = bass.AP(Zt, c % 4096, [[pstep, nparts], [1, nfree]])
        self.P.dma("pool", lambda e: e.dma_start(out=dst_ap, in_=src), r=[r_Z] + ([r_Z2] if r_Z2 is not None else []), w=[r_dst])

    def phase_nsa(self, st, g):
        nc, P = self.nc, self.P
        wsrc = self.din["w_nsa%d" % g].ap()
        qT = self.sb(st, "qT", [128, 4, S], BF16); r_qT = Reg()
        kT = self.sb(st, "kT", [128, 4, S], BF16); r_kT = [Reg() for _ in range(4)]
        vs = self.sb(st, "vs", [128, 16, 132], BF16); r_vs = Reg()
        vw = self.sb(st, "vw", [128, 16, 132], BF16); r_vw = Reg()
        gt = self.sb(st, "gt", [128, 16, 12]); r_gt = Reg()
        oacc = self.sb(st, "oacc", [128, 16, 512]); r_oacc = [Reg() for _ in range(16)]
        imp = self.sb(st, "imp", [128, 16, 32]); r_imp = [Reg() for _ in range(16)]
        negT = self.sb(st, "negT", [128, S], BF16); r_negT = Reg()
        e2c = self.sb(st, "e2c", [128, S], BF16); r_e2c = Reg()
        kcT = self.sb(st, "kcT", [128, 128], BF16); r_kcT = Reg()
        vce = self.sb(st, "vce", [128, 164], BF16); r_vce = Reg()
        P.op("pool", lambda e: e.memset(negT[:], 0.0), w=[r_negT])
        P.op("pool", lambda e: e.memset(e2c[:], 0.0), w=[r_e2c])

        with self.scope() as stw:
            wb = [self.sb(stw, "wbn%d" % i, [128, NCH, 512], BF16) for i in range(2)]
            r_wb = [Reg(), Reg()]
            self.load_w(wb[0][:], r_wb[0], wsrc[:, 0:512])
            self.load_w(wb[1][:], r_wb[1], wsrc[:, 512:1024])
            for h in range(4):
                self.proj_fm(wb[0], r_wb[0], h * 128, 128, lambda tb, h=h: qT[:, h, tb * 512:(tb + 1) * 512], r_qT, scale=128.0 ** -0.5)
            for i in range(4):
                self.proj_fm(wb[1], r_wb[1], i * 128, 128, lambda tb, i=i: kT[:, i, tb * 512:(tb + 1) * 512], r_kT[i])
            self.load_w(wb[0][:, :, 0:268], r_wb[0], wsrc[:, 1024:1292])
            P.op("pool", lambda e: e.memset(vs[:, :, 128:132], 1.0), w=[r_vs])
            P.op("pool", lambda e: e.memset(vw[:, :, 128:132], 1.0), w=[r_vw])
            self.proj_tm(wb[0], r_wb[0], 0, 128, lambda tt: vs[:, tt, 0:128], r_vs)
            self.proj_tm(wb[0], r_wb[0], 128, 128, lambda tt: vw[:, tt, 0:128], r_vw)
            self.proj_tm(wb[0], r_wb[0], 256, 12, lambda tt: gt[:, tt, :], r_gt, func=AF.Sigmoid)

        self.dbg_dump("qT%d" % g, qT[:, 0, :], [128, S], r_qT, BF16)
        self.dbg_dump("kT%d" % g, kT[:, 0, :], [128, S], r_kT[0], BF16)
        self.dbg_dump("vs%d" % g, vs[:, 0, :], [128, 132], r_vs, BF16)
        self.dbg_dump("gt%d" % g, gt[:, 0, :], [128, 12], r_gt)
        with self.scope() as st2:
            e2f = self.sb(st2, "e2f", [32, S]); r_e2f = Reg()
            P.dma("sp", lambda e: e.dma_start(out=e2f[:], in_=self.din["c_e2"].ap()), w=[r_e2f])
            P.op("dve", lambda e: e.tensor_copy(e2c[0:32, :], e2f[:]), r=[r_e2f], w=[r_e2c])

        with self.scope() as st2:
            w1 = self.sb(st2, "w1", [128, 32, 128], BF16); r_w1 = Reg()
            w2 = self.sb(st2, "w2", [128, 128], BF16); r_w2 = Reg()
            posT = self.sb(st2, "posT", [128, 32], BF16); r_posT = Reg()
            cb = self.sb(st2, "cb", [128, 1]); r_cb = Reg()
            h1s = self.sb(st2, "h1s", [128, 128], BF16); r_h1s = Reg()
            ovf = self.sb(st2, "ovf", [128, 33]); r_ovf = Reg()
            P.op("pool", lambda e: e.memset(ovf[:, 0:1], 1.0), w=[r_ovf])
            P.dma("sp", lambda e: e.dma_start(out=ovf[0:127, 1:33], in_=self.din["c_ov"].ap()), w=[r_ovf])
            P.op("dve", lambda e: e.tensor_copy(vce[0:127, 128:161], ovf[0:127, :]), r=[r_ovf], w=[r_vce])
            for which in range(2):
                nm = "kv"[which]
                P.dma("pool", lambda e, nm=nm: e.dma_start(out=w1[:], in_=self.din["w1" + nm].ap()), w=[r_w1])
                P.dma("pool", lambda e, nm=nm: e.dma_start(out=w2[:], in_=self.din["w2" + nm].ap()), w=[r_w2])
                P.dma("pool", lambda e, nm=nm: e.dma_start(out=posT[:], in_=self.din["pos%sT" % nm].ap()), w=[r_posT])
                pm, r_pm = self.next_pm()
                for l in range(32):
                    P.op("pe", lambda e, l=l, pm=pm: e.matmul(pm[:, 0:1], lhsT=w1[:, l, :], rhs=posT[:, l:l + 1], start=(l == 0), stop=(l == 31)),
                         r=[r_w1, r_posT], w=[r_pm], signal=(l == 31))
                P.op("dve", lambda e, pm=pm: e.tensor_copy(cb[:], pm[:, 0:1]), r=[r_pm], w=[r_cb])
                pm, r_pm = self.next_pm()
                for l in range(32):
                    P.op("pe", lambda e, l=l, pm=pm, which=which: e.matmul(pm[:, 0:127], lhsT=w1[:, l, :],
                                                                          rhs=kT[:, which, l:l + 16 * 126 + 1:16],
                                                                          start=(l == 0), stop=(l == 31)),
                         r=[r_w1, r_kT[which]], w=[r_pm], signal=(l == 31))
                P.op("act", lambda e, pm=pm: e.activation(out=h1s[:, 0:127], in_=pm[:, 0:127], func=AF.Silu, bias=cb[:, 0:1]),
                     r=[r_pm, r_cb], w=[r_h1s])
                pm, r_pm = self.next_pm()
                if which == 0:
                    P.op("pe", lambda e, pm=pm: e.matmul(pm[:, 0:127], lhsT=w2[:], rhs=h1s[:, 0:127], start=True, stop=True),
                         r=[r_w2, r_h1s], w=[r_pm])
                    P.op("dve", lambda e, pm=pm: e.tensor_copy(kcT[:, 0:127], pm[:, 0:127]), r=[r_pm], w=[r_kcT])
                else:
                    P.op("pe", lambda e, pm=pm: e.matmul(pm[0:127, 0:128], lhsT=h1s[:, 0:127], rhs=w2[:], start=True, stop=True),
                         r=[r_w2, r_h1s], w=[r_pm])
                    P.op("dve", lambda e, pm=pm: e.tensor_copy(vce[0:127, 0:128], pm[0:127, 0:128]), r=[r_pm], w=[r_vce])

        with self.scope() as sta:
            e1 = [self.sb(sta, "e1_%d" % i, [128, 512], BF16) for i in range(2)]
            r_e1 = [Reg(), Reg()]
            e2 = self.sb(sta, "e2", [128, 16, 512], BF16); r_e2 = [Reg() for _ in range(16)]
            sm = self.sb(sta, "sm", [128, 4, 4]); r_sm = [Reg() for _ in range(4)]
            EBc = self.sb(sta, "EBc", [128, S], BF16); r_EBc = Reg()
            EBs = self.sb(sta, "EBs", [128, 16, 512], BF16); r_EBs = Reg()
            EBw = self.sb(sta, "EBw", [128, 8, 512], BF16); r_EBw = Reg()
            ei = 0

            def gate_col(br, h):
                return br * 4 + h

            for h in range(4):
                hd = g * 4 + h
                self.toep(EBc[0:127, :], r_EBc, self.Z[hd], self.r_Z[hd], 4096 - 31, 4096 - 16, 127, S, self.r_Z2[hd])
                for Q in range(4):
                    pm, r_pm = self.next_pm()
                    P.op("pe", lambda e, pm=pm, h=h, Q=Q: e.matmul(pm[0:127, :], lhsT=kcT[:, 0:127], rhs=qT[:, h, Q * 512:(Q + 1) * 512],
                                                                  start=True, stop=True), r=[r_kcT, r_qT], w=[r_pm])
                    eb = e1[ei % 2]; r_eb = r_e1[ei % 2]; ei += 1
                    P.op("act", lambda e, pm=pm, eb=eb: e.activation(out=eb[0:127, :], in_=pm[0:127, :], func=AF.Exp), r=[r_pm], w=[r_eb])
                    P.op("dve", lambda e, eb=eb, Q=Q: e.tensor_tensor(out=e2[0:127, 0, :], in0=eb[0:127, :], in1=EBc[0:127, Q * 512:(Q + 1) * 512],
                                                                     op=ALU.mult), r=[r_eb, r_EBc], w=[r_e2[0]])
                    pms = []
                    for sq in range(4):
                        pm, r_pm = self.next_pm()
                        pms.append((pm, r_pm))
                        P.op("pe", lambda e, pm=pm, sq=sq: e.matmul(pm[:, 0:161], lhsT=e2[0:127, 0, sq * 128:(sq + 1) * 128], rhs=vce[0:127, 0:161],
                                                                   start=True, stop=True), r=[r_e2[0], r_vce], w=[r_pm])
                    for sq in range(4):
                        pm, r_pm = pms[sq]
                        P.op("dve", lambda e, pm=pm, sq=sq: e.tensor_scalar(out=sm[:, sq, 0:1], in0=pm[:, 128:129], scalar1=1e-30, scalar2=None, op0=ALU.max),
                             r=[r_pm], w=[r_sm[sq]])
                    P.op("dve", lambda e: e.reciprocal(sm[:, :, 1], sm[:, :, 0]), r=list(r_sm), w=list(r_sm))
                    P.op("dve", lambda e, h=h, Q=Q: e.tensor_tensor(out=sm[:, :, 2], in0=sm[:, :, 1], in1=gt[:, Q * 4:(Q + 1) * 4, gate_col(0, h)],
                                                                   op=ALU.mult), r=list(r_sm) + [r_gt], w=list(r_sm))
                    for sq in range(4):
                        T = Q * 4 + sq
                        pm, r_pm = pms[sq]
                        P.op("dve", lambda e, pm=pm, T=T, h=h, sq=sq: e.tensor_scalar(out=oacc[:, T, h * 128:(h + 1) * 128], in0=pm[:, 0:128],
                                                                                     scalar1=sm[:, sq, 2:3], scalar2=None, op0=ALU.mult),
                             r=[r_pm, r_sm[sq]], w=[r_oacc[T]])
                    for sq in range(4):
                        T = Q * 4 + sq
                        pm, r_pm = pms[sq]
                        if h == 0:
                            P.op("dve", lambda e, pm=pm, T=T, sq=sq: e.tensor_scalar(out=imp[:, T, :], in0=pm[:, 129:161], scalar1=sm[:, sq, 1:2], scalar2=None,
                                                                                    op0=ALU.mult), r=[r_pm, r_sm[sq]], w=[r_imp[T]])
                        else:
                            P.op("dve", lambda e, pm=pm, T=T, sq=sq: e.scalar_tensor_tensor(out=imp[:, T, :], in0=pm[:, 129:161], scalar=sm[:, sq, 1:2],
                                                                                           in1=imp[:, T, :], op0=ALU.mult, op1=ALU.add),
                                 r=[r_pm, r_sm[sq], r_imp[T]], w=[r_imp[T]])
            self.dbg_dump("imp%d" % g, imp[:], [128, 16, 32], r_imp[15])

            self.dbg_dump("kcT%d" % g, kcT[:], [128, 128], r_kcT, BF16)
            self.dbg_dump("vce%d" % g, vce[:], [128, 164], r_vce, BF16)
            self.dbg_dump("oaccc%d" % g, oacc[:, 0, :], [128, 512], r_oacc[0])
            with self.scope() as st2x:
                sc = imp
                r_sc = r_imp
                sc2 = self.sb(st2x, "sc2", [128, 16, 32]); r_sc2 = [Reg() for _ in range(16)]
                m8 = self.sb(st2x, "m8", [128, 16, 16]); r_m8 = [Reg() for _ in range(16)]
                P.dma("sp", lambda e: e.dma_start(out=sc2[:], in_=self.din["c_m1"].ap()), w=r_sc2)
                P.op("dve", lambda e: e.tensor_tensor(out=sc[:], in0=imp[:], in1=sc2[:], op=ALU.mult), r=list(r_imp) + list(r_sc2), w=list(r_sc))
                P.dma("sp", lambda e: e.dma_start(out=sc2[:], in_=self.din["c_add"].ap()), r=r_sc2, w=r_sc2)
                P.op("dve", lambda e: e.tensor_tensor(out=sc[:], in0=sc[:], in1=sc2[:], op=ALU.add), r=list(r_sc) + list(r_sc2), w=list(r_sc))
                for T in range(16):
                    P.op("dve", lambda e, T=T: e.max(out=m8[:, T, 0:8], in_=sc[:, T, :]), r=[r_sc[T]], w=[r_m8[T]])
                for T in range(16):
                    P.op("dve", lambda e, T=T: e.match_replace(out=sc2[:, T, :], in_to_replace=m8[:, T, 0:8], in_values=sc[:, T, :], imm_value=-3.0e38),
                         r=[r_sc[T], r_m8[T]], w=[r_sc2[T]])
                for T in range(16):
                    P.op("dve", lambda e, T=T: e.max(out=m8[:, T, 8:16], in_=sc2[:, T, :]), r=[r_sc2[T]], w=[r_m8[T]])
                P.op("dve", lambda e: e.tensor_tensor(out=sc2[:], in0=sc[:], in1=m8[:, :, 15:16].to_broadcast([128, 16, 32]), op=ALU.is_ge),
                     r=list(r_sc) + list(r_m8), w=list(r_sc2))
                P.op("dve", lambda e: e.tensor_scalar(out=sc2[:], in0=sc2[:], scalar1=-1.0, scalar2=30000.0, op0=ALU.add, op1=ALU.mult),
                     r=list(r_sc2), w=list(r_sc2))
                for T4 in range(4):
                    pm, r_pm = self.next_pm()
                    for j4 in range(4):
                        T = T4 * 4 + j4
                        P.op("pe", lambda e, pm=pm, T=T, j4=j4: e.transpose(pm[0:32, j4 * 128:(j4 + 1) * 128], sc2[:, T, :], self.ident[:]),
                             r=[r_sc2[T], self.r_ident], w=[r_pm], signal=(j4 == 3))
                    P.op("act", lambda e, pm=pm, T4=T4: e.activation(out=negT[0:32, T4 * 512:(T4 + 1) * 512], in_=pm[0:32, :], func=AF.Copy),
                         r=[r_pm], w=[r_negT])
            self.dbg_dump("negT%d" % g, negT[0:32, :], [32, S], r_negT, BF16)

            def load_eb(hh, which):
                hd_ = g * 4 + hh
                if which == 1:
                    src_s = bass.AP(self.Z[hd_], 4096 - 384, [[4095, 128], [128, 16], [1, 512]])
                    P.dma("pool", lambda e, src_s=src_s: e.dma_start(out=EBs[:], in_=src_s), r=[self.r_Z[hd_], self.r_Z2[hd_]], w=[r_EBs])
                else:
                    src_w = bass.AP(self.Zw[hd_], 4096 - 384, [[4095, 128], [128, 8], [1, 512]])
                    P.dma("pool", lambda e, src_w=src_w: e.dma_start(out=EBw[:], in_=src_w), r=[self.r_Zw[hd_], self.r_Zw2[hd_]], w=[r_EBw])

            load_eb(0, 1)
            load_eb(0, 2)
            for h in range(4):
                hd = g * 4 + h
                for br in (1, 2):
                    if br == 2 and h < 3:
                        load_eb(h + 1, 1)
                    for Q in range(4):
                        kt_lo = 0 if br == 1 else max(0, 4 * Q - 4)
                        kt_hi = 4 * Q + 3
                        kidx = 2 if br == 1 else 3
                        for kt in range(kt_lo, kt_hi + 1):
                            pm, r_pm = self.next_pm()
                            P.op("pe", lambda e, pm=pm, kt=kt, h=h, Q=Q, kidx=kidx, br=br: e.matmul(
                                pm[:, :], lhsT=kT[:, kidx, kt * 128:(kt + 1) * 128], rhs=qT[:, h, Q * 512:(Q + 1) * 512],
                                start=True, stop=(br == 2)), r=[r_kT[kidx], r_qT], w=[r_pm], signal=(br == 2))
                            if br == 1:
                                P.op("pe", lambda e, pm=pm, kt=kt, Q=Q: e.matmul(pm[:, :], lhsT=e2c[:, kt * 128:(kt + 1) * 128],
                                                                                rhs=negT[:, Q * 512:(Q + 1) * 512], start=False, stop=True),
                                     r=[r_e2c, r_negT], w=[r_pm])
                            eb = e1[ei % 2]; r_eb = r_e1[ei % 2]; ei += 1
                            P.op("act", lambda e, pm=pm, eb=eb: e.activation(out=eb[:], in_=pm[:, :], func=AF.Exp), r=[r_pm], w=[r_eb])
                            o = 4 * Q - kt + 3
                            EB = EBs if br == 1 else EBw
                            r_EB = r_EBs if br == 1 else r_EBw
                            P.op("dve", lambda e, eb=eb, kt=kt, o=o, EB=EB: e.tensor_tensor(out=e2[:, kt, :], in0=eb[:], in1=EB[:, o, :], op=ALU.mult),
                                 r=[r_eb, r_EB], w=[r_e2[kt]])
                        vv = vs if br == 1 else vw
                        r_vv = r_vs if br == 1 else r_vw
                        pms = []
                        for sq in range(4):
                            T = Q * 4 + sq
                            lo = 0 if br == 1 else max(0, T - 4)
                            hi = T
                            pm, r_pm = self.next_pm()
                            pms.append((pm, r_pm))
                            for kt in range(lo, hi + 1):
                                P.op("pe", lambda e, pm=pm, kt=kt, sq=sq, vv=vv, lo=lo, hi=hi: e.matmul(
                                    pm[:, 0:129], lhsT=e2[:, kt, sq * 128:(sq + 1) * 128], rhs=vv[:, kt, 0:129],
                                    start=(kt == lo), stop=(kt == hi)), r=[r_e2[kt], r_vv], w=[r_pm], signal=(kt == hi))
                        for sq in range(4):
                            pm, r_pm = pms[sq]
                            P.op("dve", lambda e, pm=pm, sq=sq: e.tensor_scalar(out=sm[:, sq, 0:1], in0=pm[:, 128:129], scalar1=1e-30, scalar2=None, op0=ALU.max),
                                 r=[r_pm], w=[r_sm[sq]])
                        P.op("dve", lambda e: e.reciprocal(sm[:, :, 1], sm[:, :, 0]), r=list(r_sm), w=list(r_sm))
                        P.op("dve", lambda e, h=h, br=br, Q=Q: e.tensor_tensor(out=sm[:, :, 2], in0=sm[:, :, 1], in1=gt[:, Q * 4:(Q + 1) * 4, gate_col(br, h)],
                                                                              op=ALU.mult), r=list(r_sm) + [r_gt], w=list(r_sm))
                        for sq in range(4):
                            T = Q * 4 + sq
                            pm, r_pm = pms[sq]
                            P.op("dve", lambda e, pm=pm, T=T, h=h, sq=sq: e.scalar_tensor_tensor(
                                out=oacc[:, T, h * 128:(h + 1) * 128], in0=pm[:, 0:128], scalar=sm[:, sq, 2:3],
                                in1=oacc[:, T, h * 128:(h + 1) * 128], op0=ALU.mult, op1=ALU.add),
                                r=[r_pm, r_sm[sq], r_oacc[T]], w=[r_oacc[T]])
                    if br == 2 and h < 3:
                        load_eb(h + 1, 2)

        with self.scope() as stf:
            wbz = self.sb(stf, "wbz", [128, NCH, 512], BF16); r_wbz = Reg()
            za = self.sb(stf, "za", [128, 16, 512], BF16); r_za = Reg()
            mixb = [self.sb(stf, "mixb%d" % i, [128, 512], BF16) for i in range(2)]
            r_mixb = [Reg(), Reg()]
            self.load_w(wbz[:], r_wbz, wsrc[:, 1292:1804])
            self.proj_tm(wbz, r_wbz, 0, 512, lambda tt: za[:, tt, :], r_za, func=AF.Silu)
            for T in range(16):
                b = T % 2
                P.op("dve", lambda e, T=T, b=b: e.tensor_tensor(out=mixb[b][:], in0=oacc[:, T, :], in1=za[:, T, :], op=ALU.mult),
                     r=[r_oacc[T], r_za], w=[r_mixb[b]])
                P.dma("sp", lambda e, T=T, b=b: e.dma_start(out=self.mixd.ap()[T * 128:(T + 1) * 128, g * 512:(g + 1) * 512], in_=mixb[b][:]),
                      r=[r_mixb[b]], w=[self.r_mixd[T]])
        if g == 1:
            self.dbg_dump("mixa", self.mixd.ap()[:, 0:1024], [S, 1024], list(self.r_mixd), BF16)

    def phase_rwkv_proj(self, st):
        nc, P = self.nc, self.P
        din = self.din
        self.rawd = nc.dram_tensor("rawd", [3, 1024, S + 1], F32)
        self.zbd = nc.dram_tensor("zbd", [S, 1024], BF16)
        self.lorad = nc.dram_tensor("lorad", [2, 64, S], F32)
        self.rw_regs = []
        wb = [self.sb(st, "wbp%d" % i, [128, NCH, 512], BF16) for i in range(2)]
        r_wb = [Reg(), Reg()]
        stg = [self.sb(st, "stg%d" % i, [128, S + 4]) for i in range(2)]
        r_stg = [Reg(), Reg()]
        for i in range(2):
            P.op("pool", lambda e, i=i: e.memset(stg[i][:, 0:1], 0.0), w=[r_stg[i]])
        k = 0
        for blk in range(6):
            b = blk % 2
            self.load_w(wb[b][:], r_wb[b], din["w_rkv"].ap()[:, blk * 512:(blk + 1) * 512])
            for j in range(4):
                ct = blk * 4 + j
                sg_, r_sg = stg[k % 2], r_stg[k % 2]
                k += 1
                self.proj_fm(wb[b], r_wb[b], j * 128, 128, lambda tb, sg_=sg_: sg_[:, 1 + tb * 512:1 + (tb + 1) * 512], r_sg)
                rr = Reg(); self.rw_regs.append(rr)
                P.dma("sp", lambda e, ct=ct, sg_=sg_: e.dma_start(out=self.rawd.ap()[ct // 8, (ct % 8) * 128:(ct % 8 + 1) * 128, 0:S + 1], in_=sg_[:, 0:S + 1]),
                      r=[r_sg], w=[rr])
        zst = [self.sb(st, "zst%d" % i, [128, 16, 512], BF16) for i in range(2)]
        r_zst = [Reg(), Reg()]
        for blk in range(2):
            self.load_w(wb[blk][:], r_wb[blk], din["w_zb"].ap()[:, blk * 512:(blk + 1) * 512])
            self.proj_tm(wb[blk], r_wb[blk], 0, 512, lambda tt, blk=blk: zst[blk][:, tt, :], r_zst[blk], func=AF.Silu)
            rr = Reg(); self.rw_regs.append(rr)
            P.dma("sp", lambda e, blk=blk: e.dma_start(out=self.zbd.ap()[:, blk * 512:(blk + 1) * 512].rearrange("(t p) c -> p t c", p=128),
                                                       in_=zst[blk][:]), r=[r_zst[blk]], w=[rr])
        mul = self.sb(st, "mul", [64, 2]); r_mul = Reg()
        P.dma("sp", lambda e: e.dma_start(out=mul[:], in_=din["mu_lora"].ap()), w=[r_mul])
        wbl = self.sb(st, "wbl", [128, NCH, 128], BF16); r_wbl = Reg()
        self.load_w(wbl[:], r_wbl, din["w_lora"].ap())
        raw = self.sb(st, "lraw", [64, S + 4]); r_raw = Reg()
        tmp = self.sb(st, "ltmp", [64, S]); r_tmp = Reg()
        lo = [self.sb(st, "lo%d" % i, [64, S]) for i in range(2)]; r_lo = [Reg(), Reg()]
        for i in range(2):
            P.op("pool", lambda e: e.memset(raw[:, 0:1], 0.0), w=[r_raw])
            self.proj_fm(wbl, r_wbl, i * 64, 64, lambda tb: raw[:, 1 + tb * 512:1 + (tb + 1) * 512], r_raw)
            P.op("dve", lambda e: e.tensor_tensor(out=tmp[:], in0=raw[:, 0:S], in1=raw[:, 1:S + 1], op=ALU.subtract), r=[r_raw], w=[r_tmp])
            P.op("dve", lambda e, i=i: e.scalar_tensor_tensor(out=lo[i][:], in0=tmp[:], scalar=mul[:, i:i + 1], in1=raw[:, 1:S + 1],
                                                             op0=ALU.mult, op1=ALU.add), r=[r_tmp, r_raw, r_mul], w=[r_lo[i]])
            if i == 0:
                P.op("act", lambda e: e.activation(out=lo[0][:], in_=lo[0][:], func=AF.Tanh), r=[r_lo[0]], w=[r_lo[0]])
            rr = Reg(); self.rw_regs.append(rr)
            P.dma("sp", lambda e, i=i: e.dma_start(out=self.lorad.ap()[i], in_=lo[i][:]), r=[r_lo[i]], w=[rr])

    def phase_rwkv(self, st):
        nc, P = self.nc, self.P
        din = self.din
        NT = 512
        F32R = mybir.dt.float32r

        def fr(ap):
            return ap.bitcast(F32R) if USE_F32R else ap

        def ld(name, shape, src, dt=F32, q="sp", r=()):
            t = self.sb(st, name, shape, dt)
            rg = Reg()
            P.dma(q, lambda e: e.dma_start(out=t[:], in_=src), r=list(r), w=[rg])
            return t, rg

        ident, r_ident = ld("identr", [128, 128], din["c_ident"].ap())
        ones = self.sb(st, "onesr", [64, 64]); r_ones = Reg()
        P.op("pool", lambda e: e.memset(ones[:], 1.0), w=[r_ones])
        msc, r_msc = ld("msc", [128, 512], din["c_mask_sc2"].ap())
        mt, r_mt = ld("mt", [128, 512], din["c_mask_t8"].ap())
        idr, r_idr = ld("idr", [128, 512], din["c_ident8"].ap())
        cmask, r_cmask = ld("cmask", [64, NT], din["c_cmask"].ap()[:, 0:NT])
        per, r_per = ld("per", [64, 16, 8], din["rw_per"].ap())
        w2, r_w2 = ld("rw2", [64, 1024], din["rw_w2"].ap())
        a2, r_a2 = ld("ra2", [64, 1024], din["rw_a2"].ap())
        twd, r_twd = ld("twd", [64, S], self.lorad.ap()[0], r=self.rw_regs)
        ads, r_ads = ld("ads", [64, S], self.lorad.ap()[1], r=self.rw_regs)
        omka = self.sb(st, "omka", [64, 16]); r_omka = Reg()
        P.op("dve", lambda e: e.tensor_scalar(out=omka[:], in0=per[:, :, 6], scalar1=-1.0, scalar2=1.0, op0=ALU.mult, op1=ALU.add),
             r=[r_per], w=[r_omka])

        def v3(ap):
            return ap.rearrange("p (c t) -> p c t", t=128)

        def make_set(si, pY, r_pY):
            sfx = "_%d" % si
            my_pm = self.pm[si * 3:(si + 1) * 3]
            my_rpm = self.r_pm[si * 3:(si + 1) * 3]
            rot = {"i": 0}

            def next_pm():
                i = rot["i"]
                rot["i"] = (i + 1) % 3
                return my_pm[i], my_rpm[i]

            Zst = self.sb(st, "Zst" + sfx, [128, 64]); r_Z = Reg()
            lnw = self.sb(st, "lnw" + sfx, [128, 64]); r_lnw = Reg()
            lnb = self.sb(st, "lnb" + sfx, [128, 64]); r_lnb = Reg()
            raws = self.sb(st, "raws" + sfx, [64, 3, NT + 4]); r_raws = Reg()
            names = ["r_s", "k_s", "v_s", "sg", "al", "kk", "k2", "Lp", "t0", "t1", "t2", "rinv", "bon"]
            RKV = self.sb(st, "RKV" + sfx, [64, 3, NT])
            T3 = self.sb(st, "T3" + sfx, [64, 3, NT])

            class _V:
                def __init__(self, t, i):
                    self.t, self.i = t, i

                def __getitem__(self, key):
                    return self.t[:, self.i, :][key]

            Tt = {n: self.sb(st, "T_" + n + sfx, [64, NT]) for n in names if n not in ("r_s", "k_s", "v_s", "t0", "t1", "t2")}
            for i_, n_ in enumerate(("r_s", "k_s", "v_s")):
                Tt[n_] = _V(RKV, i_)
            for i_, n_ in enumerate(("t0", "t1", "t2")):
                Tt[n_] = _V(T3, i_)
            R = {n: Reg() for n in names}
            AR = self.sb(st, "AR" + sfx, [128, 4, 2, 128]); r_AR = Reg()
            BK = self.sb(st, "BK" + sfx, [64, 4, 2, 128]); r_BK = Reg()
            BKh = self.sb(st, "BKh" + sfx, [64, 4, 2, 128]); r_BKh = Reg()
            WC = self.sb(st, "WC" + sfx, [64, 4]); r_WC = Reg()
            TM = self.sb(st, "TM" + sfx, [128, 4, 3, 64]); r_TM = Reg()
            SC = self.sb(st, "SC" + sfx, [128, 4, 512]); r_SC = Reg()
            PP = [self.sb(st, "PP%d" % i + sfx, [128, 512]) for i in range(2)]; r_PP = [Reg(), Reg()]
            PT = [self.sb(st, "PTt%d" % i + sfx, [128, 512]) for i in range(2)]; r_PT = [Reg(), Reg()]
            Tm = self.sb(st, "Tm" + sfx, [128, 512]); r_Tm = Reg()
            XTs = self.sb(st, "XTs" + sfx, [128, 64]); r_XTs = Reg()
            UTs = self.sb(st, "UTs" + sfx, [128, 64]); r_UTs = Reg()
            Yt = self.sb(st, "Yt" + sfx, [128, 4, 64]); r_Yt = Reg()
            yc = self.sb(st, "yc" + sfx, [128, 4, 64]); r_yc = Reg()
            ysq = self.sb(st, "ysq" + sfx, [128, 4, 64]); r_ysq = Reg()
            st4 = self.sb(st, "st4" + sfx, [128, 16]); r_st4 = Reg()
            zb = self.sb(st, "zb" + sfx, [128, 4, 64], BF16); r_zb = Reg()
            mixo = [self.sb(st, "mixo%d" % i + sfx, [128, 4, 64], BF16) for i in range(2)]; r_mixo = [Reg(), Reg()]
            state = {"mi": 0}

            def run(h):
                P.dma("sp", lambda e: e.dma_start(out=lnw[:], in_=din["ln_w"].ap()[:, h * 64:(h + 1) * 64]), w=[r_lnw])
                P.dma("sp", lambda e: e.dma_start(out=lnb[:], in_=din["ln_b"].ap()[:, h * 64:(h + 1) * 64]), w=[r_lnb])
                P.op("dve", lambda e: e.tensor_scalar(out=fr(Zst[:]), in0=mt[:, 0:64], scalar1=0.0, scalar2=None, op0=ALU.mult), r=[r_mt], w=[r_Z])
                if not state.get("ar_init"):
                    state["ar_init"] = True
                    for a_ in range(2):
                        P.op("dve", lambda e, a_=a_: e.tensor_scalar(out=fr(AR[:, a_ * 2:a_ * 2 + 2, :, :].rearrange("p a b c -> p (a b c)")), in0=mt[:],
                                                                    scalar1=0.0, scalar2=None, op0=ALU.mult), r=[r_mt], w=[r_AR])
                for G in range(4):
                    t0g = G * NT
                    P.dma("sp", lambda e, G=G: e.dma_start(out=raws[:, :, 0:NT + 1],
                                                           in_=self.rawd.ap()[:, h * 64:(h + 1) * 64, G * NT:G * NT + NT + 1].rearrange("i c t -> c i t")),
                          r=self.rw_regs, w=[r_raws])
                    P.dma("sp", lambda e, G=G: e.dma_start(out=zb[:], in_=self.zbd.ap()[G * NT:(G + 1) * NT, h * 64:(h + 1) * 64].rearrange("(t p) c -> p t c", p=128)),
                          r=self.rw_regs, w=[r_zb])
                    P.op("dve", lambda e: e.tensor_tensor(out=T3[:], in0=raws[:, :, 0:NT], in1=raws[:, :, 1:NT + 1], op=ALU.subtract),
                         r=[r_raws], w=[R["t0"], R["t1"], R["t2"]])
                    P.op("dve", lambda e: e.tensor_tensor(out=T3[:], in0=T3[:], in1=per[:, h, 0:3].unsqueeze(2).to_broadcast([64, 3, NT]), op=ALU.mult),
                         r=[R["t0"], R["t1"], R["t2"], r_per], w=[R["t0"], R["t1"], R["t2"]])
                    yield
                    P.op("dve", lambda e: e.tensor_tensor(out=RKV[:], in0=T3[:], in1=raws[:, :, 1:NT + 1], op=ALU.add),
                         r=[R["t0"], R["t1"], R["t2"], r_raws], w=[R["r_s"], R["k_s"], R["v_s"]])
                    yield
                    for (wmat, src, r_src, dst, bcol) in ((w2, twd, r_twd, "sg", 3), (a2, ads, r_ads, "al", 4)):
                        pm, r_pm = next_pm()
                        P.op("pe", lambda e, pm=pm, wmat=wmat, src=src, h=h, t0g=t0g: e.matmul(
                            pm[0:64, :], lhsT=wmat[:, h * 64:(h + 1) * 64], rhs=src[:, t0g:t0g + NT], start=True, stop=True),
                            r=[r_w2, r_a2, r_src], w=[r_pm])
                        P.op("act", lambda e, pm=pm, dst=dst, bcol=bcol, h=h: e.activation(out=Tt[dst][:], in_=pm[0:64, :], func=AF.Sigmoid,
                                                                                           bias=per[:, h, bcol:bcol + 1]),
                             r=[r_pm, r_per], w=[R[dst]])
                    P.op("act", lambda e, h=h: e.activation(out=Tt["t1"][:], in_=Tt["k_s"][:], func=AF.Square, scale=per[:, h, 5:6]),
                         r=[R["k_s"], r_per], w=[R["t1"]])
                    pm, r_pm = next_pm()
                    P.op("pe", lambda e, pm=pm: e.matmul(pm[0:64, :], lhsT=ones[:], rhs=Tt["t1"][:], start=True, stop=True),
                         r=[r_ones, R["t1"]], w=[r_pm])
                    P.op("act", lambda e, pm=pm: e.activation(out=Tt["rinv"][:], in_=pm[0:64, :], func=AF.Sqrt), r=[r_pm], w=[R["rinv"]])
                    yield
                    P.op("dve", lambda e: e.tensor_scalar(out=Tt["rinv"][:], in0=Tt["rinv"][:], scalar1=1e-12, scalar2=None, op0=ALU.max),
                         r=[R["rinv"]], w=[R["rinv"]])
                    P.op("dve", lambda e: e.reciprocal(Tt["rinv"][:], Tt["rinv"][:]), r=[R["rinv"]], w=[R["rinv"]])
                    P.op("dve", lambda e, h=h: e.scalar_tensor_tensor(out=Tt["kk"][:], in0=Tt["k_s"][:], scalar=per[:, h, 5:6], in1=Tt["rinv"][:],
                                                                     op0=ALU.mult, op1=ALU.mult), r=[R["k_s"], R["rinv"], r_per], w=[R["kk"]])
                    yield
                    P.op("dve", lambda e, h=h: e.tensor_scalar(out=Tt["t1"][:], in0=Tt["al"][:], scalar1=per[:, h, 6:7], scalar2=omka[:, h:h + 1],
                                                              op0=ALU.mult, op1=ALU.add), r=[R["al"], r_per, r_omka], w=[R["t1"]])
                    P.op("dve", lambda e: e.tensor_tensor(out=Tt["k2"][:], in0=Tt["k_s"][:], in1=Tt["t1"][:], op=ALU.mult),
                         r=[R["k_s"], R["t1"]], w=[R["k2"]])
                    P.op("dve", lambda e, h=h: e.scalar_tensor_tensor(out=Tt["t1"][:], in0=Tt["r_s"][:], scalar=per[:, h, 7:8], in1=Tt["k2"][:],
                                                                     op0=ALU.mult, op1=ALU.mult), r=[R["r_s"], R["k2"], r_per], w=[R["t1"]])
                    yield
                    pm, r_pm = next_pm()
                    P.op("pe", lambda e, pm=pm: e.matmul(pm[0:64, :], lhsT=ones[:], rhs=Tt["t1"][:], start=True, stop=True),
                         r=[r_ones, R["t1"]], w=[r_pm])
                    P.op("dve", lambda e, pm=pm: e.tensor_tensor(out=Tt["bon"][:], in0=pm[0:64, :], in1=Tt["v_s"][:], op=ALU.mult),
                         r=[r_pm, R["v_s"]], w=[R["bon"]])
                    P.op("dve", lambda e: e.tensor_tensor_scan(out=Tt["Lp"][:], data0=cmask[:], data1=Tt["sg"][:], initial=0.0,
                                                              op0=ALU.mult, op1=ALU.add), r=[r_cmask, R["sg"]], w=[R["Lp"]])
                    P.op("dve", lambda e: e.tensor_tensor(out=Tt["t1"][:], in0=Tt["Lp"][:], in1=Tt["sg"][:], op=ALU.subtract),
                         r=[R["Lp"], R["sg"]], w=[R["t1"]])
                    P.op("act", lambda e: e.activation(out=Tt["t1"][:], in_=Tt["t1"][:], func=AF.Exp, scale=-C0), r=[R["t1"]], w=[R["t1"]])
                    yield
                    P.op("dve", lambda e: e.scalar_tensor_tensor(out=fr(AR[0:64, :, 0, :]), in0=v3(Tt["kk"][:]), scalar=-1.0, in1=v3(Tt["t1"][:]),
                                                                op0=ALU.mult, op1=ALU.mult), r=[R["kk"], R["t1"]], w=[r_AR])
                    P.op("act", lambda e: e.activation(out=Tt["rinv"][:], in_=Tt["Lp"][:], func=AF.Exp, scale=-C0), r=[R["Lp"]], w=[R["rinv"]])
                    P.op("dve", lambda e: e.tensor_tensor(out=fr(AR[0:64, :, 1, :]), in0=v3(Tt["r_s"][:]), in1=v3(Tt["rinv"][:]), op=ALU.mult),
                         r=[R["r_s"], R["rinv"]], w=[r_AR])
                    yield
                    P.op("act", lambda e: e.activation(out=Tt["t1"][:], in_=Tt["Lp"][:], func=AF.Exp, scale=C0), r=[R["Lp"]], w=[R["t1"]])
                    P.op("dve", lambda e: e.tensor_tensor(out=Tt["t2"][:], in0=Tt["kk"][:], in1=Tt["al"][:], op=ALU.mult),
                         r=[R["kk"], R["al"]], w=[R["t2"]])
                    yield
                    P.op("dve", lambda e: e.tensor_tensor(out=fr(BK[:, :, 0, :]), in0=v3(Tt["t2"][:]), in1=v3(Tt["t1"][:]), op=ALU.mult),
                         r=[R["t2"], R["t1"]], w=[r_BK])
                    P.op("dve", lambda e: e.tensor_tensor(out=fr(BK[:, :, 1, :]), in0=v3(Tt["k2"][:]), in1=v3(Tt["t1"][:]), op=ALU.mult),
                         r=[R["k2"], R["t1"]], w=[r_BK])
                    yield
                    P.op("dve", lambda e: e.tensor_tensor(out=BKh[:].rearrange("p a b c -> p a (b c)"), in0=BK[:].rearrange("p a b c -> p a (b c)"),
                                                          in1=Tt["rinv"][:, 127:NT:128].unsqueeze(2).to_broadcast([64, 4, 256]), op=ALU.mult), r=[r_BK, R["rinv"]], w=[r_BKh])
                    yield
                    for c2 in range(2):
                        pm, r_pm = next_pm()
                        for cc in range(2):
                            c = c2 * 2 + cc
                            srcs = (Tt["v_s"][:, c * 128:(c + 1) * 128], BKh[:, c, 0, :], BKh[:, c, 1, :])
                            for k3 in range(3):
                                P.op("pe", lambda e, pm=pm, cc=cc, k3=k3, src=srcs[k3]: e.transpose(
                                    pm[:, (cc * 3 + k3) * 64:(cc * 3 + k3 + 1) * 64], src, ident[0:64, 0:64]),
                                    r=[R["v_s"], r_BKh, r_ident], w=[r_pm], signal=(cc == 1 and k3 == 2))
                        P.op("act", lambda e, pm=pm, c2=c2: e.activation(out=fr(TM[:, c2 * 2:c2 * 2 + 2, :, :]),
                                                                         in_=pm[:, 0:384].rearrange("p (a b c) -> p a b c", a=2, b=3), func=AF.Copy),
                             r=[r_pm], w=[r_TM])
                        yield
                    for c in range(4):
                        pm, r_pm = next_pm()
                        for k3 in range(2):
                            P.op("pe", lambda e, pm=pm, c=c, k3=k3: e.matmul(pm[:, k3 * 256:(k3 + 1) * 256], lhsT=fr(BK[:, c, k3, :]),
                                                                            rhs=fr(AR[0:64, c, :, :]), start=True, stop=True),
                                 r=[r_BK, r_AR], w=[r_pm], signal=(k3 == 1))
                        P.op("dve", lambda e, pm=pm, c=c: e.tensor_tensor(out=fr(SC[:, c, :]), in0=pm[:, :], in1=msc[:], op=ALU.mult),
                             r=[r_pm, r_msc], w=[r_SC])
                        yield
                    pm, r_pm = next_pm()
                    for c in range(4):
                        P.op("pe", lambda e, pm=pm, c=c: e.matmul(pm[:, c * 128:(c + 1) * 128], lhsT=fr(AR[0:64, c, 0, :]), rhs=fr(BK[:, c, 0, :]),
                                                                 start=True, stop=True), r=[r_AR, r_BK], w=[r_pm], signal=(c == 3))
                    P.op("dve", lambda e, pm=pm: e.tensor_tensor(out=fr(PT[0][:]), in0=pm[:, :], in1=mt[:], op=ALU.mult),
                         r=[r_pm, r_mt], w=[r_PT[0]])
                    yield
                    P.op("dve", lambda e: e.tensor_tensor(out=fr(v3(Tm[:])), in0=SC[:, :, 0:128], in1=v3(idr[:]), op=ALU.add), r=[r_SC, r_idr], w=[r_Tm])
                    yield
                    cur = 0
                    for it in range(6):
                        nxt = 1 - cur
                        if it < 5:
                            pm, r_pm = next_pm()
                            for c in range(4):
                                Pv = SC[:, c, 0:128] if it == 0 else PP[cur][:, c * 128:(c + 1) * 128]
                                P.op("pe", lambda e, pm=pm, c=c, cur=cur, Pv=Pv: e.matmul(pm[:, c * 128:(c + 1) * 128], lhsT=fr(PT[cur][:, c * 128:(c + 1) * 128]),
                                                                                         rhs=fr(Pv), start=True, stop=True),
                                     r=[r_PT[cur], r_SC if it == 0 else r_PP[cur]], w=[r_pm], signal=(c == 3))
                            P.op("act", lambda e, pm=pm, nxt=nxt: e.activation(out=fr(PP[nxt][:]), in_=pm[:, :], func=AF.Copy),
                                 r=[r_pm], w=[r_PP[nxt]])
                            yield
                        pm, r_pm = next_pm()
                        for c in range(4):
                            Pv = SC[:, c, 0:128] if it == 0 else PP[cur][:, c * 128:(c + 1) * 128]
                            P.op("pe", lambda e, pm=pm, c=c, cur=cur, Pv=Pv: e.matmul(pm[:, c * 128:(c + 1) * 128], lhsT=fr(Pv),
                                                                                     rhs=fr(PT[cur][:, c * 128:(c + 1) * 128]), start=True, stop=True),
                                 r=[r_PT[cur], r_SC if it == 0 else r_PP[cur]], w=[r_pm], signal=(c == 3))
                        P.op("dve", lambda e, pm=pm, nxt=nxt: e.tensor_copy(fr(PT[nxt][:]), pm[:, :]), r=[r_pm], w=[r_PT[nxt]])
                        yield
                        pm, r_pm = next_pm()
                        for c in range(4):
                            P.op("pe", lambda e, pm=pm, c=c, nxt=nxt: e.matmul(pm[:, c * 128:(c + 1) * 128], lhsT=fr(PT[nxt][:, c * 128:(c + 1) * 128]),
                                                                              rhs=fr(Tm[:, c * 128:(c + 1) * 128]), start=True, stop=True),
                                 r=[r_PT[nxt], r_Tm], w=[r_pm], signal=(c == 3))
                        P.op("dve", lambda e, pm=pm: e.tensor_tensor(out=fr(Tm[:]), in0=pm[:, :], in1=Tm[:], op=ALU.add), r=[r_pm, r_Tm], w=[r_Tm])
                        yield
                        cur = nxt
                    for c in range(4):
                        pm, r_pm = next_pm()
                        P.op("pe", lambda e, pm=pm, c=c: e.matmul(pm[:, 0:64], lhsT=fr(AR[:, c, 0, :]), rhs=fr(Zst[:]), start=True, stop=False),
                             r=[r_AR, r_Z], w=[r_pm], signal=False)
                        P.op("pe", lambda e, pm=pm, c=c: e.matmul(pm[:, 0:64], lhsT=fr(SC[:, c, 256:384]), rhs=fr(TM[:, c, 0, :]), start=False, stop=True),
                             r=[r_SC, r_TM], w=[r_pm])
                        P.op("act", lambda e, pm=pm: e.activation(out=fr(XTs[:]), in_=pm[:, 0:64], func=AF.Copy), r=[r_pm], w=[r_XTs])
                        yield
                        pm, r_pm = next_pm()
                        P.op("pe", lambda e, pm=pm, c=c: e.matmul(pm[:, 0:64], lhsT=fr(Tm[:, c * 128:(c + 1) * 128]), rhs=fr(XTs[:]), start=True, stop=True),
                             r=[r_Tm, r_XTs], w=[r_pm])
                        P.op("act", lambda e, pm=pm: e.activation(out=fr(UTs[:]), in_=pm[:, 0:64], func=AF.Copy), r=[r_pm], w=[r_UTs])
                        yield
                        yo = pY[:, c * 64:(c + 1) * 64]
                        P.op("pe", lambda e, yo=yo, c=c: e.matmul(yo, lhsT=fr(AR[:, c, 1, :]), rhs=fr(Zst[:]), start=True, stop=False),
                             r=[r_AR, r_Z], w=[r_pY], signal=False)
                        P.op("pe", lambda e, yo=yo, c=c: e.matmul(yo, lhsT=fr(SC[:, c, 128:256]), rhs=fr(UTs[:]), start=False, stop=False),
                             r=[r_SC, r_UTs], w=[r_pY], signal=False)
                        P.op("pe", lambda e, yo=yo, c=c: e.matmul(yo, lhsT=fr(SC[:, c, 384:512]), rhs=fr(TM[:, c, 0, :]), start=False, stop=True),
                             r=[r_SC, r_TM], w=[r_pY])
                        yield
                        pm, r_pm = next_pm()
                        P.op("pe", lambda e, pm=pm, c=c: e.matmul(pm[0:64, 0:64], lhsT=fr(TM[:, c, 1, :]), rhs=fr(UTs[:]), start=True, stop=False),
                             r=[r_TM, r_UTs], w=[r_pm], signal=False)
                        P.op("pe", lambda e, pm=pm, c=c: e.matmul(pm[0:64, 0:64], lhsT=fr(TM[:, c, 2, :]), rhs=fr(TM[:, c, 0, :]), start=False, stop=True),
                             r=[r_TM], w=[r_pm])
                        P.op("dve", lambda e, pm=pm, c=c: e.scalar_tensor_tensor(out=fr(Zst[0:64, :]), in0=Zst[0:64, :], scalar=Tt["rinv"][:, c * 128 + 127:c * 128 + 128], in1=pm[0:64, 0:64],
                                                                                op0=ALU.mult, op1=ALU.add), r=[r_Z, R["rinv"], r_pm], w=[r_Z])
                        yield
                    pYv = pY[:, 0:256].rearrange("p (a b) -> p a b", b=64)
                    P.op("dve", lambda e: e.reduce_sum(out=st4[:, 0:4], in_=pYv, axis=AX.X), r=[r_pY], w=[r_st4])
                    yield
                    P.op("dve", lambda e: e.scalar_tensor_tensor(out=yc[:], in0=st4[:, 0:4].unsqueeze(2).to_broadcast([128, 4, 64]), scalar=-1.0 / 64,
                                                                in1=pYv, op0=ALU.mult, op1=ALU.add), r=[r_pY, r_st4], w=[r_yc])
                    P.op("dve", lambda e: e.tensor_tensor(out=ysq[:], in0=yc[:], in1=yc[:], op=ALU.mult), r=[r_yc], w=[r_ysq])
                    P.op("dve", lambda e: e.reduce_sum(out=st4[:, 4:8], in_=ysq[:], axis=AX.X), r=[r_ysq], w=[r_st4])
                    yield
                    P.op("dve", lambda e: e.tensor_scalar(out=st4[:, 4:8], in0=st4[:, 4:8], scalar1=1.0 / 64, scalar2=64e-5, op0=ALU.mult, op1=ALU.add),
                         r=[r_st4], w=[r_st4])
                    P.op("act", lambda e: e.activation(out=st4[:, 4:8], in_=st4[:, 4:8], func=AF.Sqrt), r=[r_st4], w=[r_st4])
                    P.op("dve", lambda e: e.reciprocal(st4[:, 8:12], st4[:, 4:8]), r=[r_st4], w=[r_st4])
                    yield
                    pm, r_pm = next_pm()
                    for tq in range(4):
                        P.op("pe", lambda e, pm=pm, tq=tq: e.transpose(pm[:, tq * 64:(tq + 1) * 64], Tt["bon"][:, tq * 128:(tq + 1) * 128],
                                                                     ident[0:64, 0:64]), r=[R["bon"], r_ident], w=[r_pm], signal=(tq == 3))
                    P.op("dve", lambda e: e.tensor_tensor(out=yc[:], in0=yc[:], in1=st4[:, 8:12].unsqueeze(2).to_broadcast([128, 4, 64]), op=ALU.mult),
                         r=[r_yc, r_st4], w=[r_yc])
                    P.op("dve", lambda e: e.tensor_tensor(out=yc[:], in0=yc[:], in1=lnw[:, :].unsqueeze(1).to_broadcast([128, 4, 64]), op=ALU.mult),
                         r=[r_yc, r_lnw], w=[r_yc])
                    yield
                    P.op("dve", lambda e: e.tensor_tensor(out=yc[:], in0=yc[:], in1=lnb[:, :].unsqueeze(1).to_broadcast([128, 4, 64]), op=ALU.add),
                         r=[r_yc, r_lnb], w=[r_yc])
                    P.op("dve", lambda e, pm=pm: e.tensor_tensor(out=yc[:], in0=yc[:], in1=pm[:, 0:256].rearrange("p (a b) -> p a b", b=64), op=ALU.add),
                         r=[r_yc, r_pm], w=[r_yc])
                    yield
                    if h == 0 and G == 0:
                        self.dbg_dump("ob00", yc[:], [128, 4, 64], r_yc)

                    mo = mixo[state["mi"] % 2]; r_mo = r_mixo[state["mi"] % 2]; state["mi"] += 1
                    P.op("dve", lambda e, mo=mo: e.tensor_tensor(out=mo[:], in0=yc[:], in1=zb[:], op=ALU.mult), r=[r_yc, r_zb], w=[r_mo])
                    dst = self.mixd.ap()[G * NT:(G + 1) * NT, 1024 + h * 64:1024 + (h + 1) * 64].rearrange("(t p) c -> p t c", p=128)
                    rr = Reg()
                    self.rw_store_regs.append(rr)
                    P.dma("sp", lambda e, mo=mo, dst=dst: e.dma_start(out=dst, in_=mo[:]), r=[r_mo], w=[rr])
                    yield
            return run

        runs = [make_set(0, self.px, self.r_px), make_set(1, self.py2, self.r_py2)]
        for hp in range(8):
            gens = [runs[0](2 * hp), runs[1](2 * hp + 1)]
            for _ in range(RW_STAGGER):
                try:
                    next(gens[0])
                except StopIteration:
                    break
            while gens:
                for gq in list(gens):
                    try:
                        next(gq)
                    except StopIteration:
                        gens.remove(gq)
        self.dbg_dump("mixall", self.mixd.ap(), [S, 2048], list(self.r_mixd) + self.rw_store_regs, BF16)

    def phase_out(self, st):
        nc, P = self.nc, self.P
        din = self.din
        wo = self.sb(st, "wo", [128, NCH, D], BF16); r_wo = [Reg() for _ in range(4)]
        for nb in range(4):
            P.dma("pool", lambda e, nb=nb: e.dma_start(out=wo[:, :, nb * 512:(nb + 1) * 512],
                                                       in_=din["w_out"].ap()[:, nb * 512:(nb + 1) * 512].rearrange("(c p) n -> p c n", p=128)),
                  w=[r_wo[nb]])
        gpo = self.sb(st, "gpo", [128, D]); r_gpo = Reg()
        P.dma("sp", lambda e: e.dma_start(out=gpo[:], in_=din["g_post"].ap()), w=[r_gpo])
        idb = self.sb(st, "idb2", [128, 128], BF16); r_idb = Reg()
        idf = self.sb(st, "idf2", [128, 128]); r_idf = Reg()
        P.dma("sp", lambda e: e.dma_start(out=idf[:], in_=din["c_ident"].ap()), w=[r_idf])
        P.op("dve", lambda e: e.tensor_copy(idb[:], idf[:]), r=[r_idf], w=[r_idb])
        mx = [self.sb(st, "mx%d" % i, [128, D], BF16) for i in range(2)]; r_mx = [Reg(), Reg()]
        mT = [self.sb(st, "mT%d" % i, [128, NCH, 128], BF16) for i in range(2)]; r_mT = [Reg(), Reg()]
        xr = [self.sb(st, "xr%d" % i, [128, D]) for i in range(2)]; r_xr = [Reg(), Reg()]
        ysbs = [self.sb(st, "ysb%d" % i, [128, D]) for i in range(2)]; r_ysbs = [Reg(), Reg()]
        jks = [self.sb(st, "jk%d" % i, [128, D]) for i in range(2)]; r_jks = [Reg(), Reg()]
        s1s = [self.sb(st, "s1_%d" % i, [128, 4]) for i in range(2)]; r_s1s = [Reg(), Reg()]
        ob = [self.sb(st, "ob%d" % i, [128, D]) for i in range(2)]; r_ob = [Reg(), Reg()]
        mix_regs = list(self.r_mixd) + self.rw_store_regs

        def stage_a(T):
            b = T % 2
            P.dma("sp", lambda e: e.dma_start(out=mx[b][:], in_=self.mixd.ap()[T * 128:(T + 1) * 128, :]), r=mix_regs, w=[r_mx[b]])
            P.dma("sp", lambda e: e.dma_start(out=xr[b][:], in_=din["x"].ap()[T * 128:(T + 1) * 128, :]), w=[r_xr[b]])
            for gq in range(4):
                for jq in range(4):
                    c = gq * 4 + jq
                    P.op("pe", lambda e, c=c, jq=jq: e.transpose(self.ptr[:, jq, :], mx[b][:, c * 128:(c + 1) * 128], idb[:]),
                         r=[r_mx[b], r_idb], w=[self.r_ptr], signal=(jq == 3))
                P.op("dve", lambda e, gq=gq: e.tensor_copy(mT[b][:, gq * 4:(gq + 1) * 4, :], self.ptr[:]),
                     r=[self.r_ptr], w=[r_mT[b]])

        def stage_b(T):
            b = T % 2
            ysb, r_ysb = ysbs[b], r_ysbs[b]
            for nb in range(4):
                pm, r_pm = self.next_pm()
                for c in range(NCH):
                    P.op("pe", lambda e, pm=pm, c=c, nb=nb: e.matmul(pm[:, :], lhsT=mT[b][:, c, :], rhs=wo[:, c, nb * 512:(nb + 1) * 512],
                                                                   start=(c == 0), stop=(c == NCH - 1)),
                         r=[r_mT[b], r_wo[nb]], w=[r_pm], signal=(c == NCH - 1))
                P.op("act", lambda e, pm=pm, nb=nb: e.activation(out=ysb[:, nb * 512:(nb + 1) * 512], in_=pm[:, :], func=AF.Copy),
                     r=[r_pm], w=[r_ysb])

        def stage_c(T):
            b = T % 2
            ysb, r_ysb, jk, r_jk, s1, r_s1 = ysbs[b], r_ysbs[b], jks[b], r_jks[b], s1s[b], r_s1s[b]
            P.op("act", lambda e: e.activation(out=jk[:], in_=ysb[:], func=AF.Square), r=[r_ysb], w=[r_jk])
            P.op("dve", lambda e: e.reduce_sum(out=s1[:, 0:1], in_=jk[:], axis=AX.X), r=[r_jk], w=[r_s1])
            P.op("dve", lambda e: e.tensor_scalar(out=s1[:, 0:1], in0=s1[:, 0:1], scalar1=1.0 / D, scalar2=1e-6, op0=ALU.mult, op1=ALU.add),
                 r=[r_s1], w=[r_s1])
            P.op("act", lambda e: e.activation(out=s1[:, 0:1], in_=s1[:, 0:1], func=AF.Sqrt), r=[r_s1], w=[r_s1])
            P.op("dve", lambda e: e.reciprocal(s1[:, 1:2], s1[:, 0:1]), r=[r_s1], w=[r_s1])
            P.op("dve", lambda e: e.tensor_tensor(out=jk[:], in0=ysb[:], in1=gpo[:], op=ALU.mult), r=[r_ysb, r_gpo], w=[r_jk])
            P.op("dve", lambda e: e.scalar_tensor_tensor(out=ob[b][:], in0=jk[:], scalar=s1[:, 1:2], in1=xr[b][:], op0=ALU.mult, op1=ALU.add),
                 r=[r_jk, r_s1, r_xr[b]], w=[r_ob[b]])
            ro = Reg()
            self.out_regs.append(ro)
            P.dma("sp", lambda e: e.dma_start(out=self.out.ap()[T * 128:(T + 1) * 128, :], in_=ob[b][:]), r=[r_ob[b]], w=[ro])

        stage_a(0)
        for T in range(16):
            stage_b(T)
            if T + 1 < 16:
                stage_a(T + 1)
            stage_c(T)


def _build(in_shapes, dbg=()):
    b = B(in_shapes, dbg)
    nc = b.build()
    return nc, b


def kernel(**inputs):
    inputs = {k: np.asarray(v) for k, v in inputs.items()}
    consts = _consts()
    L = _layout_inputs(inputs)
    shared = dict(consts)
    shared.update(L)
    in_shapes = {"x": (S, D)}
    for k, v in shared.items():
        in_shapes[k] = v.shape
    nc, b = _build(in_shapes)
    active = [0, 1, 4, 5]
    zero_x = np.zeros((S, D), np.float32)
    in_maps = []
    for c in range(8):
        if c in active:
            m = {"x": np.ascontiguousarray(inputs["x"][active.index(c)])}
        else:
            m = {"x": zero_x}
        m.update(shared)
        in_maps.append(m)
    res = run_bass_kernel_spmd(nc, in_maps, core_ids=list(range(8)))
    out = np.stack([res.results[c]["out"] for c in active], 0)
    return out.astype(np.float32)
```

```python
import math
from contextlib import ExitStack, contextmanager
import numpy as np
import concourse.bass as bass
import concourse.mybir as mybir
from concourse.bass_utils import run_bass_kernel_spmd

F32 = mybir.dt.float32
BF16 = mybir.dt.bfloat16
ALU = mybir.AluOpType
AF = mybir.ActivationFunctionType
AX = mybir.AxisListType

ENGS = ("pe", "act", "dve", "pool", "sp")
N_DMA_SEMS = 12
S = 2048
D = 2048
NCH = 16
BIG = 1.0e30
C0 = math.exp(-0.5)
USE_F32R = True
RW_STAGGER = 0


class Reg:
    __slots__ = ("lw", "rd")

    def __init__(self):
        self.lw = None
        self.rd = {}


class Prog:
    def __init__(self, nc):
        self.nc = nc
        self.q = {e: [] for e in ENGS}
        self.cnt = {e: 0 for e in ENGS}
        self.pending = {e: False for e in ENGS}
        self.waited = {e: {} for e in ENGS}
        self.dma_k = {e: 0 for e in ENGS}

    def op(self, eng, fn, r=(), w=(), signal=True):
        deps = {}

        def need(key, val, kind):
            if key == eng:
                if eng == "pe":
                    return
            if deps.get(key, 0) < val:
                deps[key] = val

        for reg in r:
            if reg.lw is not None:
                need(reg.lw[0], reg.lw[1], "raw")
        for reg in w:
            if reg.lw is not None:
                need(reg.lw[0], reg.lw[1], "waw")
            for k, v in reg.rd.items():
                need(k, v, "war")
        waits = []
        wd = self.waited[eng]
        for k, v in deps.items():
            if wd.get(k, 0) < v:
                wd[k] = v
                waits.append((k, v))
        n = self.cnt[eng] + 1
        if signal:
            self.cnt[eng] = n
            self.pending[eng] = False
        else:
            self.pending[eng] = True
        self.q[eng].append((waits, fn, (eng, 1) if signal else None))
        for reg in r:
            reg.rd[eng] = n
        for reg in w:
            reg.lw = (eng, n)
            reg.rd = {}

    def dma(self, qe, fn, r=(), w=()):
        deps = {}
        for reg in r:
            if reg.lw is not None:
                k, v = reg.lw
                deps[k] = max(deps.get(k, 0), v)
        for reg in w:
            if reg.lw is not None:
                k, v = reg.lw
                deps[k] = max(deps.get(k, 0), v)
            for k, v in reg.rd.items():
                deps[k] = max(deps.get(k, 0), v)
        kk = self.dma_k[qe]
        self.dma_k[qe] = kk + 1
        slot = kk % N_DMA_SEMS
        key = "dma_%s_%d" % (qe, slot)
        prev = 16 * (kk // N_DMA_SEMS)
        if prev > 0:
            deps[key] = max(deps.get(key, 0), prev)
        waits = []
        wd = self.waited[qe]
        for k, v in deps.items():
            if wd.get(k, 0) < v:
                wd[k] = v
                waits.append((k, v))
        tgt = prev + 16
        self.q[qe].append((waits, fn, (key, 16)))
        for reg in r:
            reg.rd[key] = max(reg.rd.get(key, 0), tgt)
        for reg in w:
            reg.lw = (key, tgt)
            reg.rd = {}

    def barrier(self):
        deps = {e: self.cnt[e] for e in ENGS if self.cnt[e] > 0}
        for qe in ENGS:
            kk = self.dma_k[qe]
            for slot in range(min(N_DMA_SEMS, kk)):
                uses = (kk - slot + N_DMA_SEMS - 1) // N_DMA_SEMS
                deps["dma_%s_%d" % (qe, slot)] = 16 * uses
        for e in ENGS:
            assert not self.pending[e]
            waits = []
            for k, v in deps.items():
                if k != e and self.waited[e].get(k, 0) < v:
                    self.waited[e][k] = v
                    waits.append((k, v))
            if waits:
                self.q[e].append((waits, None, None))

    def final_wait(self, eng, regs):
        deps = {}
        for reg in regs:
            if reg.lw is not None:
                k, v = reg.lw
                deps[k] = max(deps.get(k, 0), v)
        self.q[eng].append((list(deps.items()), None, None))

    def emit(self):
        nc = self.nc
        with ExitStack() as st:
            sems = {}
            for e in ENGS:
                sems[e] = st.enter_context(nc.semaphore("s_" + e))
            for qe in ENGS:
                for i in range(min(N_DMA_SEMS, self.dma_k[qe])):
                    key = "dma_%s_%d" % (qe, i)
                    sems[key] = st.enter_context(nc.semaphore(key))
            block = st.enter_context(nc.Block())
            for e in ENGS:
                assert not self.pending[e], e

            def run(engname):
                def body(eng):
                    for waits, fn, inc in self.q[engname]:
                        for k, v in waits:
                            eng.wait_ge(sems[k], v)
                        if fn is not None:
                            ins = fn(eng)
                            if inc is not None:
                                ins.then_inc(sems[inc[0]], inc[1])
                return body

            block.tensor(run("pe"))
            block.scalar(run("act"))
            block.vector(run("dve"))
            block.gpsimd(run("pool"))
            block.sync(run("sp"))


def _bucket(n):
    n = np.maximum(n, 0)
    nf = np.maximum(n, 1).astype(np.float32)
    large = 16 + (np.log(nf / np.float32(16)) / np.float32(math.log(64)) * np.float32(16)).astype(np.int32)
    large = np.minimum(large, 31)
    return np.where(n < 16, n, large)


def _consts():
    c = {}
    n = np.arange(2048)
    oh = np.zeros((32, 4096), np.float32)
    oh[_bucket(n), n] = 1.0
    c["c_oh"] = oh
    c["c_ident"] = np.eye(128, dtype=np.float32)
    t = np.arange(S)
    cur = t // 64
    j = np.arange(32)
    forced = (j[None, :] == 0) | (j[None, :] == cur[:, None]) | (j[None, :] == cur[:, None] - 1)
    causal = j[None, :] <= cur[:, None]
    m1 = (causal & ~forced).astype(np.float32)
    add = np.where(forced, BIG, np.where(causal, 0.0, -BIG)).astype(np.float32)
    c["c_m1"] = np.ascontiguousarray(m1.reshape(16, 128, 32).transpose(1, 0, 2))
    c["c_add"] = np.ascontiguousarray(add.reshape(16, 128, 32).transpose(1, 0, 2))
    e2 = (np.arange(S)[None, :] // 64 == j[:, None]).astype(np.float32)
    c["c_e2"] = e2
    cs = np.arange(127) * 16
    ss = np.arange(32) * 64
    ov = ((cs[:, None] < ss[None, :] + 64) & (cs[:, None] + 32 > ss[None, :])).astype(np.float32)
    c["c_ov"] = ov
    tri_s = np.triu(np.ones((64, 64), np.float32), 1)
    tri_i = np.triu(np.ones((64, 64), np.float32), 0)
    ts2 = np.triu(np.ones((128, 128), np.float32), 1)
    ti2 = np.triu(np.ones((128, 128), np.float32), 0)
    c["c_mask_sc2"] = np.concatenate([ts2, ti2, ts2, ti2], 1)
    c["c_mask_t8"] = np.ascontiguousarray(np.tile(ts2.T, (1, 4)))
    c["c_ident8"] = np.ascontiguousarray(np.tile(np.eye(128, dtype=np.float32), (1, 4)))
    cm = np.ones((64, S), np.float32)
    cm[:, ::128] = 0.0
    c["c_cmask"] = cm
    return c


W_NSA_FM = 1024
W_NSA_TM = 780


def _layout_inputs(inp):
    L = {}
    w_in = inp["w_in"][0]
    o_kv = 1024
    o_g = 2560
    o_za = 2584
    o_f = 3608
    o_zb = 6808
    for g in range(2):
        cols = list(range(g * 512, g * 512 + 512))
        cols += list(range(o_kv + 0 * 256 + g * 128, o_kv + 0 * 256 + g * 128 + 128))
        cols += list(range(o_kv + 1 * 256 + g * 128, o_kv + 1 * 256 + g * 128 + 128))
        cols += list(range(o_kv + 2 * 256 + g * 128, o_kv + 2 * 256 + g * 128 + 128))
        cols += list(range(o_kv + 4 * 256 + g * 128, o_kv + 4 * 256 + g * 128 + 128))
        cols += list(range(o_kv + 3 * 256 + g * 128, o_kv + 3 * 256 + g * 128 + 128))
        cols += list(range(o_kv + 5 * 256 + g * 128, o_kv + 5 * 256 + g * 128 + 128))
        for br in range(3):
            cols += [o_g + br * 8 + g * 4 + h for h in range(4)]
        cols += list(range(o_za + g * 512, o_za + g * 512 + 512))
        L["w_nsa%d" % g] = np.ascontiguousarray(w_in[:, cols])
    L["w_rkv"] = np.ascontiguousarray(w_in[:, o_f:o_f + 3072])
    L["w_lora"] = np.ascontiguousarray(w_in[:, o_f + 3072:o_f + 3200])
    L["w_zb"] = np.ascontiguousarray(w_in[:, o_zb:o_zb + 1024])
    L["g_pre"] = np.ascontiguousarray(inp["pre_norm_g"][0].reshape(16, 128).T)
    L["g_post"] = np.ascontiguousarray(np.tile(inp["post_norm_g"][0][None, :], (128, 1)))
    L["tab"] = np.ascontiguousarray(inp["rel_bias_table"])
    cmp_params = {"k": (inp["cmp_pos_k"], inp["cmp_k_w1"], inp["cmp_k_w2"]),
                  "v": (inp["cmp_pos_v"], inp["cmp_v_w1"], inp["cmp_v_w2"])}
    for nm in ("k", "v"):
        pos_, w1_, w2_ = cmp_params[nm]
        L["pos%sT" % nm] = np.ascontiguousarray(pos_[0].T)
        L["w1%s" % nm] = np.ascontiguousarray(w1_[0].reshape(32, 128, 128).transpose(1, 0, 2))
        L["w2%s" % nm] = np.ascontiguousarray(w2_[0])
    mu = inp["rwkv_mu"][0]
    per = np.zeros((64, 16, 8), np.float32)
    for h in range(16):
        sl = slice(h * 64, h * 64 + 64)
        per[:, h, 0] = mu[0:1024][sl]
        per[:, h, 1] = mu[1024:2048][sl]
        per[:, h, 2] = mu[2048:3072][sl]
        per[:, h, 3] = inp["rwkv_w0"][0][sl]
        per[:, h, 4] = inp["rwkv_a0"][0][sl]
        per[:, h, 5] = inp["rwkv_k_k"][0][sl]
        per[:, h, 6] = inp["rwkv_k_a"][0][sl]
        per[:, h, 7] = inp["rwkv_r_k"][0][h]
    L["rw_per"] = per
    L["mu_lora"] = np.ascontiguousarray(mu[3072:3200].reshape(2, 64).T)
    L["rw_w2"] = np.ascontiguousarray(inp["rwkv_w2"][0])
    L["rw_a2"] = np.ascontiguousarray(inp["rwkv_a2"][0])
    L["ln_w"] = np.ascontiguousarray(np.tile(inp["rwkv_ln_w"][0][None, :], (128, 1)))
    L["ln_b"] = np.ascontiguousarray(np.tile(inp["rwkv_ln_b"][0][None, :], (128, 1)))
    L["w_out"] = np.ascontiguousarray(inp["w_out"][0])
    return L


_IN_SHAPES = None


class B:
    def __init__(self, in_shapes, dbg=()):
        self.dbg = dbg
        nc = self.nc = bass.Bass("TRN2", target_bir_lowering=False)
        self.P = Prog(nc)
        self.din = {}
        for k, shp in in_shapes.items():
            self.din[k] = nc.dram_tensor(k, list(shp), F32, kind="ExternalInput")
        self.out = nc.dram_tensor("out", [S, D], F32, kind="ExternalOutput")
        self.mixd = nc.dram_tensor("mixd", [S, 2048], BF16)
        self.Z = [nc.dram_tensor("Zs%d" % h, [132, 4096], BF16) for h in range(8)]
        self.Zw = [nc.dram_tensor("Zw%d" % h, [132, 4096], BF16) for h in range(8)]
        self.r_Z = [Reg() for _ in range(8)]
        self.r_Zw = [Reg() for _ in range(8)]
        self.r_Z2 = [None] * 8
        self.r_Zw2 = [None] * 8
        self.r_mixd = [Reg() for _ in range(16)]
        self.out_regs = []
        self.rw_store_regs = []
        self.dbg_out = {}
        self.es = ExitStack()

    @contextmanager
    def scope(self):
        with ExitStack() as st:
            yield st
        self.P.barrier()

    def sb(self, st, name, shape, dt=F32):
        self.uid = getattr(self, "uid", 0) + 1
        return st.enter_context(self.nc.sbuf_tensor("s%d_%s" % (self.uid, name), list(shape), dt))

    def ps(self, st, name, shape, dt=F32):
        self.uid = getattr(self, "uid", 0) + 1
        return st.enter_context(self.nc.psum_tensor("p%d_%s" % (self.uid, name), list(shape), dt))

    def dbg_dump(self, name, ap_src, shape, reg, dt=F32):
        if name not in self.dbg:
            return
        t = self.nc.dram_tensor("dbg_" + name, list(shape), dt, kind="ExternalOutput")
        self.dbg_out[name] = t
        ro = Reg()
        self.out_regs.append(ro)
        self.P.dma("sp", lambda e: e.dma_start(out=t.ap(), in_=ap_src), r=[reg] if not isinstance(reg, list) else reg, w=[ro])

    def build(self):
        nc, P = self.nc, self.P
        with self.scope() as st:
            self.ident = self.sb(st, "ident", [128, 128]); self.r_ident = Reg()
            self.identb = self.sb(st, "identb", [128, 128], BF16); self.r_identb = Reg()
            self.ones = self.sb(st, "ones", [128, 128]); self.r_ones = Reg()
            self.xT = self.sb(st, "xT", [128, NCH, S], BF16)
            self.r_xT = [Reg() for _ in range(16)]
            self.rstd_col = self.sb(st, "rstd_col", [128, 16]); self.r_rc = Reg()
            self.rstd_bc = self.sb(st, "rstd_bc", [128, S]); self.r_rb = Reg()
            self.gpre = self.sb(st, "gpre", [128, 16]); self.r_gpre = Reg()
            self.pm = [self.ps(st, "pm%d" % i, [128, 512]) for i in range(6)]
            self.r_pm = [Reg() for _ in range(6)]
            self.ptr = self.ps(st, "ptr", [128, 4, 128], BF16); self.r_ptr = Reg()
            self.px = self.ps(st, "px", [128, 512]); self.r_px = Reg()
            self.pm_i = 0

            P.dma("sp", lambda e: e.dma_start(out=self.ident[:], in_=self.din["c_ident"].ap()), w=[self.r_ident])
            P.op("dve", lambda e: e.tensor_copy(self.identb[:], self.ident[:]), r=[self.r_ident], w=[self.r_identb])
            P.op("pool", lambda e: e.memset(self.ones[:], 1.0), w=[self.r_ones])
            P.dma("sp", lambda e: e.dma_start(out=self.gpre[:], in_=self.din["g_pre"].ap()), w=[self.r_gpre])

            self.phase0(st)
            self.phase_eb()
            for g in range(2):
                with self.scope() as st2:
                    self.phase_nsa(st2, g)
            with self.scope() as st2:
                self.phase_rwkv_proj(st2)
        with self.scope() as st:
            self.pm = [self.ps(st, "rm%d" % i, [128, 512]) for i in range(6)]
            self.r_pm = [Reg() for _ in range(6)]
            self.px = self.ps(st, "rpx", [128, 512]); self.r_px = Reg()
            self.py2 = self.ps(st, "rpy2", [128, 512]); self.r_py2 = Reg()
            self.phase_rwkv(st)
        with self.scope() as st:
            self.pm = [self.ps(st, "qm%d" % i, [128, 512]) for i in range(6)]
            self.r_pm = [Reg() for _ in range(6)]
            self.ptr = self.ps(st, "qtr", [128, 4, 128], BF16); self.r_ptr = Reg()
            self.phase_out(st)
            P.final_wait("sp", self.out_regs)
            P.emit()
        return nc

    def next_pm(self):
        i = self.pm_i
        self.pm_i = (i + 1) % len(self.pm)
        return self.pm[i], self.r_pm[i]

    def phase0(self, st0):
        nc, P = self.nc, self.P
        x = self.din["x"].ap()
        with self.scope() as st:
            xt = [self.sb(st, "xt%d" % i, [128, D]) for i in range(2)]
            r_xt = [Reg(), Reg()]
            xb = [self.sb(st, "xb%d" % i, [128, D], BF16) for i in range(2)]
            r_xb = [Reg(), Reg()]
            junk = self.sb(st, "junk", [128, D]); r_junk = Reg()
            ss = self.sb(st, "ss", [128, 16]); r_ss = Reg()
            dg = self.sb(st, "dg", [128, 128]); r_dg = Reg()
            for tt in range(16):
                b = tt % 2
                P.dma("sp", lambda e, b=b, tt=tt: e.dma_start(out=xt[b][:], in_=x[tt * 128:(tt + 1) * 128, :]), w=[r_xt[b]])
                P.op("act", lambda e, b=b, tt=tt: e.activation(out=junk[:], in_=xt[b][:], func=AF.Square), r=[r_xt[b]], w=[r_junk])
                P.op("dve", lambda e, tt=tt: e.reduce_sum(out=ss[:, tt:tt + 1], in_=junk[:], axis=AX.X), r=[r_junk], w=[r_ss])
                P.op("pool", lambda e, b=b: e.tensor_copy(xb[b][:], xt[b][:]), r=[r_xt[b]], w=[r_xb[b]])
                for gq in range(4):
                    for jq in range(4):
                        c = gq * 4 + jq
                        P.op("pe", lambda e, b=b, c=c, jq=jq: e.transpose(self.ptr[:, jq, :], xb[b][:, c * 128:(c + 1) * 128], self.identb[:]),
                             r=[r_xb[b], self.r_identb], w=[self.r_ptr], signal=(jq == 3))
                    for jq in range(4):
                        c = gq * 4 + jq
                        P.op("dve", lambda e, c=c, jq=jq, tt=tt: e.tensor_scalar(
                            out=self.xT[:, c, tt * 128:(tt + 1) * 128], in0=self.ptr[:, jq, :],
                            scalar1=self.gpre[:, c:c + 1], scalar2=None, op0=ALU.mult),
                            r=[self.r_ptr, self.r_gpre], w=[self.r_xT[tt]])
            P.op("dve", lambda e: e.tensor_scalar(out=ss[:], in0=ss[:], scalar1=1.0 / D, scalar2=1e-6, op0=ALU.mult, op1=ALU.add),
                 r=[r_ss], w=[r_ss])
            P.op("act", lambda e: e.activation(out=ss[:], in_=ss[:], func=AF.Sqrt), r=[r_ss], w=[r_ss])
            P.op("dve", lambda e: e.reciprocal(self.rstd_col[:], ss[:]), r=[r_ss], w=[self.r_rc])
            for tt in range(16):
                P.op("dve", lambda e, tt=tt: e.tensor_scalar(out=dg[:], in0=self.ident[:], scalar1=self.rstd_col[:, tt:tt + 1],
                                                            scalar2=None, op0=ALU.mult), r=[self.r_ident, self.r_rc], w=[r_dg])
                P.op("pe", lambda e: e.matmul(self.px[:, 0:128], lhsT=self.ones[:], rhs=dg[:], start=True, stop=True),
                     r=[self.r_ones, r_dg], w=[self.r_px])
                P.op("act", lambda e, tt=tt: e.activation(out=self.rstd_bc[:, tt * 128:(tt + 1) * 128], in_=self.px[:, 0:128], func=AF.Copy),
                     r=[self.r_px], w=[self.r_rb])

        self.dbg_dump("rstd_col", self.rstd_col[:], [128, 16], self.r_rc)
        self.dbg_dump("rstd_bc", self.rstd_bc[:], [128, S], self.r_rb)
        self.dbg_dump("xT0", self.xT[:, 0, :], [128, S], self.r_xT, BF16)

    def load_w(self, wb, r_wb, src_ap):
        self.P.dma("pool", lambda e: e.dma_start(out=wb, in_=src_ap.rearrange("(c p) n -> p c n", p=128)), w=[r_wb])

    def proj_fm(self, wb, r_wb, j0, ncols, dst_fn, r_dst, scale=None, evac="dve"):
        P = self.P
        for tb in range(4):
            pm, r_pm = self.next_pm()
            for c in range(NCH):
                P.op("pe", lambda e, c=c, tb=tb, pm=pm: e.matmul(pm[0:ncols, :], lhsT=wb[:, c, j0:j0 + ncols],
                                                                 rhs=self.xT[:, c, tb * 512:(tb + 1) * 512],
                                                                 start=(c == 0), stop=(c == NCH - 1)),
                     r=[r_wb] + self.r_xT[tb * 4:tb * 4 + 4], w=[r_pm], signal=(c == NCH - 1))
            if scale is None:
                P.op("dve", lambda e, tb=tb, pm=pm: e.tensor_tensor(out=dst_fn(tb), in0=pm[0:ncols, :],
                                                                   in1=self.rstd_bc[0:ncols, tb * 512:(tb + 1) * 512], op=ALU.mult),
                     r=[r_pm, self.r_rb], w=[r_dst])
            else:
                P.op("dve", lambda e, tb=tb, pm=pm: e.scalar_tensor_tensor(out=dst_fn(tb), in0=pm[0:ncols, :], scalar=scale,
                                                                          in1=self.rstd_bc[0:ncols, tb * 512:(tb + 1) * 512],
                                                                          op0=ALU.mult, op1=ALU.mult),
                     r=[r_pm, self.r_rb], w=[r_dst])

    def proj_tm(self, wb, r_wb, j0, ncols, dst_fn, r_dst, func=AF.Copy):
        P = self.P
        for tt in range(16):
            pm, r_pm = self.next_pm()
            for c in range(NCH):
                P.op("pe", lambda e, c=c, tt=tt, pm=pm: e.matmul(pm[:, 0:ncols], lhsT=self.xT[:, c, tt * 128:(tt + 1) * 128],
                                                                 rhs=wb[:, c, j0:j0 + ncols], start=(c == 0), stop=(c == NCH - 1)),
                     r=[r_wb, self.r_xT[tt]], w=[r_pm], signal=(c == NCH - 1))
            P.op("act", lambda e, tt=tt, pm=pm: e.activation(out=dst_fn(tt), in_=pm[:, 0:ncols], func=func,
                                                            scale=self.rstd_col[:, tt:tt + 1]),
                 r=[r_pm, self.r_rc], w=[r_dst])

    def phase_eb(self):
        nc, P = self.nc, self.P
        with self.scope() as st:
            oh = self.sb(st, "oh", [32, 4096]); r_oh = Reg()
            tab = self.sb(st, "tab", [32, 8]); r_tab = Reg()
            zrow = [self.sb(st, "zrow%d" % i, [128, 4096], BF16) for i in range(2)]
            r_zrow = [Reg(), Reg()]
            P.dma("sp", lambda e: e.dma_start(out=oh[:], in_=self.din["c_oh"].ap()), w=[r_oh])
            P.dma("sp", lambda e: e.dma_start(out=tab[:], in_=self.din["tab"].ap()), w=[r_tab])
            tabrep = self.sb(st, "tabrep", [32, 8, 128]); r_tabrep = Reg()
            for h in range(8):
                P.op("dve", lambda e, h=h: e.tensor_scalar(out=tabrep[:, h, :], in0=self.ones[0:32, :], scalar1=tab[:, h:h + 1], scalar2=None,
                                                          op0=ALU.mult), r=[self.r_ones, r_tab], w=[r_tabrep])
            k = 0
            for h in range(8):
                for win in range(2):
                    zb = zrow[k % 2]; r_zb = r_zrow[k % 2]
                    k += 1
                    nblk = 1 if win else 4
                    if True:
                        P.op("pool", lambda e, zb=zb: e.memset(zb[:, 512 * nblk:], 0.0), w=[r_zb])
                    for blk in range(nblk):
                        pm, r_pm = self.next_pm()
                        P.op("pe", lambda e, h=h, blk=blk, pm=pm: e.matmul(pm[:, :], lhsT=tabrep[:, h, :],
                                                                          rhs=oh[:, blk * 512:(blk + 1) * 512], start=True, stop=True),
                             r=[r_tabrep, r_oh], w=[r_pm])
                        P.op("act", lambda e, blk=blk, pm=pm, zb=zb: e.activation(out=zb[:, blk * 512:(blk + 1) * 512], in_=pm[:, :], func=AF.Exp),
                             r=[r_pm], w=[r_zb])
                    dst = (self.Zw if win else self.Z)[h]
                    r_dst = (self.r_Zw if win else self.r_Z)[h]
                    P.dma("sp", lambda e, dst=dst, zb=zb: e.dma_start(out=dst.ap()[0:128, :], in_=zb[:]), r=[r_zb], w=[r_dst])
                    r_dst2 = Reg()
                    P.dma("sp", lambda e, dst=dst, zb=zb: e.dma_start(out=dst.ap()[128:132, :], in_=zb[0:4, :]), r=[r_zb], w=[r_dst2])
                    (self.r_Zw2 if win else self.r_Z2)[h] = r_dst2

    def toep(self, dst_ap, r_dst, Zt, r_Z, c, pstep, nparts, nfree, r_Z2=None):
        src = bass.AP(Zt, c % 4096, [[pstep, nparts], [1, nfree]])
        self.P.dma("pool", lambda e: e.dma_start(out=dst_ap, in_=src), r=[r_Z] + ([r_Z2] if r_Z2 is not None else []), w=[r_dst])

    def phase_nsa(self, st, g):
        nc, P = self.nc, self.P
        wsrc = self.din["w_nsa%d" % g].ap()
        qT = self.sb(st, "qT", [128, 4, S], BF16); r_qT = Reg()
        kT = self.sb(st, "kT", [128, 4, S], BF16); r_kT = [Reg() for _ in range(4)]
        vs = self.sb(st, "vs", [128, 16, 132], BF16); r_vs = Reg()
        vw = self.sb(st, "vw", [128, 16, 132], BF16); r_vw = Reg()
        gt = self.sb(st, "gt", [128, 16, 12]); r_gt = Reg()
        oacc = self.sb(st, "oacc", [128, 16, 512]); r_oacc = [Reg() for _ in range(16)]
        imp = self.sb(st, "imp", [128, 16, 32]); r_imp = [Reg() for _ in range(16)]
        negT = self.sb(st, "negT", [128, S], BF16); r_negT = Reg()
        e2c = self.sb(st, "e2c", [128, S], BF16); r_e2c = Reg()
        kcT = self.sb(st, "kcT", [128, 128], BF16); r_kcT = Reg()
        vce = self.sb(st, "vce", [128, 164], BF16); r_vce = Reg()
        P.op("pool", lambda e: e.memset(negT[:], 0.0), w=[r_negT])
        P.op("pool", lambda e: e.memset(e2c[:], 0.0), w=[r_e2c])

        with self.scope() as stw:
            wb = [self.sb(stw, "wbn%d" % i, [128, NCH, 512], BF16) for i in range(2)]
            r_wb = [Reg(), Reg()]
            self.load_w(wb[0][:], r_wb[0], wsrc[:, 0:512])
            self.load_w(wb[1][:], r_wb[1], wsrc[:, 512:1024])
            for h in range(4):
                self.proj_fm(wb[0], r_wb[0], h * 128, 128, lambda tb, h=h: qT[:, h, tb * 512:(tb + 1) * 512], r_qT, scale=128.0 ** -0.5)
            for i in range(4):
                self.proj_fm(wb[1], r_wb[1], i * 128, 128, lambda tb, i=i: kT[:, i, tb * 512:(tb + 1) * 512], r_kT[i])
            self.load_w(wb[0][:, :, 0:268], r_wb[0], wsrc[:, 1024:1292])
            P.op("pool", lambda e: e.memset(vs[:, :, 128:132], 1.0), w=[r_vs])
            P.op("pool", lambda e: e.memset(vw[:, :, 128:132], 1.0), w=[r_vw])
            self.proj_tm(wb[0], r_wb[0], 0, 128, lambda tt: vs[:, tt, 0:128], r_vs)
            self.proj_tm(wb[0], r_wb[0], 128, 128, lambda tt: vw[:, tt, 0:128], r_vw)
            self.proj_tm(wb[0], r_wb[0], 256, 12, lambda tt: gt[:, tt, :], r_gt, func=AF.Sigmoid)

        self.dbg_dump("qT%d" % g, qT[:, 0, :], [128, S], r_qT, BF16)
        self.dbg_dump("kT%d" % g, kT[:, 0, :], [128, S], r_kT[0], BF16)
        self.dbg_dump("vs%d" % g, vs[:, 0, :], [128, 132], r_vs, BF16)
        self.dbg_dump("gt%d" % g, gt[:, 0, :], [128, 12], r_gt)
        with self.scope() as st2:
            e2f = self.sb(st2, "e2f", [32, S]); r_e2f = Reg()
            P.dma("sp", lambda e: e.dma_start(out=e2f[:], in_=self.din["c_e2"].ap()), w=[r_e2f])
            P.op("dve", lambda e: e.tensor_copy(e2c[0:32, :], e2f[:]), r=[r_e2f], w=[r_e2c])

        with self.scope() as st2:
            w1 = self.sb(st2, "w1", [128, 32, 128], BF16); r_w1 = Reg()
            w2 = self.sb(st2, "w2", [128, 128], BF16); r_w2 = Reg()
            posT = self.sb(st2, "posT", [128, 32], BF16); r_posT = Reg()
            cb = self.sb(st2, "cb", [128, 1]); r_cb = Reg()
            h1s = self.sb(st2, "h1s", [128, 128], BF16); r_h1s = Reg()
            ovf = self.sb(st2, "ovf", [128, 33]); r_ovf = Reg()
            P.op("pool", lambda e: e.memset(ovf[:, 0:1], 1.0), w=[r_ovf])
            P.dma("sp", lambda e: e.dma_start(out=ovf[0:127, 1:33], in_=self.din["c_ov"].ap()), w=[r_ovf])
            P.op("dve", lambda e: e.tensor_copy(vce[0:127, 128:161], ovf[0:127, :]), r=[r_ovf], w=[r_vce])
            for which in range(2):
                nm = "kv"[which]
                P.dma("pool", lambda e, nm=nm: e.dma_start(out=w1[:], in_=self.din["w1" + nm].ap()), w=[r_w1])
                P.dma("pool", lambda e, nm=nm: e.dma_start(out=w2[:], in_=self.din["w2" + nm].ap()), w=[r_w2])
                P.dma("pool", lambda e, nm=nm: e.dma_start(out=posT[:], in_=self.din["pos%sT" % nm].ap()), w=[r_posT])
                pm, r_pm = self.next_pm()
                for l in range(32):
                    P.op("pe", lambda e, l=l, pm=pm: e.matmul(pm[:, 0:1], lhsT=w1[:, l, :], rhs=posT[:, l:l + 1], start=(l == 0), stop=(l == 31)),
                         r=[r_w1, r_posT], w=[r_pm], signal=(l == 31))
                P.op("dve", lambda e, pm=pm: e.tensor_copy(cb[:], pm[:, 0:1]), r=[r_pm], w=[r_cb])
                pm, r_pm = self.next_pm()
                for l in range(32):
                    P.op("pe", lambda e, l=l, pm=pm, which=which: e.matmul(pm[:, 0:127], lhsT=w1[:, l, :],
                                                                          rhs=kT[:, which, l:l + 16 * 126 + 1:16],
                                                                          start=(l == 0), stop=(l == 31)),
                         r=[r_w1, r_kT[which]], w=[r_pm], signal=(l == 31))
                P.op("act", lambda e, pm=pm: e.activation(out=h1s[:, 0:127], in_=pm[:, 0:127], func=AF.Silu, bias=cb[:, 0:1]),
                     r=[r_pm, r_cb], w=[r_h1s])
                pm, r_pm = self.next_pm()
                if which == 0:
                    P.op("pe", lambda e, pm=pm: e.matmul(pm[:, 0:127], lhsT=w2[:], rhs=h1s[:, 0:127], start=True, stop=True),
                         r=[r_w2, r_h1s], w=[r_pm])
                    P.op("dve", lambda e, pm=pm: e.tensor_copy(kcT[:, 0:127], pm[:, 0:127]), r=[r_pm], w=[r_kcT])
                else:
                    P.op("pe", lambda e, pm=pm: e.matmul(pm[0:127, 0:128], lhsT=h1s[:, 0:127], rhs=w2[:], start=True, stop=True),
                         r=[r_w2, r_h1s], w=[r_pm])
                    P.op("dve", lambda e, pm=pm: e.tensor_copy(vce[0:127, 0:128], pm[0:127, 0:128]), r=[r_pm], w=[r_vce])

        with self.scope() as sta:
            e1 = [self.sb(sta, "e1_%d" % i, [128, 512], BF16) for i in range(2)]
            r_e1 = [Reg(), Reg()]
            e2 = self.sb(sta, "e2", [128, 16, 512], BF16); r_e2 = [Reg() for _ in range(16)]
            sm = self.sb(sta, "sm", [128, 4, 4]); r_sm = [Reg() for _ in range(4)]
            EBc = self.sb(sta, "EBc", [128, S], BF16); r_EBc = Reg()
            EBs = self.sb(sta, "EBs", [128, 16, 512], BF16); r_EBs = Reg()
            EBw = self.sb(sta, "EBw", [128, 8, 512], BF16); r_EBw = Reg()
            ei = 0

            def gate_col(br, h):
                return br * 4 + h

            for h in range(4):
                hd = g * 4 + h
                self.toep(EBc[0:127, :], r_EBc, self.Z[hd], self.r_Z[hd], 4096 - 31, 4096 - 16, 127, S, self.r_Z2[hd])
                for Q in range(4):
                    pm, r_pm = self.next_pm()
                    P.op("pe", lambda e, pm=pm, h=h, Q=Q: e.matmul(pm[0:127, :], lhsT=kcT[:, 0:127], rhs=qT[:, h, Q * 512:(Q + 1) * 512],
                                                                  start=True, stop=True), r=[r_kcT, r_qT], w=[r_pm])
                    eb = e1[ei % 2]; r_eb = r_e1[ei % 2]; ei += 1
                    P.op("act", lambda e, pm=pm, eb=eb: e.activation(out=eb[0:127, :], in_=pm[0:127, :], func=AF.Exp), r=[r_pm], w=[r_eb])
                    P.op("dve", lambda e, eb=eb, Q=Q: e.tensor_tensor(out=e2[0:127, 0, :], in0=eb[0:127, :], in1=EBc[0:127, Q * 512:(Q + 1) * 512],
                                                                     op=ALU.mult), r=[r_eb, r_EBc], w=[r_e2[0]])
                    pms = []
                    for sq in range(4):
                        pm, r_pm = self.next_pm()
                        pms.append((pm, r_pm))
                        P.op("pe", lambda e, pm=pm, sq=sq: e.matmul(pm[:, 0:161], lhsT=e2[0:127, 0, sq * 128:(sq + 1) * 128], rhs=vce[0:127, 0:161],
                                                                   start=True, stop=True), r=[r_e2[0], r_vce], w=[r_pm])
                    for sq in range(4):
                        pm, r_pm = pms[sq]
                        P.op("dve", lambda e, pm=pm, sq=sq: e.tensor_scalar(out=sm[:, sq, 0:1], in0=pm[:, 128:129], scalar1=1e-30, scalar2=None, op0=ALU.max),
                             r=[r_pm], w=[r_sm[sq]])
                    P.op("dve", lambda e: e.reciprocal(sm[:, :, 1], sm[:, :, 0]), r=list(r_sm), w=list(r_sm))
                    P.op("dve", lambda e, h=h, Q=Q: e.tensor_tensor(out=sm[:, :, 2], in0=sm[:, :, 1], in1=gt[:, Q * 4:(Q + 1) * 4, gate_col(0, h)],
                                                                   op=ALU.mult), r=list(r_sm) + [r_gt], w=list(r_sm))
                    for sq in range(4):
                        T = Q * 4 + sq
                        pm, r_pm = pms[sq]
                        P.op("dve", lambda e, pm=pm, T=T, h=h, sq=sq: e.tensor_scalar(out=oacc[:, T, h * 128:(h + 1) * 128], in0=pm[:, 0:128],
                                                                                     scalar1=sm[:, sq, 2:3], scalar2=None, op0=ALU.mult),
                             r=[r_pm, r_sm[sq]], w=[r_oacc[T]])
                    for sq in range(4):
                        T = Q * 4 + sq
                        pm, r_pm = pms[sq]
                        if h == 0:
                            P.op("dve", lambda e, pm=pm, T=T, sq=sq: e.tensor_scalar(out=imp[:, T, :], in0=pm[:, 129:161], scalar1=sm[:, sq, 1:2], scalar2=None,
                                                                                    op0=ALU.mult), r=[r_pm, r_sm[sq]], w=[r_imp[T]])
                        else:
                            P.op("dve", lambda e, pm=pm, T=T, sq=sq: e.scalar_tensor_tensor(out=imp[:, T, :], in0=pm[:, 129:161], scalar=sm[:, sq, 1:2],
                                                                                           in1=imp[:, T, :], op0=ALU.mult, op1=ALU.add),
                                 r=[r_pm, r_sm[sq], r_imp[T]], w=[r_imp[T]])
            self.dbg_dump("imp%d" % g, imp[:], [128, 16, 32], r_imp[15])

            self.dbg_dump("kcT%d" % g, kcT[:], [128, 128], r_kcT, BF16)
            self.dbg_dump("vce%d" % g, vce[:], [128, 164], r_vce, BF16)
            self.dbg_dump("oaccc%d" % g, oacc[:, 0, :], [128, 512], r_oacc[0])
            with self.scope() as st2x:
                sc = imp
                r_sc = r_imp
                sc2 = self.sb(st2x, "sc2", [128, 16, 32]); r_sc2 = [Reg() for _ in range(16)]
                m8 = self.sb(st2x, "m8", [128, 16, 16]); r_m8 = [Reg() for _ in range(16)]
                P.dma("sp", lambda e: e.dma_start(out=sc2[:], in_=self.din["c_m1"].ap()), w=r_sc2)
                P.op("dve", lambda e: e.tensor_tensor(out=sc[:], in0=imp[:], in1=sc2[:], op=ALU.mult), r=list(r_imp) + list(r_sc2), w=list(r_sc))
                P.dma("sp", lambda e: e.dma_start(out=sc2[:], in_=self.din["c_add"].ap()), r=r_sc2, w=r_sc2)
                P.op("dve", lambda e: e.tensor_tensor(out=sc[:], in0=sc[:], in1=sc2[:], op=ALU.add), r=list(r_sc) + list(r_sc2), w=list(r_sc))
                for T in range(16):
                    P.op("dve", lambda e, T=T: e.max(out=m8[:, T, 0:8], in_=sc[:, T, :]), r=[r_sc[T]], w=[r_m8[T]])
                for T in range(16):
                    P.op("dve", lambda e, T=T: e.match_replace(out=sc2[:, T, :], in_to_replace=m8[:, T, 0:8], in_values=sc[:, T, :], imm_value=-3.0e38),
                         r=[r_sc[T], r_m8[T]], w=[r_sc2[T]])
                for T in range(16):
                    P.op("dve", lambda e, T=T: e.max(out=m8[:, T, 8:16], in_=sc2[:, T, :]), r=[r_sc2[T]], w=[r_m8[T]])
                P.op("dve", lambda e: e.tensor_tensor(out=sc2[:], in0=sc[:], in1=m8[:, :, 15:16].to_broadcast([128, 16, 32]), op=ALU.is_ge),
                     r=list(r_sc) + list(r_m8), w=list(r_sc2))
                P.op("dve", lambda e: e.tensor_scalar(out=sc2[:], in0=sc2[:], scalar1=-1.0, scalar2=30000.0, op0=ALU.add, op1=ALU.mult),
                     r=list(r_sc2), w=list(r_sc2))
                for T4 in range(4):
                    pm, r_pm = self.next_pm()
                    for j4 in range(4):
                        T = T4 * 4 + j4
                        P.op("pe", lambda e, pm=pm, T=T, j4=j4: e.transpose(pm[0:32, j4 * 128:(j4 + 1) * 128], sc2[:, T, :], self.ident[:]),
                             r=[r_sc2[T], self.r_ident], w=[r_pm], signal=(j4 == 3))
                    P.op("act", lambda e, pm=pm, T4=T4: e.activation(out=negT[0:32, T4 * 512:(T4 + 1) * 512], in_=pm[0:32, :], func=AF.Copy),
                         r=[r_pm], w=[r_negT])
            self.dbg_dump("negT%d" % g, negT[0:32, :], [32, S], r_negT, BF16)

            def load_eb(hh, which):
                hd_ = g * 4 + hh
                if which == 1:
                    src_s = bass.AP(self.Z[hd_], 4096 - 384, [[4095, 128], [128, 16], [1, 512]])
                    P.dma("pool", lambda e, src_s=src_s: e.dma_start(out=EBs[:], in_=src_s), r=[self.r_Z[hd_], self.r_Z2[hd_]], w=[r_EBs])
                else:
                    src_w = bass.AP(self.Zw[hd_], 4096 - 384, [[4095, 128], [128, 8], [1, 512]])
                    P.dma("pool", lambda e, src_w=src_w: e.dma_start(out=EBw[:], in_=src_w), r=[self.r_Zw[hd_], self.r_Zw2[hd_]], w=[r_EBw])

            load_eb(0, 1)
            load_eb(0, 2)
            for h in range(4):
                hd = g * 4 + h
                for br in (1, 2):
                    if br == 2 and h < 3:
                        load_eb(h + 1, 1)
                    for Q in range(4):
                        kt_lo = 0 if br == 1 else max(0, 4 * Q - 4)
                        kt_hi = 4 * Q + 3
                        kidx = 2 if br == 1 else 3
                        for kt in range(kt_lo, kt_hi + 1):
                            pm, r_pm = self.next_pm()
                            P.op("pe", lambda e, pm=pm, kt=kt, h=h, Q=Q, kidx=kidx, br=br: e.matmul(
                                pm[:, :], lhsT=kT[:, kidx, kt * 128:(kt + 1) * 128], rhs=qT[:, h, Q * 512:(Q + 1) * 512],
                                start=True, stop=(br == 2)), r=[r_kT[kidx], r_qT], w=[r_pm], signal=(br == 2))
                            if br == 1:
                                P.op("pe", lambda e, pm=pm, kt=kt, Q=Q: e.matmul(pm[:, :], lhsT=e2c[:, kt * 128:(kt + 1) * 128],
                                                                                rhs=negT[:, Q * 512:(Q + 1) * 512], start=False, stop=True),
                                     r=[r_e2c, r_negT], w=[r_pm])
                            eb = e1[ei % 2]; r_eb = r_e1[ei % 2]; ei += 1
                            P.op("act", lambda e, pm=pm, eb=eb: e.activation(out=eb[:], in_=pm[:, :], func=AF.Exp), r=[r_pm], w=[r_eb])
                            o = 4 * Q - kt + 3
                            EB = EBs if br == 1 else EBw
                            r_EB = r_EBs if br == 1 else r_EBw
                            P.op("dve", lambda e, eb=eb, kt=kt, o=o, EB=EB: e.tensor_tensor(out=e2[:, kt, :], in0=eb[:], in1=EB[:, o, :], op=ALU.mult),
                                 r=[r_eb, r_EB], w=[r_e2[kt]])
                        vv = vs if br == 1 else vw
                        r_vv = r_vs if br == 1 else r_vw
                        pms = []
                        for sq in range(4):
                            T = Q * 4 + sq
                            lo = 0 if br == 1 else max(0, T - 4)
                            hi = T
                            pm, r_pm = self.next_pm()
                            pms.append((pm, r_pm))
                            for kt in range(lo, hi + 1):
                                P.op("pe", lambda e, pm=pm, kt=kt, sq=sq, vv=vv, lo=lo, hi=hi: e.matmul(
                                    pm[:, 0:129], lhsT=e2[:, kt, sq * 128:(sq + 1) * 128], rhs=vv[:, kt, 0:129],
                                    start=(kt == lo), stop=(kt == hi)), r=[r_e2[kt], r_vv], w=[r_pm], signal=(kt == hi))
                        for sq in range(4):
                            pm, r_pm = pms[sq]
                            P.op("dve", lambda e, pm=pm, sq=sq: e.tensor_scalar(out=sm[:, sq, 0:1], in0=pm[:, 128:129], scalar1=1e-30, scalar2=None, op0=ALU.max),
                                 r=[r_pm], w=[r_sm[sq]])
                        P.op("dve", lambda e: e.reciprocal(sm[:, :, 1], sm[:, :, 0]), r=list(r_sm), w=list(r_sm))
                        P.op("dve", lambda e, h=h, br=br, Q=Q: e.tensor_tensor(out=sm[:, :, 2], in0=sm[:, :, 1], in1=gt[:, Q * 4:(Q + 1) * 4, gate_col(br, h)],
                                                                              op=ALU.mult), r=list(r_sm) + [r_gt], w=list(r_sm))
                        for sq in range(4):
                            T = Q * 4 + sq
                            pm, r_pm = pms[sq]
                            P.op("dve", lambda e, pm=pm, T=T, h=h, sq=sq: e.scalar_tensor_tensor(
                                out=oacc[:, T, h * 128:(h + 1) * 128], in0=pm[:, 0:128], scalar=sm[:, sq, 2:3],
                                in1=oacc[:, T, h * 128:(h + 1) * 128], op0=ALU.mult, op1=ALU.add),
                                r=[r_pm, r_sm[sq], r_oacc[T]], w=[r_oacc[T]])
                    if br == 2 and h < 3:
                        load_eb(h + 1, 2)

        with self.scope() as stf:
            wbz = self.sb(stf, "wbz", [128, NCH, 512], BF16); r_wbz = Reg()
            za = self.sb(stf, "za", [128, 16, 512], BF16); r_za = Reg()
            mixb = [self.sb(stf, "mixb%d" % i, [128, 512], BF16) for i in range(2)]
            r_mixb = [Reg(), Reg()]
            self.load_w(wbz[:], r_wbz, wsrc[:, 1292:1804])
            self.proj_tm(wbz, r_wbz, 0, 512, lambda tt: za[:, tt, :], r_za, func=AF.Silu)
            for T in range(16):
                b = T % 2
                P.op("dve", lambda e, T=T, b=b: e.tensor_tensor(out=mixb[b][:], in0=oacc[:, T, :], in1=za[:, T, :], op=ALU.mult),
                     r=[r_oacc[T], r_za], w=[r_mixb[b]])
                P.dma("sp", lambda e, T=T, b=b: e.dma_start(out=self.mixd.ap()[T * 128:(T + 1) * 128, g * 512:(g + 1) * 512], in_=mixb[b][:]),
                      r=[r_mixb[b]], w=[self.r_mixd[T]])
        if g == 1:
            self.dbg_dump("mixa", self.mixd.ap()[:, 0:1024], [S, 1024], list(self.r_mixd), BF16)

    def phase_rwkv_proj(self, st):
        nc, P = self.nc, self.P
        din = self.din
        self.rawd = nc.dram_tensor("rawd", [3, 1024, S + 1], F32)
        self.zbd = nc.dram_tensor("zbd", [S, 1024], BF16)
        self.lorad = nc.dram_tensor("lorad", [2, 64, S], F32)
        self.rw_regs = []
        wb = [self.sb(st, "wbp%d" % i, [128, NCH, 512], BF16) for i in range(2)]
        r_wb = [Reg(), Reg()]
        stg = [self.sb(st, "stg%d" % i, [128, S + 4]) for i in range(2)]
        r_stg = [Reg(), Reg()]
        for i in range(2):
            P.op("pool", lambda e, i=i: e.memset(stg[i][:, 0:1], 0.0), w=[r_stg[i]])
        k = 0
        for blk in range(6):
            b = blk % 2
            self.load_w(wb[b][:], r_wb[b], din["w_rkv"].ap()[:, blk * 512:(blk + 1) * 512])
            for j in range(4):
                ct = blk * 4 + j
                sg_, r_sg = stg[k % 2], r_stg[k % 2]
                k += 1
                self.proj_fm(wb[b], r_wb[b], j * 128, 128, lambda tb, sg_=sg_: sg_[:, 1 + tb * 512:1 + (tb + 1) * 512], r_sg)
                rr = Reg(); self.rw_regs.append(rr)
                P.dma("sp", lambda e, ct=ct, sg_=sg_: e.dma_start(out=self.rawd.ap()[ct // 8, (ct % 8) * 128:(ct % 8 + 1) * 128, 0:S + 1], in_=sg_[:, 0:S + 1]),
                      r=[r_sg], w=[rr])
        zst = [self.sb(st, "zst%d" % i, [128, 16, 512], BF16) for i in range(2)]
        r_zst = [Reg(), Reg()]
        for blk in range(2):
            self.load_w(wb[blk][:], r_wb[blk], din["w_zb"].ap()[:, blk * 512:(blk + 1) * 512])
            self.proj_tm(wb[blk], r_wb[blk], 0, 512, lambda tt, blk=blk: zst[blk][:, tt, :], r_zst[blk], func=AF.Silu)
            rr = Reg(); self.rw_regs.append(rr)
            P.dma("sp", lambda e, blk=blk: e.dma_start(out=self.zbd.ap()[:, blk * 512:(blk + 1) * 512].rearrange("(t p) c -> p t c", p=128),
                                                       in_=zst[blk][:]), r=[r_zst[blk]], w=[rr])
        mul = self.sb(st, "mul", [64, 2]); r_mul = Reg()
        P.dma("sp", lambda e: e.dma_start(out=mul[:], in_=din["mu_lora"].ap()), w=[r_mul])
        wbl = self.sb(st, "wbl", [128, NCH, 128], BF16); r_wbl = Reg()
        self.load_w(wbl[:], r_wbl, din["w_lora"].ap())
        raw = self.sb(st, "lraw", [64, S + 4]); r_raw = Reg()
        tmp = self.sb(st, "ltmp", [64, S]); r_tmp = Reg()
        lo = [self.sb(st, "lo%d" % i, [64, S]) for i in range(2)]; r_lo = [Reg(), Reg()]
        for i in range(2):
            P.op("pool", lambda e: e.memset(raw[:, 0:1], 0.0), w=[r_raw])
            self.proj_fm(wbl, r_wbl, i * 64, 64, lambda tb: raw[:, 1 + tb * 512:1 + (tb + 1) * 512], r_raw)
            P.op("dve", lambda e: e.tensor_tensor(out=tmp[:], in0=raw[:, 0:S], in1=raw[:, 1:S + 1], op=ALU.subtract), r=[r_raw], w=[r_tmp])
            P.op("dve", lambda e, i=i: e.scalar_tensor_tensor(out=lo[i][:], in0=tmp[:], scalar=mul[:, i:i + 1], in1=raw[:, 1:S + 1],
                                                             op0=ALU.mult, op1=ALU.add), r=[r_tmp, r_raw, r_mul], w=[r_lo[i]])
            if i == 0:
                P.op("act", lambda e: e.activation(out=lo[0][:], in_=lo[0][:], func=AF.Tanh), r=[r_lo[0]], w=[r_lo[0]])
            rr = Reg(); self.rw_regs.append(rr)
            P.dma("sp", lambda e, i=i: e.dma_start(out=self.lorad.ap()[i], in_=lo[i][:]), r=[r_lo[i]], w=[rr])

    def phase_rwkv(self, st):
        nc, P = self.nc, self.P
        din = self.din
        NT = 512
        F32R = mybir.dt.float32r

        def fr(ap):
            return ap.bitcast(F32R) if USE_F32R else ap

        def ld(name, shape, src, dt=F32, q="sp", r=()):
            t = self.sb(st, name, shape, dt)
            rg = Reg()
            P.dma(q, lambda e: e.dma_start(out=t[:], in_=src), r=list(r), w=[rg])
            return t, rg

        ident, r_ident = ld("identr", [128, 128], din["c_ident"].ap())
        ones = self.sb(st, "onesr", [64, 64]); r_ones = Reg()
        P.op("pool", lambda e: e.memset(ones[:], 1.0), w=[r_ones])
        msc, r_msc = ld("msc", [128, 512], din["c_mask_sc2"].ap())
        mt, r_mt = ld("mt", [128, 512], din["c_mask_t8"].ap())
        idr, r_idr = ld("idr", [128, 512], din["c_ident8"].ap())
        cmask, r_cmask = ld("cmask", [64, NT], din["c_cmask"].ap()[:, 0:NT])
        per, r_per = ld("per", [64, 16, 8], din["rw_per"].ap())
        w2, r_w2 = ld("rw2", [64, 1024], din["rw_w2"].ap())
        a2, r_a2 = ld("ra2", [64, 1024], din["rw_a2"].ap())
        twd, r_twd = ld("twd", [64, S], self.lorad.ap()[0], r=self.rw_regs)
        ads, r_ads = ld("ads", [64, S], self.lorad.ap()[1], r=self.rw_regs)
        omka = self.sb(st, "omka", [64, 16]); r_omka = Reg()
        P.op("dve", lambda e: e.tensor_scalar(out=omka[:], in0=per[:, :, 6], scalar1=-1.0, scalar2=1.0, op0=ALU.mult, op1=ALU.add),
             r=[r_per], w=[r_omka])

        def v3(ap):
            return ap.rearrange("p (c t) -> p c t", t=128)

        def make_set(si, pY, r_pY):
            sfx = "_%d" % si
            my_pm = self.pm[si * 3:(si + 1) * 3]
            my_rpm = self.r_pm[si * 3:(si + 1) * 3]
            rot = {"i": 0}

            def next_pm():
                i = rot["i"]
                rot["i"] = (i + 1) % 3
                return my_pm[i], my_rpm[i]

            Zst = self.sb(st, "Zst" + sfx, [128, 64]); r_Z = Reg()
            lnw = self.sb(st, "lnw" + sfx, [128, 64]); r_lnw = Reg()
            lnb = self.sb(st, "lnb" + sfx, [128, 64]); r_lnb = Reg()
            raws = self.sb(st, "raws" + sfx, [64, 3, NT + 4]); r_raws = Reg()
            names = ["r_s", "k_s", "v_s", "sg", "al", "kk", "k2", "Lp", "t0", "t1", "t2", "rinv", "bon"]
            Tt = {n: self.sb(st, "T_" + n + sfx, [64, NT]) for n in names}
            R = {n: Reg() for n in names}
            AR = self.sb(st, "AR" + sfx, [128, 4, 2, 128]); r_AR = Reg()
            BK = self.sb(st, "BK" + sfx, [64, 4, 2, 128]); r_BK = Reg()
            BKh = self.sb(st, "BKh" + sfx, [64, 4, 2, 128]); r_BKh = Reg()
            WC = self.sb(st, "WC" + sfx, [64, 4]); r_WC = Reg()
            TM = self.sb(st, "TM" + sfx, [128, 4, 3, 64]); r_TM = Reg()
            SC = self.sb(st, "SC" + sfx, [128, 4, 512]); r_SC = Reg()
            PP = [self.sb(st, "PP%d" % i + sfx, [128, 512]) for i in range(2)]; r_PP = [Reg(), Reg()]
            PT = [self.sb(st, "PTt%d" % i + sfx, [128, 512]) for i in range(2)]; r_PT = [Reg(), Reg()]
            Tm = self.sb(st, "Tm" + sfx, [128, 512]); r_Tm = Reg()
            XTs = self.sb(st, "XTs" + sfx, [128, 64]); r_XTs = Reg()
            UTs = self.sb(st, "UTs" + sfx, [128, 64]); r_UTs = Reg()
            Yt = self.sb(st, "Yt" + sfx, [128, 4, 64]); r_Yt = Reg()
            yc = self.sb(st, "yc" + sfx, [128, 4, 64]); r_yc = Reg()
            ysq = self.sb(st, "ysq" + sfx, [128, 4, 64]); r_ysq = Reg()
            st4 = self.sb(st, "st4" + sfx, [128, 16]); r_st4 = Reg()
            zb = self.sb(st, "zb" + sfx, [128, 4, 64], BF16); r_zb = Reg()
            mixo = [self.sb(st, "mixo%d" % i + sfx, [128, 4, 64], BF16) for i in range(2)]; r_mixo = [Reg(), Reg()]
            state = {"mi": 0}

            def run(h):
                P.dma("sp", lambda e: e.dma_start(out=lnw[:], in_=din["ln_w"].ap()[:, h * 64:(h + 1) * 64]), w=[r_lnw])
                P.dma("sp", lambda e: e.dma_start(out=lnb[:], in_=din["ln_b"].ap()[:, h * 64:(h + 1) * 64]), w=[r_lnb])
                P.op("dve", lambda e: e.tensor_scalar(out=fr(Zst[:]), in0=mt[:, 0:64], scalar1=0.0, scalar2=None, op0=ALU.mult), r=[r_mt], w=[r_Z])
                if not state.get("ar_init"):
                    state["ar_init"] = True
                    for a_ in range(2):
                        P.op("dve", lambda e, a_=a_: e.tensor_scalar(out=fr(AR[:, a_ * 2:a_ * 2 + 2, :, :].rearrange("p a b c -> p (a b c)")), in0=mt[:],
                                                                    scalar1=0.0, scalar2=None, op0=ALU.mult), r=[r_mt], w=[r_AR])
                for G in range(4):
                    t0g = G * NT
                    P.dma("sp", lambda e, G=G: e.dma_start(out=raws[:, :, 0:NT + 1],
                                                           in_=self.rawd.ap()[:, h * 64:(h + 1) * 64, G * NT:G * NT + NT + 1].rearrange("i c t -> c i t")),
                          r=self.rw_regs, w=[r_raws])
                    P.dma("sp", lambda e, G=G: e.dma_start(out=zb[:], in_=self.zbd.ap()[G * NT:(G + 1) * NT, h * 64:(h + 1) * 64].rearrange("(t p) c -> p t c", p=128)),
                          r=self.rw_regs, w=[r_zb])
                    for i in range(3):
                        nm = ("r_s", "k_s", "v_s")[i]
                        P.op("dve", lambda e, i=i: e.tensor_tensor(out=Tt["t0"][:], in0=raws[:, i, 0:NT], in1=raws[:, i, 1:NT + 1], op=ALU.subtract),
                             r=[r_raws], w=[R["t0"]])
                        P.op("dve", lambda e, i=i, nm=nm: e.scalar_tensor_tensor(out=Tt[nm][:], in0=Tt["t0"][:], scalar=per[:, h, i:i + 1],
                                                                                in1=raws[:, i, 1:NT + 1], op0=ALU.mult, op1=ALU.add),
                             r=[R["t0"], r_raws, r_per], w=[R[nm]])
                        yield
                    for (wmat, src, r_src, dst, bcol) in ((w2, twd, r_twd, "sg", 3), (a2, ads, r_ads, "al", 4)):
                        pm, r_pm = next_pm()
                        P.op("pe", lambda e, pm=pm, wmat=wmat, src=src, h=h, t0g=t0g: e.matmul(
                            pm[0:64, :], lhsT=wmat[:, h * 64:(h + 1) * 64], rhs=src[:, t0g:t0g + NT], start=True, stop=True),
                            r=[r_w2, r_a2, r_src], w=[r_pm])
                        P.op("act", lambda e, pm=pm, dst=dst, bcol=bcol, h=h: e.activation(out=Tt[dst][:], in_=pm[0:64, :], func=AF.Sigmoid,
                                                                                           bias=per[:, h, bcol:bcol + 1]),
                             r=[r_pm, r_per], w=[R[dst]])
                    P.op("act", lambda e, h=h: e.activation(out=Tt["t1"][:], in_=Tt["k_s"][:], func=AF.Square, scale=per[:, h, 5:6]),
                         r=[R["k_s"], r_per], w=[R["t1"]])
                    pm, r_pm = next_pm()
                    P.op("pe", lambda e, pm=pm: e.matmul(pm[0:64, :], lhsT=ones[:], rhs=Tt["t1"][:], start=True, stop=True),
                         r=[r_ones, R["t1"]], w=[r_pm])
                    P.op("act", lambda e, pm=pm: e.activation(out=Tt["rinv"][:], in_=pm[0:64, :], func=AF.Sqrt), r=[r_pm], w=[R["rinv"]])
                    yield
                    P.op("dve", lambda e: e.tensor_scalar(out=Tt["rinv"][:], in0=Tt["rinv"][:], scalar1=1e-12, scalar2=None, op0=ALU.max),
                         r=[R["rinv"]], w=[R["rinv"]])
                    P.op("dve", lambda e: e.reciprocal(Tt["rinv"][:], Tt["rinv"][:]), r=[R["rinv"]], w=[R["rinv"]])
                    P.op("dve", lambda e, h=h: e.scalar_tensor_tensor(out=Tt["kk"][:], in0=Tt["k_s"][:], scalar=per[:, h, 5:6], in1=Tt["rinv"][:],
                                                                     op0=ALU.mult, op1=ALU.mult), r=[R["k_s"], R["rinv"], r_per], w=[R["kk"]])
                    yield
                    P.op("dve", lambda e, h=h: e.tensor_scalar(out=Tt["t1"][:], in0=Tt["al"][:], scalar1=per[:, h, 6:7], scalar2=omka[:, h:h + 1],
                                                              op0=ALU.mult, op1=ALU.add), r=[R["al"], r_per, r_omka], w=[R["t1"]])
                    P.op("dve", lambda e: e.tensor_tensor(out=Tt["k2"][:], in0=Tt["k_s"][:], in1=Tt["t1"][:], op=ALU.mult),
                         r=[R["k_s"], R["t1"]], w=[R["k2"]])
                    P.op("dve", lambda e, h=h: e.scalar_tensor_tensor(out=Tt["t1"][:], in0=Tt["r_s"][:], scalar=per[:, h, 7:8], in1=Tt["k2"][:],
                                                                     op0=ALU.mult, op1=ALU.mult), r=[R["r_s"], R["k2"], r_per], w=[R["t1"]])
                    yield
                    pm, r_pm = next_pm()
                    P.op("pe", lambda e, pm=pm: e.matmul(pm[0:64, :], lhsT=ones[:], rhs=Tt["t1"][:], start=True, stop=True),
                         r=[r_ones, R["t1"]], w=[r_pm])
                    P.op("dve", lambda e, pm=pm: e.tensor_tensor(out=Tt["bon"][:], in0=pm[0:64, :], in1=Tt["v_s"][:], op=ALU.mult),
                         r=[r_pm, R["v_s"]], w=[R["bon"]])
                    P.op("dve", lambda e: e.tensor_tensor_scan(out=Tt["Lp"][:], data0=cmask[:], data1=Tt["sg"][:], initial=0.0,
                                                              op0=ALU.mult, op1=ALU.add), r=[r_cmask, R["sg"]], w=[R["Lp"]])
                    P.op("dve", lambda e: e.tensor_tensor(out=Tt["t1"][:], in0=Tt["Lp"][:], in1=Tt["sg"][:], op=ALU.subtract),
                         r=[R["Lp"], R["sg"]], w=[R["t1"]])
                    P.op("act", lambda e: e.activation(out=Tt["t1"][:], in_=Tt["t1"][:], func=AF.Exp, scale=-C0), r=[R["t1"]], w=[R["t1"]])
                    yield
                    P.op("dve", lambda e: e.scalar_tensor_tensor(out=fr(AR[0:64, :, 0, :]), in0=v3(Tt["kk"][:]), scalar=-1.0, in1=v3(Tt["t1"][:]),
                                                                op0=ALU.mult, op1=ALU.mult), r=[R["kk"], R["t1"]], w=[r_AR])
                    P.op("act", lambda e: e.activation(out=Tt["rinv"][:], in_=Tt["Lp"][:], func=AF.Exp, scale=-C0), r=[R["Lp"]], w=[R["rinv"]])
                    P.op("dve", lambda e: e.tensor_tensor(out=fr(AR[0:64, :, 1, :]), in0=v3(Tt["r_s"][:]), in1=v3(Tt["rinv"][:]), op=ALU.mult),
                         r=[R["r_s"], R["rinv"]], w=[r_AR])
                    yield
                    P.op("act", lambda e: e.activation(out=Tt["t1"][:], in_=Tt["Lp"][:], func=AF.Exp, scale=C0), r=[R["Lp"]], w=[R["t1"]])
                    P.op("dve", lambda e: e.tensor_tensor(out=Tt["t2"][:], in0=Tt["kk"][:], in1=Tt["al"][:], op=ALU.mult),
                         r=[R["kk"], R["al"]], w=[R["t2"]])
                    yield
                    P.op("dve", lambda e: e.tensor_tensor(out=fr(BK[:, :, 0, :]), in0=v3(Tt["t2"][:]), in1=v3(Tt["t1"][:]), op=ALU.mult),
                         r=[R["t2"], R["t1"]], w=[r_BK])
                    P.op("dve", lambda e: e.tensor_tensor(out=fr(BK[:, :, 1, :]), in0=v3(Tt["k2"][:]), in1=v3(Tt["t1"][:]), op=ALU.mult),
                         r=[R["k2"], R["t1"]], w=[r_BK])
                    yield
                    P.op("dve", lambda e: e.tensor_tensor(out=BKh[:].rearrange("p a b c -> p a (b c)"), in0=BK[:].rearrange("p a b c -> p a (b c)"),
                                                          in1=Tt["rinv"][:, 127:NT:128].unsqueeze(2).to_broadcast([64, 4, 256]), op=ALU.mult), r=[r_BK, R["rinv"]], w=[r_BKh])
                    yield
                    for c2 in range(2):
                        pm, r_pm = next_pm()
                        for cc in range(2):
                            c = c2 * 2 + cc
                            srcs = (Tt["v_s"][:, c * 128:(c + 1) * 128], BKh[:, c, 0, :], BKh[:, c, 1, :])
                            for k3 in range(3):
                                P.op("pe", lambda e, pm=pm, cc=cc, k3=k3, src=srcs[k3]: e.transpose(
                                    pm[:, (cc * 3 + k3) * 64:(cc * 3 + k3 + 1) * 64], src, ident[0:64, 0:64]),
                                    r=[R["v_s"], r_BKh, r_ident], w=[r_pm], signal=(cc == 1 and k3 == 2))
                        P.op("act", lambda e, pm=pm, c2=c2: e.activation(out=fr(TM[:, c2 * 2:c2 * 2 + 2, :, :]),
                                                                         in_=pm[:, 0:384].rearrange("p (a b c) -> p a b c", a=2, b=3), func=AF.Copy),
                             r=[r_pm], w=[r_TM])
                        yield
                    for c in range(4):
                        pm, r_pm = next_pm()
                        for k3 in range(2):
                            P.op("pe", lambda e, pm=pm, c=c, k3=k3: e.matmul(pm[:, k3 * 256:(k3 + 1) * 256], lhsT=fr(BK[:, c, k3, :]),
                                                                            rhs=fr(AR[0:64, c, :, :]), start=True, stop=True),
                                 r=[r_BK, r_AR], w=[r_pm], signal=(k3 == 1))
                        P.op("dve", lambda e, pm=pm, c=c: e.tensor_tensor(out=fr(SC[:, c, :]), in0=pm[:, :], in1=msc[:], op=ALU.mult),
                             r=[r_pm, r_msc], w=[r_SC])
                        yield
                    pm, r_pm = next_pm()
                    for c in range(4):
                        P.op("pe", lambda e, pm=pm, c=c: e.matmul(pm[:, c * 128:(c + 1) * 128], lhsT=fr(AR[0:64, c, 0, :]), rhs=fr(BK[:, c, 0, :]),
                                                                 start=True, stop=True), r=[r_AR, r_BK], w=[r_pm], signal=(c == 3))
                    P.op("dve", lambda e, pm=pm: e.tensor_tensor(out=fr(PT[0][:]), in0=pm[:, :], in1=mt[:], op=ALU.mult),
                         r=[r_pm, r_mt], w=[r_PT[0]])
                    yield
                    P.op("dve", lambda e: e.tensor_tensor(out=fr(v3(Tm[:])), in0=SC[:, :, 0:128], in1=v3(idr[:]), op=ALU.add), r=[r_SC, r_idr], w=[r_Tm])
                    yield
                    cur = 0
                    for it in range(6):
                        nxt = 1 - cur
                        if it < 5:
                            pm, r_pm = next_pm()
                            for c in range(4):
                                Pv = SC[:, c, 0:128] if it == 0 else PP[cur][:, c * 128:(c + 1) * 128]
                                P.op("pe", lambda e, pm=pm, c=c, cur=cur, Pv=Pv: e.matmul(pm[:, c * 128:(c + 1) * 128], lhsT=fr(PT[cur][:, c * 128:(c + 1) * 128]),
                                                                                         rhs=fr(Pv), start=True, stop=True),
                                     r=[r_PT[cur], r_SC if it == 0 else r_PP[cur]], w=[r_pm], signal=(c == 3))
                            P.op("act", lambda e, pm=pm, nxt=nxt: e.activation(out=fr(PP[nxt][:]), in_=pm[:, :], func=AF.Copy),
                                 r=[r_pm], w=[r_PP[nxt]])
                            yield
                        pm, r_pm = next_pm()
                        for c in range(4):
                            Pv = SC[:, c, 0:128] if it == 0 else PP[cur][:, c * 128:(c + 1) * 128]
                            P.op("pe", lambda e, pm=pm, c=c, cur=cur, Pv=Pv: e.matmul(pm[:, c * 128:(c + 1) * 128], lhsT=fr(Pv),
                                                                                     rhs=fr(PT[cur][:, c * 128:(c + 1) * 128]), start=True, stop=True),
                                 r=[r_PT[cur], r_SC if it == 0 else r_PP[cur]], w=[r_pm], signal=(c == 3))
                        P.op("dve", lambda e, pm=pm, nxt=nxt: e.tensor_copy(fr(PT[nxt][:]), pm[:, :]), r=[r_pm], w=[r_PT[nxt]])
                        yield
                        pm, r_pm = next_pm()
                        for c in range(4):
                            P.op("pe", lambda e, pm=pm, c=c, nxt=nxt: e.matmul(pm[:, c * 128:(c + 1) * 128], lhsT=fr(PT[nxt][:, c * 128:(c + 1) * 128]),
                                                                              rhs=fr(Tm[:, c * 128:(c + 1) * 128]), start=True, stop=True),
                                 r=[r_PT[nxt], r_Tm], w=[r_pm], signal=(c == 3))
                        P.op("dve", lambda e, pm=pm: e.tensor_tensor(out=fr(Tm[:]), in0=pm[:, :], in1=Tm[:], op=ALU.add), r=[r_pm, r_Tm], w=[r_Tm])
                        yield
                        cur = nxt
                    for c in range(4):
                        pm, r_pm = next_pm()
                        P.op("pe", lambda e, pm=pm, c=c: e.matmul(pm[:, 0:64], lhsT=fr(AR[:, c, 0, :]), rhs=fr(Zst[:]), start=True, stop=False),
                             r=[r_AR, r_Z], w=[r_pm], signal=False)
                        P.op("pe", lambda e, pm=pm, c=c: e.matmul(pm[:, 0:64], lhsT=fr(SC[:, c, 256:384]), rhs=fr(TM[:, c, 0, :]), start=False, stop=True),
                             r=[r_SC, r_TM], w=[r_pm])
                        P.op("act", lambda e, pm=pm: e.activation(out=fr(XTs[:]), in_=pm[:, 0:64], func=AF.Copy), r=[r_pm], w=[r_XTs])
                        yield
                        pm, r_pm = next_pm()
                        P.op("pe", lambda e, pm=pm, c=c: e.matmul(pm[:, 0:64], lhsT=fr(Tm[:, c * 128:(c + 1) * 128]), rhs=fr(XTs[:]), start=True, stop=True),
                             r=[r_Tm, r_XTs], w=[r_pm])
                        P.op("act", lambda e, pm=pm: e.activation(out=fr(UTs[:]), in_=pm[:, 0:64], func=AF.Copy), r=[r_pm], w=[r_UTs])
                        yield
                        yo = pY[:, c * 64:(c + 1) * 64]
                        P.op("pe", lambda e, yo=yo, c=c: e.matmul(yo, lhsT=fr(AR[:, c, 1, :]), rhs=fr(Zst[:]), start=True, stop=False),
                             r=[r_AR, r_Z], w=[r_pY], signal=False)
                        P.op("pe", lambda e, yo=yo, c=c: e.matmul(yo, lhsT=fr(SC[:, c, 128:256]), rhs=fr(UTs[:]), start=False, stop=False),
                             r=[r_SC, r_UTs], w=[r_pY], signal=False)
                        P.op("pe", lambda e, yo=yo, c=c: e.matmul(yo, lhsT=fr(SC[:, c, 384:512]), rhs=fr(TM[:, c, 0, :]), start=False, stop=True),
                             r=[r_SC, r_TM], w=[r_pY])
                        yield
                        pm, r_pm = next_pm()
                        P.op("pe", lambda e, pm=pm, c=c: e.matmul(pm[0:64, 0:64], lhsT=fr(TM[:, c, 1, :]), rhs=fr(UTs[:]), start=True, stop=False),
                             r=[r_TM, r_UTs], w=[r_pm], signal=False)
                        P.op("pe", lambda e, pm=pm, c=c: e.matmul(pm[0:64, 0:64], lhsT=fr(TM[:, c, 2, :]), rhs=fr(TM[:, c, 0, :]), start=False, stop=True),
                             r=[r_TM], w=[r_pm])
                        P.op("dve", lambda e, pm=pm, c=c: e.scalar_tensor_tensor(out=fr(Zst[0:64, :]), in0=Zst[0:64, :], scalar=Tt["rinv"][:, c * 128 + 127:c * 128 + 128], in1=pm[0:64, 0:64],
                                                                                op0=ALU.mult, op1=ALU.add), r=[r_Z, R["rinv"], r_pm], w=[r_Z])
                        yield
                    P.op("act", lambda e: e.activation(out=Yt[:], in_=pY[:, 0:256].rearrange("p (a b) -> p a b", b=64), func=AF.Copy), r=[r_pY], w=[r_Yt])
                    if h == 0 and G == 0:
                        self.dbg_dump("yraw", Yt[:], [128, 4, 64], r_Yt)
                    P.op("dve", lambda e: e.reduce_sum(out=st4[:, 0:4], in_=Yt[:], axis=AX.X), r=[r_Yt], w=[r_st4])
                    yield
                    P.op("dve", lambda e: e.scalar_tensor_tensor(out=yc[:], in0=st4[:, 0:4].unsqueeze(2).to_broadcast([128, 4, 64]), scalar=-1.0 / 64,
                                                                in1=Yt[:], op0=ALU.mult, op1=ALU.add), r=[r_Yt, r_st4], w=[r_yc])
                    P.op("dve", lambda e: e.tensor_tensor(out=ysq[:], in0=yc[:], in1=yc[:], op=ALU.mult), r=[r_yc], w=[r_ysq])
                    P.op("dve", lambda e: e.reduce_sum(out=st4[:, 4:8], in_=ysq[:], axis=AX.X), r=[r_ysq], w=[r_st4])
                    yield
                    P.op("dve", lambda e: e.tensor_scalar(out=st4[:, 4:8], in0=st4[:, 4:8], scalar1=1.0 / 64, scalar2=64e-5, op0=ALU.mult, op1=ALU.add),
                         r=[r_st4], w=[r_st4])
                    P.op("act", lambda e: e.activation(out=st4[:, 4:8], in_=st4[:, 4:8], func=AF.Sqrt), r=[r_st4], w=[r_st4])
                    P.op("dve", lambda e: e.reciprocal(st4[:, 8:12], st4[:, 4:8]), r=[r_st4], w=[r_st4])
                    yield
                    pm, r_pm = next_pm()
                    for tq in range(4):
                        P.op("pe", lambda e, pm=pm, tq=tq: e.transpose(pm[:, tq * 64:(tq + 1) * 64], Tt["bon"][:, tq * 128:(tq + 1) * 128],
                                                                     ident[0:64, 0:64]), r=[R["bon"], r_ident], w=[r_pm], signal=(tq == 3))
                    P.op("dve", lambda e: e.tensor_tensor(out=yc[:], in0=yc[:], in1=st4[:, 8:12].unsqueeze(2).to_broadcast([128, 4, 64]), op=ALU.mult),
                         r=[r_yc, r_st4], w=[r_yc])
                    P.op("dve", lambda e: e.tensor_tensor(out=yc[:], in0=yc[:], in1=lnw[:, :].unsqueeze(1).to_broadcast([128, 4, 64]), op=ALU.mult),
                         r=[r_yc, r_lnw], w=[r_yc])
                    yield
                    P.op("dve", lambda e: e.tensor_tensor(out=yc[:], in0=yc[:], in1=lnb[:, :].unsqueeze(1).to_broadcast([128, 4, 64]), op=ALU.add),
                         r=[r_yc, r_lnb], w=[r_yc])
                    P.op("dve", lambda e, pm=pm: e.tensor_tensor(out=yc[:], in0=yc[:], in1=pm[:, 0:256].rearrange("p (a b) -> p a b", b=64), op=ALU.add),
                         r=[r_yc, r_pm], w=[r_yc])
                    yield
                    if h == 0 and G == 0:
                        self.dbg_dump("ob00", yc[:], [128, 4, 64], r_yc)

                    mo = mixo[state["mi"] % 2]; r_mo = r_mixo[state["mi"] % 2]; state["mi"] += 1
                    P.op("dve", lambda e, mo=mo: e.tensor_tensor(out=mo[:], in0=yc[:], in1=zb[:], op=ALU.mult), r=[r_yc, r_zb], w=[r_mo])
                    dst = self.mixd.ap()[G * NT:(G + 1) * NT, 1024 + h * 64:1024 + (h + 1) * 64].rearrange("(t p) c -> p t c", p=128)
                    rr = Reg()
                    self.rw_store_regs.append(rr)
                    P.dma("sp", lambda e, mo=mo, dst=dst: e.dma_start(out=dst, in_=mo[:]), r=[r_mo], w=[rr])
                    yield
            return run

        runs = [make_set(0, self.px, self.r_px), make_set(1, self.py2, self.r_py2)]
        for hp in range(8):
            gens = [runs[0](2 * hp), runs[1](2 * hp + 1)]
            for _ in range(RW_STAGGER):
                try:
                    next(gens[0])
                except StopIteration:
                    break
            while gens:
                for gq in list(gens):
                    try:
                        next(gq)
                    except StopIteration:
                        gens.remove(gq)
        self.dbg_dump("mixall", self.mixd.ap(), [S, 2048], list(self.r_mixd) + self.rw_store_regs, BF16)

    def phase_out(self, st):
        nc, P = self.nc, self.P
        din = self.din
        wo = self.sb(st, "wo", [128, NCH, D], BF16); r_wo = [Reg() for _ in range(4)]
        for nb in range(4):
            P.dma("pool", lambda e, nb=nb: e.dma_start(out=wo[:, :, nb * 512:(nb + 1) * 512],
                                                       in_=din["w_out"].ap()[:, nb * 512:(nb + 1) * 512].rearrange("(c p) n -> p c n", p=128)),
                  w=[r_wo[nb]])
        gpo = self.sb(st, "gpo", [128, D]); r_gpo = Reg()
        P.dma("sp", lambda e: e.dma_start(out=gpo[:], in_=din["g_post"].ap()), w=[r_gpo])
        idb = self.sb(st, "idb2", [128, 128], BF16); r_idb = Reg()
        idf = self.sb(st, "idf2", [128, 128]); r_idf = Reg()
        P.dma("sp", lambda e: e.dma_start(out=idf[:], in_=din["c_ident"].ap()), w=[r_idf])
        P.op("dve", lambda e: e.tensor_copy(idb[:], idf[:]), r=[r_idf], w=[r_idb])
        mx = [self.sb(st, "mx%d" % i, [128, D], BF16) for i in range(2)]; r_mx = [Reg(), Reg()]
        mT = [self.sb(st, "mT%d" % i, [128, NCH, 128], BF16) for i in range(2)]; r_mT = [Reg(), Reg()]
        xr = [self.sb(st, "xr%d" % i, [128, D]) for i in range(2)]; r_xr = [Reg(), Reg()]
        ysbs = [self.sb(st, "ysb%d" % i, [128, D]) for i in range(2)]; r_ysbs = [Reg(), Reg()]
        jks = [self.sb(st, "jk%d" % i, [128, D]) for i in range(2)]; r_jks = [Reg(), Reg()]
        s1s = [self.sb(st, "s1_%d" % i, [128, 4]) for i in range(2)]; r_s1s = [Reg(), Reg()]
        ob = [self.sb(st, "ob%d" % i, [128, D]) for i in range(2)]; r_ob = [Reg(), Reg()]
        mix_regs = list(self.r_mixd) + self.rw_store_regs

        def stage_a(T):
            b = T % 2
            P.dma("sp", lambda e: e.dma_start(out=mx[b][:], in_=self.mixd.ap()[T * 128:(T + 1) * 128, :]), r=mix_regs, w=[r_mx[b]])
            P.dma("sp", lambda e: e.dma_start(out=xr[b][:], in_=din["x"].ap()[T * 128:(T + 1) * 128, :]), w=[r_xr[b]])
            for gq in range(4):
                for jq in range(4):
                    c = gq * 4 + jq
                    P.op("pe", lambda e, c=c, jq=jq: e.transpose(self.ptr[:, jq, :], mx[b][:, c * 128:(c + 1) * 128], idb[:]),
                         r=[r_mx[b], r_idb], w=[self.r_ptr], signal=(jq == 3))
                P.op("dve", lambda e, gq=gq: e.tensor_copy(mT[b][:, gq * 4:(gq + 1) * 4, :], self.ptr[:]),
                     r=[self.r_ptr], w=[r_mT[b]])

        def stage_b(T):
            b = T % 2
            ysb, r_ysb = ysbs[b], r_ysbs[b]
            for nb in range(4):
                pm, r_pm = self.next_pm()
                for c in range(NCH):
                    P.op("pe", lambda e, pm=pm, c=c, nb=nb: e.matmul(pm[:, :], lhsT=mT[b][:, c, :], rhs=wo[:, c, nb * 512:(nb + 1) * 512],
                                                                   start=(c == 0), stop=(c == NCH - 1)),
                         r=[r_mT[b], r_wo[nb]], w=[r_pm], signal=(c == NCH - 1))
                P.op("act", lambda e, pm=pm, nb=nb: e.activation(out=ysb[:, nb * 512:(nb + 1) * 512], in_=pm[:, :], func=AF.Copy),
                     r=[r_pm], w=[r_ysb])

        def stage_c(T):
            b = T % 2
            ysb, r_ysb, jk, r_jk, s1, r_s1 = ysbs[b], r_ysbs[b], jks[b], r_jks[b], s1s[b], r_s1s[b]
            P.op("act", lambda e: e.activation(out=jk[:], in_=ysb[:], func=AF.Square), r=[r_ysb], w=[r_jk])
            P.op("dve", lambda e: e.reduce_sum(out=s1[:, 0:1], in_=jk[:], axis=AX.X), r=[r_jk], w=[r_s1])
            P.op("dve", lambda e: e.tensor_scalar(out=s1[:, 0:1], in0=s1[:, 0:1], scalar1=1.0 / D, scalar2=1e-6, op0=ALU.mult, op1=ALU.add),
                 r=[r_s1], w=[r_s1])
            P.op("act", lambda e: e.activation(out=s1[:, 0:1], in_=s1[:, 0:1], func=AF.Sqrt), r=[r_s1], w=[r_s1])
            P.op("dve", lambda e: e.reciprocal(s1[:, 1:2], s1[:, 0:1]), r=[r_s1], w=[r_s1])
            P.op("dve", lambda e: e.tensor_tensor(out=jk[:], in0=ysb[:], in1=gpo[:], op=ALU.mult), r=[r_ysb, r_gpo], w=[r_jk])
            P.op("dve", lambda e: e.scalar_tensor_tensor(out=ob[b][:], in0=jk[:], scalar=s1[:, 1:2], in1=xr[b][:], op0=ALU.mult, op1=ALU.add),
                 r=[r_jk, r_s1, r_xr[b]], w=[r_ob[b]])
            ro = Reg()
            self.out_regs.append(ro)
            P.dma("sp", lambda e: e.dma_start(out=self.out.ap()[T * 128:(T + 1) * 128, :], in_=ob[b][:]), r=[r_ob[b]], w=[ro])

        stage_a(0)
        for T in range(16):
            stage_b(T)
            if T + 1 < 16:
                stage_a(T + 1)
            stage_c(T)


def _build(in_shapes, dbg=()):
    b = B(in_shapes, dbg)
    nc = b.build()
    return nc, b


def kernel(**inputs):
    inputs = {k: np.asarray(v) for k, v in inputs.items()}
    consts = _consts()
    L = _layout_inputs(inputs)
    shared = dict(consts)
    shared.update(L)
    in_shapes = {"x": (S, D)}
    for k, v in shared.items():
        in_shapes[k] = v.shape
    nc, b = _build(in_shapes)
    active = [0, 1, 4, 5]
    zero_x = np.zeros((S, D), np.float32)
    in_maps = []
    for c in range(8):
        if c in active:
            m = {"x": np.ascontiguousarray(inputs["x"][active.index(c)])}
        else:
            m = {"x": zero_x}
        m.update(shared)
        in_maps.append(m)
    res = run_bass_kernel_spmd(nc, in_maps, core_ids=list(range(8)))
    out = np.stack([res.results[c]["out"] for c in active], 0)
    return out.astype(np.float32)
```

```python
import math
from contextlib import ExitStack, contextmanager
import numpy as np
import concourse.bass as bass
import concourse.mybir as mybir
from concourse.bass_utils import run_bass_kernel_spmd

F32 = mybir.dt.float32
BF16 = mybir.dt.bfloat16
ALU = mybir.AluOpType
AF = mybir.ActivationFunctionType
AX = mybir.AxisListType

ENGS = ("pe", "act", "dve", "pool", "sp")
N_DMA_SEMS = 12
S = 2048
D = 2048
NCH = 16
BIG = 1.0e30
C0 = math.exp(-0.5)
USE_F32R = True
RW_STAGGER = 0


class Reg:
    __slots__ = ("lw", "rd")

    def __init__(self):
        self.lw = None
        self.rd = {}


class Prog:
    def __init__(self, nc):
        self.nc = nc
        self.q = {e: [] for e in ENGS}
        self.cnt = {e: 0 for e in ENGS}
        self.pending = {e: False for e in ENGS}
        self.waited = {e: {} for e in ENGS}
        self.dma_k = {e: 0 for e in ENGS}

    def op(self, eng, fn, r=(), w=(), signal=True):
        deps = {}

        def need(key, val, kind):
            if key == eng:
                if eng == "pe":
                    return
            if deps.get(key, 0) < val:
                deps[key] = val

        for reg in r:
            if reg.lw is not None:
                need(reg.lw[0], reg.lw[1], "raw")
        for reg in w:
            if reg.lw is not None:
                need(reg.lw[0], reg.lw[1], "waw")
            for k, v in reg.rd.items():
                need(k, v, "war")
        waits = []
        wd = self.waited[eng]
        for k, v in deps.items():
            if wd.get(k, 0) < v:
                wd[k] = v
                waits.append((k, v))
        n = self.cnt[eng] + 1
        if signal:
            self.cnt[eng] = n
            self.pending[eng] = False
        else:
            self.pending[eng] = True
        self.q[eng].append((waits, fn, (eng, 1) if signal else None))
        for reg in r:
            reg.rd[eng] = n
        for reg in w:
            reg.lw = (eng, n)
            reg.rd = {}

    def dma(self, qe, fn, r=(), w=()):
        deps = {}
        for reg in r:
            if reg.lw is not None:
                k, v = reg.lw
                deps[k] = max(deps.get(k, 0), v)
        for reg in w:
            if reg.lw is not None:
                k, v = reg.lw
                deps[k] = max(deps.get(k, 0), v)
            for k, v in reg.rd.items():
                deps[k] = max(deps.get(k, 0), v)
        kk = self.dma_k[qe]
        self.dma_k[qe] = kk + 1
        slot = kk % N_DMA_SEMS
        key = "dma_%s_%d" % (qe, slot)
        prev = 16 * (kk // N_DMA_SEMS)
        if prev > 0:
            deps[key] = max(deps.get(key, 0), prev)
        waits = []
        wd = self.waited[qe]
        for k, v in deps.items():
            if wd.get(k, 0) < v:
                wd[k] = v
                waits.append((k, v))
        tgt = prev + 16
        self.q[qe].append((waits, fn, (key, 16)))
        for reg in r:
            reg.rd[key] = max(reg.rd.get(key, 0), tgt)
        for reg in w:
            reg.lw = (key, tgt)
            reg.rd = {}

    def barrier(self):
        deps = {e: self.cnt[e] for e in ENGS if self.cnt[e] > 0}
        for qe in ENGS:
            kk = self.dma_k[qe]
            for slot in range(min(N_DMA_SEMS, kk)):
                uses = (kk - slot + N_DMA_SEMS - 1) // N_DMA_SEMS
                deps["dma_%s_%d" % (qe, slot)] = 16 * uses
        for e in ENGS:
            assert not self.pending[e]
            waits = []
            for k, v in deps.items():
                if k != e and self.waited[e].get(k, 0) < v:
                    self.waited[e][k] = v
                    waits.append((k, v))
            if waits:
                self.q[e].append((waits, None, None))

    def final_wait(self, eng, regs):
        deps = {}
        for reg in regs:
            if reg.lw is not None:
                k, v = reg.lw
                deps[k] = max(deps.get(k, 0), v)
        self.q[eng].append((list(deps.items()), None, None))

    def emit(self):
        nc = self.nc
        with ExitStack() as st:
            sems = {}
            for e in ENGS:
                sems[e] = st.enter_context(nc.semaphore("s_" + e))
            for qe in ENGS:
                for i in range(min(N_DMA_SEMS, self.dma_k[qe])):
                    key = "dma_%s_%d" % (qe, i)
                    sems[key] = st.enter_context(nc.semaphore(key))
            block = st.enter_context(nc.Block())
            for e in ENGS:
                assert not self.pending[e], e

            def run(engname):
                def body(eng):
                    for waits, fn, inc in self.q[engname]:
                        for k, v in waits:
                            eng.wait_ge(sems[k], v)
                        if fn is not None:
                            ins = fn(eng)
                            if inc is not None:
                                ins.then_inc(sems[inc[0]], inc[1])
                return body

            block.tensor(run("pe"))
            block.scalar(run("act"))
            block.vector(run("dve"))
            block.gpsimd(run("pool"))
            block.sync(run("sp"))


def _bucket(n):
    n = np.maximum(n, 0)
    nf = np.maximum(n, 1).astype(np.float32)
    large = 16 + (np.log(nf / np.float32(16)) / np.float32(math.log(64)) * np.float32(16)).astype(np.int32)
    large = np.minimum(large, 31)
    return np.where(n < 16, n, large)


def _consts():
    c = {}
    n = np.arange(2048)
    oh = np.zeros((32, 4096), np.float32)
    oh[_bucket(n), n] = 1.0
    c["c_oh"] = oh
    c["c_ident"] = np.eye(128, dtype=np.float32)
    t = np.arange(S)
    cur = t // 64
    j = np.arange(32)
    forced = (j[None, :] == 0) | (j[None, :] == cur[:, None]) | (j[None, :] == cur[:, None] - 1)
    causal = j[None, :] <= cur[:, None]
    m1 = (causal & ~forced).astype(np.float32)
    add = np.where(forced, BIG, np.where(causal, 0.0, -BIG)).astype(np.float32)
    c["c_m1"] = np.ascontiguousarray(m1.reshape(16, 128, 32).transpose(1, 0, 2))
    c["c_add"] = np.ascontiguousarray(add.reshape(16, 128, 32).transpose(1, 0, 2))
    e2 = (np.arange(S)[None, :] // 64 == j[:, None]).astype(np.float32)
    c["c_e2"] = e2
    cs = np.arange(127) * 16
    ss = np.arange(32) * 64
    ov = ((cs[:, None] < ss[None, :] + 64) & (cs[:, None] + 32 > ss[None, :])).astype(np.float32)
    c["c_ov"] = ov
    tri_s = np.triu(np.ones((64, 64), np.float32), 1)
    tri_i = np.triu(np.ones((64, 64), np.float32), 0)
    ts2 = np.triu(np.ones((128, 128), np.float32), 1)
    ti2 = np.triu(np.ones((128, 128), np.float32), 0)
    c["c_mask_sc2"] = np.concatenate([ts2, ti2, ts2, ti2], 1)
    c["c_mask_t8"] = np.ascontiguousarray(np.tile(ts2.T, (1, 4)))
    c["c_ident8"] = np.ascontiguousarray(np.tile(np.eye(128, dtype=np.float32), (1, 4)))
    cm = np.ones((64, S), np.float32)
    cm[:, ::128] = 0.0
    c["c_cmask"] = cm
    return c


W_NSA_FM = 1024
W_NSA_TM = 780


def _layout_inputs(inp):
    L = {}
    w_in = inp["w_in"][0]
    o_kv = 1024
    o_g = 2560
    o_za = 2584
    o_f = 3608
    o_zb = 6808
    for g in range(2):
        cols = list(range(g * 512, g * 512 + 512))
        cols += list(range(o_kv + 0 * 256 + g * 128, o_kv + 0 * 256 + g * 128 + 128))
        cols += list(range(o_kv + 1 * 256 + g * 128, o_kv + 1 * 256 + g * 128 + 128))
        cols += list(range(o_kv + 2 * 256 + g * 128, o_kv + 2 * 256 + g * 128 + 128))
        cols += list(range(o_kv + 4 * 256 + g * 128, o_kv + 4 * 256 + g * 128 + 128))
        cols += list(range(o_kv + 3 * 256 + g * 128, o_kv + 3 * 256 + g * 128 + 128))
        cols += list(range(o_kv + 5 * 256 + g * 128, o_kv + 5 * 256 + g * 128 + 128))
        for br in range(3):
            cols += [o_g + br * 8 + g * 4 + h for h in range(4)]
        cols += list(range(o_za + g * 512, o_za + g * 512 + 512))
        L["w_nsa%d" % g] = np.ascontiguousarray(w_in[:, cols])
    L["w_rkv"] = np.ascontiguousarray(w_in[:, o_f:o_f + 3072])
    L["w_lora"] = np.ascontiguousarray(w_in[:, o_f + 3072:o_f + 3200])
    L["w_zb"] = np.ascontiguousarray(w_in[:, o_zb:o_zb + 1024])
    L["g_pre"] = np.ascontiguousarray(inp["pre_norm_g"][0].reshape(16, 128).T)
    L["g_post"] = np.ascontiguousarray(np.tile(inp["post_norm_g"][0][None, :], (128, 1)))
    L["tab"] = np.ascontiguousarray(inp["rel_bias_table"])
    cmp_params = {"k": (inp["cmp_pos_k"], inp["cmp_k_w1"], inp["cmp_k_w2"]),
                  "v": (inp["cmp_pos_v"], inp["cmp_v_w1"], inp["cmp_v_w2"])}
    for nm in ("k", "v"):
        pos_, w1_, w2_ = cmp_params[nm]
        L["pos%sT" % nm] = np.ascontiguousarray(pos_[0].T)
        L["w1%s" % nm] = np.ascontiguousarray(w1_[0].reshape(32, 128, 128).transpose(1, 0, 2))
        L["w2%s" % nm] = np.ascontiguousarray(w2_[0])
    mu = inp["rwkv_mu"][0]
    per = np.zeros((64, 16, 8), np.float32)
    for h in range(16):
        sl = slice(h * 64, h * 64 + 64)
        per[:, h, 0] = mu[0:1024][sl]
        per[:, h, 1] = mu[1024:2048][sl]
        per[:, h, 2] = mu[2048:3072][sl]
        per[:, h, 3] = inp["rwkv_w0"][0][sl]
        per[:, h, 4] = inp["rwkv_a0"][0][sl]
        per[:, h, 5] = inp["rwkv_k_k"][0][sl]
        per[:, h, 6] = inp["rwkv_k_a"][0][sl]
        per[:, h, 7] = inp["rwkv_r_k"][0][h]
    L["rw_per"] = per
    L["mu_lora"] = np.ascontiguousarray(mu[3072:3200].reshape(2, 64).T)
    L["rw_w2"] = np.ascontiguousarray(inp["rwkv_w2"][0])
    L["rw_a2"] = np.ascontiguousarray(inp["rwkv_a2"][0])
    L["ln_w"] = np.ascontiguousarray(np.tile(inp["rwkv_ln_w"][0][None, :], (128, 1)))
    L["ln_b"] = np.ascontiguousarray(np.tile(inp["rwkv_ln_b"][0][None, :], (128, 1)))
    L["w_out"] = np.ascontiguousarray(inp["w_out"][0])
    return L


_IN_SHAPES = None


class B:
    def __init__(self, in_shapes, dbg=()):
        self.dbg = dbg
        nc = self.nc = bass.Bass("TRN2", target_bir_lowering=False)
        self.P = Prog(nc)
        self.din = {}
        for k, shp in in_shapes.items():
            self.din[k] = nc.dram_tensor(k, list(shp), F32, kind="ExternalInput")
        self.out = nc.dram_tensor("out", [S, D], F32, kind="ExternalOutput")
        self.mixd = nc.dram_tensor("mixd", [S, 2048], BF16)
        self.Z = [nc.dram_tensor("Zs%d" % h, [132, 4096], BF16) for h in range(8)]
        self.Zw = [nc.dram_tensor("Zw%d" % h, [132, 4096], BF16) for h in range(8)]
        self.r_Z = [Reg() for _ in range(8)]
        self.r_Zw = [Reg() for _ in range(8)]
        self.r_Z2 = [None] * 8
        self.r_Zw2 = [None] * 8
        self.r_mixd = [Reg() for _ in range(16)]
        self.out_regs = []
        self.rw_store_regs = []
        self.dbg_out = {}
        self.es = ExitStack()

    @contextmanager
    def scope(self):
        with ExitStack() as st:
            yield st
        self.P.barrier()

    def sb(self, st, name, shape, dt=F32):
        self.uid = getattr(self, "uid", 0) + 1
        return st.enter_context(self.nc.sbuf_tensor("s%d_%s" % (self.uid, name), list(shape), dt))

    def ps(self, st, name, shape, dt=F32):
        self.uid = getattr(self, "uid", 0) + 1
        return st.enter_context(self.nc.psum_tensor("p%d_%s" % (self.uid, name), list(shape), dt))

    def dbg_dump(self, name, ap_src, shape, reg, dt=F32):
        if name not in self.dbg:
            return
        t = self.nc.dram_tensor("dbg_" + name, list(shape), dt, kind="ExternalOutput")
        self.dbg_out[name] = t
        ro = Reg()
        self.out_regs.append(ro)
        self.P.dma("sp", lambda e: e.dma_start(out=t.ap(), in_=ap_src), r=[reg] if not isinstance(reg, list) else reg, w=[ro])

    def build(self):
        nc, P = self.nc, self.P
        with self.scope() as st:
            self.ident = self.sb(st, "ident", [128, 128]); self.r_ident = Reg()
            self.identb = self.sb(st, "identb", [128, 128], BF16); self.r_identb = Reg()
            self.ones = self.sb(st, "ones", [128, 128]); self.r_ones = Reg()
            self.xT = self.sb(st, "xT", [128, NCH, S], BF16)
            self.r_xT = [Reg() for _ in range(16)]
            self.rstd_col = self.sb(st, "rstd_col", [128, 16]); self.r_rc = Reg()
            self.rstd_bc = self.sb(st, "rstd_bc", [128, S]); self.r_rb = Reg()
            self.gpre = self.sb(st, "gpre", [128, 16]); self.r_gpre = Reg()
            self.pm = [self.ps(st, "pm%d" % i, [128, 512]) for i in range(6)]
            self.r_pm = [Reg() for _ in range(6)]
            self.ptr = self.ps(st, "ptr", [128, 4, 128], BF16); self.r_ptr = Reg()
            self.px = self.ps(st, "px", [128, 512]); self.r_px = Reg()
            self.pm_i = 0

            P.dma("sp", lambda e: e.dma_start(out=self.ident[:], in_=self.din["c_ident"].ap()), w=[self.r_ident])
            P.op("dve", lambda e: e.tensor_copy(self.identb[:], self.ident[:]), r=[self.r_ident], w=[self.r_identb])
            P.op("pool", lambda e: e.memset(self.ones[:], 1.0), w=[self.r_ones])
            P.dma("sp", lambda e: e.dma_start(out=self.gpre[:], in_=self.din["g_pre"].ap()), w=[self.r_gpre])

            self.phase0(st)
            self.phase_eb()
            for g in range(2):
                with self.scope() as st2:
                    self.phase_nsa(st2, g)
            with self.scope() as st2:
                self.phase_rwkv_proj(st2)
        with ExitStack() as stx:
            self.wo0 = self.sb(stx, "wo0", [128, NCH, 512], BF16); self.r_wo0 = Reg()
            P.dma("pool", lambda e: e.dma_start(out=self.wo0[:], in_=self.din["w_out"].ap()[:, 0:512].rearrange("(c p) n -> p c n", p=128)),
                  w=[self.r_wo0])
            with self.scope() as st:
                self.pm = [self.ps(st, "rm%d" % i, [128, 512]) for i in range(6)]
                self.r_pm = [Reg() for _ in range(6)]
                self.px = self.ps(st, "rpx", [128, 512]); self.r_px = Reg()
                self.py2 = self.ps(st, "rpy2", [128, 512]); self.r_py2 = Reg()
                self.phase_rwkv(st)
            with self.scope() as st:
                self.pm = [self.ps(st, "qm%d" % i, [128, 512]) for i in range(6)]
                self.r_pm = [Reg() for _ in range(6)]
                self.ptr = self.ps(st, "qtr", [128, 4, 128], BF16); self.r_ptr = Reg()
                self.phase_out(st)
                P.final_wait("sp", self.out_regs)
                P.emit()
        return nc

    def next_pm(self):
        i = self.pm_i
        self.pm_i = (i + 1) % len(self.pm)
        return self.pm[i], self.r_pm[i]

    def phase0(self, st0):
        nc, P = self.nc, self.P
        x = self.din["x"].ap()
        with self.scope() as st:
            xt = [self.sb(st, "xt%d" % i, [128, D]) for i in range(2)]
            r_xt = [Reg(), Reg()]
            xb = [self.sb(st, "xb%d" % i, [128, D], BF16) for i in range(2)]
            r_xb = [Reg(), Reg()]
            junk = self.sb(st, "junk", [128, D]); r_junk = Reg()
            ss = self.sb(st, "ss", [128, 16]); r_ss = Reg()
            dg = self.sb(st, "dg", [128, 128]); r_dg = Reg()
            for tt in range(16):
                b = tt % 2
                P.dma("sp", lambda e, b=b, tt=tt: e.dma_start(out=xt[b][:], in_=x[tt * 128:(tt + 1) * 128, :]), w=[r_xt[b]])
                P.op("act", lambda e, b=b, tt=tt: e.activation(out=junk[:], in_=xt[b][:], func=AF.Square), r=[r_xt[b]], w=[r_junk])
                P.op("dve", lambda e, tt=tt: e.reduce_sum(out=ss[:, tt:tt + 1], in_=junk[:], axis=AX.X), r=[r_junk], w=[r_ss])
                P.op("pool", lambda e, b=b: e.tensor_copy(xb[b][:], xt[b][:]), r=[r_xt[b]], w=[r_xb[b]])
                for gq in range(4):
                    for jq in range(4):
                        c = gq * 4 + jq
                        P.op("pe", lambda e, b=b, c=c, jq=jq: e.transpose(self.ptr[:, jq, :], xb[b][:, c * 128:(c + 1) * 128], self.identb[:]),
                             r=[r_xb[b], self.r_identb], w=[self.r_ptr], signal=(jq == 3))
                    for jq in range(4):
                        c = gq * 4 + jq
                        P.op("dve", lambda e, c=c, jq=jq, tt=tt: e.tensor_scalar(
                            out=self.xT[:, c, tt * 128:(tt + 1) * 128], in0=self.ptr[:, jq, :],
                            scalar1=self.gpre[:, c:c + 1], scalar2=None, op0=ALU.mult),
                            r=[self.r_ptr, self.r_gpre], w=[self.r_xT[tt]])
            P.op("dve", lambda e: e.tensor_scalar(out=ss[:], in0=ss[:], scalar1=1.0 / D, scalar2=1e-6, op0=ALU.mult, op1=ALU.add),
                 r=[r_ss], w=[r_ss])
            P.op("act", lambda e: e.activation(out=ss[:], in_=ss[:], func=AF.Sqrt), r=[r_ss], w=[r_ss])
            P.op("dve", lambda e: e.reciprocal(self.rstd_col[:], ss[:]), r=[r_ss], w=[self.r_rc])
            for tt in range(16):
                P.op("dve", lambda e, tt=tt: e.tensor_scalar(out=dg[:], in0=self.ident[:], scalar1=self.rstd_col[:, tt:tt + 1],
                                                            scalar2=None, op0=ALU.mult), r=[self.r_ident, self.r_rc], w=[r_dg])
                P.op("pe", lambda e: e.matmul(self.px[:, 0:128], lhsT=self.ones[:], rhs=dg[:], start=True, stop=True),
                     r=[self.r_ones, r_dg], w=[self.r_px])
                P.op("act", lambda e, tt=tt: e.activation(out=self.rstd_bc[:, tt * 128:(tt + 1) * 128], in_=self.px[:, 0:128], func=AF.Copy),
                     r=[self.r_px], w=[self.r_rb])

        self.dbg_dump("rstd_col", self.rstd_col[:], [128, 16], self.r_rc)
        self.dbg_dump("rstd_bc", self.rstd_bc[:], [128, S], self.r_rb)
        self.dbg_dump("xT0", self.xT[:, 0, :], [128, S], self.r_xT, BF16)

    def load_w(self, wb, r_wb, src_ap):
        self.P.dma("pool", lambda e: e.dma_start(out=wb, in_=src_ap.rearrange("(c p) n -> p c n", p=128)), w=[r_wb])

    def proj_fm(self, wb, r_wb, j0, ncols, dst_fn, r_dst, scale=None, evac="dve"):
        P = self.P
        for tb in range(4):
            pm, r_pm = self.next_pm()
            for c in range(NCH):
                P.op("pe", lambda e, c=c, tb=tb, pm=pm: e.matmul(pm[0:ncols, :], lhsT=wb[:, c, j0:j0 + ncols],
                                                                 rhs=self.xT[:, c, tb * 512:(tb + 1) * 512],
                                                                 start=(c == 0), stop=(c == NCH - 1)),
                     r=[r_wb] + self.r_xT[tb * 4:tb * 4 + 4], w=[r_pm], signal=(c == NCH - 1))
            if scale is None:
                P.op("dve", lambda e, tb=tb, pm=pm: e.tensor_tensor(out=dst_fn(tb), in0=pm[0:ncols, :],
                                                                   in1=self.rstd_bc[0:ncols, tb * 512:(tb + 1) * 512], op=ALU.mult),
                     r=[r_pm, self.r_rb], w=[r_dst])
            else:
                P.op("dve", lambda e, tb=tb, pm=pm: e.scalar_tensor_tensor(out=dst_fn(tb), in0=pm[0:ncols, :], scalar=scale,
                                                                          in1=self.rstd_bc[0:ncols, tb * 512:(tb + 1) * 512],
                                                                          op0=ALU.mult, op1=ALU.mult),
                     r=[r_pm, self.r_rb], w=[r_dst])

    def proj_tm(self, wb, r_wb, j0, ncols, dst_fn, r_dst, func=AF.Copy):
        P = self.P
        for tt in range(16):
            pm, r_pm = self.next_pm()
            for c in range(NCH):
                P.op("pe", lambda e, c=c, tt=tt, pm=pm: e.matmul(pm[:, 0:ncols], lhsT=self.xT[:, c, tt * 128:(tt + 1) * 128],
                                                                 rhs=wb[:, c, j0:j0 + ncols], start=(c == 0), stop=(c == NCH - 1)),
                     r=[r_wb, self.r_xT[tt]], w=[r_pm], signal=(c == NCH - 1))
            P.op("act", lambda e, tt=tt, pm=pm: e.activation(out=dst_fn(tt), in_=pm[:, 0:ncols], func=func,
                                                            scale=self.rstd_col[:, tt:tt + 1]),
                 r=[r_pm, self.r_rc], w=[r_dst])

    def phase_eb(self):
        nc, P = self.nc, self.P
        with self.scope() as st:
            oh = self.sb(st, "oh", [32, 4096]); r_oh = Reg()
            tab = self.sb(st, "tab", [32, 8]); r_tab = Reg()
            zrow = [self.sb(st, "zrow%d" % i, [128, 4096], BF16) for i in range(2)]
            r_zrow = [Reg(), Reg()]
            P.dma("sp", lambda e: e.dma_start(out=oh[:], in_=self.din["c_oh"].ap()), w=[r_oh])
            P.dma("sp", lambda e: e.dma_start(out=tab[:], in_=self.din["tab"].ap()), w=[r_tab])
            tabrep = self.sb(st, "tabrep", [32, 8, 128]); r_tabrep = Reg()
            for h in range(8):
                P.op("dve", lambda e, h=h: e.tensor_scalar(out=tabrep[:, h, :], in0=self.ones[0:32, :], scalar1=tab[:, h:h + 1], scalar2=None,
                                                          op0=ALU.mult), r=[self.r_ones, r_tab], w=[r_tabrep])
            k = 0
            for h in range(8):
                for win in range(2):
                    zb = zrow[k % 2]; r_zb = r_zrow[k % 2]
                    k += 1
                    nblk = 1 if win else 4
                    if True:
                        P.op("pool", lambda e, zb=zb: e.memset(zb[:, 512 * nblk:], 0.0), w=[r_zb])
                    for blk in range(nblk):
                        pm, r_pm = self.next_pm()
                        P.op("pe", lambda e, h=h, blk=blk, pm=pm: e.matmul(pm[:, :], lhsT=tabrep[:, h, :],
                                                                          rhs=oh[:, blk * 512:(blk + 1) * 512], start=True, stop=True),
                             r=[r_tabrep, r_oh], w=[r_pm])
                        P.op("act", lambda e, blk=blk, pm=pm, zb=zb: e.activation(out=zb[:, blk * 512:(blk + 1) * 512], in_=pm[:, :], func=AF.Exp),
                             r=[r_pm], w=[r_zb])
                    dst = (self.Zw if win else self.Z)[h]
                    r_dst = (self.r_Zw if win else self.r_Z)[h]
                    P.dma("sp", lambda e, dst=dst, zb=zb: e.dma_start(out=dst.ap()[0:128, :], in_=zb[:]), r=[r_zb], w=[r_dst])
                    r_dst2 = Reg()
                    P.dma("sp", lambda e, dst=dst, zb=zb: e.dma_start(out=dst.ap()[128:132, :], in_=zb[0:4, :]), r=[r_zb], w=[r_dst2])
                    (self.r_Zw2 if win else self.r_Z2)[h] = r_dst2

    def toep(self, dst_ap, r_dst, Zt, r_Z, c, pstep, nparts, nfree, r_Z2=None):
        src = bass.AP(Zt, c % 4096, [[pstep, nparts], [1, nfree]])
        self.P.dma("pool", lambda e: e.dma_start(out=dst_ap, in_=src), r=[r_Z] + ([r_Z2] if r_Z2 is not None else []), w=[r_dst])

    def phase_nsa(self, st, g):
        nc, P = self.nc, self.P
        wsrc = self.din["w_nsa%d" % g].ap()
        qT = self.sb(st, "qT", [128, 4, S], BF16); r_qT = Reg()
        kT = self.sb(st, "kT", [128, 4, S], BF16); r_kT = [Reg() for _ in range(4)]
        vs = self.sb(st, "vs", [128, 16, 132], BF16); r_vs = Reg()
        vw = self.sb(st, "vw", [128, 16, 132], BF16); r_vw = Reg()
        gt = self.sb(st, "gt", [128, 16, 12]); r_gt = Reg()
        oacc = self.sb(st, "oacc", [128, 16, 512]); r_oacc = [Reg() for _ in range(16)]
        imp = self.sb(st, "imp", [128, 16, 32]); r_imp = [Reg() for _ in range(16)]
        negT = self.sb(st, "negT", [128, S], BF16); r_negT = Reg()
        e2c = self.sb(st, "e2c", [128, S], BF16); r_e2c = Reg()
        kcT = self.sb(st, "kcT", [128, 128], BF16); r_kcT = Reg()
        vce = self.sb(st, "vce", [128, 164], BF16); r_vce = Reg()
        P.op("pool", lambda e: e.memset(negT[:], 0.0), w=[r_negT])
        P.op("pool", lambda e: e.memset(e2c[:], 0.0), w=[r_e2c])

        with self.scope() as stw:
            wb = [self.sb(stw, "wbn%d" % i, [128, NCH, 512], BF16) for i in range(2)]
            r_wb = [Reg(), Reg()]
            self.load_w(wb[0][:], r_wb[0], wsrc[:, 0:512])
            self.load_w(wb[1][:], r_wb[1], wsrc[:, 512:1024])
            for h in range(4):
                self.proj_fm(wb[0], r_wb[0], h * 128, 128, lambda tb, h=h: qT[:, h, tb * 512:(tb + 1) * 512], r_qT, scale=128.0 ** -0.5)
            for i in range(4):
                self.proj_fm(wb[1], r_wb[1], i * 128, 128, lambda tb, i=i: kT[:, i, tb * 512:(tb + 1) * 512], r_kT[i])
            self.load_w(wb[0][:, :, 0:268], r_wb[0], wsrc[:, 1024:1292])
            P.op("pool", lambda e: e.memset(vs[:, :, 128:132], 1.0), w=[r_vs])
            P.op("pool", lambda e: e.memset(vw[:, :, 128:132], 1.0), w=[r_vw])
            self.proj_tm(wb[0], r_wb[0], 0, 128, lambda tt: vs[:, tt, 0:128], r_vs)
            self.proj_tm(wb[0], r_wb[0], 128, 128, lambda tt: vw[:, tt, 0:128], r_vw)
            self.proj_tm(wb[0], r_wb[0], 256, 12, lambda tt: gt[:, tt, :], r_gt, func=AF.Sigmoid)

        self.dbg_dump("qT%d" % g, qT[:, 0, :], [128, S], r_qT, BF16)
        self.dbg_dump("kT%d" % g, kT[:, 0, :], [128, S], r_kT[0], BF16)
        self.dbg_dump("vs%d" % g, vs[:, 0, :], [128, 132], r_vs, BF16)
        self.dbg_dump("gt%d" % g, gt[:, 0, :], [128, 12], r_gt)
        with self.scope() as st2:
            e2f = self.sb(st2, "e2f", [32, S]); r_e2f = Reg()
            P.dma("sp", lambda e: e.dma_start(out=e2f[:], in_=self.din["c_e2"].ap()), w=[r_e2f])
            P.op("dve", lambda e: e.tensor_copy(e2c[0:32, :], e2f[:]), r=[r_e2f], w=[r_e2c])

        with self.scope() as st2:
            w1 = self.sb(st2, "w1", [128, 32, 128], BF16); r_w1 = Reg()
            w2 = self.sb(st2, "w2", [128, 128], BF16); r_w2 = Reg()
            posT = self.sb(st2, "posT", [128, 32], BF16); r_posT = Reg()
            cb = self.sb(st2, "cb", [128, 1]); r_cb = Reg()
            h1s = self.sb(st2, "h1s", [128, 128], BF16); r_h1s = Reg()
            ovf = self.sb(st2, "ovf", [128, 33]); r_ovf = Reg()
            P.op("pool", lambda e: e.memset(ovf[:, 0:1], 1.0), w=[r_ovf])
            P.dma("sp", lambda e: e.dma_start(out=ovf[0:127, 1:33], in_=self.din["c_ov"].ap()), w=[r_ovf])
            P.op("dve", lambda e: e.tensor_copy(vce[0:127, 128:161], ovf[0:127, :]), r=[r_ovf], w=[r_vce])
            for which in range(2):
                nm = "kv"[which]
                P.dma("pool", lambda e, nm=nm: e.dma_start(out=w1[:], in_=self.din["w1" + nm].ap()), w=[r_w1])
                P.dma("pool", lambda e, nm=nm: e.dma_start(out=w2[:], in_=self.din["w2" + nm].ap()), w=[r_w2])
                P.dma("pool", lambda e, nm=nm: e.dma_start(out=posT[:], in_=self.din["pos%sT" % nm].ap()), w=[r_posT])
                pm, r_pm = self.next_pm()
                for l in range(32):
                    P.op("pe", lambda e, l=l, pm=pm: e.matmul(pm[:, 0:1], lhsT=w1[:, l, :], rhs=posT[:, l:l + 1], start=(l == 0), stop=(l == 31)),
                         r=[r_w1, r_posT], w=[r_pm], signal=(l == 31))
                P.op("dve", lambda e, pm=pm: e.tensor_copy(cb[:], pm[:, 0:1]), r=[r_pm], w=[r_cb])
                pm, r_pm = self.next_pm()
                for l in range(32):
                    P.op("pe", lambda e, l=l, pm=pm, which=which: e.matmul(pm[:, 0:127], lhsT=w1[:, l, :],
                                                                          rhs=kT[:, which, l:l + 16 * 126 + 1:16],
                                                                          start=(l == 0), stop=(l == 31)),
                         r=[r_w1, r_kT[which]], w=[r_pm], signal=(l == 31))
                P.op("act", lambda e, pm=pm: e.activation(out=h1s[:, 0:127], in_=pm[:, 0:127], func=AF.Silu, bias=cb[:, 0:1]),
                     r=[r_pm, r_cb], w=[r_h1s])
                pm, r_pm = self.next_pm()
                if which == 0:
                    P.op("pe", lambda e, pm=pm: e.matmul(pm[:, 0:127], lhsT=w2[:], rhs=h1s[:, 0:127], start=True, stop=True),
                         r=[r_w2, r_h1s], w=[r_pm])
                    P.op("dve", lambda e, pm=pm: e.tensor_copy(kcT[:, 0:127], pm[:, 0:127]), r=[r_pm], w=[r_kcT])
                else:
                    P.op("pe", lambda e, pm=pm: e.matmul(pm[0:127, 0:128], lhsT=h1s[:, 0:127], rhs=w2[:], start=True, stop=True),
                         r=[r_w2, r_h1s], w=[r_pm])
                    P.op("dve", lambda e, pm=pm: e.tensor_copy(vce[0:127, 0:128], pm[0:127, 0:128]), r=[r_pm], w=[r_vce])

        with self.scope() as sta:
            e1 = [self.sb(sta, "e1_%d" % i, [128, 512], BF16) for i in range(2)]
            r_e1 = [Reg(), Reg()]
            e2 = self.sb(sta, "e2", [128, 16, 512], BF16); r_e2 = [Reg() for _ in range(16)]
            sm = self.sb(sta, "sm", [128, 4, 4]); r_sm = [Reg() for _ in range(4)]
            EBc = self.sb(sta, "EBc", [128, S], BF16); r_EBc = Reg()
            EBs = self.sb(sta, "EBs", [128, 16, 512], BF16); r_EBs = Reg()
            EBw = self.sb(sta, "EBw", [128, 8, 512], BF16); r_EBw = Reg()
            ei = 0

            def gate_col(br, h):
                return br * 4 + h

            for h in range(4):
                hd = g * 4 + h
                self.toep(EBc[0:127, :], r_EBc, self.Z[hd], self.r_Z[hd], 4096 - 31, 4096 - 16, 127, S, self.r_Z2[hd])
                for Q in range(4):
                    pm, r_pm = self.next_pm()
                    P.op("pe", lambda e, pm=pm, h=h, Q=Q: e.matmul(pm[0:127, :], lhsT=kcT[:, 0:127], rhs=qT[:, h, Q * 512:(Q + 1) * 512],
                                                                  start=True, stop=True), r=[r_kcT, r_qT], w=[r_pm])
                    eb = e1[ei % 2]; r_eb = r_e1[ei % 2]; ei += 1
                    P.op("act", lambda e, pm=pm, eb=eb: e.activation(out=eb[0:127, :], in_=pm[0:127, :], func=AF.Exp), r=[r_pm], w=[r_eb])
                    P.op("dve", lambda e, eb=eb, Q=Q: e.tensor_tensor(out=e2[0:127, 0, :], in0=eb[0:127, :], in1=EBc[0:127, Q * 512:(Q + 1) * 512],
                                                                     op=ALU.mult), r=[r_eb, r_EBc], w=[r_e2[0]])
                    pms = []
                    for sq in range(4):
                        pm, r_pm = self.next_pm()
                        pms.append((pm, r_pm))
                        P.op("pe", lambda e, pm=pm, sq=sq: e.matmul(pm[:, 0:161], lhsT=e2[0:127, 0, sq * 128:(sq + 1) * 128], rhs=vce[0:127, 0:161],
                                                                   start=True, stop=True), r=[r_e2[0], r_vce], w=[r_pm])
                    for sq in range(4):
                        pm, r_pm = pms[sq]
                        P.op("dve", lambda e, pm=pm, sq=sq: e.tensor_scalar(out=sm[:, sq, 0:1], in0=pm[:, 128:129], scalar1=1e-30, scalar2=None, op0=ALU.max),
                             r=[r_pm], w=[r_sm[sq]])
                    P.op("dve", lambda e: e.reciprocal(sm[:, :, 1], sm[:, :, 0]), r=list(r_sm), w=list(r_sm))
                    P.op("dve", lambda e, h=h, Q=Q: e.tensor_tensor(out=sm[:, :, 2], in0=sm[:, :, 1], in1=gt[:, Q * 4:(Q + 1) * 4, gate_col(0, h)],
                                                                   op=ALU.mult), r=list(r_sm) + [r_gt], w=list(r_sm))
                    for sq in range(4):
                        T = Q * 4 + sq
                        pm, r_pm = pms[sq]
                        P.op("dve", lambda e, pm=pm, T=T, h=h, sq=sq: e.tensor_scalar(out=oacc[:, T, h * 128:(h + 1) * 128], in0=pm[:, 0:128],
                                                                                     scalar1=sm[:, sq, 2:3], scalar2=None, op0=ALU.mult),
                             r=[r_pm, r_sm[sq]], w=[r_oacc[T]])
                    for sq in range(4):
                        T = Q * 4 + sq
                        pm, r_pm = pms[sq]
                        if h == 0:
                            P.op("dve", lambda e, pm=pm, T=T, sq=sq: e.tensor_scalar(out=imp[:, T, :], in0=pm[:, 129:161], scalar1=sm[:, sq, 1:2], scalar2=None,
                                                                                    op0=ALU.mult), r=[r_pm, r_sm[sq]], w=[r_imp[T]])
                        else:
                            P.op("dve", lambda e, pm=pm, T=T, sq=sq: e.scalar_tensor_tensor(out=imp[:, T, :], in0=pm[:, 129:161], scalar=sm[:, sq, 1:2],
                                                                                           in1=imp[:, T, :], op0=ALU.mult, op1=ALU.add),
                                 r=[r_pm, r_sm[sq], r_imp[T]], w=[r_imp[T]])
            self.dbg_dump("imp%d" % g, imp[:], [128, 16, 32], r_imp[15])

            self.dbg_dump("kcT%d" % g, kcT[:], [128, 128], r_kcT, BF16)
            self.dbg_dump("vce%d" % g, vce[:], [128, 164], r_vce, BF16)
            self.dbg_dump("oaccc%d" % g, oacc[:, 0, :], [128, 512], r_oacc[0])
            with self.scope() as st2x:
                sc = imp
                r_sc = r_imp
                sc2 = self.sb(st2x, "sc2", [128, 16, 32]); r_sc2 = [Reg() for _ in range(16)]
                m8 = self.sb(st2x, "m8", [128, 16, 16]); r_m8 = [Reg() for _ in range(16)]
                P.dma("sp", lambda e: e.dma_start(out=sc2[:], in_=self.din["c_m1"].ap()), w=r_sc2)
                P.op("dve", lambda e: e.tensor_tensor(out=sc[:], in0=imp[:], in1=sc2[:], op=ALU.mult), r=list(r_imp) + list(r_sc2), w=list(r_sc))
                P.dma("sp", lambda e: e.dma_start(out=sc2[:], in_=self.din["c_add"].ap()), r=r_sc2, w=r_sc2)
                P.op("dve", lambda e: e.tensor_tensor(out=sc[:], in0=sc[:], in1=sc2[:], op=ALU.add), r=list(r_sc) + list(r_sc2), w=list(r_sc))
                for T in range(16):
                    P.op("dve", lambda e, T=T: e.max(out=m8[:, T, 0:8], in_=sc[:, T, :]), r=[r_sc[T]], w=[r_m8[T]])
                for T in range(16):
                    P.op("dve", lambda e, T=T: e.match_replace(out=sc2[:, T, :], in_to_replace=m8[:, T, 0:8], in_values=sc[:, T, :], imm_value=-3.0e38),
                         r=[r_sc[T], r_m8[T]], w=[r_sc2[T]])
                for T in range(16):
                    P.op("dve", lambda e, T=T: e.max(out=m8[:, T, 8:16], in_=sc2[:, T, :]), r=[r_sc2[T]], w=[r_m8[T]])
                P.op("dve", lambda e: e.tensor_tensor(out=sc2[:], in0=sc[:], in1=m8[:, :, 15:16].to_broadcast([128, 16, 32]), op=ALU.is_ge),
                     r=list(r_sc) + list(r_m8), w=list(r_sc2))
                P.op("dve", lambda e: e.tensor_scalar(out=sc2[:], in0=sc2[:], scalar1=-1.0, scalar2=30000.0, op0=ALU.add, op1=ALU.mult),
                     r=list(r_sc2), w=list(r_sc2))
                for T4 in range(4):
                    pm, r_pm = self.next_pm()
                    for j4 in range(4):
                        T = T4 * 4 + j4
                        P.op("pe", lambda e, pm=pm, T=T, j4=j4: e.transpose(pm[0:32, j4 * 128:(j4 + 1) * 128], sc2[:, T, :], self.ident[:]),
                             r=[r_sc2[T], self.r_ident], w=[r_pm], signal=(j4 == 3))
                    P.op("act", lambda e, pm=pm, T4=T4: e.activation(out=negT[0:32, T4 * 512:(T4 + 1) * 512], in_=pm[0:32, :], func=AF.Copy),
                         r=[r_pm], w=[r_negT])
            self.dbg_dump("negT%d" % g, negT[0:32, :], [32, S], r_negT, BF16)

            def load_eb(hh, which):
                hd_ = g * 4 + hh
                if which == 1:
                    src_s = bass.AP(self.Z[hd_], 4096 - 384, [[4095, 128], [128, 16], [1, 512]])
                    P.dma("pool", lambda e, src_s=src_s: e.dma_start(out=EBs[:], in_=src_s), r=[self.r_Z[hd_], self.r_Z2[hd_]], w=[r_EBs])
                else:
                    src_w = bass.AP(self.Zw[hd_], 4096 - 384, [[4095, 128], [128, 8], [1, 512]])
                    P.dma("pool", lambda e, src_w=src_w: e.dma_start(out=EBw[:], in_=src_w), r=[self.r_Zw[hd_], self.r_Zw2[hd_]], w=[r_EBw])

            load_eb(0, 1)
            load_eb(0, 2)
            for h in range(4):
                hd = g * 4 + h
                for br in (1, 2):
                    if br == 2 and h < 3:
                        load_eb(h + 1, 1)
                    for Q in range(4):
                        kt_lo = 0 if br == 1 else max(0, 4 * Q - 4)
                        kt_hi = 4 * Q + 3
                        kidx = 2 if br == 1 else 3
                        for kt in range(kt_lo, kt_hi + 1):
                            pm, r_pm = self.next_pm()
                            P.op("pe", lambda e, pm=pm, kt=kt, h=h, Q=Q, kidx=kidx, br=br: e.matmul(
                                pm[:, :], lhsT=kT[:, kidx, kt * 128:(kt + 1) * 128], rhs=qT[:, h, Q * 512:(Q + 1) * 512],
                                start=True, stop=(br == 2)), r=[r_kT[kidx], r_qT], w=[r_pm], signal=(br == 2))
                            if br == 1:
                                P.op("pe", lambda e, pm=pm, kt=kt, Q=Q: e.matmul(pm[:, :], lhsT=e2c[:, kt * 128:(kt + 1) * 128],
                                                                                rhs=negT[:, Q * 512:(Q + 1) * 512], start=False, stop=True),
                                     r=[r_e2c, r_negT], w=[r_pm])
                            eb = e1[ei % 2]; r_eb = r_e1[ei % 2]; ei += 1
                            P.op("act", lambda e, pm=pm, eb=eb: e.activation(out=eb[:], in_=pm[:, :], func=AF.Exp), r=[r_pm], w=[r_eb])
                            o = 4 * Q - kt + 3
                            EB = EBs if br == 1 else EBw
                            r_EB = r_EBs if br == 1 else r_EBw
                            P.op("dve", lambda e, eb=eb, kt=kt, o=o, EB=EB: e.tensor_tensor(out=e2[:, kt, :], in0=eb[:], in1=EB[:, o, :], op=ALU.mult),
                                 r=[r_eb, r_EB], w=[r_e2[kt]])
                        vv = vs if br == 1 else vw
                        r_vv = r_vs if br == 1 else r_vw
                        pms = []
                        for sq in range(4):
                            T = Q * 4 + sq
                            lo = 0 if br == 1 else max(0, T - 4)
                            hi = T
                            pm, r_pm = self.next_pm()
                            pms.append((pm, r_pm))
                            for kt in range(lo, hi + 1):
                                P.op("pe", lambda e, pm=pm, kt=kt, sq=sq, vv=vv, lo=lo, hi=hi: e.matmul(
                                    pm[:, 0:129], lhsT=e2[:, kt, sq * 128:(sq + 1) * 128], rhs=vv[:, kt, 0:129],
                                    start=(kt == lo), stop=(kt == hi)), r=[r_e2[kt], r_vv], w=[r_pm], signal=(kt == hi))
                        for sq in range(4):
                            pm, r_pm = pms[sq]
                            P.op("dve", lambda e, pm=pm, sq=sq: e.tensor_scalar(out=sm[:, sq, 0:1], in0=pm[:, 128:129], scalar1=1e-30, scalar2=None, op0=ALU.max),
                                 r=[r_pm], w=[r_sm[sq]])
                        P.op("dve", lambda e: e.reciprocal(sm[:, :, 1], sm[:, :, 0]), r=list(r_sm), w=list(r_sm))
                        P.op("dve", lambda e, h=h, br=br, Q=Q: e.tensor_tensor(out=sm[:, :, 2], in0=sm[:, :, 1], in1=gt[:, Q * 4:(Q + 1) * 4, gate_col(br, h)],
                                                                              op=ALU.mult), r=list(r_sm) + [r_gt], w=list(r_sm))
                        for sq in range(4):
                            T = Q * 4 + sq
                            pm, r_pm = pms[sq]
                            P.op("dve", lambda e, pm=pm, T=T, h=h, sq=sq: e.scalar_tensor_tensor(
                                out=oacc[:, T, h * 128:(h + 1) * 128], in0=pm[:, 0:128], scalar=sm[:, sq, 2:3],
                                in1=oacc[:, T, h * 128:(h + 1) * 128], op0=ALU.mult, op1=ALU.add),
                                r=[r_pm, r_sm[sq], r_oacc[T]], w=[r_oacc[T]])
                    if br == 2 and h < 3:
                        load_eb(h + 1, 2)

        with self.scope() as stf:
            wbz = self.sb(stf, "wbz", [128, NCH, 512], BF16); r_wbz = Reg()
            za = self.sb(stf, "za", [128, 16, 512], BF16); r_za = Reg()
            mixb = [self.sb(stf, "mixb%d" % i, [128, 512], BF16) for i in range(2)]
            r_mixb = [Reg(), Reg()]
            self.load_w(wbz[:], r_wbz, wsrc[:, 1292:1804])
            self.proj_tm(wbz, r_wbz, 0, 512, lambda tt: za[:, tt, :], r_za, func=AF.Silu)
            for T in range(16):
                b = T % 2
                P.op("dve", lambda e, T=T, b=b: e.tensor_tensor(out=mixb[b][:], in0=oacc[:, T, :], in1=za[:, T, :], op=ALU.mult),
                     r=[r_oacc[T], r_za], w=[r_mixb[b]])
                P.dma("sp", lambda e, T=T, b=b: e.dma_start(out=self.mixd.ap()[T * 128:(T + 1) * 128, g * 512:(g + 1) * 512], in_=mixb[b][:]),
                      r=[r_mixb[b]], w=[self.r_mixd[T]])
        if g == 1:
            self.dbg_dump("mixa", self.mixd.ap()[:, 0:1024], [S, 1024], list(self.r_mixd), BF16)

    def phase_rwkv_proj(self, st):
        nc, P = self.nc, self.P
        din = self.din
        self.rawd = nc.dram_tensor("rawd", [3, 1024, S + 1], F32)
        self.zbd = nc.dram_tensor("zbd", [S, 1024], BF16)
        self.lorad = nc.dram_tensor("lorad", [2, 64, S], F32)
        self.rw_regs = []
        wb = [self.sb(st, "wbp%d" % i, [128, NCH, 512], BF16) for i in range(2)]
        r_wb = [Reg(), Reg()]
        stg = [self.sb(st, "stg%d" % i, [128, S + 4]) for i in range(2)]
        r_stg = [Reg(), Reg()]
        for i in range(2):
            P.op("pool", lambda e, i=i: e.memset(stg[i][:, 0:1], 0.0), w=[r_stg[i]])
        k = 0
        for blk in range(6):
            b = blk % 2
            self.load_w(wb[b][:], r_wb[b], din["w_rkv"].ap()[:, blk * 512:(blk + 1) * 512])
            for j in range(4):
                ct = blk * 4 + j
                sg_, r_sg = stg[k % 2], r_stg[k % 2]
                k += 1
                self.proj_fm(wb[b], r_wb[b], j * 128, 128, lambda tb, sg_=sg_: sg_[:, 1 + tb * 512:1 + (tb + 1) * 512], r_sg)
                rr = Reg(); self.rw_regs.append(rr)
                P.dma("sp", lambda e, ct=ct, sg_=sg_: e.dma_start(out=self.rawd.ap()[ct // 8, (ct % 8) * 128:(ct % 8 + 1) * 128, 0:S + 1], in_=sg_[:, 0:S + 1]),
                      r=[r_sg], w=[rr])
        zst = [self.sb(st, "zst%d" % i, [128, 16, 512], BF16) for i in range(2)]
        r_zst = [Reg(), Reg()]
        for blk in range(2):
            self.load_w(wb[blk][:], r_wb[blk], din["w_zb"].ap()[:, blk * 512:(blk + 1) * 512])
            self.proj_tm(wb[blk], r_wb[blk], 0, 512, lambda tt, blk=blk: zst[blk][:, tt, :], r_zst[blk], func=AF.Silu)
            rr = Reg(); self.rw_regs.append(rr)
            P.dma("sp", lambda e, blk=blk: e.dma_start(out=self.zbd.ap()[:, blk * 512:(blk + 1) * 512].rearrange("(t p) c -> p t c", p=128),
                                                       in_=zst[blk][:]), r=[r_zst[blk]], w=[rr])
        mul = self.sb(st, "mul", [64, 2]); r_mul = Reg()
        P.dma("sp", lambda e: e.dma_start(out=mul[:], in_=din["mu_lora"].ap()), w=[r_mul])
        wbl = self.sb(st, "wbl", [128, NCH, 128], BF16); r_wbl = Reg()
        self.load_w(wbl[:], r_wbl, din["w_lora"].ap())
        raw = self.sb(st, "lraw", [64, S + 4]); r_raw = Reg()
        tmp = self.sb(st, "ltmp", [64, S]); r_tmp = Reg()
        lo = [self.sb(st, "lo%d" % i, [64, S]) for i in range(2)]; r_lo = [Reg(), Reg()]
        for i in range(2):
            P.op("pool", lambda e: e.memset(raw[:, 0:1], 0.0), w=[r_raw])
            self.proj_fm(wbl, r_wbl, i * 64, 64, lambda tb: raw[:, 1 + tb * 512:1 + (tb + 1) * 512], r_raw)
            P.op("dve", lambda e: e.tensor_tensor(out=tmp[:], in0=raw[:, 0:S], in1=raw[:, 1:S + 1], op=ALU.subtract), r=[r_raw], w=[r_tmp])
            P.op("dve", lambda e, i=i: e.scalar_tensor_tensor(out=lo[i][:], in0=tmp[:], scalar=mul[:, i:i + 1], in1=raw[:, 1:S + 1],
                                                             op0=ALU.mult, op1=ALU.add), r=[r_tmp, r_raw, r_mul], w=[r_lo[i]])
            if i == 0:
                P.op("act", lambda e: e.activation(out=lo[0][:], in_=lo[0][:], func=AF.Tanh), r=[r_lo[0]], w=[r_lo[0]])
            rr = Reg(); self.rw_regs.append(rr)
            P.dma("sp", lambda e, i=i: e.dma_start(out=self.lorad.ap()[i], in_=lo[i][:]), r=[r_lo[i]], w=[rr])

    def phase_rwkv(self, st):
        nc, P = self.nc, self.P
        din = self.din
        NT = 512
        F32R = mybir.dt.float32r

        def fr(ap):
            return ap.bitcast(F32R) if USE_F32R else ap

        def ld(name, shape, src, dt=F32, q="sp", r=()):
            t = self.sb(st, name, shape, dt)
            rg = Reg()
            P.dma(q, lambda e: e.dma_start(out=t[:], in_=src), r=list(r), w=[rg])
            return t, rg

        ident, r_ident = ld("identr", [128, 128], din["c_ident"].ap())
        ones = self.sb(st, "onesr", [64, 64]); r_ones = Reg()
        P.op("pool", lambda e: e.memset(ones[:], 1.0), w=[r_ones])
        msc, r_msc = ld("msc", [128, 512], din["c_mask_sc2"].ap())
        mt, r_mt = ld("mt", [128, 512], din["c_mask_t8"].ap())
        idr, r_idr = ld("idr", [128, 512], din["c_ident8"].ap())
        cmask, r_cmask = ld("cmask", [64, NT], din["c_cmask"].ap()[:, 0:NT])
        per, r_per = ld("per", [64, 16, 8], din["rw_per"].ap())
        w2, r_w2 = ld("rw2", [64, 1024], din["rw_w2"].ap())
        a2, r_a2 = ld("ra2", [64, 1024], din["rw_a2"].ap())
        twd, r_twd = ld("twd", [64, S], self.lorad.ap()[0], r=self.rw_regs)
        ads, r_ads = ld("ads", [64, S], self.lorad.ap()[1], r=self.rw_regs)
        omka = self.sb(st, "omka", [64, 16]); r_omka = Reg()
        P.op("dve", lambda e: e.tensor_scalar(out=omka[:], in0=per[:, :, 6], scalar1=-1.0, scalar2=1.0, op0=ALU.mult, op1=ALU.add),
             r=[r_per], w=[r_omka])

        def v3(ap):
            return ap.rearrange("p (c t) -> p c t", t=128)

        def make_set(si, pY, r_pY):
            sfx = "_%d" % si
            my_pm = self.pm[si * 3:(si + 1) * 3]
            my_rpm = self.r_pm[si * 3:(si + 1) * 3]
            rot = {"i": 0}

            def next_pm():
                i = rot["i"]
                rot["i"] = (i + 1) % 3
                return my_pm[i], my_rpm[i]

            Zst = self.sb(st, "Zst" + sfx, [128, 64]); r_Z = Reg()
            lnw = self.sb(st, "lnw" + sfx, [128, 64]); r_lnw = Reg()
            lnb = self.sb(st, "lnb" + sfx, [128, 64]); r_lnb = Reg()
            raws = self.sb(st, "raws" + sfx, [64, 3, NT + 4]); r_raws = Reg()
            names = ["r_s", "k_s", "v_s", "sg", "al", "kk", "k2", "Lp", "t0", "t1", "t2", "rinv", "bon"]
            Tt = {n: self.sb(st, "T_" + n + sfx, [64, NT]) for n in names}
            R = {n: Reg() for n in names}
            AR = self.sb(st, "AR" + sfx, [128, 4, 2, 128]); r_AR = Reg()
            BK = self.sb(st, "BK" + sfx, [64, 4, 2, 128]); r_BK = Reg()
            BKh = self.sb(st, "BKh" + sfx, [64, 4, 2, 128]); r_BKh = Reg()
            WC = self.sb(st, "WC" + sfx, [64, 4]); r_WC = Reg()
            TM = self.sb(st, "TM" + sfx, [128, 4, 3, 64]); r_TM = Reg()
            SC = self.sb(st, "SC" + sfx, [128, 4, 512]); r_SC = Reg()
            PP = [self.sb(st, "PP%d" % i + sfx, [128, 512]) for i in range(2)]; r_PP = [Reg(), Reg()]
            PT = [self.sb(st, "PTt%d" % i + sfx, [128, 512]) for i in range(2)]; r_PT = [Reg(), Reg()]
            Tm = self.sb(st, "Tm" + sfx, [128, 512]); r_Tm = Reg()
            XTs = self.sb(st, "XTs" + sfx, [128, 64]); r_XTs = Reg()
            UTs = self.sb(st, "UTs" + sfx, [128, 64]); r_UTs = Reg()
            Yt = self.sb(st, "Yt" + sfx, [128, 4, 64]); r_Yt = Reg()
            yc = self.sb(st, "yc" + sfx, [128, 4, 64]); r_yc = Reg()
            ysq = self.sb(st, "ysq" + sfx, [128, 4, 64]); r_ysq = Reg()
            st4 = self.sb(st, "st4" + sfx, [128, 16]); r_st4 = Reg()
            zb = self.sb(st, "zb" + sfx, [128, 4, 64], BF16); r_zb = Reg()
            mixo = [self.sb(st, "mixo%d" % i + sfx, [128, 4, 64], BF16) for i in range(2)]; r_mixo = [Reg(), Reg()]
            state = {"mi": 0}

            def run(h):
                P.dma("sp", lambda e: e.dma_start(out=lnw[:], in_=din["ln_w"].ap()[:, h * 64:(h + 1) * 64]), w=[r_lnw])
                P.dma("sp", lambda e: e.dma_start(out=lnb[:], in_=din["ln_b"].ap()[:, h * 64:(h + 1) * 64]), w=[r_lnb])
                P.op("dve", lambda e: e.tensor_scalar(out=fr(Zst[:]), in0=mt[:, 0:64], scalar1=0.0, scalar2=None, op0=ALU.mult), r=[r_mt], w=[r_Z])
                if not state.get("ar_init"):
                    state["ar_init"] = True
                    for a_ in range(2):
                        P.op("dve", lambda e, a_=a_: e.tensor_scalar(out=fr(AR[:, a_ * 2:a_ * 2 + 2, :, :].rearrange("p a b c -> p (a b c)")), in0=mt[:],
                                                                    scalar1=0.0, scalar2=None, op0=ALU.mult), r=[r_mt], w=[r_AR])
                for G in range(4):
                    t0g = G * NT
                    P.dma("sp", lambda e, G=G: e.dma_start(out=raws[:, :, 0:NT + 1],
                                                           in_=self.rawd.ap()[:, h * 64:(h + 1) * 64, G * NT:G * NT + NT + 1].rearrange("i c t -> c i t")),
                          r=self.rw_regs, w=[r_raws])
                    P.dma("sp", lambda e, G=G: e.dma_start(out=zb[:], in_=self.zbd.ap()[G * NT:(G + 1) * NT, h * 64:(h + 1) * 64].rearrange("(t p) c -> p t c", p=128)),
                          r=self.rw_regs, w=[r_zb])
                    for i in range(3):
                        nm = ("r_s", "k_s", "v_s")[i]
                        P.op("dve", lambda e, i=i: e.tensor_tensor(out=Tt["t0"][:], in0=raws[:, i, 0:NT], in1=raws[:, i, 1:NT + 1], op=ALU.subtract),
                             r=[r_raws], w=[R["t0"]])
                        P.op("dve", lambda e, i=i, nm=nm: e.scalar_tensor_tensor(out=Tt[nm][:], in0=Tt["t0"][:], scalar=per[:, h, i:i + 1],
                                                                                in1=raws[:, i, 1:NT + 1], op0=ALU.mult, op1=ALU.add),
                             r=[R["t0"], r_raws, r_per], w=[R[nm]])
                        yield
                    for (wmat, src, r_src, dst, bcol) in ((w2, twd, r_twd, "sg", 3), (a2, ads, r_ads, "al", 4)):
                        pm, r_pm = next_pm()
                        P.op("pe", lambda e, pm=pm, wmat=wmat, src=src, h=h, t0g=t0g: e.matmul(
                            pm[0:64, :], lhsT=wmat[:, h * 64:(h + 1) * 64], rhs=src[:, t0g:t0g + NT], start=True, stop=True),
                            r=[r_w2, r_a2, r_src], w=[r_pm])
                        P.op("act", lambda e, pm=pm, dst=dst, bcol=bcol, h=h: e.activation(out=Tt[dst][:], in_=pm[0:64, :], func=AF.Sigmoid,
                                                                                           bias=per[:, h, bcol:bcol + 1]),
                             r=[r_pm, r_per], w=[R[dst]])
                    P.op("act", lambda e, h=h: e.activation(out=Tt["t1"][:], in_=Tt["k_s"][:], func=AF.Square, scale=per[:, h, 5:6]),
                         r=[R["k_s"], r_per], w=[R["t1"]])
                    pm, r_pm = next_pm()
                    P.op("pe", lambda e, pm=pm: e.matmul(pm[0:64, :], lhsT=ones[:], rhs=Tt["t1"][:], start=True, stop=True),
                         r=[r_ones, R["t1"]], w=[r_pm])
                    P.op("act", lambda e, pm=pm: e.activation(out=Tt["rinv"][:], in_=pm[0:64, :], func=AF.Sqrt), r=[r_pm], w=[R["rinv"]])
                    yield
                    P.op("dve", lambda e: e.tensor_scalar(out=Tt["rinv"][:], in0=Tt["rinv"][:], scalar1=1e-12, scalar2=None, op0=ALU.max),
                         r=[R["rinv"]], w=[R["rinv"]])
                    P.op("dve", lambda e: e.reciprocal(Tt["rinv"][:], Tt["rinv"][:]), r=[R["rinv"]], w=[R["rinv"]])
                    P.op("dve", lambda e, h=h: e.scalar_tensor_tensor(out=Tt["kk"][:], in0=Tt["k_s"][:], scalar=per[:, h, 5:6], in1=Tt["rinv"][:],
                                                                     op0=ALU.mult, op1=ALU.mult), r=[R["k_s"], R["rinv"], r_per], w=[R["kk"]])
                    yield
                    P.op("dve", lambda e, h=h: e.tensor_scalar(out=Tt["t1"][:], in0=Tt["al"][:], scalar1=per[:, h, 6:7], scalar2=omka[:, h:h + 1],
                                                              op0=ALU.mult, op1=ALU.add), r=[R["al"], r_per, r_omka], w=[R["t1"]])
                    P.op("dve", lambda e: e.tensor_tensor(out=Tt["k2"][:], in0=Tt["k_s"][:], in1=Tt["t1"][:], op=ALU.mult),
                         r=[R["k_s"], R["t1"]], w=[R["k2"]])
                    P.op("dve", lambda e, h=h: e.scalar_tensor_tensor(out=Tt["t1"][:], in0=Tt["r_s"][:], scalar=per[:, h, 7:8], in1=Tt["k2"][:],
                                                                     op0=ALU.mult, op1=ALU.mult), r=[R["r_s"], R["k2"], r_per], w=[R["t1"]])
                    yield
                    pm, r_pm = next_pm()
                    P.op("pe", lambda e, pm=pm: e.matmul(pm[0:64, :], lhsT=ones[:], rhs=Tt["t1"][:], start=True, stop=True),
                         r=[r_ones, R["t1"]], w=[r_pm])
                    P.op("dve", lambda e, pm=pm: e.tensor_tensor(out=Tt["bon"][:], in0=pm[0:64, :], in1=Tt["v_s"][:], op=ALU.mult),
                         r=[r_pm, R["v_s"]], w=[R["bon"]])
                    P.op("dve", lambda e: e.tensor_tensor_scan(out=Tt["Lp"][:], data0=cmask[:], data1=Tt["sg"][:], initial=0.0,
                                                              op0=ALU.mult, op1=ALU.add), r=[r_cmask, R["sg"]], w=[R["Lp"]])
                    P.op("dve", lambda e: e.tensor_tensor(out=Tt["t1"][:], in0=Tt["Lp"][:], in1=Tt["sg"][:], op=ALU.subtract),
                         r=[R["Lp"], R["sg"]], w=[R["t1"]])
                    P.op("act", lambda e: e.activation(out=Tt["t1"][:], in_=Tt["t1"][:], func=AF.Exp, scale=-C0), r=[R["t1"]], w=[R["t1"]])
                    yield
                    P.op("dve", lambda e: e.scalar_tensor_tensor(out=fr(AR[0:64, :, 0, :]), in0=v3(Tt["kk"][:]), scalar=-1.0, in1=v3(Tt["t1"][:]),
                                                                op0=ALU.mult, op1=ALU.mult), r=[R["kk"], R["t1"]], w=[r_AR])
                    P.op("act", lambda e: e.activation(out=Tt["rinv"][:], in_=Tt["Lp"][:], func=AF.Exp, scale=-C0), r=[R["Lp"]], w=[R["rinv"]])
                    P.op("dve", lambda e: e.tensor_tensor(out=fr(AR[0:64, :, 1, :]), in0=v3(Tt["r_s"][:]), in1=v3(Tt["rinv"][:]), op=ALU.mult),
                         r=[R["r_s"], R["rinv"]], w=[r_AR])
                    yield
                    P.op("act", lambda e: e.activation(out=Tt["t1"][:], in_=Tt["Lp"][:], func=AF.Exp, scale=C0), r=[R["Lp"]], w=[R["t1"]])
                    P.op("dve", lambda e: e.tensor_tensor(out=Tt["t2"][:], in0=Tt["kk"][:], in1=Tt["al"][:], op=ALU.mult),
                         r=[R["kk"], R["al"]], w=[R["t2"]])
                    yield
                    P.op("dve", lambda e: e.tensor_tensor(out=fr(BK[:, :, 0, :]), in0=v3(Tt["t2"][:]), in1=v3(Tt["t1"][:]), op=ALU.mult),
                         r=[R["t2"], R["t1"]], w=[r_BK])
                    P.op("dve", lambda e: e.tensor_tensor(out=fr(BK[:, :, 1, :]), in0=v3(Tt["k2"][:]), in1=v3(Tt["t1"][:]), op=ALU.mult),
                         r=[R["k2"], R["t1"]], w=[r_BK])
                    yield
                    P.op("dve", lambda e: e.tensor_tensor(out=BKh[:].rearrange("p a b c -> p a (b c)"), in0=BK[:].rearrange("p a b c -> p a (b c)"),
                                                          in1=Tt["rinv"][:, 127:NT:128].unsqueeze(2).to_broadcast([64, 4, 256]), op=ALU.mult), r=[r_BK, R["rinv"]], w=[r_BKh])
                    yield
                    for c2 in range(2):
                        pm, r_pm = next_pm()
                        for cc in range(2):
                            c = c2 * 2 + cc
                            srcs = (Tt["v_s"][:, c * 128:(c + 1) * 128], BKh[:, c, 0, :], BKh[:, c, 1, :])
                            for k3 in range(3):
                                P.op("pe", lambda e, pm=pm, cc=cc, k3=k3, src=srcs[k3]: e.transpose(
                                    pm[:, (cc * 3 + k3) * 64:(cc * 3 + k3 + 1) * 64], src, ident[0:64, 0:64]),
                                    r=[R["v_s"], r_BKh, r_ident], w=[r_pm], signal=(cc == 1 and k3 == 2))
                        P.op("act", lambda e, pm=pm, c2=c2: e.activation(out=fr(TM[:, c2 * 2:c2 * 2 + 2, :, :]),
                                                                         in_=pm[:, 0:384].rearrange("p (a b c) -> p a b c", a=2, b=3), func=AF.Copy),
                             r=[r_pm], w=[r_TM])
                        yield
                    for c in range(4):
                        pm, r_pm = next_pm()
                        for k3 in range(2):
                            P.op("pe", lambda e, pm=pm, c=c, k3=k3: e.matmul(pm[:, k3 * 256:(k3 + 1) * 256], lhsT=fr(BK[:, c, k3, :]),
                                                                            rhs=fr(AR[0:64, c, :, :]), start=True, stop=True),
                                 r=[r_BK, r_AR], w=[r_pm], signal=(k3 == 1))
                        P.op("dve", lambda e, pm=pm, c=c: e.tensor_tensor(out=fr(SC[:, c, :]), in0=pm[:, :], in1=msc[:], op=ALU.mult),
                             r=[r_pm, r_msc], w=[r_SC])
                        yield
                    pm, r_pm = next_pm()
                    for c in range(4):
                        P.op("pe", lambda e, pm=pm, c=c: e.matmul(pm[:, c * 128:(c + 1) * 128], lhsT=fr(AR[0:64, c, 0, :]), rhs=fr(BK[:, c, 0, :]),
                                                                 start=True, stop=True), r=[r_AR, r_BK], w=[r_pm], signal=(c == 3))
                    P.op("dve", lambda e, pm=pm: e.tensor_tensor(out=fr(PT[0][:]), in0=pm[:, :], in1=mt[:], op=ALU.mult),
                         r=[r_pm, r_mt], w=[r_PT[0]])
                    yield
                    P.op("dve", lambda e: e.tensor_tensor(out=fr(v3(Tm[:])), in0=SC[:, :, 0:128], in1=v3(idr[:]), op=ALU.add), r=[r_SC, r_idr], w=[r_Tm])
                    yield
                    cur = 0
                    for it in range(6):
                        nxt = 1 - cur
                        if it < 5:
                            pm, r_pm = next_pm()
                            for c in range(4):
                                Pv = SC[:, c, 0:128] if it == 0 else PP[cur][:, c * 128:(c + 1) * 128]
                                P.op("pe", lambda e, pm=pm, c=c, cur=cur, Pv=Pv: e.matmul(pm[:, c * 128:(c + 1) * 128], lhsT=fr(PT[cur][:, c * 128:(c + 1) * 128]),
                                                                                         rhs=fr(Pv), start=True, stop=True),
                                     r=[r_PT[cur], r_SC if it == 0 else r_PP[cur]], w=[r_pm], signal=(c == 3))
                            P.op("act", lambda e, pm=pm, nxt=nxt: e.activation(out=fr(PP[nxt][:]), in_=pm[:, :], func=AF.Copy),
                                 r=[r_pm], w=[r_PP[nxt]])
                            yield
                        pm, r_pm = next_pm()
                        for c in range(4):
                            Pv = SC[:, c, 0:128] if it == 0 else PP[cur][:, c * 128:(c + 1) * 128]
                            P.op("pe", lambda e, pm=pm, c=c, cur=cur, Pv=Pv: e.matmul(pm[:, c * 128:(c + 1) * 128], lhsT=fr(Pv),
                                                                                     rhs=fr(PT[cur][:, c * 128:(c + 1) * 128]), start=True, stop=True),
                                 r=[r_PT[cur], r_SC if it == 0 else r_PP[cur]], w=[r_pm], signal=(c == 3))
                        P.op("dve", lambda e, pm=pm, nxt=nxt: e.tensor_copy(fr(PT[nxt][:]), pm[:, :]), r=[r_pm], w=[r_PT[nxt]])
                        yield
                        pm, r_pm = next_pm()
                        for c in range(4):
                            P.op("pe", lambda e, pm=pm, c=c, nxt=nxt: e.matmul(pm[:, c * 128:(c + 1) * 128], lhsT=fr(PT[nxt][:, c * 128:(c + 1) * 128]),
                                                                              rhs=fr(Tm[:, c * 128:(c + 1) * 128]), start=True, stop=True),
                                 r=[r_PT[nxt], r_Tm], w=[r_pm], signal=(c == 3))
                        P.op("dve", lambda e, pm=pm: e.tensor_tensor(out=fr(Tm[:]), in0=pm[:, :], in1=Tm[:], op=ALU.add), r=[r_pm, r_Tm], w=[r_Tm])
                        yield
                        cur = nxt
                    for c in range(4):
                        pm, r_pm = next_pm()
                        P.op("pe", lambda e, pm=pm, c=c: e.matmul(pm[:, 0:64], lhsT=fr(AR[:, c, 0, :]), rhs=fr(Zst[:]), start=True, stop=False),
                             r=[r_AR, r_Z], w=[r_pm], signal=False)
                        P.op("pe", lambda e, pm=pm, c=c: e.matmul(pm[:, 0:64], lhsT=fr(SC[:, c, 256:384]), rhs=fr(TM[:, c, 0, :]), start=False, stop=True),
                             r=[r_SC, r_TM], w=[r_pm])
                        P.op("act", lambda e, pm=pm: e.activation(out=fr(XTs[:]), in_=pm[:, 0:64], func=AF.Copy), r=[r_pm], w=[r_XTs])
                        yield
                        pm, r_pm = next_pm()
                        P.op("pe", lambda e, pm=pm, c=c: e.matmul(pm[:, 0:64], lhsT=fr(Tm[:, c * 128:(c + 1) * 128]), rhs=fr(XTs[:]), start=True, stop=True),
                             r=[r_Tm, r_XTs], w=[r_pm])
                        P.op("act", lambda e, pm=pm: e.activation(out=fr(UTs[:]), in_=pm[:, 0:64], func=AF.Copy), r=[r_pm], w=[r_UTs])
                        yield
                        yo = pY[:, c * 64:(c + 1) * 64]
                        P.op("pe", lambda e, yo=yo, c=c: e.matmul(yo, lhsT=fr(AR[:, c, 1, :]), rhs=fr(Zst[:]), start=True, stop=False),
                             r=[r_AR, r_Z], w=[r_pY], signal=False)
                        P.op("pe", lambda e, yo=yo, c=c: e.matmul(yo, lhsT=fr(SC[:, c, 128:256]), rhs=fr(UTs[:]), start=False, stop=False),
                             r=[r_SC, r_UTs], w=[r_pY], signal=False)
                        P.op("pe", lambda e, yo=yo, c=c: e.matmul(yo, lhsT=fr(SC[:, c, 384:512]), rhs=fr(TM[:, c, 0, :]), start=False, stop=True),
                             r=[r_SC, r_TM], w=[r_pY])
                        yield
                        pm, r_pm = next_pm()
                        P.op("pe", lambda e, pm=pm, c=c: e.matmul(pm[0:64, 0:64], lhsT=fr(TM[:, c, 1, :]), rhs=fr(UTs[:]), start=True, stop=False),
                             r=[r_TM, r_UTs], w=[r_pm], signal=False)
                        P.op("pe", lambda e, pm=pm, c=c: e.matmul(pm[0:64, 0:64], lhsT=fr(TM[:, c, 2, :]), rhs=fr(TM[:, c, 0, :]), start=False, stop=True),
                             r=[r_TM], w=[r_pm])
                        P.op("dve", lambda e, pm=pm, c=c: e.scalar_tensor_tensor(out=fr(Zst[0:64, :]), in0=Zst[0:64, :], scalar=Tt["rinv"][:, c * 128 + 127:c * 128 + 128], in1=pm[0:64, 0:64],
                                                                                op0=ALU.mult, op1=ALU.add), r=[r_Z, R["rinv"], r_pm], w=[r_Z])
                        yield
                    P.op("act", lambda e: e.activation(out=Yt[:], in_=pY[:, 0:256].rearrange("p (a b) -> p a b", b=64), func=AF.Copy), r=[r_pY], w=[r_Yt])
                    if h == 0 and G == 0:
                        self.dbg_dump("yraw", Yt[:], [128, 4, 64], r_Yt)
                    P.op("dve", lambda e: e.reduce_sum(out=st4[:, 0:4], in_=Yt[:], axis=AX.X), r=[r_Yt], w=[r_st4])
                    yield
                    P.op("dve", lambda e: e.scalar_tensor_tensor(out=yc[:], in0=st4[:, 0:4].unsqueeze(2).to_broadcast([128, 4, 64]), scalar=-1.0 / 64,
                                                                in1=Yt[:], op0=ALU.mult, op1=ALU.add), r=[r_Yt, r_st4], w=[r_yc])
                    P.op("dve", lambda e: e.tensor_tensor(out=ysq[:], in0=yc[:], in1=yc[:], op=ALU.mult), r=[r_yc], w=[r_ysq])
                    P.op("dve", lambda e: e.reduce_sum(out=st4[:, 4:8], in_=ysq[:], axis=AX.X), r=[r_ysq], w=[r_st4])
                    yield
                    P.op("dve", lambda e: e.tensor_scalar(out=st4[:, 4:8], in0=st4[:, 4:8], scalar1=1.0 / 64, scalar2=64e-5, op0=ALU.mult, op1=ALU.add),
                         r=[r_st4], w=[r_st4])
                    P.op("act", lambda e: e.activation(out=st4[:, 4:8], in_=st4[:, 4:8], func=AF.Sqrt), r=[r_st4], w=[r_st4])
                    P.op("dve", lambda e: e.reciprocal(st4[:, 8:12], st4[:, 4:8]), r=[r_st4], w=[r_st4])
                    yield
                    pm, r_pm = next_pm()
                    for tq in range(4):
                        P.op("pe", lambda e, pm=pm, tq=tq: e.transpose(pm[:, tq * 64:(tq + 1) * 64], Tt["bon"][:, tq * 128:(tq + 1) * 128],
                                                                     ident[0:64, 0:64]), r=[R["bon"], r_ident], w=[r_pm], signal=(tq == 3))
                    P.op("dve", lambda e: e.tensor_tensor(out=yc[:], in0=yc[:], in1=st4[:, 8:12].unsqueeze(2).to_broadcast([128, 4, 64]), op=ALU.mult),
                         r=[r_yc, r_st4], w=[r_yc])
                    P.op("dve", lambda e: e.tensor_tensor(out=yc[:], in0=yc[:], in1=lnw[:, :].unsqueeze(1).to_broadcast([128, 4, 64]), op=ALU.mult),
                         r=[r_yc, r_lnw], w=[r_yc])
                    yield
                    P.op("dve", lambda e: e.tensor_tensor(out=yc[:], in0=yc[:], in1=lnb[:, :].unsqueeze(1).to_broadcast([128, 4, 64]), op=ALU.add),
                         r=[r_yc, r_lnb], w=[r_yc])
                    P.op("dve", lambda e, pm=pm: e.tensor_tensor(out=yc[:], in0=yc[:], in1=pm[:, 0:256].rearrange("p (a b) -> p a b", b=64), op=ALU.add),
                         r=[r_yc, r_pm], w=[r_yc])
                    yield
                    if h == 0 and G == 0:
                        self.dbg_dump("ob00", yc[:], [128, 4, 64], r_yc)

                    mo = mixo[state["mi"] % 2]; r_mo = r_mixo[state["mi"] % 2]; state["mi"] += 1
                    P.op("dve", lambda e, mo=mo: e.tensor_tensor(out=mo[:], in0=yc[:], in1=zb[:], op=ALU.mult), r=[r_yc, r_zb], w=[r_mo])
                    dst = self.mixd.ap()[G * NT:(G + 1) * NT, 1024 + h * 64:1024 + (h + 1) * 64].rearrange("(t p) c -> p t c", p=128)
                    rr = Reg()
                    self.rw_store_regs.append(rr)
                    P.dma("sp", lambda e, mo=mo, dst=dst: e.dma_start(out=dst, in_=mo[:]), r=[r_mo], w=[rr])
                    yield
            return run

        runs = [make_set(0, self.px, self.r_px), make_set(1, self.py2, self.r_py2)]
        for hp in range(8):
            gens = [runs[0](2 * hp), runs[1](2 * hp + 1)]
            for _ in range(RW_STAGGER):
                try:
                    next(gens[0])
                except StopIteration:
                    break
            while gens:
                for gq in list(gens):
                    try:
                        next(gq)
                    except StopIteration:
                        gens.remove(gq)
        self.dbg_dump("mixall", self.mixd.ap(), [S, 2048], list(self.r_mixd) + self.rw_store_regs, BF16)

    def phase_out(self, st):
        nc, P = self.nc, self.P
        din = self.din
        wo = self.sb(st, "wo", [128, NCH, D - 512], BF16); r_wo = [self.r_wo0] + [Reg() for _ in range(3)]
        for nb in range(1, 4):
            P.dma("pool", lambda e, nb=nb: e.dma_start(out=wo[:, :, (nb - 1) * 512:nb * 512],
                                                       in_=din["w_out"].ap()[:, nb * 512:(nb + 1) * 512].rearrange("(c p) n -> p c n", p=128)),
                  w=[r_wo[nb]])

        def wo_blk(c, nb):
            return self.wo0[:, c, :] if nb == 0 else wo[:, c, (nb - 1) * 512:nb * 512]
        gpo = self.sb(st, "gpo", [128, D]); r_gpo = Reg()
        P.dma("sp", lambda e: e.dma_start(out=gpo[:], in_=din["g_post"].ap()), w=[r_gpo])
        idb = self.sb(st, "idb2", [128, 128], BF16); r_idb = Reg()
        idf = self.sb(st, "idf2", [128, 128]); r_idf = Reg()
        P.dma("sp", lambda e: e.dma_start(out=idf[:], in_=din["c_ident"].ap()), w=[r_idf])
        P.op("dve", lambda e: e.tensor_copy(idb[:], idf[:]), r=[r_idf], w=[r_idb])
        mx = [self.sb(st, "mx%d" % i, [128, D], BF16) for i in range(2)]; r_mx = [Reg(), Reg()]
        mT = [self.sb(st, "mT%d" % i, [128, NCH, 128], BF16) for i in range(2)]; r_mT = [Reg(), Reg()]
        xr = [self.sb(st, "xr%d" % i, [128, D]) for i in range(2)]; r_xr = [Reg(), Reg()]
        ysbs = [self.sb(st, "ysb%d" % i, [128, D]) for i in range(2)]; r_ysbs = [Reg(), Reg()]
        jks = [self.sb(st, "jk%d" % i, [128, D]) for i in range(2)]; r_jks = [Reg(), Reg()]
        s1s = [self.sb(st, "s1_%d" % i, [128, 4]) for i in range(2)]; r_s1s = [Reg(), Reg()]
        ob = [self.sb(st, "ob%d" % i, [128, D]) for i in range(2)]; r_ob = [Reg(), Reg()]
        mix_regs = list(self.r_mixd) + self.rw_store_regs

        def stage_a(T):
            b = T % 2
            P.dma("sp", lambda e: e.dma_start(out=mx[b][:], in_=self.mixd.ap()[T * 128:(T + 1) * 128, :]), r=mix_regs, w=[r_mx[b]])
            P.dma("sp", lambda e: e.dma_start(out=xr[b][:], in_=din["x"].ap()[T * 128:(T + 1) * 128, :]), w=[r_xr[b]])
            for gq in range(4):
                for jq in range(4):
                    c = gq * 4 + jq
                    P.op("pe", lambda e, c=c, jq=jq: e.transpose(self.ptr[:, jq, :], mx[b][:, c * 128:(c + 1) * 128], idb[:]),
                         r=[r_mx[b], r_idb], w=[self.r_ptr], signal=(jq == 3))
                P.op("dve", lambda e, gq=gq: e.tensor_copy(mT[b][:, gq * 4:(gq + 1) * 4, :], self.ptr[:]),
                     r=[self.r_ptr], w=[r_mT[b]])

        def stage_b(T):
            b = T % 2
            ysb, r_ysb = ysbs[b], r_ysbs[b]
            for nb in range(4):
                pm, r_pm = self.next_pm()
                for c in range(NCH):
                    P.op("pe", lambda e, pm=pm, c=c, nb=nb: e.matmul(pm[:, :], lhsT=mT[b][:, c, :], rhs=wo_blk(c, nb),
                                                                   start=(c == 0), stop=(c == NCH - 1)),
                         r=[r_mT[b], r_wo[nb]], w=[r_pm], signal=(c == NCH - 1))
                P.op("act", lambda e, pm=pm, nb=nb: e.activation(out=ysb[:, nb * 512:(nb + 1) * 512], in_=pm[:, :], func=AF.Copy),
                     r=[r_pm], w=[r_ysb])

        def stage_c(T):
            b = T % 2
            ysb, r_ysb, jk, r_jk, s1, r_s1 = ysbs[b], r_ysbs[b], jks[b], r_jks[b], s1s[b], r_s1s[b]
            P.op("act", lambda e: e.activation(out=jk[:], in_=ysb[:], func=AF.Square), r=[r_ysb], w=[r_jk])
            P.op("dve", lambda e: e.reduce_sum(out=s1[:, 0:1], in_=jk[:], axis=AX.X), r=[r_jk], w=[r_s1])
            P.op("dve", lambda e: e.tensor_scalar(out=s1[:, 0:1], in0=s1[:, 0:1], scalar1=1.0 / D, scalar2=1e-6, op0=ALU.mult, op1=ALU.add),
                 r=[r_s1], w=[r_s1])
            P.op("act", lambda e: e.activation(out=s1[:, 0:1], in_=s1[:, 0:1], func=AF.Sqrt), r=[r_s1], w=[r_s1])
            P.op("dve", lambda e: e.reciprocal(s1[:, 1:2], s1[:, 0:1]), r=[r_s1], w=[r_s1])
            P.op("dve", lambda e: e.tensor_tensor(out=jk[:], in0=ysb[:], in1=gpo[:], op=ALU.mult), r=[r_ysb, r_gpo], w=[r_jk])
            P.op("dve", lambda e: e.scalar_tensor_tensor(out=ob[b][:], in0=jk[:], scalar=s1[:, 1:2], in1=xr[b][:], op0=ALU.mult, op1=ALU.add),
                 r=[r_jk, r_s1, r_xr[b]], w=[r_ob[b]])
            ro = Reg()
            self.out_regs.append(ro)
            P.dma("sp", lambda e: e.dma_start(out=self.out.ap()[T * 128:(T + 1) * 128, :], in_=ob[b][:]), r=[r_ob[b]], w=[ro])

        stage_a(0)
        for T in range(16):
            stage_b(T)
            if T + 1 < 16:
                stage_a(T + 1)
            stage_c(T)


def _build(in_shapes, dbg=()):
    b = B(in_shapes, dbg)
    nc = b.build()
    return nc, b


def kernel(**inputs):
    inputs = {k: np.asarray(v) for k, v in inputs.items()}
    consts = _consts()
    L = _layout_inputs(inputs)
    shared = dict(consts)
    shared.update(L)
    in_shapes = {"x": (S, D)}
    for k, v in shared.items():
        in_shapes[k] = v.shape
    nc, b = _build(in_shapes)
    active = [0, 1, 4, 5]
    zero_x = np.zeros((S, D), np.float32)
    in_maps = []
    for c in range(8):
        if c in active:
            m = {"x": np.ascontiguousarray(inputs["x"][active.index(c)])}
        else:
            m = {"x": zero_x}
        m.update(shared)
        in_maps.append(m)
    res = run_bass_kernel_spmd(nc, in_maps, core_ids=list(range(8)))
    out = np.stack([res.results[c]["out"] for c in active], 0)
    return out.astype(np.float32)
```

```python
import math
from contextlib import ExitStack, contextmanager
import numpy as np
import concourse.bass as bass
import concourse.mybir as mybir
from concourse.bass_utils import run_bass_kernel_spmd

F32 = mybir.dt.float32
BF16 = mybir.dt.bfloat16
ALU = mybir.AluOpType
AF = mybir.ActivationFunctionType
AX = mybir.AxisListType

ENGS = ("pe", "act", "dve", "pool", "sp")
N_DMA_SEMS = 12
S = 2048
D = 2048
NCH = 16
BIG = 1.0e30
C0 = math.exp(-0.5)
USE_F32R = True
RW_STAGGER = 0


class Reg:
    __slots__ = ("lw", "rd")

    def __init__(self):
        self.lw = None
        self.rd = {}


class Prog:
    def __init__(self, nc):
        self.nc = nc
        self.q = {e: [] for e in ENGS}
        self.cnt = {e: 0 for e in ENGS}
        self.pending = {e: False for e in ENGS}
        self.waited = {e: {} for e in ENGS}
        self.dma_k = {e: 0 for e in ENGS}

    def op(self, eng, fn, r=(), w=(), signal=True):
        deps = {}

        def need(key, val, kind):
            if key == eng:
                if eng == "pe":
                    return
            if deps.get(key, 0) < val:
                deps[key] = val

        for reg in r:
            if reg.lw is not None:
                need(reg.lw[0], reg.lw[1], "raw")
        for reg in w:
            if reg.lw is not None:
                need(reg.lw[0], reg.lw[1], "waw")
            for k, v in reg.rd.items():
                need(k, v, "war")
        waits = []
        wd = self.waited[eng]
        for k, v in deps.items():
            if wd.get(k, 0) < v:
                wd[k] = v
                waits.append((k, v))
        n = self.cnt[eng] + 1
        if signal:
            self.cnt[eng] = n
            self.pending[eng] = False
        else:
            self.pending[eng] = True
        self.q[eng].append((waits, fn, (eng, 1) if signal else None))
        for reg in r:
            reg.rd[eng] = n
        for reg in w:
            reg.lw = (eng, n)
            reg.rd = {}

    def dma(self, qe, fn, r=(), w=()):
        deps = {}
        for reg in r:
            if reg.lw is not None:
                k, v = reg.lw
                deps[k] = max(deps.get(k, 0), v)
        for reg in w:
            if reg.lw is not None:
                k, v = reg.lw
                deps[k] = max(deps.get(k, 0), v)
            for k, v in reg.rd.items():
                deps[k] = max(deps.get(k, 0), v)
        kk = self.dma_k[qe]
        self.dma_k[qe] = kk + 1
        slot = kk % N_DMA_SEMS
        key = "dma_%s_%d" % (qe, slot)
        prev = 16 * (kk // N_DMA_SEMS)
        if prev > 0:
            deps[key] = max(deps.get(key, 0), prev)
        waits = []
        wd = self.waited[qe]
        for k, v in deps.items():
            if wd.get(k, 0) < v:
                wd[k] = v
                waits.append((k, v))
        tgt = prev + 16
        self.q[qe].append((waits, fn, (key, 16)))
        for reg in r:
            reg.rd[key] = max(reg.rd.get(key, 0), tgt)
        for reg in w:
            reg.lw = (key, tgt)
            reg.rd = {}

    def barrier(self):
        deps = {e: self.cnt[e] for e in ENGS if self.cnt[e] > 0}
        for qe in ENGS:
            kk = self.dma_k[qe]
            for slot in range(min(N_DMA_SEMS, kk)):
                uses = (kk - slot + N_DMA_SEMS - 1) // N_DMA_SEMS
                deps["dma_%s_%d" % (qe, slot)] = 16 * uses
        for e in ENGS:
            assert not self.pending[e]
            waits = []
            for k, v in deps.items():
                if k != e and self.waited[e].get(k, 0) < v:
                    self.waited[e][k] = v
                    waits.append((k, v))
            if waits:
                self.q[e].append((waits, None, None))

    def final_wait(self, eng, regs):
        deps = {}
        for reg in regs:
            if reg.lw is not None:
                k, v = reg.lw
                deps[k] = max(deps.get(k, 0), v)
        self.q[eng].append((list(deps.items()), None, None))

    def emit(self):
        nc = self.nc
        with ExitStack() as st:
            sems = {}
            for e in ENGS:
                sems[e] = st.enter_context(nc.semaphore("s_" + e))
            for qe in ENGS:
                for i in range(min(N_DMA_SEMS, self.dma_k[qe])):
                    key = "dma_%s_%d" % (qe, i)
                    sems[key] = st.enter_context(nc.semaphore(key))
            block = st.enter_context(nc.Block())
            for e in ENGS:
                assert not self.pending[e], e

            def run(engname):
                def body(eng):
                    for waits, fn, inc in self.q[engname]:
                        for k, v in waits:
                            eng.wait_ge(sems[k], v)
                        if fn is not None:
                            ins = fn(eng)
                            if inc is not None:
                                ins.then_inc(sems[inc[0]], inc[1])
                return body

            block.tensor(run("pe"))
            block.scalar(run("act"))
            block.vector(run("dve"))
            block.gpsimd(run("pool"))
            block.sync(run("sp"))


def _bucket(n):
    n = np.maximum(n, 0)
    nf = np.maximum(n, 1).astype(np.float32)
    large = 16 + (np.log(nf / np.float32(16)) / np.float32(math.log(64)) * np.float32(16)).astype(np.int32)
    large = np.minimum(large, 31)
    return np.where(n < 16, n, large)


def _consts():
    c = {}
    n = np.arange(2048)
    oh = np.zeros((32, 4096), np.float32)
    oh[_bucket(n), n] = 1.0
    c["c_oh"] = oh
    c["c_ident"] = np.eye(128, dtype=np.float32)
    t = np.arange(S)
    cur = t // 64
    j = np.arange(32)
    forced = (j[None, :] == 0) | (j[None, :] == cur[:, None]) | (j[None, :] == cur[:, None] - 1)
    causal = j[None, :] <= cur[:, None]
    m1 = (causal & ~forced).astype(np.float32)
    add = np.where(forced, BIG, np.where(causal, 0.0, -BIG)).astype(np.float32)
    c["c_m1"] = np.ascontiguousarray(m1.reshape(16, 128, 32).transpose(1, 0, 2))
    c["c_add"] = np.ascontiguousarray(add.reshape(16, 128, 32).transpose(1, 0, 2))
    e2 = (np.arange(S)[None, :] // 64 == j[:, None]).astype(np.float32)
    c["c_e2"] = e2
    cs = np.arange(127) * 16
    ss = np.arange(32) * 64
    ov = ((cs[:, None] < ss[None, :] + 64) & (cs[:, None] + 32 > ss[None, :])).astype(np.float32)
    c["c_ov"] = ov
    tri_s = np.triu(np.ones((64, 64), np.float32), 1)
    tri_i = np.triu(np.ones((64, 64), np.float32), 0)
    ts2 = np.triu(np.ones((128, 128), np.float32), 1)
    ti2 = np.triu(np.ones((128, 128), np.float32), 0)
    c["c_mask_sc2"] = np.concatenate([ts2, ti2, ts2, ti2], 1)
    c["c_mask_t8"] = np.ascontiguousarray(np.tile(ts2.T, (1, 4)))
    c["c_ident8"] = np.ascontiguousarray(np.tile(np.eye(128, dtype=np.float32), (1, 4)))
    cm = np.ones((64, S), np.float32)
    cm[:, ::128] = 0.0
    c["c_cmask"] = cm
    return c


W_NSA_FM = 1024
W_NSA_TM = 780


def _layout_inputs(inp):
    L = {}
    w_in = inp["w_in"][0]
    o_kv = 1024
    o_g = 2560
    o_za = 2584
    o_f = 3608
    o_zb = 6808
    for g in range(2):
        cols = list(range(g * 512, g * 512 + 512))
        cols += list(range(o_kv + 0 * 256 + g * 128, o_kv + 0 * 256 + g * 128 + 128))
        cols += list(range(o_kv + 1 * 256 + g * 128, o_kv + 1 * 256 + g * 128 + 128))
        cols += list(range(o_kv + 2 * 256 + g * 128, o_kv + 2 * 256 + g * 128 + 128))
        cols += list(range(o_kv + 4 * 256 + g * 128, o_kv + 4 * 256 + g * 128 + 128))
        cols += list(range(o_kv + 3 * 256 + g * 128, o_kv + 3 * 256 + g * 128 + 128))
        cols += list(range(o_kv + 5 * 256 + g * 128, o_kv + 5 * 256 + g * 128 + 128))
        for br in range(3):
            cols += [o_g + br * 8 + g * 4 + h for h in range(4)]
        cols += list(range(o_za + g * 512, o_za + g * 512 + 512))
        L["w_nsa%d" % g] = np.ascontiguousarray(w_in[:, cols])
    L["w_rkv"] = np.ascontiguousarray(w_in[:, o_f:o_f + 3072])
    L["w_lora"] = np.ascontiguousarray(w_in[:, o_f + 3072:o_f + 3200])
    L["w_zb"] = np.ascontiguousarray(w_in[:, o_zb:o_zb + 1024])
    L["g_pre"] = np.ascontiguousarray(inp["pre_norm_g"][0].reshape(16, 128).T)
    L["g_post"] = np.ascontiguousarray(np.tile(inp["post_norm_g"][0][None, :], (128, 1)))
    L["tab"] = np.ascontiguousarray(inp["rel_bias_table"])
    cmp_params = {"k": (inp["cmp_pos_k"], inp["cmp_k_w1"], inp["cmp_k_w2"]),
                  "v": (inp["cmp_pos_v"], inp["cmp_v_w1"], inp["cmp_v_w2"])}
    for nm in ("k", "v"):
        pos_, w1_, w2_ = cmp_params[nm]
        L["pos%sT" % nm] = np.ascontiguousarray(pos_[0].T)
        L["w1%s" % nm] = np.ascontiguousarray(w1_[0].reshape(32, 128, 128).transpose(1, 0, 2))
        L["w2%s" % nm] = np.ascontiguousarray(w2_[0])
    mu = inp["rwkv_mu"][0]
    per = np.zeros((64, 16, 8), np.float32)
    for h in range(16):
        sl = slice(h * 64, h * 64 + 64)
        per[:, h, 0] = mu[0:1024][sl]
        per[:, h, 1] = mu[1024:2048][sl]
        per[:, h, 2] = mu[2048:3072][sl]
        per[:, h, 3] = inp["rwkv_w0"][0][sl]
        per[:, h, 4] = inp["rwkv_a0"][0][sl]
        per[:, h, 5] = inp["rwkv_k_k"][0][sl]
        per[:, h, 6] = inp["rwkv_k_a"][0][sl]
        per[:, h, 7] = inp["rwkv_r_k"][0][h]
    L["rw_per"] = per
    L["mu_lora"] = np.ascontiguousarray(mu[3072:3200].reshape(2, 64).T)
    L["rw_w2"] = np.ascontiguousarray(inp["rwkv_w2"][0])
    L["rw_a2"] = np.ascontiguousarray(inp["rwkv_a2"][0])
    L["ln_w"] = np.ascontiguousarray(np.tile(inp["rwkv_ln_w"][0][None, :], (128, 1)))
    L["ln_b"] = np.ascontiguousarray(np.tile(inp["rwkv_ln_b"][0][None, :], (128, 1)))
    L["w_out"] = np.ascontiguousarray(inp["w_out"][0])
    return L


_IN_SHAPES = None


class B:
    def __init__(self, in_shapes, dbg=()):
        self.dbg = dbg
        nc = self.nc = bass.Bass("TRN2", target_bir_lowering=False)
        self.P = Prog(nc)
        self.din = {}
        for k, shp in in_shapes.items():
            self.din[k] = nc.dram_tensor(k, list(shp), F32, kind="ExternalInput")
        self.out = nc.dram_tensor("out", [S, D], F32, kind="ExternalOutput")
        self.mixd = nc.dram_tensor("mixd", [S, 2048], BF16)
        self.Z = [nc.dram_tensor("Zs%d" % h, [132, 4096], BF16) for h in range(8)]
        self.Zw = [nc.dram_tensor("Zw%d" % h, [132, 4096], BF16) for h in range(8)]
        self.r_Z = [Reg() for _ in range(8)]
        self.r_Zw = [Reg() for _ in range(8)]
        self.r_Z2 = [None] * 8
        self.r_Zw2 = [None] * 8
        self.r_mixd = [Reg() for _ in range(16)]
        self.out_regs = []
        self.rw_store_regs = []
        self.dbg_out = {}
        self.es = ExitStack()

    @contextmanager
    def scope(self):
        with ExitStack() as st:
            yield st
        self.P.barrier()

    def sb(self, st, name, shape, dt=F32):
        self.uid = getattr(self, "uid", 0) + 1
        return st.enter_context(self.nc.sbuf_tensor("s%d_%s" % (self.uid, name), list(shape), dt))

    def ps(self, st, name, shape, dt=F32):
        self.uid = getattr(self, "uid", 0) + 1
        return st.enter_context(self.nc.psum_tensor("p%d_%s" % (self.uid, name), list(shape), dt))

    def dbg_dump(self, name, ap_src, shape, reg, dt=F32):
        if name not in self.dbg:
            return
        t = self.nc.dram_tensor("dbg_" + name, list(shape), dt, kind="ExternalOutput")
        self.dbg_out[name] = t
        ro = Reg()
        self.out_regs.append(ro)
        self.P.dma("sp", lambda e: e.dma_start(out=t.ap(), in_=ap_src), r=[reg] if not isinstance(reg, list) else reg, w=[ro])

    def build(self):
        nc, P = self.nc, self.P
        with self.scope() as st:
            self.ident = self.sb(st, "ident", [128, 128]); self.r_ident = Reg()
            self.identb = self.sb(st, "identb", [128, 128], BF16); self.r_identb = Reg()
            self.ones = self.sb(st, "ones", [128, 128]); self.r_ones = Reg()
            self.xT = self.sb(st, "xT", [128, NCH, S], BF16)
            self.r_xT = [Reg() for _ in range(16)]
            self.rstd_col = self.sb(st, "rstd_col", [128, 16]); self.r_rc = Reg()
            self.rstd_bc = self.sb(st, "rstd_bc", [128, S]); self.r_rb = Reg()
            self.gpre = self.sb(st, "gpre", [128, 16]); self.r_gpre = Reg()
            self.pm = [self.ps(st, "pm%d" % i, [128, 512]) for i in range(6)]
            self.r_pm = [Reg() for _ in range(6)]
            self.ptr = self.ps(st, "ptr", [128, 4, 128], BF16); self.r_ptr = Reg()
            self.px = self.ps(st, "px", [128, 512]); self.r_px = Reg()
            self.pm_i = 0

            P.dma("sp", lambda e: e.dma_start(out=self.ident[:], in_=self.din["c_ident"].ap()), w=[self.r_ident])
            P.op("dve", lambda e: e.tensor_copy(self.identb[:], self.ident[:]), r=[self.r_ident], w=[self.r_identb])
            P.op("pool", lambda e: e.memset(self.ones[:], 1.0), w=[self.r_ones])
            P.dma("sp", lambda e: e.dma_start(out=self.gpre[:], in_=self.din["g_pre"].ap()), w=[self.r_gpre])

            self.phase0(st)
            self.phase_eb()
            for g in range(2):
                with self.scope() as st2:
                    self.phase_nsa(st2, g)
            with self.scope() as st2:
                self.phase_rwkv_proj(st2)
        with ExitStack() as stx:
            self.wo0 = self.sb(stx, "wo0", [128, NCH, 512], BF16); self.r_wo0 = Reg()
            P.dma("pool", lambda e: e.dma_start(out=self.wo0[:], in_=self.din["w_out"].ap()[:, 0:512].rearrange("(c p) n -> p c n", p=128)),
                  w=[self.r_wo0])
            with self.scope() as st:
                self.pm = [self.ps(st, "rm%d" % i, [128, 512]) for i in range(6)]
                self.r_pm = [Reg() for _ in range(6)]
                self.px = self.ps(st, "rpx", [128, 512]); self.r_px = Reg()
                self.py2 = self.ps(st, "rpy2", [128, 512]); self.r_py2 = Reg()
                self.phase_rwkv(st)
            with self.scope() as st:
                self.pm = [self.ps(st, "qm%d" % i, [128, 512]) for i in range(6)]
                self.r_pm = [Reg() for _ in range(6)]
                self.ptr = self.ps(st, "qtr", [128, 4, 128], BF16); self.r_ptr = Reg()
                self.phase_out(st)
                P.final_wait("sp", self.out_regs)
                P.emit()
        return nc

    def next_pm(self):
        i = self.pm_i
        self.pm_i = (i + 1) % len(self.pm)
        return self.pm[i], self.r_pm[i]

    def phase0(self, st0):
        nc, P = self.nc, self.P
        x = self.din["x"].ap()
        with self.scope() as st:
            xt = [self.sb(st, "xt%d" % i, [128, D]) for i in range(2)]
            r_xt = [Reg(), Reg()]
            xb = [self.sb(st, "xb%d" % i, [128, D], BF16) for i in range(2)]
            r_xb = [Reg(), Reg()]
            junk = self.sb(st, "junk", [128, D]); r_junk = Reg()
            ss = self.sb(st, "ss", [128, 16]); r_ss = Reg()
            dg = self.sb(st, "dg", [128, 128]); r_dg = Reg()
            for tt in range(16):
                b = tt % 2
                P.dma("sp", lambda e, b=b, tt=tt: e.dma_start(out=xt[b][:], in_=x[tt * 128:(tt + 1) * 128, :]), w=[r_xt[b]])
                P.op("act", lambda e, b=b, tt=tt: e.activation(out=junk[:], in_=xt[b][:], func=AF.Square), r=[r_xt[b]], w=[r_junk])
                P.op("dve", lambda e, tt=tt: e.reduce_sum(out=ss[:, tt:tt + 1], in_=junk[:], axis=AX.X), r=[r_junk], w=[r_ss])
                P.op("pool", lambda e, b=b: e.tensor_copy(xb[b][:], xt[b][:]), r=[r_xt[b]], w=[r_xb[b]])
                for gq in range(4):
                    for jq in range(4):
                        c = gq * 4 + jq
                        P.op("pe", lambda e, b=b, c=c, jq=jq: e.transpose(self.ptr[:, jq, :], xb[b][:, c * 128:(c + 1) * 128], self.identb[:]),
                             r=[r_xb[b], self.r_identb], w=[self.r_ptr], signal=(jq == 3))
                    for jq in range(4):
                        c = gq * 4 + jq
                        P.op("dve", lambda e, c=c, jq=jq, tt=tt: e.tensor_scalar(
                            out=self.xT[:, c, tt * 128:(tt + 1) * 128], in0=self.ptr[:, jq, :],
                            scalar1=self.gpre[:, c:c + 1], scalar2=None, op0=ALU.mult),
                            r=[self.r_ptr, self.r_gpre], w=[self.r_xT[tt]])
            P.op("dve", lambda e: e.tensor_scalar(out=ss[:], in0=ss[:], scalar1=1.0 / D, scalar2=1e-6, op0=ALU.mult, op1=ALU.add),
                 r=[r_ss], w=[r_ss])
            P.op("act", lambda e: e.activation(out=ss[:], in_=ss[:], func=AF.Sqrt), r=[r_ss], w=[r_ss])
            P.op("dve", lambda e: e.reciprocal(self.rstd_col[:], ss[:]), r=[r_ss], w=[self.r_rc])
            for tt in range(16):
                P.op("dve", lambda e, tt=tt: e.tensor_scalar(out=dg[:], in0=self.ident[:], scalar1=self.rstd_col[:, tt:tt + 1],
                                                            scalar2=None, op0=ALU.mult), r=[self.r_ident, self.r_rc], w=[r_dg])
                P.op("pe", lambda e: e.matmul(self.px[:, 0:128], lhsT=self.ones[:], rhs=dg[:], start=True, stop=True),
                     r=[self.r_ones, r_dg], w=[self.r_px])
                P.op("act", lambda e, tt=tt: e.activation(out=self.rstd_bc[:, tt * 128:(tt + 1) * 128], in_=self.px[:, 0:128], func=AF.Copy),
                     r=[self.r_px], w=[self.r_rb])

        self.dbg_dump("rstd_col", self.rstd_col[:], [128, 16], self.r_rc)
        self.dbg_dump("rstd_bc", self.rstd_bc[:], [128, S], self.r_rb)
        self.dbg_dump("xT0", self.xT[:, 0, :], [128, S], self.r_xT, BF16)

    def load_w(self, wb, r_wb, src_ap):
        self.P.dma("pool", lambda e: e.dma_start(out=wb, in_=src_ap.rearrange("(c p) n -> p c n", p=128)), w=[r_wb])

    def proj_fm(self, wb, r_wb, j0, ncols, dst_fn, r_dst, scale=None, evac="dve"):
        P = self.P
        for tb in range(4):
            pm, r_pm = self.next_pm()
            for c in range(NCH):
                P.op("pe", lambda e, c=c, tb=tb, pm=pm: e.matmul(pm[0:ncols, :], lhsT=wb[:, c, j0:j0 + ncols],
                                                                 rhs=self.xT[:, c, tb * 512:(tb + 1) * 512],
                                                                 start=(c == 0), stop=(c == NCH - 1)),
                     r=[r_wb] + self.r_xT[tb * 4:tb * 4 + 4], w=[r_pm], signal=(c == NCH - 1))
            if scale is None:
                P.op("dve", lambda e, tb=tb, pm=pm: e.tensor_tensor(out=dst_fn(tb), in0=pm[0:ncols, :],
                                                                   in1=self.rstd_bc[0:ncols, tb * 512:(tb + 1) * 512], op=ALU.mult),
                     r=[r_pm, self.r_rb], w=[r_dst])
            else:
                P.op("dve", lambda e, tb=tb, pm=pm: e.scalar_tensor_tensor(out=dst_fn(tb), in0=pm[0:ncols, :], scalar=scale,
                                                                          in1=self.rstd_bc[0:ncols, tb * 512:(tb + 1) * 512],
                                                                          op0=ALU.mult, op1=ALU.mult),
                     r=[r_pm, self.r_rb], w=[r_dst])

    def proj_tm(self, wb, r_wb, j0, ncols, dst_fn, r_dst, func=AF.Copy):
        P = self.P
        for tt in range(16):
            pm, r_pm = self.next_pm()
            for c in range(NCH):
                P.op("pe", lambda e, c=c, tt=tt, pm=pm: e.matmul(pm[:, 0:ncols], lhsT=self.xT[:, c, tt * 128:(tt + 1) * 128],
                                                                 rhs=wb[:, c, j0:j0 + ncols], start=(c == 0), stop=(c == NCH - 1)),
                     r=[r_wb, self.r_xT[tt]], w=[r_pm], signal=(c == NCH - 1))
            P.op("act", lambda e, tt=tt, pm=pm: e.activation(out=dst_fn(tt), in_=pm[:, 0:ncols], func=func,
                                                            scale=self.rstd_col[:, tt:tt + 1]),
                 r=[r_pm, self.r_rc], w=[r_dst])

    def phase_eb(self):
        nc, P = self.nc, self.P
        with self.scope() as st:
            oh = self.sb(st, "oh", [32, 4096]); r_oh = Reg()
            tab = self.sb(st, "tab", [32, 8]); r_tab = Reg()
            zrow = [self.sb(st, "zrow%d" % i, [128, 4096], BF16) for i in range(2)]
            r_zrow = [Reg(), Reg()]
            P.dma("sp", lambda e: e.dma_start(out=oh[:], in_=self.din["c_oh"].ap()), w=[r_oh])
            P.dma("sp", lambda e: e.dma_start(out=tab[:], in_=self.din["tab"].ap()), w=[r_tab])
            tabrep = self.sb(st, "tabrep", [32, 8, 128]); r_tabrep = Reg()
            for h in range(8):
                P.op("dve", lambda e, h=h: e.tensor_scalar(out=tabrep[:, h, :], in0=self.ones[0:32, :], scalar1=tab[:, h:h + 1], scalar2=None,
                                                          op0=ALU.mult), r=[self.r_ones, r_tab], w=[r_tabrep])
            k = 0
            for h in range(8):
                for win in range(2):
                    zb = zrow[k % 2]; r_zb = r_zrow[k % 2]
                    k += 1
                    nblk = 1 if win else 4
                    if True:
                        P.op("pool", lambda e, zb=zb: e.memset(zb[:, 512 * nblk:], 0.0), w=[r_zb])
                    for blk in range(nblk):
                        pm, r_pm = self.next_pm()
                        P.op("pe", lambda e, h=h, blk=blk, pm=pm: e.matmul(pm[:, :], lhsT=tabrep[:, h, :],
                                                                          rhs=oh[:, blk * 512:(blk + 1) * 512], start=True, stop=True),
                             r=[r_tabrep, r_oh], w=[r_pm])
                        P.op("act", lambda e, blk=blk, pm=pm, zb=zb: e.activation(out=zb[:, blk * 512:(blk + 1) * 512], in_=pm[:, :], func=AF.Exp),
                             r=[r_pm], w=[r_zb])
                    dst = (self.Zw if win else self.Z)[h]
                    r_dst = (self.r_Zw if win else self.r_Z)[h]
                    P.dma("sp", lambda e, dst=dst, zb=zb: e.dma_start(out=dst.ap()[0:128, :], in_=zb[:]), r=[r_zb], w=[r_dst])
                    r_dst2 = Reg()
                    P.dma("sp", lambda e, dst=dst, zb=zb: e.dma_start(out=dst.ap()[128:132, :], in_=zb[0:4, :]), r=[r_zb], w=[r_dst2])
                    (self.r_Zw2 if win else self.r_Z2)[h] = r_dst2

    def toep(self, dst_ap, r_dst, Zt, r_Z, c, pstep, nparts, nfree, r_Z2=None):
        src = bass.AP(Zt, c % 4096, [[pstep, nparts], [1, nfree]])
        self.P.dma("pool", lambda e: e.dma_start(out=dst_ap, in_=src), r=[r_Z] + ([r_Z2] if r_Z2 is not None else []), w=[r_dst])

    def phase_nsa(self, st, g):
        nc, P = self.nc, self.P
        wsrc = self.din["w_nsa%d" % g].ap()
        qT = self.sb(st, "qT", [128, 4, S], BF16); r_qT = Reg()
        kT = self.sb(st, "kT", [128, 4, S], BF16); r_kT = [Reg() for _ in range(4)]
        vs = self.sb(st, "vs", [128, 16, 132], BF16); r_vs = Reg()
        vw = self.sb(st, "vw", [128, 16, 132], BF16); r_vw = Reg()
        gt = self.sb(st, "gt", [128, 16, 12]); r_gt = Reg()
        oacc = self.sb(st, "oacc", [128, 16, 512]); r_oacc = [Reg() for _ in range(16)]
        imp = self.sb(st, "imp", [128, 16, 32]); r_imp = [Reg() for _ in range(16)]
        negT = self.sb(st, "negT", [128, S], BF16); r_negT = Reg()
        e2c = self.sb(st, "e2c", [128, S], BF16); r_e2c = Reg()
        kcT = self.sb(st, "kcT", [128, 128], BF16); r_kcT = Reg()
        vce = self.sb(st, "vce", [128, 164], BF16); r_vce = Reg()
        P.op("pool", lambda e: e.memset(negT[:], 0.0), w=[r_negT])
        P.op("pool", lambda e: e.memset(e2c[:], 0.0), w=[r_e2c])

        with self.scope() as stw:
            wb = [self.sb(stw, "wbn%d" % i, [128, NCH, 512], BF16) for i in range(2)]
            r_wb = [Reg(), Reg()]
            self.load_w(wb[0][:], r_wb[0], wsrc[:, 0:512])
            self.load_w(wb[1][:], r_wb[1], wsrc[:, 512:1024])
            for h in range(4):
                self.proj_fm(wb[0], r_wb[0], h * 128, 128, lambda tb, h=h: qT[:, h, tb * 512:(tb + 1) * 512], r_qT, scale=128.0 ** -0.5)
            for i in range(4):
                self.proj_fm(wb[1], r_wb[1], i * 128, 128, lambda tb, i=i: kT[:, i, tb * 512:(tb + 1) * 512], r_kT[i])
            self.load_w(wb[0][:, :, 0:268], r_wb[0], wsrc[:, 1024:1292])
            P.op("pool", lambda e: e.memset(vs[:, :, 128:132], 1.0), w=[r_vs])
            P.op("pool", lambda e: e.memset(vw[:, :, 128:132], 1.0), w=[r_vw])
            self.proj_tm(wb[0], r_wb[0], 0, 128, lambda tt: vs[:, tt, 0:128], r_vs)
            self.proj_tm(wb[0], r_wb[0], 128, 128, lambda tt: vw[:, tt, 0:128], r_vw)
            self.proj_tm(wb[0], r_wb[0], 256, 12, lambda tt: gt[:, tt, :], r_gt, func=AF.Sigmoid)

        self.dbg_dump("qT%d" % g, qT[:, 0, :], [128, S], r_qT, BF16)
        self.dbg_dump("kT%d" % g, kT[:, 0, :], [128, S], r_kT[0], BF16)
        self.dbg_dump("vs%d" % g, vs[:, 0, :], [128, 132], r_vs, BF16)
        self.dbg_dump("gt%d" % g, gt[:, 0, :], [128, 12], r_gt)
        with self.scope() as st2:
            e2f = self.sb(st2, "e2f", [32, S]); r_e2f = Reg()
            P.dma("sp", lambda e: e.dma_start(out=e2f[:], in_=self.din["c_e2"].ap()), w=[r_e2f])
            P.op("dve", lambda e: e.tensor_copy(e2c[0:32, :], e2f[:]), r=[r_e2f], w=[r_e2c])

        with self.scope() as st2:
            w1 = self.sb(st2, "w1", [128, 32, 128], BF16); r_w1 = Reg()
            w2 = self.sb(st2, "w2", [128, 128], BF16); r_w2 = Reg()
            posT = self.sb(st2, "posT", [128, 32], BF16); r_posT = Reg()
            cb = self.sb(st2, "cb", [128, 1]); r_cb = Reg()
            h1s = self.sb(st2, "h1s", [128, 128], BF16); r_h1s = Reg()
            ovf = self.sb(st2, "ovf", [128, 33]); r_ovf = Reg()
            P.op("pool", lambda e: e.memset(ovf[:, 0:1], 1.0), w=[r_ovf])
            P.dma("sp", lambda e: e.dma_start(out=ovf[0:127, 1:33], in_=self.din["c_ov"].ap()), w=[r_ovf])
            P.op("dve", lambda e: e.tensor_copy(vce[0:127, 128:161], ovf[0:127, :]), r=[r_ovf], w=[r_vce])
            for which in range(2):
                nm = "kv"[which]
                P.dma("pool", lambda e, nm=nm: e.dma_start(out=w1[:], in_=self.din["w1" + nm].ap()), w=[r_w1])
                P.dma("pool", lambda e, nm=nm: e.dma_start(out=w2[:], in_=self.din["w2" + nm].ap()), w=[r_w2])
                P.dma("pool", lambda e, nm=nm: e.dma_start(out=posT[:], in_=self.din["pos%sT" % nm].ap()), w=[r_posT])
                pm, r_pm = self.next_pm()
                for l in range(32):
                    P.op("pe", lambda e, l=l, pm=pm: e.matmul(pm[:, 0:1], lhsT=w1[:, l, :], rhs=posT[:, l:l + 1], start=(l == 0), stop=(l == 31)),
                         r=[r_w1, r_posT], w=[r_pm], signal=(l == 31))
                P.op("dve", lambda e, pm=pm: e.tensor_copy(cb[:], pm[:, 0:1]), r=[r_pm], w=[r_cb])
                pm, r_pm = self.next_pm()
                for l in range(32):
                    P.op("pe", lambda e, l=l, pm=pm, which=which: e.matmul(pm[:, 0:127], lhsT=w1[:, l, :],
                                                                          rhs=kT[:, which, l:l + 16 * 126 + 1:16],
                                                                          start=(l == 0), stop=(l == 31)),
                         r=[r_w1, r_kT[which]], w=[r_pm], signal=(l == 31))
                P.op("act", lambda e, pm=pm: e.activation(out=h1s[:, 0:127], in_=pm[:, 0:127], func=AF.Silu, bias=cb[:, 0:1]),
                     r=[r_pm, r_cb], w=[r_h1s])
                pm, r_pm = self.next_pm()
                if which == 0:
                    P.op("pe", lambda e, pm=pm: e.matmul(pm[:, 0:127], lhsT=w2[:], rhs=h1s[:, 0:127], start=True, stop=True),
                         r=[r_w2, r_h1s], w=[r_pm])
                    P.op("dve", lambda e, pm=pm: e.tensor_copy(kcT[:, 0:127], pm[:, 0:127]), r=[r_pm], w=[r_kcT])
                else:
                    P.op("pe", lambda e, pm=pm: e.matmul(pm[0:127, 0:128], lhsT=h1s[:, 0:127], rhs=w2[:], start=True, stop=True),
                         r=[r_w2, r_h1s], w=[r_pm])
                    P.op("dve", lambda e, pm=pm: e.tensor_copy(vce[0:127, 0:128], pm[0:127, 0:128]), r=[r_pm], w=[r_vce])

        with self.scope() as sta:
            e1 = [self.sb(sta, "e1_%d" % i, [128, 512], BF16) for i in range(2)]
            r_e1 = [Reg(), Reg()]
            e2 = self.sb(sta, "e2", [128, 16, 512], BF16); r_e2 = [Reg() for _ in range(16)]
            sm = self.sb(sta, "sm", [128, 4, 4]); r_sm = [Reg() for _ in range(4)]
            EBc = self.sb(sta, "EBc", [128, S], BF16); r_EBc = Reg()
            EBs = self.sb(sta, "EBs", [128, 16, 512], BF16); r_EBs = Reg()
            EBw = self.sb(sta, "EBw", [128, 8, 512], BF16); r_EBw = Reg()
            ei = 0

            def gate_col(br, h):
                return br * 4 + h

            for h in range(4):
                hd = g * 4 + h
                self.toep(EBc[0:127, :], r_EBc, self.Z[hd], self.r_Z[hd], 4096 - 31, 4096 - 16, 127, S, self.r_Z2[hd])
                for Q in range(4):
                    pm, r_pm = self.next_pm()
                    P.op("pe", lambda e, pm=pm, h=h, Q=Q: e.matmul(pm[0:127, :], lhsT=kcT[:, 0:127], rhs=qT[:, h, Q * 512:(Q + 1) * 512],
                                                                  start=True, stop=True), r=[r_kcT, r_qT], w=[r_pm])
                    eb = e1[ei % 2]; r_eb = r_e1[ei % 2]; ei += 1
                    P.op("act", lambda e, pm=pm, eb=eb: e.activation(out=eb[0:127, :], in_=pm[0:127, :], func=AF.Exp), r=[r_pm], w=[r_eb])
                    P.op("dve", lambda e, eb=eb, Q=Q: e.tensor_tensor(out=e2[0:127, 0, :], in0=eb[0:127, :], in1=EBc[0:127, Q * 512:(Q + 1) * 512],
                                                                     op=ALU.mult), r=[r_eb, r_EBc], w=[r_e2[0]])
                    pms = []
                    for sq in range(4):
                        pm, r_pm = self.next_pm()
                        pms.append((pm, r_pm))
                        P.op("pe", lambda e, pm=pm, sq=sq: e.matmul(pm[:, 0:161], lhsT=e2[0:127, 0, sq * 128:(sq + 1) * 128], rhs=vce[0:127, 0:161],
                                                                   start=True, stop=True), r=[r_e2[0], r_vce], w=[r_pm])
                    for sq in range(4):
                        pm, r_pm = pms[sq]
                        P.op("dve", lambda e, pm=pm, sq=sq: e.tensor_scalar(out=sm[:, sq, 0:1], in0=pm[:, 128:129], scalar1=1e-30, scalar2=None, op0=ALU.max),
                             r=[r_pm], w=[r_sm[sq]])
                    P.op("dve", lambda e: e.reciprocal(sm[:, :, 1], sm[:, :, 0]), r=list(r_sm), w=list(r_sm))
                    P.op("dve", lambda e, h=h, Q=Q: e.tensor_tensor(out=sm[:, :, 2], in0=sm[:, :, 1], in1=gt[:, Q * 4:(Q + 1) * 4, gate_col(0, h)],
                                                                   op=ALU.mult), r=list(r_sm) + [r_gt], w=list(r_sm))
                    for sq in range(4):
                        T = Q * 4 + sq
                        pm, r_pm = pms[sq]
                        P.op("dve", lambda e, pm=pm, T=T, h=h, sq=sq: e.tensor_scalar(out=oacc[:, T, h * 128:(h + 1) * 128], in0=pm[:, 0:128],
                                                                                     scalar1=sm[:, sq, 2:3], scalar2=None, op0=ALU.mult),
                             r=[r_pm, r_sm[sq]], w=[r_oacc[T]])
                    for sq in range(4):
                        T = Q * 4 + sq
                        pm, r_pm = pms[sq]
                        if h == 0:
                            P.op("dve", lambda e, pm=pm, T=T, sq=sq: e.tensor_scalar(out=imp[:, T, :], in0=pm[:, 129:161], scalar1=sm[:, sq, 1:2], scalar2=None,
                                                                                    op0=ALU.mult), r=[r_pm, r_sm[sq]], w=[r_imp[T]])
                        else:
                            P.op("dve", lambda e, pm=pm, T=T, sq=sq: e.scalar_tensor_tensor(out=imp[:, T, :], in0=pm[:, 129:161], scalar=sm[:, sq, 1:2],
                                                                                           in1=imp[:, T, :], op0=ALU.mult, op1=ALU.add),
                                 r=[r_pm, r_sm[sq], r_imp[T]], w=[r_imp[T]])
            self.dbg_dump("imp%d" % g, imp[:], [128, 16, 32], r_imp[15])

            self.dbg_dump("kcT%d" % g, kcT[:], [128, 128], r_kcT, BF16)
            self.dbg_dump("vce%d" % g, vce[:], [128, 164], r_vce, BF16)
            self.dbg_dump("oaccc%d" % g, oacc[:, 0, :], [128, 512], r_oacc[0])
            with self.scope() as st2x:
                sc = imp
                r_sc = r_imp
                sc2 = self.sb(st2x, "sc2", [128, 16, 32]); r_sc2 = [Reg() for _ in range(16)]
                m8 = self.sb(st2x, "m8", [128, 16, 16]); r_m8 = [Reg() for _ in range(16)]
                P.dma("sp", lambda e: e.dma_start(out=sc2[:], in_=self.din["c_m1"].ap()), w=r_sc2)
                P.op("dve", lambda e: e.tensor_tensor(out=sc[:], in0=imp[:], in1=sc2[:], op=ALU.mult), r=list(r_imp) + list(r_sc2), w=list(r_sc))
                P.dma("sp", lambda e: e.dma_start(out=sc2[:], in_=self.din["c_add"].ap()), r=r_sc2, w=r_sc2)
                P.op("dve", lambda e: e.tensor_tensor(out=sc[:], in0=sc[:], in1=sc2[:], op=ALU.add), r=list(r_sc) + list(r_sc2), w=list(r_sc))
                for T in range(16):
                    P.op("dve", lambda e, T=T: e.max(out=m8[:, T, 0:8], in_=sc[:, T, :]), r=[r_sc[T]], w=[r_m8[T]])
                for T in range(16):
                    P.op("dve", lambda e, T=T: e.match_replace(out=sc2[:, T, :], in_to_replace=m8[:, T, 0:8], in_values=sc[:, T, :], imm_value=-3.0e38),
                         r=[r_sc[T], r_m8[T]], w=[r_sc2[T]])
                for T in range(16):
                    P.op("dve", lambda e, T=T: e.max(out=m8[:, T, 8:16], in_=sc2[:, T, :]), r=[r_sc2[T]], w=[r_m8[T]])
                P.op("dve", lambda e: e.tensor_tensor(out=sc2[:], in0=sc[:], in1=m8[:, :, 15:16].to_broadcast([128, 16, 32]), op=ALU.is_ge),
                     r=list(r_sc) + list(r_m8), w=list(r_sc2))
                P.op("dve", lambda e: e.tensor_scalar(out=sc2[:], in0=sc2[:], scalar1=-1.0, scalar2=30000.0, op0=ALU.add, op1=ALU.mult),
                     r=list(r_sc2), w=list(r_sc2))
                for T4 in range(4):
                    pm, r_pm = self.next_pm()
                    for j4 in range(4):
                        T = T4 * 4 + j4
                        P.op("pe", lambda e, pm=pm, T=T, j4=j4: e.transpose(pm[0:32, j4 * 128:(j4 + 1) * 128], sc2[:, T, :], self.ident[:]),
                             r=[r_sc2[T], self.r_ident], w=[r_pm], signal=(j4 == 3))
                    P.op("act", lambda e, pm=pm, T4=T4: e.activation(out=negT[0:32, T4 * 512:(T4 + 1) * 512], in_=pm[0:32, :], func=AF.Copy),
                         r=[r_pm], w=[r_negT])
            self.dbg_dump("negT%d" % g, negT[0:32, :], [32, S], r_negT, BF16)

            def load_eb(hh, which):
                hd_ = g * 4 + hh
                if which == 1:
                    src_s = bass.AP(self.Z[hd_], 4096 - 384, [[4095, 128], [128, 16], [1, 512]])
                    P.dma("pool", lambda e, src_s=src_s: e.dma_start(out=EBs[:], in_=src_s), r=[self.r_Z[hd_], self.r_Z2[hd_]], w=[r_EBs])
                else:
                    src_w = bass.AP(self.Zw[hd_], 4096 - 384, [[4095, 128], [128, 8], [1, 512]])
                    P.dma("pool", lambda e, src_w=src_w: e.dma_start(out=EBw[:], in_=src_w), r=[self.r_Zw[hd_], self.r_Zw2[hd_]], w=[r_EBw])

            load_eb(0, 1)
            load_eb(0, 2)
            for h in range(4):
                hd = g * 4 + h
                for br in (1, 2):
                    if br == 2 and h < 3:
                        load_eb(h + 1, 1)
                    for Q in range(4):
                        kt_lo = 0 if br == 1 else max(0, 4 * Q - 4)
                        kt_hi = 4 * Q + 3
                        kidx = 2 if br == 1 else 3
                        for kt in range(kt_lo, kt_hi + 1):
                            pm, r_pm = self.next_pm()
                            P.op("pe", lambda e, pm=pm, kt=kt, h=h, Q=Q, kidx=kidx, br=br: e.matmul(
                                pm[:, :], lhsT=kT[:, kidx, kt * 128:(kt + 1) * 128], rhs=qT[:, h, Q * 512:(Q + 1) * 512],
                                start=True, stop=(br == 2)), r=[r_kT[kidx], r_qT], w=[r_pm], signal=(br == 2))
                            if br == 1:
                                P.op("pe", lambda e, pm=pm, kt=kt, Q=Q: e.matmul(pm[:, :], lhsT=e2c[:, kt * 128:(kt + 1) * 128],
                                                                                rhs=negT[:, Q * 512:(Q + 1) * 512], start=False, stop=True),
                                     r=[r_e2c, r_negT], w=[r_pm])
                            eb = e1[ei % 2]; r_eb = r_e1[ei % 2]; ei += 1
                            P.op("act", lambda e, pm=pm, eb=eb: e.activation(out=eb[:], in_=pm[:, :], func=AF.Exp), r=[r_pm], w=[r_eb])
                            o = 4 * Q - kt + 3
                            EB = EBs if br == 1 else EBw
                            r_EB = r_EBs if br == 1 else r_EBw
                            P.op("dve", lambda e, eb=eb, kt=kt, o=o, EB=EB: e.tensor_tensor(out=e2[:, kt, :], in0=eb[:], in1=EB[:, o, :], op=ALU.mult),
                                 r=[r_eb, r_EB], w=[r_e2[kt]])
                        vv = vs if br == 1 else vw
                        r_vv = r_vs if br == 1 else r_vw
                        pms = []
                        for sq in range(4):
                            T = Q * 4 + sq
                            lo = 0 if br == 1 else max(0, T - 4)
                            hi = T
                            pm, r_pm = self.next_pm()
                            pms.append((pm, r_pm))
                            for kt in range(lo, hi + 1):
                                P.op("pe", lambda e, pm=pm, kt=kt, sq=sq, vv=vv, lo=lo, hi=hi: e.matmul(
                                    pm[:, 0:129], lhsT=e2[:, kt, sq * 128:(sq + 1) * 128], rhs=vv[:, kt, 0:129],
                                    start=(kt == lo), stop=(kt == hi)), r=[r_e2[kt], r_vv], w=[r_pm], signal=(kt == hi))
                        for sq in range(4):
                            pm, r_pm = pms[sq]
                            P.op("dve", lambda e, pm=pm, sq=sq: e.tensor_scalar(out=sm[:, sq, 0:1], in0=pm[:, 128:129], scalar1=1e-30, scalar2=None, op0=ALU.max),
                                 r=[r_pm], w=[r_sm[sq]])
                        P.op("dve", lambda e: e.reciprocal(sm[:, :, 1], sm[:, :, 0]), r=list(r_sm), w=list(r_sm))
                        P.op("dve", lambda e, h=h, br=br, Q=Q: e.tensor_tensor(out=sm[:, :, 2], in0=sm[:, :, 1], in1=gt[:, Q * 4:(Q + 1) * 4, gate_col(br, h)],
                                                                              op=ALU.mult), r=list(r_sm) + [r_gt], w=list(r_sm))
                        for sq in range(4):
                            T = Q * 4 + sq
                            pm, r_pm = pms[sq]
                            P.op("dve", lambda e, pm=pm, T=T, h=h, sq=sq: e.scalar_tensor_tensor(
                                out=oacc[:, T, h * 128:(h + 1) * 128], in0=pm[:, 0:128], scalar=sm[:, sq, 2:3],
                                in1=oacc[:, T, h * 128:(h + 1) * 128], op0=ALU.mult, op1=ALU.add),
                                r=[r_pm, r_sm[sq], r_oacc[T]], w=[r_oacc[T]])
                    if br == 2 and h < 3:
                        load_eb(h + 1, 2)

        with self.scope() as stf:
            wbz = self.sb(stf, "wbz", [128, NCH, 512], BF16); r_wbz = Reg()
            za = self.sb(stf, "za", [128, 16, 512], BF16); r_za = Reg()
            mixb = [self.sb(stf, "mixb%d" % i, [128, 512], BF16) for i in range(2)]
            r_mixb = [Reg(), Reg()]
            self.load_w(wbz[:], r_wbz, wsrc[:, 1292:1804])
            self.proj_tm(wbz, r_wbz, 0, 512, lambda tt: za[:, tt, :], r_za, func=AF.Silu)
            for T in range(16):
                b = T % 2
                P.op("dve", lambda e, T=T, b=b: e.tensor_tensor(out=mixb[b][:], in0=oacc[:, T, :], in1=za[:, T, :], op=ALU.mult),
                     r=[r_oacc[T], r_za], w=[r_mixb[b]])
                P.dma("sp", lambda e, T=T, b=b: e.dma_start(out=self.mixd.ap()[T * 128:(T + 1) * 128, g * 512:(g + 1) * 512], in_=mixb[b][:]),
                      r=[r_mixb[b]], w=[self.r_mixd[T]])
        if g == 1:
            self.dbg_dump("mixa", self.mixd.ap()[:, 0:1024], [S, 1024], list(self.r_mixd), BF16)

    def phase_rwkv_proj(self, st):
        nc, P = self.nc, self.P
        din = self.din
        self.rawd = nc.dram_tensor("rawd", [3, 1024, S + 1], F32)
        self.zbd = nc.dram_tensor("zbd", [S, 1024], BF16)
        self.lorad = nc.dram_tensor("lorad", [2, 64, S], F32)
        self.rw_regs = []
        wb = [self.sb(st, "wbp%d" % i, [128, NCH, 512], BF16) for i in range(2)]
        r_wb = [Reg(), Reg()]
        stg = [self.sb(st, "stg%d" % i, [128, S + 4]) for i in range(2)]
        r_stg = [Reg(), Reg()]
        for i in range(2):
            P.op("pool", lambda e, i=i: e.memset(stg[i][:, 0:1], 0.0), w=[r_stg[i]])
        k = 0
        for blk in range(6):
            b = blk % 2
            self.load_w(wb[b][:], r_wb[b], din["w_rkv"].ap()[:, blk * 512:(blk + 1) * 512])
            for j in range(4):
                ct = blk * 4 + j
                sg_, r_sg = stg[k % 2], r_stg[k % 2]
                k += 1
                self.proj_fm(wb[b], r_wb[b], j * 128, 128, lambda tb, sg_=sg_: sg_[:, 1 + tb * 512:1 + (tb + 1) * 512], r_sg)
                rr = Reg(); self.rw_regs.append(rr)
                P.dma("sp", lambda e, ct=ct, sg_=sg_: e.dma_start(out=self.rawd.ap()[ct // 8, (ct % 8) * 128:(ct % 8 + 1) * 128, 0:S + 1], in_=sg_[:, 0:S + 1]),
                      r=[r_sg], w=[rr])
        zst = [self.sb(st, "zst%d" % i, [128, 16, 512], BF16) for i in range(2)]
        r_zst = [Reg(), Reg()]
        for blk in range(2):
            self.load_w(wb[blk][:], r_wb[blk], din["w_zb"].ap()[:, blk * 512:(blk + 1) * 512])
            self.proj_tm(wb[blk], r_wb[blk], 0, 512, lambda tt, blk=blk: zst[blk][:, tt, :], r_zst[blk], func=AF.Silu)
            rr = Reg(); self.rw_regs.append(rr)
            P.dma("sp", lambda e, blk=blk: e.dma_start(out=self.zbd.ap()[:, blk * 512:(blk + 1) * 512].rearrange("(t p) c -> p t c", p=128),
                                                       in_=zst[blk][:]), r=[r_zst[blk]], w=[rr])
        mul = self.sb(st, "mul", [64, 2]); r_mul = Reg()
        P.dma("sp", lambda e: e.dma_start(out=mul[:], in_=din["mu_lora"].ap()), w=[r_mul])
        wbl = self.sb(st, "wbl", [128, NCH, 128], BF16); r_wbl = Reg()
        self.load_w(wbl[:], r_wbl, din["w_lora"].ap())
        raw = self.sb(st, "lraw", [64, S + 4]); r_raw = Reg()
        tmp = self.sb(st, "ltmp", [64, S]); r_tmp = Reg()
        lo = [self.sb(st, "lo%d" % i, [64, S]) for i in range(2)]; r_lo = [Reg(), Reg()]
        for i in range(2):
            P.op("pool", lambda e: e.memset(raw[:, 0:1], 0.0), w=[r_raw])
            self.proj_fm(wbl, r_wbl, i * 64, 64, lambda tb: raw[:, 1 + tb * 512:1 + (tb + 1) * 512], r_raw)
            P.op("dve", lambda e: e.tensor_tensor(out=tmp[:], in0=raw[:, 0:S], in1=raw[:, 1:S + 1], op=ALU.subtract), r=[r_raw], w=[r_tmp])
            P.op("dve", lambda e, i=i: e.scalar_tensor_tensor(out=lo[i][:], in0=tmp[:], scalar=mul[:, i:i + 1], in1=raw[:, 1:S + 1],
                                                             op0=ALU.mult, op1=ALU.add), r=[r_tmp, r_raw, r_mul], w=[r_lo[i]])
            if i == 0:
                P.op("act", lambda e: e.activation(out=lo[0][:], in_=lo[0][:], func=AF.Tanh), r=[r_lo[0]], w=[r_lo[0]])
            rr = Reg(); self.rw_regs.append(rr)
            P.dma("sp", lambda e, i=i: e.dma_start(out=self.lorad.ap()[i], in_=lo[i][:]), r=[r_lo[i]], w=[rr])

    def phase_rwkv(self, st):
        nc, P = self.nc, self.P
        din = self.din
        NT = 512
        F32R = mybir.dt.float32r

        def fr(ap):
            return ap.bitcast(F32R) if USE_F32R else ap

        def ld(name, shape, src, dt=F32, q="sp", r=()):
            t = self.sb(st, name, shape, dt)
            rg = Reg()
            P.dma(q, lambda e: e.dma_start(out=t[:], in_=src), r=list(r), w=[rg])
            return t, rg

        ident, r_ident = ld("identr", [128, 128], din["c_ident"].ap())
        ones = self.sb(st, "onesr", [64, 64]); r_ones = Reg()
        P.op("pool", lambda e: e.memset(ones[:], 1.0), w=[r_ones])
        msc, r_msc = ld("msc", [128, 512], din["c_mask_sc2"].ap())
        mt, r_mt = ld("mt", [128, 512], din["c_mask_t8"].ap())
        idr, r_idr = ld("idr", [128, 512], din["c_ident8"].ap())
        cmask, r_cmask = ld("cmask", [64, NT], din["c_cmask"].ap()[:, 0:NT])
        per, r_per = ld("per", [64, 16, 8], din["rw_per"].ap())
        w2, r_w2 = ld("rw2", [64, 1024], din["rw_w2"].ap())
        a2, r_a2 = ld("ra2", [64, 1024], din["rw_a2"].ap())
        twd, r_twd = ld("twd", [64, S], self.lorad.ap()[0], r=self.rw_regs)
        ads, r_ads = ld("ads", [64, S], self.lorad.ap()[1], r=self.rw_regs)
        omka = self.sb(st, "omka", [64, 16]); r_omka = Reg()
        P.op("dve", lambda e: e.tensor_scalar(out=omka[:], in0=per[:, :, 6], scalar1=-1.0, scalar2=1.0, op0=ALU.mult, op1=ALU.add),
             r=[r_per], w=[r_omka])

        def v3(ap):
            return ap.rearrange("p (c t) -> p c t", t=128)

        def make_set(si, pY, r_pY):
            sfx = "_%d" % si
            my_pm = self.pm[si * 3:(si + 1) * 3]
            my_rpm = self.r_pm[si * 3:(si + 1) * 3]
            rot = {"i": 0}

            def next_pm():
                i = rot["i"]
                rot["i"] = (i + 1) % 3
                return my_pm[i], my_rpm[i]

            Zst = self.sb(st, "Zst" + sfx, [128, 64]); r_Z = Reg()
            lnw = self.sb(st, "lnw" + sfx, [128, 64]); r_lnw = Reg()
            lnb = self.sb(st, "lnb" + sfx, [128, 64]); r_lnb = Reg()
            raws = self.sb(st, "raws" + sfx, [64, 3, NT + 4]); r_raws = Reg()
            names = ["r_s", "k_s", "v_s", "sg", "al", "kk", "k2", "Lp", "t0", "t1", "t2", "rinv", "bon"]
            Tt = {n: self.sb(st, "T_" + n + sfx, [64, NT]) for n in names}
            R = {n: Reg() for n in names}
            AR = self.sb(st, "AR" + sfx, [128, 4, 2, 128]); r_AR = Reg()
            BK = self.sb(st, "BK" + sfx, [64, 4, 2, 128]); r_BK = Reg()
            BKh = self.sb(st, "BKh" + sfx, [64, 4, 2, 128]); r_BKh = Reg()
            WC = self.sb(st, "WC" + sfx, [64, 4]); r_WC = Reg()
            TM = self.sb(st, "TM" + sfx, [128, 4, 3, 64]); r_TM = Reg()
            SC = self.sb(st, "SC" + sfx, [128, 4, 512]); r_SC = Reg()
            PP = [self.sb(st, "PP%d" % i + sfx, [128, 512]) for i in range(2)]; r_PP = [Reg(), Reg()]
            PT = [self.sb(st, "PTt%d" % i + sfx, [128, 512]) for i in range(2)]; r_PT = [Reg(), Reg()]
            Tm = self.sb(st, "Tm" + sfx, [128, 512]); r_Tm = Reg()
            XTs = self.sb(st, "XTs" + sfx, [128, 64]); r_XTs = Reg()
            UTs = self.sb(st, "UTs" + sfx, [128, 64]); r_UTs = Reg()
            Yt = self.sb(st, "Yt" + sfx, [128, 4, 64]); r_Yt = Reg()
            yc = self.sb(st, "yc" + sfx, [128, 4, 64]); r_yc = Reg()
            ysq = self.sb(st, "ysq" + sfx, [128, 4, 64]); r_ysq = Reg()
            st4 = self.sb(st, "st4" + sfx, [128, 16]); r_st4 = Reg()
            zb = self.sb(st, "zb" + sfx, [128, 4, 64], BF16); r_zb = Reg()
            mixo = [self.sb(st, "mixo%d" % i + sfx, [128, 4, 64], BF16) for i in range(2)]; r_mixo = [Reg(), Reg()]
            state = {"mi": 0}

            def run(h):
                P.dma("sp", lambda e: e.dma_start(out=lnw[:], in_=din["ln_w"].ap()[:, h * 64:(h + 1) * 64]), w=[r_lnw])
                P.dma("sp", lambda e: e.dma_start(out=lnb[:], in_=din["ln_b"].ap()[:, h * 64:(h + 1) * 64]), w=[r_lnb])
                P.op("dve", lambda e: e.tensor_scalar(out=fr(Zst[:]), in0=mt[:, 0:64], scalar1=0.0, scalar2=None, op0=ALU.mult), r=[r_mt], w=[r_Z])
                if not state.get("ar_init"):
                    state["ar_init"] = True
                    for a_ in range(2):
                        P.op("dve", lambda e, a_=a_: e.tensor_scalar(out=fr(AR[:, a_ * 2:a_ * 2 + 2, :, :].rearrange("p a b c -> p (a b c)")), in0=mt[:],
                                                                    scalar1=0.0, scalar2=None, op0=ALU.mult), r=[r_mt], w=[r_AR])
                for G in range(4):
                    t0g = G * NT
                    P.dma("sp", lambda e, G=G: e.dma_start(out=raws[:, :, 0:NT + 1],
                                                           in_=self.rawd.ap()[:, h * 64:(h + 1) * 64, G * NT:G * NT + NT + 1].rearrange("i c t -> c i t")),
                          r=self.rw_regs, w=[r_raws])
                    P.dma("sp", lambda e, G=G: e.dma_start(out=zb[:], in_=self.zbd.ap()[G * NT:(G + 1) * NT, h * 64:(h + 1) * 64].rearrange("(t p) c -> p t c", p=128)),
                          r=self.rw_regs, w=[r_zb])
                    for i in range(3):
                        nm = ("r_s", "k_s", "v_s")[i]
                        P.op("dve", lambda e, i=i: e.tensor_tensor(out=Tt["t0"][:], in0=raws[:, i, 0:NT], in1=raws[:, i, 1:NT + 1], op=ALU.subtract),
                             r=[r_raws], w=[R["t0"]])
                        P.op("dve", lambda e, i=i, nm=nm: e.scalar_tensor_tensor(out=Tt[nm][:], in0=Tt["t0"][:], scalar=per[:, h, i:i + 1],
                                                                                in1=raws[:, i, 1:NT + 1], op0=ALU.mult, op1=ALU.add),
                             r=[R["t0"], r_raws, r_per], w=[R[nm]])
                        yield
                    for (wmat, src, r_src, dst, bcol) in ((w2, twd, r_twd, "sg", 3), (a2, ads, r_ads, "al", 4)):
                        pm, r_pm = next_pm()
                        P.op("pe", lambda e, pm=pm, wmat=wmat, src=src, h=h, t0g=t0g: e.matmul(
                            pm[0:64, :], lhsT=wmat[:, h * 64:(h + 1) * 64], rhs=src[:, t0g:t0g + NT], start=True, stop=True),
                            r=[r_w2, r_a2, r_src], w=[r_pm])
                        P.op("act", lambda e, pm=pm, dst=dst, bcol=bcol, h=h: e.activation(out=Tt[dst][:], in_=pm[0:64, :], func=AF.Sigmoid,
                                                                                           bias=per[:, h, bcol:bcol + 1]),
                             r=[r_pm, r_per], w=[R[dst]])
                    P.op("act", lambda e, h=h: e.activation(out=Tt["t1"][:], in_=Tt["k_s"][:], func=AF.Square, scale=per[:, h, 5:6]),
                         r=[R["k_s"], r_per], w=[R["t1"]])
                    pm, r_pm = next_pm()
                    P.op("pe", lambda e, pm=pm: e.matmul(pm[0:64, :], lhsT=ones[:], rhs=Tt["t1"][:], start=True, stop=True),
                         r=[r_ones, R["t1"]], w=[r_pm])
                    P.op("act", lambda e, pm=pm: e.activation(out=Tt["rinv"][:], in_=pm[0:64, :], func=AF.Sqrt), r=[r_pm], w=[R["rinv"]])
                    yield
                    P.op("dve", lambda e: e.tensor_scalar(out=Tt["rinv"][:], in0=Tt["rinv"][:], scalar1=1e-12, scalar2=None, op0=ALU.max),
                         r=[R["rinv"]], w=[R["rinv"]])
                    P.op("dve", lambda e: e.reciprocal(Tt["rinv"][:], Tt["rinv"][:]), r=[R["rinv"]], w=[R["rinv"]])
                    P.op("dve", lambda e, h=h: e.scalar_tensor_tensor(out=Tt["kk"][:], in0=Tt["k_s"][:], scalar=per[:, h, 5:6], in1=Tt["rinv"][:],
                                                                     op0=ALU.mult, op1=ALU.mult), r=[R["k_s"], R["rinv"], r_per], w=[R["kk"]])
                    yield
                    P.op("dve", lambda e, h=h: e.tensor_scalar(out=Tt["t1"][:], in0=Tt["al"][:], scalar1=per[:, h, 6:7], scalar2=omka[:, h:h + 1],
                                                              op0=ALU.mult, op1=ALU.add), r=[R["al"], r_per, r_omka], w=[R["t1"]])
                    P.op("dve", lambda e: e.tensor_tensor(out=Tt["k2"][:], in0=Tt["k_s"][:], in1=Tt["t1"][:], op=ALU.mult),
                         r=[R["k_s"], R["t1"]], w=[R["k2"]])
                    P.op("dve", lambda e, h=h: e.scalar_tensor_tensor(out=Tt["t1"][:], in0=Tt["r_s"][:], scalar=per[:, h, 7:8], in1=Tt["k2"][:],
                                                                     op0=ALU.mult, op1=ALU.mult), r=[R["r_s"], R["k2"], r_per], w=[R["t1"]])
                    yield
                    pm, r_pm = next_pm()
                    P.op("pe", lambda e, pm=pm: e.matmul(pm[0:64, :], lhsT=ones[:], rhs=Tt["t1"][:], start=True, stop=True),
                         r=[r_ones, R["t1"]], w=[r_pm])
                    P.op("dve", lambda e, pm=pm: e.tensor_tensor(out=Tt["bon"][:], in0=pm[0:64, :], in1=Tt["v_s"][:], op=ALU.mult),
                         r=[r_pm, R["v_s"]], w=[R["bon"]])
                    P.op("dve", lambda e: e.tensor_tensor_scan(out=Tt["Lp"][:], data0=cmask[:], data1=Tt["sg"][:], initial=0.0,
                                                              op0=ALU.mult, op1=ALU.add), r=[r_cmask, R["sg"]], w=[R["Lp"]])
                    P.op("dve", lambda e: e.tensor_tensor(out=Tt["t1"][:], in0=Tt["Lp"][:], in1=Tt["sg"][:], op=ALU.subtract),
                         r=[R["Lp"], R["sg"]], w=[R["t1"]])
                    P.op("act", lambda e: e.activation(out=Tt["t1"][:], in_=Tt["t1"][:], func=AF.Exp, scale=-C0), r=[R["t1"]], w=[R["t1"]])
                    yield
                    P.op("dve", lambda e: e.scalar_tensor_tensor(out=fr(AR[0:64, :, 0, :]), in0=v3(Tt["kk"][:]), scalar=-1.0, in1=v3(Tt["t1"][:]),
                                                                op0=ALU.mult, op1=ALU.mult), r=[R["kk"], R["t1"]], w=[r_AR])
                    P.op("act", lambda e: e.activation(out=Tt["rinv"][:], in_=Tt["Lp"][:], func=AF.Exp, scale=-C0), r=[R["Lp"]], w=[R["rinv"]])
                    P.op("dve", lambda e: e.tensor_tensor(out=fr(AR[0:64, :, 1, :]), in0=v3(Tt["r_s"][:]), in1=v3(Tt["rinv"][:]), op=ALU.mult),
                         r=[R["r_s"], R["rinv"]], w=[r_AR])
                    yield
                    P.op("act", lambda e: e.activation(out=Tt["t1"][:], in_=Tt["Lp"][:], func=AF.Exp, scale=C0), r=[R["Lp"]], w=[R["t1"]])
                    P.op("dve", lambda e: e.tensor_tensor(out=Tt["t2"][:], in0=Tt["kk"][:], in1=Tt["al"][:], op=ALU.mult),
                         r=[R["kk"], R["al"]], w=[R["t2"]])
                    yield
                    P.op("dve", lambda e: e.tensor_tensor(out=fr(BK[:, :, 0, :]), in0=v3(Tt["t2"][:]), in1=v3(Tt["t1"][:]), op=ALU.mult),
                         r=[R["t2"], R["t1"]], w=[r_BK])
                    P.op("dve", lambda e: e.tensor_tensor(out=fr(BK[:, :, 1, :]), in0=v3(Tt["k2"][:]), in1=v3(Tt["t1"][:]), op=ALU.mult),
                         r=[R["k2"], R["t1"]], w=[r_BK])
                    yield
                    P.op("dve", lambda e: e.tensor_tensor(out=BKh[:].rearrange("p a b c -> p a (b c)"), in0=BK[:].rearrange("p a b c -> p a (b c)"),
                                                          in1=Tt["rinv"][:, 127:NT:128].unsqueeze(2).to_broadcast([64, 4, 256]), op=ALU.mult), r=[r_BK, R["rinv"]], w=[r_BKh])
                    yield
                    for c2 in range(2):
                        pm, r_pm = next_pm()
                        for cc in range(2):
                            c = c2 * 2 + cc
                            srcs = (Tt["v_s"][:, c * 128:(c + 1) * 128], BKh[:, c, 0, :], BKh[:, c, 1, :])
                            for k3 in range(3):
                                P.op("pe", lambda e, pm=pm, cc=cc, k3=k3, src=srcs[k3]: e.transpose(
                                    pm[:, (cc * 3 + k3) * 64:(cc * 3 + k3 + 1) * 64], src, ident[0:64, 0:64]),
                                    r=[R["v_s"], r_BKh, r_ident], w=[r_pm], signal=(cc == 1 and k3 == 2))
                        P.op("act", lambda e, pm=pm, c2=c2: e.activation(out=fr(TM[:, c2 * 2:c2 * 2 + 2, :, :]),
                                                                         in_=pm[:, 0:384].rearrange("p (a b c) -> p a b c", a=2, b=3), func=AF.Copy),
                             r=[r_pm], w=[r_TM])
                        yield
                    for c in range(4):
                        pm, r_pm = next_pm()
                        for k3 in range(2):
                            P.op("pe", lambda e, pm=pm, c=c, k3=k3: e.matmul(pm[:, k3 * 256:(k3 + 1) * 256], lhsT=fr(BK[:, c, k3, :]),
                                                                            rhs=fr(AR[0:64, c, :, :]), start=True, stop=True),
                                 r=[r_BK, r_AR], w=[r_pm], signal=(k3 == 1))
                        P.op("dve", lambda e, pm=pm, c=c: e.tensor_tensor(out=fr(SC[:, c, :]), in0=pm[:, :], in1=msc[:], op=ALU.mult),
                             r=[r_pm, r_msc], w=[r_SC])
                        yield
                    pm, r_pm = next_pm()
                    for c in range(4):
                        P.op("pe", lambda e, pm=pm, c=c: e.matmul(pm[:, c * 128:(c + 1) * 128], lhsT=fr(AR[0:64, c, 0, :]), rhs=fr(BK[:, c, 0, :]),
                                                                 start=True, stop=True), r=[r_AR, r_BK], w=[r_pm], signal=(c == 3))
                    P.op("dve", lambda e, pm=pm: e.tensor_tensor(out=fr(PT[0][:]), in0=pm[:, :], in1=mt[:], op=ALU.mult),
                         r=[r_pm, r_mt], w=[r_PT[0]])
                    yield
                    P.op("dve", lambda e: e.tensor_tensor(out=fr(v3(Tm[:])), in0=SC[:, :, 0:128], in1=v3(idr[:]), op=ALU.add), r=[r_SC, r_idr], w=[r_Tm])
                    yield
                    cur = 0
                    for it in range(6):
                        nxt = 1 - cur
                        if it < 5:
                            pm, r_pm = next_pm()
                            for c in range(4):
                                Pv = SC[:, c, 0:128] if it == 0 else PP[cur][:, c * 128:(c + 1) * 128]
                                P.op("pe", lambda e, pm=pm, c=c, cur=cur, Pv=Pv: e.matmul(pm[:, c * 128:(c + 1) * 128], lhsT=fr(PT[cur][:, c * 128:(c + 1) * 128]),
                                                                                         rhs=fr(Pv), start=True, stop=True),
                                     r=[r_PT[cur], r_SC if it == 0 else r_PP[cur]], w=[r_pm], signal=(c == 3))
                            P.op("act", lambda e, pm=pm, nxt=nxt: e.activation(out=fr(PP[nxt][:]), in_=pm[:, :], func=AF.Copy),
                                 r=[r_pm], w=[r_PP[nxt]])
                            yield
                        pm, r_pm = next_pm()
                        for c in range(4):
                            Pv = SC[:, c, 0:128] if it == 0 else PP[cur][:, c * 128:(c + 1) * 128]
                            P.op("pe", lambda e, pm=pm, c=c, cur=cur, Pv=Pv: e.matmul(pm[:, c * 128:(c + 1) * 128], lhsT=fr(Pv),
                                                                                     rhs=fr(PT[cur][:, c * 128:(c + 1) * 128]), start=True, stop=True),
                                 r=[r_PT[cur], r_SC if it == 0 else r_PP[cur]], w=[r_pm], signal=(c == 3))
                        P.op("dve", lambda e, pm=pm, nxt=nxt: e.tensor_copy(fr(PT[nxt][:]), pm[:, :]), r=[r_pm], w=[r_PT[nxt]])
                        yield
                        pm, r_pm = next_pm()
                        for c in range(4):
                            P.op("pe", lambda e, pm=pm, c=c, nxt=nxt: e.matmul(pm[:, c * 128:(c + 1) * 128], lhsT=fr(PT[nxt][:, c * 128:(c + 1) * 128]),
                                                                              rhs=fr(Tm[:, c * 128:(c + 1) * 128]), start=True, stop=True),
                                 r=[r_PT[nxt], r_Tm], w=[r_pm], signal=(c == 3))
                        P.op("dve", lambda e, pm=pm: e.tensor_tensor(out=fr(Tm[:]), in0=pm[:, :], in1=Tm[:], op=ALU.add), r=[r_pm, r_Tm], w=[r_Tm])
                        yield
                        cur = nxt
                    for c in range(4):
                        pm, r_pm = next_pm()
                        P.op("pe", lambda e, pm=pm, c=c: e.matmul(pm[:, 0:64], lhsT=fr(AR[:, c, 0, :]), rhs=fr(Zst[:]), start=True, stop=False),
                             r=[r_AR, r_Z], w=[r_pm], signal=False)
                        P.op("pe", lambda e, pm=pm, c=c: e.matmul(pm[:, 0:64], lhsT=fr(SC[:, c, 256:384]), rhs=fr(TM[:, c, 0, :]), start=False, stop=True),
                             r=[r_SC, r_TM], w=[r_pm])
                        P.op("act", lambda e, pm=pm: e.activation(out=fr(XTs[:]), in_=pm[:, 0:64], func=AF.Copy), r=[r_pm], w=[r_XTs])
                        yield
                        pm, r_pm = next_pm()
                        P.op("pe", lambda e, pm=pm, c=c: e.matmul(pm[:, 0:64], lhsT=fr(Tm[:, c * 128:(c + 1) * 128]), rhs=fr(XTs[:]), start=True, stop=True),
                             r=[r_Tm, r_XTs], w=[r_pm])
                        P.op("act", lambda e, pm=pm: e.activation(out=fr(UTs[:]), in_=pm[:, 0:64], func=AF.Copy), r=[r_pm], w=[r_UTs])
                        yield
                        yo = pY[:, c * 64:(c + 1) * 64]
                        P.op("pe", lambda e, yo=yo, c=c: e.matmul(yo, lhsT=fr(AR[:, c, 1, :]), rhs=fr(Zst[:]), start=True, stop=False),
                             r=[r_AR, r_Z], w=[r_pY], signal=False)
                        P.op("pe", lambda e, yo=yo, c=c: e.matmul(yo, lhsT=fr(SC[:, c, 128:256]), rhs=fr(UTs[:]), start=False, stop=False),
                             r=[r_SC, r_UTs], w=[r_pY], signal=False)
                        P.op("pe", lambda e, yo=yo, c=c: e.matmul(yo, lhsT=fr(SC[:, c, 384:512]), rhs=fr(TM[:, c, 0, :]), start=False, stop=True),
                             r=[r_SC, r_TM], w=[r_pY])
                        yield
                        pm, r_pm = next_pm()
                        P.op("pe", lambda e, pm=pm, c=c: e.matmul(pm[0:64, 0:64], lhsT=fr(TM[:, c, 1, :]), rhs=fr(UTs[:]), start=True, stop=False),
                             r=[r_TM, r_UTs], w=[r_pm], signal=False)
                        P.op("pe", lambda e, pm=pm, c=c: e.matmul(pm[0:64, 0:64], lhsT=fr(TM[:, c, 2, :]), rhs=fr(TM[:, c, 0, :]), start=False, stop=True),
                             r=[r_TM], w=[r_pm])
                        P.op("dve", lambda e, pm=pm, c=c: e.scalar_tensor_tensor(out=fr(Zst[0:64, :]), in0=Zst[0:64, :], scalar=Tt["rinv"][:, c * 128 + 127:c * 128 + 128], in1=pm[0:64, 0:64],
                                                                                op0=ALU.mult, op1=ALU.add), r=[r_Z, R["rinv"], r_pm], w=[r_Z])
                        yield
                    P.op("act", lambda e: e.activation(out=Yt[:], in_=pY[:, 0:256].rearrange("p (a b) -> p a b", b=64), func=AF.Copy), r=[r_pY], w=[r_Yt])
                    if h == 0 and G == 0:
                        self.dbg_dump("yraw", Yt[:], [128, 4, 64], r_Yt)
                    P.op("dve", lambda e: e.reduce_sum(out=st4[:, 0:4], in_=Yt[:], axis=AX.X), r=[r_Yt], w=[r_st4])
                    yield
                    P.op("dve", lambda e: e.scalar_tensor_tensor(out=yc[:], in0=st4[:, 0:4].unsqueeze(2).to_broadcast([128, 4, 64]), scalar=-1.0 / 64,
                                                                in1=Yt[:], op0=ALU.mult, op1=ALU.add), r=[r_Yt, r_st4], w=[r_yc])
                    P.op("dve", lambda e: e.tensor_tensor(out=ysq[:], in0=yc[:], in1=yc[:], op=ALU.mult), r=[r_yc], w=[r_ysq])
                    P.op("dve", lambda e: e.reduce_sum(out=st4[:, 4:8], in_=ysq[:], axis=AX.X), r=[r_ysq], w=[r_st4])
                    yield
                    P.op("dve", lambda e: e.tensor_scalar(out=st4[:, 4:8], in0=st4[:, 4:8], scalar1=1.0 / 64, scalar2=64e-5, op0=ALU.mult, op1=ALU.add),
                         r=[r_st4], w=[r_st4])
                    P.op("act", lambda e: e.activation(out=st4[:, 4:8], in_=st4[:, 4:8], func=AF.Sqrt), r=[r_st4], w=[r_st4])
                    P.op("dve", lambda e: e.reciprocal(st4[:, 8:12], st4[:, 4:8]), r=[r_st4], w=[r_st4])
                    yield
                    pm, r_pm = next_pm()
                    for tq in range(4):
                        P.op("pe", lambda e, pm=pm, tq=tq: e.transpose(pm[:, tq * 64:(tq + 1) * 64], Tt["bon"][:, tq * 128:(tq + 1) * 128],
                                                                     ident[0:64, 0:64]), r=[R["bon"], r_ident], w=[r_pm], signal=(tq == 3))
                    P.op("dve", lambda e: e.tensor_tensor(out=yc[:], in0=yc[:], in1=st4[:, 8:12].unsqueeze(2).to_broadcast([128, 4, 64]), op=ALU.mult),
                         r=[r_yc, r_st4], w=[r_yc])
                    P.op("dve", lambda e: e.tensor_tensor(out=yc[:], in0=yc[:], in1=lnw[:, :].unsqueeze(1).to_broadcast([128, 4, 64]), op=ALU.mult),
                         r=[r_yc, r_lnw], w=[r_yc])
                    yield
                    P.op("dve", lambda e: e.tensor_tensor(out=yc[:], in0=yc[:], in1=lnb[:, :].unsqueeze(1).to_broadcast([128, 4, 64]), op=ALU.add),
                         r=[r_yc, r_lnb], w=[r_yc])
                    P.op("dve", lambda e, pm=pm: e.tensor_tensor(out=yc[:], in0=yc[:], in1=pm[:, 0:256].rearrange("p (a b) -> p a b", b=64), op=ALU.add),
                         r=[r_yc, r_pm], w=[r_yc])
                    yield
                    if h == 0 and G == 0:
                        self.dbg_dump("ob00", yc[:], [128, 4, 64], r_yc)

                    mo = mixo[state["mi"] % 2]; r_mo = r_mixo[state["mi"] % 2]; state["mi"] += 1
                    P.op("dve", lambda e, mo=mo: e.tensor_tensor(out=mo[:], in0=yc[:], in1=zb[:], op=ALU.mult), r=[r_yc, r_zb], w=[r_mo])
                    dst = self.mixd.ap()[G * NT:(G + 1) * NT, 1024 + h * 64:1024 + (h + 1) * 64].rearrange("(t p) c -> p t c", p=128)
                    rr = Reg()
                    self.rw_store_regs.append(rr)
                    P.dma("pool", lambda e, mo=mo, dst=dst: e.dma_start(out=dst, in_=mo[:]), r=[r_mo], w=[rr])
                    yield
            return run

        runs = [make_set(0, self.px, self.r_px), make_set(1, self.py2, self.r_py2)]
        for hp in range(8):
            gens = [runs[0](2 * hp), runs[1](2 * hp + 1)]
            for _ in range(RW_STAGGER):
                try:
                    next(gens[0])
                except StopIteration:
                    break
            while gens:
                for gq in list(gens):
                    try:
                        next(gq)
                    except StopIteration:
                        gens.remove(gq)
        self.dbg_dump("mixall", self.mixd.ap(), [S, 2048], list(self.r_mixd) + self.rw_store_regs, BF16)

    def phase_out(self, st):
        nc, P = self.nc, self.P
        din = self.din
        wo = self.sb(st, "wo", [128, NCH, D - 512], BF16); r_wo = [self.r_wo0] + [Reg() for _ in range(3)]
        for nb in range(1, 4):
            P.dma("pool", lambda e, nb=nb: e.dma_start(out=wo[:, :, (nb - 1) * 512:nb * 512],
                                                       in_=din["w_out"].ap()[:, nb * 512:(nb + 1) * 512].rearrange("(c p) n -> p c n", p=128)),
                  w=[r_wo[nb]])

        def wo_blk(c, nb):
            return self.wo0[:, c, :] if nb == 0 else wo[:, c, (nb - 1) * 512:nb * 512]
        gpo = self.sb(st, "gpo", [128, D]); r_gpo = Reg()
        P.dma("sp", lambda e: e.dma_start(out=gpo[:], in_=din["g_post"].ap()), w=[r_gpo])
        idb = self.sb(st, "idb2", [128, 128], BF16); r_idb = Reg()
        idf = self.sb(st, "idf2", [128, 128]); r_idf = Reg()
        P.dma("sp", lambda e: e.dma_start(out=idf[:], in_=din["c_ident"].ap()), w=[r_idf])
        P.op("dve", lambda e: e.tensor_copy(idb[:], idf[:]), r=[r_idf], w=[r_idb])
        mx = [self.sb(st, "mx%d" % i, [128, D], BF16) for i in range(2)]; r_mx = [Reg(), Reg()]
        mT = [self.sb(st, "mT%d" % i, [128, NCH, 128], BF16) for i in range(2)]; r_mT = [Reg(), Reg()]
        xr = [self.sb(st, "xr%d" % i, [128, D]) for i in range(2)]; r_xr = [Reg(), Reg()]
        ysbs = [self.sb(st, "ysb%d" % i, [128, D]) for i in range(2)]; r_ysbs = [Reg(), Reg()]
        jks = [self.sb(st, "jk%d" % i, [128, D]) for i in range(2)]; r_jks = [Reg(), Reg()]
        s1s = [self.sb(st, "s1_%d" % i, [128, 4]) for i in range(2)]; r_s1s = [Reg(), Reg()]
        ob = [self.sb(st, "ob%d" % i, [128, D]) for i in range(2)]; r_ob = [Reg(), Reg()]
        mix_regs = list(self.r_mixd) + self.rw_store_regs

        def stage_a(T):
            b = T % 2
            P.dma("sp", lambda e: e.dma_start(out=mx[b][:], in_=self.mixd.ap()[T * 128:(T + 1) * 128, :]), r=mix_regs, w=[r_mx[b]])
            P.dma("sp", lambda e: e.dma_start(out=xr[b][:], in_=din["x"].ap()[T * 128:(T + 1) * 128, :]), w=[r_xr[b]])
            for gq in range(4):
                for jq in range(4):
                    c = gq * 4 + jq
                    P.op("pe", lambda e, c=c, jq=jq: e.transpose(self.ptr[:, jq, :], mx[b][:, c * 128:(c + 1) * 128], idb[:]),
                         r=[r_mx[b], r_idb], w=[self.r_ptr], signal=(jq == 3))
                P.op("dve", lambda e, gq=gq: e.tensor_copy(mT[b][:, gq * 4:(gq + 1) * 4, :], self.ptr[:]),
                     r=[self.r_ptr], w=[r_mT[b]])

        def stage_b(T):
            b = T % 2
            ysb, r_ysb = ysbs[b], r_ysbs[b]
            for nb in range(4):
                pm, r_pm = self.next_pm()
                for c in range(NCH):
                    P.op("pe", lambda e, pm=pm, c=c, nb=nb: e.matmul(pm[:, :], lhsT=mT[b][:, c, :], rhs=wo_blk(c, nb),
                                                                   start=(c == 0), stop=(c == NCH - 1)),
                         r=[r_mT[b], r_wo[nb]], w=[r_pm], signal=(c == NCH - 1))
                P.op("act", lambda e, pm=pm, nb=nb: e.activation(out=ysb[:, nb * 512:(nb + 1) * 512], in_=pm[:, :], func=AF.Copy),
                     r=[r_pm], w=[r_ysb])

        def stage_c(T):
            b = T % 2
            ysb, r_ysb, jk, r_jk, s1, r_s1 = ysbs[b], r_ysbs[b], jks[b], r_jks[b], s1s[b], r_s1s[b]
            P.op("act", lambda e: e.activation(out=jk[:], in_=ysb[:], func=AF.Square), r=[r_ysb], w=[r_jk])
            P.op("dve", lambda e: e.reduce_sum(out=s1[:, 0:1], in_=jk[:], axis=AX.X), r=[r_jk], w=[r_s1])
            P.op("dve", lambda e: e.tensor_scalar(out=s1[:, 0:1], in0=s1[:, 0:1], scalar1=1.0 / D, scalar2=1e-6, op0=ALU.mult, op1=ALU.add),
                 r=[r_s1], w=[r_s1])
            P.op("act", lambda e: e.activation(out=s1[:, 0:1], in_=s1[:, 0:1], func=AF.Sqrt), r=[r_s1], w=[r_s1])
            P.op("dve", lambda e: e.reciprocal(s1[:, 1:2], s1[:, 0:1]), r=[r_s1], w=[r_s1])
            P.op("dve", lambda e: e.tensor_tensor(out=jk[:], in0=ysb[:], in1=gpo[:], op=ALU.mult), r=[r_ysb, r_gpo], w=[r_jk])
            P.op("dve", lambda e: e.scalar_tensor_tensor(out=ob[b][:], in0=jk[:], scalar=s1[:, 1:2], in1=xr[b][:], op0=ALU.mult, op1=ALU.add),
                 r=[r_jk, r_s1, r_xr[b]], w=[r_ob[b]])
            ro = Reg()
            self.out_regs.append(ro)
            P.dma("pool", lambda e: e.dma_start(out=self.out.ap()[T * 128:(T + 1) * 128, :], in_=ob[b][:]), r=[r_ob[b]], w=[ro])

        stage_a(0)
        for T in range(16):
            stage_b(T)
            if T + 1 < 16:
                stage_a(T + 1)
            stage_c(T)


def _build(in_shapes, dbg=()):
    b = B(in_shapes, dbg)
    nc = b.build()
    return nc, b


def kernel(**inputs):
    inputs = {k: np.asarray(v) for k, v in inputs.items()}
    consts = _consts()
    L = _layout_inputs(inputs)
    shared = dict(consts)
    shared.update(L)
    in_shapes = {"x": (S, D)}
    for k, v in shared.items():
        in_shapes[k] = v.shape
    nc, b = _build(in_shapes)
    active = [0, 1, 4, 5]
    zero_x = np.zeros((S, D), np.float32)
    in_maps = []
    for c in range(8):
        if c in active:
            m = {"x": np.ascontiguousarray(inputs["x"][active.index(c)])}
        else:
            m = {"x": zero_x}
        m.update(shared)
        in_maps.append(m)
    res = run_bass_kernel_spmd(nc, in_maps, core_ids=list(range(8)))
    out = np.stack([res.results[c]["out"] for c in active], 0)
    return out.astype(np.float32)
```

```python
import math
from contextlib import ExitStack, contextmanager
import numpy as np
import concourse.bass as bass
import concourse.mybir as mybir
from concourse.bass_utils import run_bass_kernel_spmd

F32 = mybir.dt.float32
BF16 = mybir.dt.bfloat16
ALU = mybir.AluOpType
AF = mybir.ActivationFunctionType
AX = mybir.AxisListType

ENGS = ("pe", "act", "dve", "pool", "sp")
N_DMA_SEMS = 12
S = 2048
D = 2048
NCH = 16
BIG = 1.0e30
C0 = math.exp(-0.5)
USE_F32R = True
RW_STAGGER = 0


class Reg:
    __slots__ = ("lw", "rd")

    def __init__(self):
        self.lw = None
        self.rd = {}


class Prog:
    def __init__(self, nc):
        self.nc = nc
        self.q = {e: [] for e in ENGS}
        self.cnt = {e: 0 for e in ENGS}
        self.pending = {e: False for e in ENGS}
        self.waited = {e: {} for e in ENGS}
        self.dma_k = {e: 0 for e in ENGS}

    def op(self, eng, fn, r=(), w=(), signal=True):
        deps = {}

        def need(key, val, kind):
            if key == eng:
                if eng == "pe":
                    return
            if deps.get(key, 0) < val:
                deps[key] = val

        for reg in r:
            if reg.lw is not None:
                need(reg.lw[0], reg.lw[1], "raw")
        for reg in w:
            if reg.lw is not None:
                need(reg.lw[0], reg.lw[1], "waw")
            for k, v in reg.rd.items():
                need(k, v, "war")
        waits = []
        wd = self.waited[eng]
        for k, v in deps.items():
            if wd.get(k, 0) < v:
                wd[k] = v
                waits.append((k, v))
        n = self.cnt[eng] + 1
        if signal:
            self.cnt[eng] = n
            self.pending[eng] = False
        else:
            self.pending[eng] = True
        self.q[eng].append((waits, fn, (eng, 1) if signal else None))
        for reg in r:
            reg.rd[eng] = n
        for reg in w:
            reg.lw = (eng, n)
            reg.rd = {}

    def dma(self, qe, fn, r=(), w=()):
        deps = {}
        for reg in r:
            if reg.lw is not None:
                k, v = reg.lw
                deps[k] = max(deps.get(k, 0), v)
        for reg in w:
            if reg.lw is not None:
                k, v = reg.lw
                deps[k] = max(deps.get(k, 0), v)
            for k, v in reg.rd.items():
                deps[k] = max(deps.get(k, 0), v)
        kk = self.dma_k[qe]
        self.dma_k[qe] = kk + 1
        slot = kk % N_DMA_SEMS
        key = "dma_%s_%d" % (qe, slot)
        prev = 16 * (kk // N_DMA_SEMS)
        if prev > 0:
            deps[key] = max(deps.get(key, 0), prev)
        waits = []
        wd = self.waited[qe]
        for k, v in deps.items():
            if wd.get(k, 0) < v:
                wd[k] = v
                waits.append((k, v))
        tgt = prev + 16
        self.q[qe].append((waits, fn, (key, 16)))
        for reg in r:
            reg.rd[key] = max(reg.rd.get(key, 0), tgt)
        for reg in w:
            reg.lw = (key, tgt)
            reg.rd = {}

    def barrier(self):
        deps = {e: self.cnt[e] for e in ENGS if self.cnt[e] > 0}
        for qe in ENGS:
            kk = self.dma_k[qe]
            for slot in range(min(N_DMA_SEMS, kk)):
                uses = (kk - slot + N_DMA_SEMS - 1) // N_DMA_SEMS
                deps["dma_%s_%d" % (qe, slot)] = 16 * uses
        for e in ENGS:
            assert not self.pending[e]
            waits = []
            for k, v in deps.items():
                if k != e and self.waited[e].get(k, 0) < v:
                    self.waited[e][k] = v
                    waits.append((k, v))
            if waits:
                self.q[e].append((waits, None, None))

    def final_wait(self, eng, regs):
        deps = {}
        for reg in regs:
            if reg.lw is not None:
                k, v = reg.lw
                deps[k] = max(deps.get(k, 0), v)
        self.q[eng].append((list(deps.items()), None, None))

    def emit(self):
        nc = self.nc
        with ExitStack() as st:
            sems = {}
            for e in ENGS:
                sems[e] = st.enter_context(nc.semaphore("s_" + e))
            for qe in ENGS:
                for i in range(min(N_DMA_SEMS, self.dma_k[qe])):
                    key = "dma_%s_%d" % (qe, i)
                    sems[key] = st.enter_context(nc.semaphore(key))
            block = st.enter_context(nc.Block())
            for e in ENGS:
                assert not self.pending[e], e

            def run(engname):
                def body(eng):
                    for waits, fn, inc in self.q[engname]:
                        for k, v in waits:
                            eng.wait_ge(sems[k], v)
                        if fn is not None:
                            ins = fn(eng)
                            if inc is not None:
                                ins.then_inc(sems[inc[0]], inc[1])
                return body

            block.tensor(run("pe"))
            block.scalar(run("act"))
            block.vector(run("dve"))
            block.gpsimd(run("pool"))
            block.sync(run("sp"))


def _bucket(n):
    n = np.maximum(n, 0)
    nf = np.maximum(n, 1).astype(np.float32)
    large = 16 + (np.log(nf / np.float32(16)) / np.float32(math.log(64)) * np.float32(16)).astype(np.int32)
    large = np.minimum(large, 31)
    return np.where(n < 16, n, large)


def _consts():
    c = {}
    n = np.arange(2048)
    oh = np.zeros((32, 4096), np.float32)
    oh[_bucket(n), n] = 1.0
    c["c_oh"] = oh
    c["c_ident"] = np.eye(128, dtype=np.float32)
    t = np.arange(S)
    cur = t // 64
    j = np.arange(32)
    forced = (j[None, :] == 0) | (j[None, :] == cur[:, None]) | (j[None, :] == cur[:, None] - 1)
    causal = j[None, :] <= cur[:, None]
    m1 = (causal & ~forced).astype(np.float32)
    add = np.where(forced, BIG, np.where(causal, 0.0, -BIG)).astype(np.float32)
    c["c_m1"] = np.ascontiguousarray(m1.reshape(16, 128, 32).transpose(1, 0, 2))
    c["c_add"] = np.ascontiguousarray(add.reshape(16, 128, 32).transpose(1, 0, 2))
    e2 = (np.arange(S)[None, :] // 64 == j[:, None]).astype(np.float32)
    c["c_e2"] = e2
    cs = np.arange(127) * 16
    ss = np.arange(32) * 64
    ov = ((cs[:, None] < ss[None, :] + 64) & (cs[:, None] + 32 > ss[None, :])).astype(np.float32)
    c["c_ov"] = ov
    tri_s = np.triu(np.ones((64, 64), np.float32), 1)
    tri_i = np.triu(np.ones((64, 64), np.float32), 0)
    ts2 = np.triu(np.ones((128, 128), np.float32), 1)
    ti2 = np.triu(np.ones((128, 128), np.float32), 0)
    c["c_mask_sc2"] = np.concatenate([ts2, ti2, ts2, ti2], 1)
    c["c_mask_t8"] = np.ascontiguousarray(np.tile(ts2.T, (1, 4)))
    c["c_ident8"] = np.ascontiguousarray(np.tile(np.eye(128, dtype=np.float32), (1, 4)))
    cm = np.ones((64, S), np.float32)
    cm[:, ::128] = 0.0
    c["c_cmask"] = cm
    return c


W_NSA_FM = 1024
W_NSA_TM = 780


def _layout_inputs(inp):
    L = {}
    w_in = inp["w_in"][0]
    o_kv = 1024
    o_g = 2560
    o_za = 2584
    o_f = 3608
    o_zb = 6808
    for g in range(2):
        cols = list(range(g * 512, g * 512 + 512))
        cols += list(range(o_kv + 0 * 256 + g * 128, o_kv + 0 * 256 + g * 128 + 128))
        cols += list(range(o_kv + 1 * 256 + g * 128, o_kv + 1 * 256 + g * 128 + 128))
        cols += list(range(o_kv + 2 * 256 + g * 128, o_kv + 2 * 256 + g * 128 + 128))
        cols += list(range(o_kv + 4 * 256 + g * 128, o_kv + 4 * 256 + g * 128 + 128))
        cols += list(range(o_kv + 3 * 256 + g * 128, o_kv + 3 * 256 + g * 128 + 128))
        cols += list(range(o_kv + 5 * 256 + g * 128, o_kv + 5 * 256 + g * 128 + 128))
        for br in range(3):
            cols += [o_g + br * 8 + g * 4 + h for h in range(4)]
        cols += list(range(o_za + g * 512, o_za + g * 512 + 512))
        L["w_nsa%d" % g] = np.ascontiguousarray(w_in[:, cols])
    L["w_rkv"] = np.ascontiguousarray(w_in[:, o_f:o_f + 3072])
    L["w_lora"] = np.ascontiguousarray(w_in[:, o_f + 3072:o_f + 3200])
    L["w_zb"] = np.ascontiguousarray(w_in[:, o_zb:o_zb + 1024])
    L["g_pre"] = np.ascontiguousarray(inp["pre_norm_g"][0].reshape(16, 128).T)
    L["g_post"] = np.ascontiguousarray(np.tile(inp["post_norm_g"][0][None, :], (128, 1)))
    L["tab"] = np.ascontiguousarray(inp["rel_bias_table"])
    cmp_params = {"k": (inp["cmp_pos_k"], inp["cmp_k_w1"], inp["cmp_k_w2"]),
                  "v": (inp["cmp_pos_v"], inp["cmp_v_w1"], inp["cmp_v_w2"])}
    for nm in ("k", "v"):
        pos_, w1_, w2_ = cmp_params[nm]
        L["pos%sT" % nm] = np.ascontiguousarray(pos_[0].T)
        L["w1%s" % nm] = np.ascontiguousarray(w1_[0].reshape(32, 128, 128).transpose(1, 0, 2))
        L["w2%s" % nm] = np.ascontiguousarray(w2_[0])
    mu = inp["rwkv_mu"][0]
    per = np.zeros((64, 16, 8), np.float32)
    for h in range(16):
        sl = slice(h * 64, h * 64 + 64)
        per[:, h, 0] = mu[0:1024][sl]
        per[:, h, 1] = mu[1024:2048][sl]
        per[:, h, 2] = mu[2048:3072][sl]
        per[:, h, 3] = inp["rwkv_w0"][0][sl]
        per[:, h, 4] = inp["rwkv_a0"][0][sl]
        per[:, h, 5] = inp["rwkv_k_k"][0][sl]
        per[:, h, 6] = inp["rwkv_k_a"][0][sl]
        per[:, h, 7] = inp["rwkv_r_k"][0][h]
    L["rw_per"] = per
    L["mu_lora"] = np.ascontiguousarray(mu[3072:3200].reshape(2, 64).T)
    L["rw_w2"] = np.ascontiguousarray(inp["rwkv_w2"][0])
    L["rw_a2"] = np.ascontiguousarray(inp["rwkv_a2"][0])
    L["ln_w"] = np.ascontiguousarray(np.tile(inp["rwkv_ln_w"][0][None, :], (128, 1)))
    L["ln_b"] = np.ascontiguousarray(np.tile(inp["rwkv_ln_b"][0][None, :], (128, 1)))
    L["w_out"] = np.ascontiguousarray(inp["w_out"][0])
    return L


_IN_SHAPES = None


class B:
    def __init__(self, in_shapes, dbg=()):
        self.dbg = dbg
        nc = self.nc = bass.Bass("TRN2", target_bir_lowering=False)
        self.P = Prog(nc)
        self.din = {}
        for k, shp in in_shapes.items():
            self.din[k] = nc.dram_tensor(k, list(shp), F32, kind="ExternalInput")
        self.out = nc.dram_tensor("out", [S, D], F32, kind="ExternalOutput")
        self.mixd = nc.dram_tensor("mixd", [S, 2048], BF16)
        self.Z = [nc.dram_tensor("Zs%d" % h, [132, 4096], BF16) for h in range(8)]
        self.Zw = [nc.dram_tensor("Zw%d" % h, [132, 4096], BF16) for h in range(8)]
        self.r_Z = [Reg() for _ in range(8)]
        self.r_Zw = [Reg() for _ in range(8)]
        self.r_Z2 = [None] * 8
        self.r_Zw2 = [None] * 8
        self.r_mixd = [Reg() for _ in range(16)]
        self.out_regs = []
        self.rw_store_regs = []
        self.dbg_out = {}
        self.es = ExitStack()

    @contextmanager
    def scope(self):
        with ExitStack() as st:
            yield st
        self.P.barrier()

    def sb(self, st, name, shape, dt=F32):
        self.uid = getattr(self, "uid", 0) + 1
        return st.enter_context(self.nc.sbuf_tensor("s%d_%s" % (self.uid, name), list(shape), dt))

    def ps(self, st, name, shape, dt=F32):
        self.uid = getattr(self, "uid", 0) + 1
        return st.enter_context(self.nc.psum_tensor("p%d_%s" % (self.uid, name), list(shape), dt))

    def dbg_dump(self, name, ap_src, shape, reg, dt=F32):
        if name not in self.dbg:
            return
        t = self.nc.dram_tensor("dbg_" + name, list(shape), dt, kind="ExternalOutput")
        self.dbg_out[name] = t
        ro = Reg()
        self.out_regs.append(ro)
        self.P.dma("sp", lambda e: e.dma_start(out=t.ap(), in_=ap_src), r=[reg] if not isinstance(reg, list) else reg, w=[ro])

    def build(self):
        nc, P = self.nc, self.P
        with self.scope() as st:
            self.ident = self.sb(st, "ident", [128, 128]); self.r_ident = Reg()
            self.identb = self.sb(st, "identb", [128, 128], BF16); self.r_identb = Reg()
            self.ones = self.sb(st, "ones", [128, 128]); self.r_ones = Reg()
            self.xT = self.sb(st, "xT", [128, NCH, S], BF16)
            self.r_xT = [Reg() for _ in range(16)]
            self.rstd_col = self.sb(st, "rstd_col", [128, 16]); self.r_rc = Reg()
            self.rstd_bc = self.sb(st, "rstd_bc", [128, S]); self.r_rb = Reg()
            self.gpre = self.sb(st, "gpre", [128, 16]); self.r_gpre = Reg()
            self.pm = [self.ps(st, "pm%d" % i, [128, 512]) for i in range(6)]
            self.r_pm = [Reg() for _ in range(6)]
            self.ptr = self.ps(st, "ptr", [128, 4, 128], BF16); self.r_ptr = Reg()
            self.px = self.ps(st, "px", [128, 512]); self.r_px = Reg()
            self.pm_i = 0

            P.dma("sp", lambda e: e.dma_start(out=self.ident[:], in_=self.din["c_ident"].ap()), w=[self.r_ident])
            P.op("dve", lambda e: e.tensor_copy(self.identb[:], self.ident[:]), r=[self.r_ident], w=[self.r_identb])
            P.op("pool", lambda e: e.memset(self.ones[:], 1.0), w=[self.r_ones])
            P.dma("sp", lambda e: e.dma_start(out=self.gpre[:], in_=self.din["g_pre"].ap()), w=[self.r_gpre])

            self.phase0(st)
            self.phase_eb()
            for g in range(2):
                with self.scope() as st2:
                    self.phase_nsa(st2, g)
            with self.scope() as st2:
                self.phase_rwkv_proj(st2)
        with ExitStack() as stx:
            self.wo0 = self.sb(stx, "wo0", [128, NCH, 512], BF16); self.r_wo0 = Reg()
            P.dma("pool", lambda e: e.dma_start(out=self.wo0[:], in_=self.din["w_out"].ap()[:, 0:512].rearrange("(c p) n -> p c n", p=128)),
                  w=[self.r_wo0])
            with self.scope() as st:
                self.pm = [self.ps(st, "rm%d" % i, [128, 512]) for i in range(6)]
                self.r_pm = [Reg() for _ in range(6)]
                self.px = self.ps(st, "rpx", [128, 512]); self.r_px = Reg()
                self.py2 = self.ps(st, "rpy2", [128, 512]); self.r_py2 = Reg()
                self.phase_rwkv(st)
            with self.scope() as st:
                self.pm = [self.ps(st, "qm%d" % i, [128, 512]) for i in range(6)]
                self.r_pm = [Reg() for _ in range(6)]
                self.ptr = self.ps(st, "qtr", [128, 4, 128], BF16); self.r_ptr = Reg()
                self.phase_out(st)
                P.final_wait("sp", self.out_regs)
                P.emit()
        return nc

    def next_pm(self):
        i = self.pm_i
        self.pm_i = (i + 1) % len(self.pm)
        return self.pm[i], self.r_pm[i]

    def phase0(self, st0):
        nc, P = self.nc, self.P
        x = self.din["x"].ap()
        with self.scope() as st:
            xt = [self.sb(st, "xt%d" % i, [128, D]) for i in range(2)]
            r_xt = [Reg(), Reg()]
            xb = [self.sb(st, "xb%d" % i, [128, D], BF16) for i in range(2)]
            r_xb = [Reg(), Reg()]
            junk = self.sb(st, "junk", [128, D]); r_junk = Reg()
            ss = self.sb(st, "ss", [128, 16]); r_ss = Reg()
            dg = self.sb(st, "dg", [128, 128]); r_dg = Reg()
            for tt in range(16):
                b = tt % 2
                P.dma("sp", lambda e, b=b, tt=tt: e.dma_start(out=xt[b][:], in_=x[tt * 128:(tt + 1) * 128, :]), w=[r_xt[b]])
                P.op("act", lambda e, b=b, tt=tt: e.activation(out=junk[:], in_=xt[b][:], func=AF.Square), r=[r_xt[b]], w=[r_junk])
                P.op("dve", lambda e, tt=tt: e.reduce_sum(out=ss[:, tt:tt + 1], in_=junk[:], axis=AX.X), r=[r_junk], w=[r_ss])
                P.op("pool", lambda e, b=b: e.tensor_copy(xb[b][:], xt[b][:]), r=[r_xt[b]], w=[r_xb[b]])
                for gq in range(4):
                    for jq in range(4):
                        c = gq * 4 + jq
                        P.op("pe", lambda e, b=b, c=c, jq=jq: e.transpose(self.ptr[:, jq, :], xb[b][:, c * 128:(c + 1) * 128], self.identb[:]),
                             r=[r_xb[b], self.r_identb], w=[self.r_ptr], signal=(jq == 3))
                    for jq in range(4):
                        c = gq * 4 + jq
                        P.op("dve", lambda e, c=c, jq=jq, tt=tt: e.tensor_scalar(
                            out=self.xT[:, c, tt * 128:(tt + 1) * 128], in0=self.ptr[:, jq, :],
                            scalar1=self.gpre[:, c:c + 1], scalar2=None, op0=ALU.mult),
                            r=[self.r_ptr, self.r_gpre], w=[self.r_xT[tt]])
            P.op("dve", lambda e: e.tensor_scalar(out=ss[:], in0=ss[:], scalar1=1.0 / D, scalar2=1e-6, op0=ALU.mult, op1=ALU.add),
                 r=[r_ss], w=[r_ss])
            P.op("act", lambda e: e.activation(out=ss[:], in_=ss[:], func=AF.Sqrt), r=[r_ss], w=[r_ss])
            P.op("dve", lambda e: e.reciprocal(self.rstd_col[:], ss[:]), r=[r_ss], w=[self.r_rc])
            for tt in range(16):
                P.op("dve", lambda e, tt=tt: e.tensor_scalar(out=dg[:], in0=self.ident[:], scalar1=self.rstd_col[:, tt:tt + 1],
                                                            scalar2=None, op0=ALU.mult), r=[self.r_ident, self.r_rc], w=[r_dg])
                P.op("pe", lambda e: e.matmul(self.px[:, 0:128], lhsT=self.ones[:], rhs=dg[:], start=True, stop=True),
                     r=[self.r_ones, r_dg], w=[self.r_px])
                P.op("act", lambda e, tt=tt: e.activation(out=self.rstd_bc[:, tt * 128:(tt + 1) * 128], in_=self.px[:, 0:128], func=AF.Copy),
                     r=[self.r_px], w=[self.r_rb])

        self.dbg_dump("rstd_col", self.rstd_col[:], [128, 16], self.r_rc)
        self.dbg_dump("rstd_bc", self.rstd_bc[:], [128, S], self.r_rb)
        self.dbg_dump("xT0", self.xT[:, 0, :], [128, S], self.r_xT, BF16)

    def load_w(self, wb, r_wb, src_ap):
        self.P.dma("pool", lambda e: e.dma_start(out=wb, in_=src_ap.rearrange("(c p) n -> p c n", p=128)), w=[r_wb])

    def proj_fm(self, wb, r_wb, j0, ncols, dst_fn, r_dst, scale=None, evac="dve"):
        P = self.P
        for tb in range(4):
            pm, r_pm = self.next_pm()
            for c in range(NCH):
                P.op("pe", lambda e, c=c, tb=tb, pm=pm: e.matmul(pm[0:ncols, :], lhsT=wb[:, c, j0:j0 + ncols],
                                                                 rhs=self.xT[:, c, tb * 512:(tb + 1) * 512],
                                                                 start=(c == 0), stop=(c == NCH - 1)),
                     r=[r_wb] + self.r_xT[tb * 4:tb * 4 + 4], w=[r_pm], signal=(c == NCH - 1))
            if scale is None:
                P.op("dve", lambda e, tb=tb, pm=pm: e.tensor_tensor(out=dst_fn(tb), in0=pm[0:ncols, :],
                                                                   in1=self.rstd_bc[0:ncols, tb * 512:(tb + 1) * 512], op=ALU.mult),
                     r=[r_pm, self.r_rb], w=[r_dst])
            else:
                P.op("dve", lambda e, tb=tb, pm=pm: e.scalar_tensor_tensor(out=dst_fn(tb), in0=pm[0:ncols, :], scalar=scale,
                                                                          in1=self.rstd_bc[0:ncols, tb * 512:(tb + 1) * 512],
                                                                          op0=ALU.mult, op1=ALU.mult),
                     r=[r_pm, self.r_rb], w=[r_dst])

    def proj_tm(self, wb, r_wb, j0, ncols, dst_fn, r_dst, func=AF.Copy):
        P = self.P
        for tt in range(16):
            pm, r_pm = self.next_pm()
            for c in range(NCH):
                P.op("pe", lambda e, c=c, tt=tt, pm=pm: e.matmul(pm[:, 0:ncols], lhsT=self.xT[:, c, tt * 128:(tt + 1) * 128],
                                                                 rhs=wb[:, c, j0:j0 + ncols], start=(c == 0), stop=(c == NCH - 1)),
                     r=[r_wb, self.r_xT[tt]], w=[r_pm], signal=(c == NCH - 1))
            P.op("act", lambda e, tt=tt, pm=pm: e.activation(out=dst_fn(tt), in_=pm[:, 0:ncols], func=func,
                                                            scale=self.rstd_col[:, tt:tt + 1]),
                 r=[r_pm, self.r_rc], w=[r_dst])

    def phase_eb(self):
        nc, P = self.nc, self.P
        with self.scope() as st:
            oh = self.sb(st, "oh", [32, 4096]); r_oh = Reg()
            tab = self.sb(st, "tab", [32, 8]); r_tab = Reg()
            zrow = [self.sb(st, "zrow%d" % i, [128, 4096], BF16) for i in range(2)]
            r_zrow = [Reg(), Reg()]
            P.dma("sp", lambda e: e.dma_start(out=oh[:], in_=self.din["c_oh"].ap()), w=[r_oh])
            P.dma("sp", lambda e: e.dma_start(out=tab[:], in_=self.din["tab"].ap()), w=[r_tab])
            tabrep = self.sb(st, "tabrep", [32, 8, 128]); r_tabrep = Reg()
            for h in range(8):
                P.op("dve", lambda e, h=h: e.tensor_scalar(out=tabrep[:, h, :], in0=self.ones[0:32, :], scalar1=tab[:, h:h + 1], scalar2=None,
                                                          op0=ALU.mult), r=[self.r_ones, r_tab], w=[r_tabrep])
            k = 0
            for h in range(8):
                for win in range(2):
                    zb = zrow[k % 2]; r_zb = r_zrow[k % 2]
                    k += 1
                    nblk = 1 if win else 4
                    if True:
                        P.op("pool", lambda e, zb=zb: e.memset(zb[:, 512 * nblk:], 0.0), w=[r_zb])
                    for blk in range(nblk):
                        pm, r_pm = self.next_pm()
                        P.op("pe", lambda e, h=h, blk=blk, pm=pm: e.matmul(pm[:, :], lhsT=tabrep[:, h, :],
                                                                          rhs=oh[:, blk * 512:(blk + 1) * 512], start=True, stop=True),
                             r=[r_tabrep, r_oh], w=[r_pm])
                        P.op("act", lambda e, blk=blk, pm=pm, zb=zb: e.activation(out=zb[:, blk * 512:(blk + 1) * 512], in_=pm[:, :], func=AF.Exp),
                             r=[r_pm], w=[r_zb])
                    dst = (self.Zw if win else self.Z)[h]
                    r_dst = (self.r_Zw if win else self.r_Z)[h]
                    P.dma("sp", lambda e, dst=dst, zb=zb: e.dma_start(out=dst.ap()[0:128, :], in_=zb[:]), r=[r_zb], w=[r_dst])
                    r_dst2 = Reg()
                    P.dma("sp", lambda e, dst=dst, zb=zb: e.dma_start(out=dst.ap()[128:132, :], in_=zb[0:4, :]), r=[r_zb], w=[r_dst2])
                    (self.r_Zw2 if win else self.r_Z2)[h] = r_dst2

    def toep(self, dst_ap, r_dst, Zt, r_Z, c, pstep, nparts, nfree, r_Z2=None):
        src = bass.AP(Zt, c % 4096, [[pstep, nparts], [1, nfree]])
        self.P.dma("pool", lambda e: e.dma_start(out=dst_ap, in_=src), r=[r_Z] + ([r_Z2] if r_Z2 is not None else []), w=[r_dst])

    def phase_nsa(self, st, g):
        nc, P = self.nc, self.P
        wsrc = self.din["w_nsa%d" % g].ap()
        qT = self.sb(st, "qT", [128, 4, S], BF16); r_qT = Reg()
        kT = self.sb(st, "kT", [128, 4, S], BF16); r_kT = [Reg() for _ in range(4)]
        vs = self.sb(st, "vs", [128, 16, 132], BF16); r_vs = Reg()
        vw = self.sb(st, "vw", [128, 16, 132], BF16); r_vw = Reg()
        gt = self.sb(st, "gt", [128, 16, 12]); r_gt = Reg()
        oacc = self.sb(st, "oacc", [128, 16, 512]); r_oacc = [Reg() for _ in range(16)]
        imp = self.sb(st, "imp", [128, 16, 32]); r_imp = [Reg() for _ in range(16)]
        negT = self.sb(st, "negT", [128, S], BF16); r_negT = Reg()
        e2c = self.sb(st, "e2c", [128, S], BF16); r_e2c = Reg()
        kcT = self.sb(st, "kcT", [128, 128], BF16); r_kcT = Reg()
        vce = self.sb(st, "vce", [128, 164], BF16); r_vce = Reg()
        P.op("pool", lambda e: e.memset(negT[:], 0.0), w=[r_negT])
        P.op("pool", lambda e: e.memset(e2c[:], 0.0), w=[r_e2c])

        with self.scope() as stw:
            wb = [self.sb(stw, "wbn%d" % i, [128, NCH, 512], BF16) for i in range(2)]
            r_wb = [Reg(), Reg()]
            self.load_w(wb[0][:], r_wb[0], wsrc[:, 0:512])
            self.load_w(wb[1][:], r_wb[1], wsrc[:, 512:1024])
            for h in range(4):
                self.proj_fm(wb[0], r_wb[0], h * 128, 128, lambda tb, h=h: qT[:, h, tb * 512:(tb + 1) * 512], r_qT, scale=128.0 ** -0.5)
            for i in range(4):
                self.proj_fm(wb[1], r_wb[1], i * 128, 128, lambda tb, i=i: kT[:, i, tb * 512:(tb + 1) * 512], r_kT[i])
            self.load_w(wb[0][:, :, 0:268], r_wb[0], wsrc[:, 1024:1292])
            P.op("pool", lambda e: e.memset(vs[:, :, 128:132], 1.0), w=[r_vs])
            P.op("pool", lambda e: e.memset(vw[:, :, 128:132], 1.0), w=[r_vw])
            self.proj_tm(wb[0], r_wb[0], 0, 128, lambda tt: vs[:, tt, 0:128], r_vs)
            self.proj_tm(wb[0], r_wb[0], 128, 128, lambda tt: vw[:, tt, 0:128], r_vw)
            self.proj_tm(wb[0], r_wb[0], 256, 12, lambda tt: gt[:, tt, :], r_gt, func=AF.Sigmoid)

        self.dbg_dump("qT%d" % g, qT[:, 0, :], [128, S], r_qT, BF16)
        self.dbg_dump("kT%d" % g, kT[:, 0, :], [128, S], r_kT[0], BF16)
        self.dbg_dump("vs%d" % g, vs[:, 0, :], [128, 132], r_vs, BF16)
        self.dbg_dump("gt%d" % g, gt[:, 0, :], [128, 12], r_gt)
        with self.scope() as st2:
            e2f = self.sb(st2, "e2f", [32, S]); r_e2f = Reg()
            P.dma("sp", lambda e: e.dma_start(out=e2f[:], in_=self.din["c_e2"].ap()), w=[r_e2f])
            P.op("dve", lambda e: e.tensor_copy(e2c[0:32, :], e2f[:]), r=[r_e2f], w=[r_e2c])

        with self.scope() as st2:
            w1 = self.sb(st2, "w1", [128, 32, 128], BF16); r_w1 = Reg()
            w2 = self.sb(st2, "w2", [128, 128], BF16); r_w2 = Reg()
            posT = self.sb(st2, "posT", [128, 32], BF16); r_posT = Reg()
            cb = self.sb(st2, "cb", [128, 1]); r_cb = Reg()
            h1s = self.sb(st2, "h1s", [128, 128], BF16); r_h1s = Reg()
            ovf = self.sb(st2, "ovf", [128, 33]); r_ovf = Reg()
            P.op("pool", lambda e: e.memset(ovf[:, 0:1], 1.0), w=[r_ovf])
            P.dma("sp", lambda e: e.dma_start(out=ovf[0:127, 1:33], in_=self.din["c_ov"].ap()), w=[r_ovf])
            P.op("dve", lambda e: e.tensor_copy(vce[0:127, 128:161], ovf[0:127, :]), r=[r_ovf], w=[r_vce])
            for which in range(2):
                nm = "kv"[which]
                P.dma("pool", lambda e, nm=nm: e.dma_start(out=w1[:], in_=self.din["w1" + nm].ap()), w=[r_w1])
                P.dma("pool", lambda e, nm=nm: e.dma_start(out=w2[:], in_=self.din["w2" + nm].ap()), w=[r_w2])
                P.dma("pool", lambda e, nm=nm: e.dma_start(out=posT[:], in_=self.din["pos%sT" % nm].ap()), w=[r_posT])
                pm, r_pm = self.next_pm()
                for l in range(32):
                    P.op("pe", lambda e, l=l, pm=pm: e.matmul(pm[:, 0:1], lhsT=w1[:, l, :], rhs=posT[:, l:l + 1], start=(l == 0), stop=(l == 31)),
                         r=[r_w1, r_posT], w=[r_pm], signal=(l == 31))
                P.op("dve", lambda e, pm=pm: e.tensor_copy(cb[:], pm[:, 0:1]), r=[r_pm], w=[r_cb])
                pm, r_pm = self.next_pm()
                for l in range(32):
                    P.op("pe", lambda e, l=l, pm=pm, which=which: e.matmul(pm[:, 0:127], lhsT=w1[:, l, :],
                                                                          rhs=kT[:, which, l:l + 16 * 126 + 1:16],
                                                                          start=(l == 0), stop=(l == 31)),
                         r=[r_w1, r_kT[which]], w=[r_pm], signal=(l == 31))
                P.op("act", lambda e, pm=pm: e.activation(out=h1s[:, 0:127], in_=pm[:, 0:127], func=AF.Silu, bias=cb[:, 0:1]),
                     r=[r_pm, r_cb], w=[r_h1s])
                pm, r_pm = self.next_pm()
                if which == 0:
                    P.op("pe", lambda e, pm=pm: e.matmul(pm[:, 0:127], lhsT=w2[:], rhs=h1s[:, 0:127], start=True, stop=True),
                         r=[r_w2, r_h1s], w=[r_pm])
                    P.op("dve", lambda e, pm=pm: e.tensor_copy(kcT[:, 0:127], pm[:, 0:127]), r=[r_pm], w=[r_kcT])
                else:
                    P.op("pe", lambda e, pm=pm: e.matmul(pm[0:127, 0:128], lhsT=h1s[:, 0:127], rhs=w2[:], start=True, stop=True),
                         r=[r_w2, r_h1s], w=[r_pm])
                    P.op("dve", lambda e, pm=pm: e.tensor_copy(vce[0:127, 0:128], pm[0:127, 0:128]), r=[r_pm], w=[r_vce])

        with self.scope() as sta:
            e1 = [self.sb(sta, "e1_%d" % i, [128, 512], BF16) for i in range(2)]
            r_e1 = [Reg(), Reg()]
            e2 = self.sb(sta, "e2", [128, 16, 512], BF16); r_e2 = [Reg() for _ in range(16)]
            sm = self.sb(sta, "sm", [128, 4, 4]); r_sm = [Reg() for _ in range(4)]
            EBc = self.sb(sta, "EBc", [128, S], BF16); r_EBc = Reg()
            EBs = self.sb(sta, "EBs", [128, 16, 512], BF16); r_EBs = Reg()
            EBw = self.sb(sta, "EBw", [128, 8, 512], BF16); r_EBw = Reg()
            ei = 0

            def gate_col(br, h):
                return br * 4 + h

            for h in range(4):
                hd = g * 4 + h
                self.toep(EBc[0:127, :], r_EBc, self.Z[hd], self.r_Z[hd], 4096 - 31, 4096 - 16, 127, S, self.r_Z2[hd])
                for Q in range(4):
                    pm, r_pm = self.next_pm()
                    P.op("pe", lambda e, pm=pm, h=h, Q=Q: e.matmul(pm[0:127, :], lhsT=kcT[:, 0:127], rhs=qT[:, h, Q * 512:(Q + 1) * 512],
                                                                  start=True, stop=True), r=[r_kcT, r_qT], w=[r_pm])
                    eb = e1[ei % 2]; r_eb = r_e1[ei % 2]; ei += 1
                    P.op("act", lambda e, pm=pm, eb=eb: e.activation(out=eb[0:127, :], in_=pm[0:127, :], func=AF.Exp), r=[r_pm], w=[r_eb])
                    P.op("dve", lambda e, eb=eb, Q=Q: e.tensor_tensor(out=e2[0:127, 0, :], in0=eb[0:127, :], in1=EBc[0:127, Q * 512:(Q + 1) * 512],
                                                                     op=ALU.mult), r=[r_eb, r_EBc], w=[r_e2[0]])
                    pms = []
                    for sq in range(4):
                        pm, r_pm = self.next_pm()
                        pms.append((pm, r_pm))
                        P.op("pe", lambda e, pm=pm, sq=sq: e.matmul(pm[:, 0:161], lhsT=e2[0:127, 0, sq * 128:(sq + 1) * 128], rhs=vce[0:127, 0:161],
                                                                   start=True, stop=True), r=[r_e2[0], r_vce], w=[r_pm])
                    for sq in range(4):
                        pm, r_pm = pms[sq]
                        P.op("dve", lambda e, pm=pm, sq=sq: e.tensor_scalar(out=sm[:, sq, 0:1], in0=pm[:, 128:129], scalar1=1e-30, scalar2=None, op0=ALU.max),
                             r=[r_pm], w=[r_sm[sq]])
                    P.op("dve", lambda e: e.reciprocal(sm[:, :, 1], sm[:, :, 0]), r=list(r_sm), w=list(r_sm))
                    P.op("dve", lambda e, h=h, Q=Q: e.tensor_tensor(out=sm[:, :, 2], in0=sm[:, :, 1], in1=gt[:, Q * 4:(Q + 1) * 4, gate_col(0, h)],
                                                                   op=ALU.mult), r=list(r_sm) + [r_gt], w=list(r_sm))
                    for sq in range(4):
                        T = Q * 4 + sq
                        pm, r_pm = pms[sq]
                        P.op("dve", lambda e, pm=pm, T=T, h=h, sq=sq: e.tensor_scalar(out=oacc[:, T, h * 128:(h + 1) * 128], in0=pm[:, 0:128],
                                                                                     scalar1=sm[:, sq, 2:3], scalar2=None, op0=ALU.mult),
                             r=[r_pm, r_sm[sq]], w=[r_oacc[T]])
                    for sq in range(4):
                        T = Q * 4 + sq
                        pm, r_pm = pms[sq]
                        if h == 0:
                            P.op("dve", lambda e, pm=pm, T=T, sq=sq: e.tensor_scalar(out=imp[:, T, :], in0=pm[:, 129:161], scalar1=sm[:, sq, 1:2], scalar2=None,
                                                                                    op0=ALU.mult), r=[r_pm, r_sm[sq]], w=[r_imp[T]])
                        else:
                            P.op("dve", lambda e, pm=pm, T=T, sq=sq: e.scalar_tensor_tensor(out=imp[:, T, :], in0=pm[:, 129:161], scalar=sm[:, sq, 1:2],
                                                                                           in1=imp[:, T, :], op0=ALU.mult, op1=ALU.add),
                                 r=[r_pm, r_sm[sq], r_imp[T]], w=[r_imp[T]])
            self.dbg_dump("imp%d" % g, imp[:], [128, 16, 32], r_imp[15])

            self.dbg_dump("kcT%d" % g, kcT[:], [128, 128], r_kcT, BF16)
            self.dbg_dump("vce%d" % g, vce[:], [128, 164], r_vce, BF16)
            self.dbg_dump("oaccc%d" % g, oacc[:, 0, :], [128, 512], r_oacc[0])
            with self.scope() as st2x:
                sc = imp
                r_sc = r_imp
                sc2 = self.sb(st2x, "sc2", [128, 16, 32]); r_sc2 = [Reg() for _ in range(16)]
                m8 = self.sb(st2x, "m8", [128, 16, 16]); r_m8 = [Reg() for _ in range(16)]
                P.dma("sp", lambda e: e.dma_start(out=sc2[:], in_=self.din["c_m1"].ap()), w=r_sc2)
                P.op("dve", lambda e: e.tensor_tensor(out=sc[:], in0=imp[:], in1=sc2[:], op=ALU.mult), r=list(r_imp) + list(r_sc2), w=list(r_sc))
                P.dma("sp", lambda e: e.dma_start(out=sc2[:], in_=self.din["c_add"].ap()), r=r_sc2, w=r_sc2)
                P.op("dve", lambda e: e.tensor_tensor(out=sc[:], in0=sc[:], in1=sc2[:], op=ALU.add), r=list(r_sc) + list(r_sc2), w=list(r_sc))
                for T in range(16):
                    P.op("dve", lambda e, T=T: e.max(out=m8[:, T, 0:8], in_=sc[:, T, :]), r=[r_sc[T]], w=[r_m8[T]])
                for T in range(16):
                    P.op("dve", lambda e, T=T: e.match_replace(out=sc2[:, T, :], in_to_replace=m8[:, T, 0:8], in_values=sc[:, T, :], imm_value=-3.0e38),
                         r=[r_sc[T], r_m8[T]], w=[r_sc2[T]])
                for T in range(16):
                    P.op("dve", lambda e, T=T: e.max(out=m8[:, T, 8:16], in_=sc2[:, T, :]), r=[r_sc2[T]], w=[r_m8[T]])
                P.op("dve", lambda e: e.tensor_tensor(out=sc2[:], in0=sc[:], in1=m8[:, :, 15:16].to_broadcast([128, 16, 32]), op=ALU.is_ge),
                     r=list(r_sc) + list(r_m8), w=list(r_sc2))
                P.op("dve", lambda e: e.tensor_scalar(out=sc2[:], in0=sc2[:], scalar1=-1.0, scalar2=30000.0, op0=ALU.add, op1=ALU.mult),
                     r=list(r_sc2), w=list(r_sc2))
                for T4 in range(4):
                    pm, r_pm = self.next_pm()
                    for j4 in range(4):
                        T = T4 * 4 + j4
                        P.op("pe", lambda e, pm=pm, T=T, j4=j4: e.transpose(pm[0:32, j4 * 128:(j4 + 1) * 128], sc2[:, T, :], self.ident[:]),
                             r=[r_sc2[T], self.r_ident], w=[r_pm], signal=(j4 == 3))
                    P.op("act", lambda e, pm=pm, T4=T4: e.activation(out=negT[0:32, T4 * 512:(T4 + 1) * 512], in_=pm[0:32, :], func=AF.Copy),
                         r=[r_pm], w=[r_negT])
            self.dbg_dump("negT%d" % g, negT[0:32, :], [32, S], r_negT, BF16)

            def load_eb(hh, which):
                hd_ = g * 4 + hh
                if which == 1:
                    src_s = bass.AP(self.Z[hd_], 4096 - 384, [[4095, 128], [128, 16], [1, 512]])
                    P.dma("pool", lambda e, src_s=src_s: e.dma_start(out=EBs[:], in_=src_s), r=[self.r_Z[hd_], self.r_Z2[hd_]], w=[r_EBs])
                else:
                    src_w = bass.AP(self.Zw[hd_], 4096 - 384, [[4095, 128], [128, 8], [1, 512]])
                    P.dma("pool", lambda e, src_w=src_w: e.dma_start(out=EBw[:], in_=src_w), r=[self.r_Zw[hd_], self.r_Zw2[hd_]], w=[r_EBw])

            load_eb(0, 1)
            load_eb(0, 2)
            for h in range(4):
                hd = g * 4 + h
                for br in (1, 2):
                    if br == 2 and h < 3:
                        load_eb(h + 1, 1)
                    for Q in range(4):
                        kt_lo = 0 if br == 1 else max(0, 4 * Q - 4)
                        kt_hi = 4 * Q + 3
                        kidx = 2 if br == 1 else 3
                        for kt in range(kt_lo, kt_hi + 1):
                            pm, r_pm = self.next_pm()
                            P.op("pe", lambda e, pm=pm, kt=kt, h=h, Q=Q, kidx=kidx, br=br: e.matmul(
                                pm[:, :], lhsT=kT[:, kidx, kt * 128:(kt + 1) * 128], rhs=qT[:, h, Q * 512:(Q + 1) * 512],
                                start=True, stop=(br == 2)), r=[r_kT[kidx], r_qT], w=[r_pm], signal=(br == 2))
                            if br == 1:
                                P.op("pe", lambda e, pm=pm, kt=kt, Q=Q: e.matmul(pm[:, :], lhsT=e2c[:, kt * 128:(kt + 1) * 128],
                                                                                rhs=negT[:, Q * 512:(Q + 1) * 512], start=False, stop=True),
                                     r=[r_e2c, r_negT], w=[r_pm])
                            eb = e1[ei % 2]; r_eb = r_e1[ei % 2]; ei += 1
                            P.op("act", lambda e, pm=pm, eb=eb: e.activation(out=eb[:], in_=pm[:, :], func=AF.Exp), r=[r_pm], w=[r_eb])
                            o = 4 * Q - kt + 3
                            EB = EBs if br == 1 else EBw
                            r_EB = r_EBs if br == 1 else r_EBw
                            P.op("dve", lambda e, eb=eb, kt=kt, o=o, EB=EB: e.tensor_tensor(out=e2[:, kt, :], in0=eb[:], in1=EB[:, o, :], op=ALU.mult),
                                 r=[r_eb, r_EB], w=[r_e2[kt]])
                        vv = vs if br == 1 else vw
                        r_vv = r_vs if br == 1 else r_vw
                        pms = []
                        for sq in range(4):
                            T = Q * 4 + sq
                            lo = 0 if br == 1 else max(0, T - 4)
                            hi = T
                            pm, r_pm = self.next_pm()
                            pms.append((pm, r_pm))
                            for kt in range(lo, hi + 1):
                                P.op("pe", lambda e, pm=pm, kt=kt, sq=sq, vv=vv, lo=lo, hi=hi: e.matmul(
                                    pm[:, 0:129], lhsT=e2[:, kt, sq * 128:(sq + 1) * 128], rhs=vv[:, kt, 0:129],
                                    start=(kt == lo), stop=(kt == hi)), r=[r_e2[kt], r_vv], w=[r_pm], signal=(kt == hi))
                        for sq in range(4):
                            pm, r_pm = pms[sq]
                            P.op("dve", lambda e, pm=pm, sq=sq: e.tensor_scalar(out=sm[:, sq, 0:1], in0=pm[:, 128:129], scalar1=1e-30, scalar2=None, op0=ALU.max),
                                 r=[r_pm], w=[r_sm[sq]])
                        P.op("dve", lambda e: e.reciprocal(sm[:, :, 1], sm[:, :, 0]), r=list(r_sm), w=list(r_sm))
                        P.op("dve", lambda e, h=h, br=br, Q=Q: e.tensor_tensor(out=sm[:, :, 2], in0=sm[:, :, 1], in1=gt[:, Q * 4:(Q + 1) * 4, gate_col(br, h)],
                                                                              op=ALU.mult), r=list(r_sm) + [r_gt], w=list(r_sm))
                        for sq in range(4):
                            T = Q * 4 + sq
                            pm, r_pm = pms[sq]
                            P.op("dve", lambda e, pm=pm, T=T, h=h, sq=sq: e.scalar_tensor_tensor(
                                out=oacc[:, T, h * 128:(h + 1) * 128], in0=pm[:, 0:128], scalar=sm[:, sq, 2:3],
                                in1=oacc[:, T, h * 128:(h + 1) * 128], op0=ALU.mult, op1=ALU.add),
                                r=[r_pm, r_sm[sq], r_oacc[T]], w=[r_oacc[T]])
                    if br == 2 and h < 3:
                        load_eb(h + 1, 2)

        with self.scope() as stf:
            wbz = self.sb(stf, "wbz", [128, NCH, 512], BF16); r_wbz = Reg()
            za = self.sb(stf, "za", [128, 16, 512], BF16); r_za = Reg()
            mixb = [self.sb(stf, "mixb%d" % i, [128, 512], BF16) for i in range(2)]
            r_mixb = [Reg(), Reg()]
            self.load_w(wbz[:], r_wbz, wsrc[:, 1292:1804])
            self.proj_tm(wbz, r_wbz, 0, 512, lambda tt: za[:, tt, :], r_za, func=AF.Silu)
            for T in range(16):
                b = T % 2
                P.op("dve", lambda e, T=T, b=b: e.tensor_tensor(out=mixb[b][:], in0=oacc[:, T, :], in1=za[:, T, :], op=ALU.mult),
                     r=[r_oacc[T], r_za], w=[r_mixb[b]])
                P.dma("sp", lambda e, T=T, b=b: e.dma_start(out=self.mixd.ap()[T * 128:(T + 1) * 128, g * 512:(g + 1) * 512], in_=mixb[b][:]),
                      r=[r_mixb[b]], w=[self.r_mixd[T]])
        if g == 1:
            self.dbg_dump("mixa", self.mixd.ap()[:, 0:1024], [S, 1024], list(self.r_mixd), BF16)

    def phase_rwkv_proj(self, st):
        nc, P = self.nc, self.P
        din = self.din
        self.rawd = nc.dram_tensor("rawd", [3, 1024, S + 1], F32)
        self.zbd = nc.dram_tensor("zbd", [S, 1024], BF16)
        self.lorad = nc.dram_tensor("lorad", [2, 64, S], F32)
        self.rw_regs = []
        wb = [self.sb(st, "wbp%d" % i, [128, NCH, 512], BF16) for i in range(2)]
        r_wb = [Reg(), Reg()]
        stg = [self.sb(st, "stg%d" % i, [128, S + 4]) for i in range(2)]
        r_stg = [Reg(), Reg()]
        for i in range(2):
            P.op("pool", lambda e, i=i: e.memset(stg[i][:, 0:1], 0.0), w=[r_stg[i]])
        k = 0
        for blk in range(6):
            b = blk % 2
            self.load_w(wb[b][:], r_wb[b], din["w_rkv"].ap()[:, blk * 512:(blk + 1) * 512])
            for j in range(4):
                ct = blk * 4 + j
                sg_, r_sg = stg[k % 2], r_stg[k % 2]
                k += 1
                self.proj_fm(wb[b], r_wb[b], j * 128, 128, lambda tb, sg_=sg_: sg_[:, 1 + tb * 512:1 + (tb + 1) * 512], r_sg)
                rr = Reg(); self.rw_regs.append(rr)
                P.dma("sp", lambda e, ct=ct, sg_=sg_: e.dma_start(out=self.rawd.ap()[ct // 8, (ct % 8) * 128:(ct % 8 + 1) * 128, 0:S + 1], in_=sg_[:, 0:S + 1]),
                      r=[r_sg], w=[rr])
        zst = [self.sb(st, "zst%d" % i, [128, 16, 512], BF16) for i in range(2)]
        r_zst = [Reg(), Reg()]
        for blk in range(2):
            self.load_w(wb[blk][:], r_wb[blk], din["w_zb"].ap()[:, blk * 512:(blk + 1) * 512])
            self.proj_tm(wb[blk], r_wb[blk], 0, 512, lambda tt, blk=blk: zst[blk][:, tt, :], r_zst[blk], func=AF.Silu)
            rr = Reg(); self.rw_regs.append(rr)
            P.dma("sp", lambda e, blk=blk: e.dma_start(out=self.zbd.ap()[:, blk * 512:(blk + 1) * 512].rearrange("(t p) c -> p t c", p=128),
                                                       in_=zst[blk][:]), r=[r_zst[blk]], w=[rr])
        mul = self.sb(st, "mul", [64, 2]); r_mul = Reg()
        P.dma("sp", lambda e: e.dma_start(out=mul[:], in_=din["mu_lora"].ap()), w=[r_mul])
        wbl = self.sb(st, "wbl", [128, NCH, 128], BF16); r_wbl = Reg()
        self.load_w(wbl[:], r_wbl, din["w_lora"].ap())
        raw = self.sb(st, "lraw", [64, S + 4]); r_raw = Reg()
        tmp = self.sb(st, "ltmp", [64, S]); r_tmp = Reg()
        lo = [self.sb(st, "lo%d" % i, [64, S]) for i in range(2)]; r_lo = [Reg(), Reg()]
        for i in range(2):
            P.op("pool", lambda e: e.memset(raw[:, 0:1], 0.0), w=[r_raw])
            self.proj_fm(wbl, r_wbl, i * 64, 64, lambda tb: raw[:, 1 + tb * 512:1 + (tb + 1) * 512], r_raw)
            P.op("dve", lambda e: e.tensor_tensor(out=tmp[:], in0=raw[:, 0:S], in1=raw[:, 1:S + 1], op=ALU.subtract), r=[r_raw], w=[r_tmp])
            P.op("dve", lambda e, i=i: e.scalar_tensor_tensor(out=lo[i][:], in0=tmp[:], scalar=mul[:, i:i + 1], in1=raw[:, 1:S + 1],
                                                             op0=ALU.mult, op1=ALU.add), r=[r_tmp, r_raw, r_mul], w=[r_lo[i]])
            if i == 0:
                P.op("act", lambda e: e.activation(out=lo[0][:], in_=lo[0][:], func=AF.Tanh), r=[r_lo[0]], w=[r_lo[0]])
            rr = Reg(); self.rw_regs.append(rr)
            P.dma("sp", lambda e, i=i: e.dma_start(out=self.lorad.ap()[i], in_=lo[i][:]), r=[r_lo[i]], w=[rr])

    def phase_rwkv(self, st):
        nc, P = self.nc, self.P
        din = self.din
        NT = 512
        F32R = mybir.dt.float32r

        def fr(ap):
            return ap.bitcast(F32R) if USE_F32R else ap

        def ld(name, shape, src, dt=F32, q="sp", r=()):
            t = self.sb(st, name, shape, dt)
            rg = Reg()
            P.dma(q, lambda e: e.dma_start(out=t[:], in_=src), r=list(r), w=[rg])
            return t, rg

        ident, r_ident = ld("identr", [128, 128], din["c_ident"].ap())
        ones = self.sb(st, "onesr", [64, 64]); r_ones = Reg()
        P.op("pool", lambda e: e.memset(ones[:], 1.0), w=[r_ones])
        msc, r_msc = ld("msc", [128, 512], din["c_mask_sc2"].ap())
        mt, r_mt = ld("mt", [128, 512], din["c_mask_t8"].ap())
        idr, r_idr = ld("idr", [128, 512], din["c_ident8"].ap())
        cmask, r_cmask = ld("cmask", [64, NT], din["c_cmask"].ap()[:, 0:NT])
        per, r_per = ld("per", [64, 16, 8], din["rw_per"].ap())
        w2, r_w2 = ld("rw2", [64, 1024], din["rw_w2"].ap())
        a2, r_a2 = ld("ra2", [64, 1024], din["rw_a2"].ap())
        twd, r_twd = ld("twd", [64, S], self.lorad.ap()[0], r=self.rw_regs)
        ads, r_ads = ld("ads", [64, S], self.lorad.ap()[1], r=self.rw_regs)
        omka = self.sb(st, "omka", [64, 16]); r_omka = Reg()
        P.op("dve", lambda e: e.tensor_scalar(out=omka[:], in0=per[:, :, 6], scalar1=-1.0, scalar2=1.0, op0=ALU.mult, op1=ALU.add),
             r=[r_per], w=[r_omka])

        def v3(ap):
            return ap.rearrange("p (c t) -> p c t", t=128)

        def make_set(si, pY, r_pY):
            sfx = "_%d" % si
            my_pm = self.pm[si * 3:(si + 1) * 3]
            my_rpm = self.r_pm[si * 3:(si + 1) * 3]
            rot = {"i": 0}

            def next_pm():
                i = rot["i"]
                rot["i"] = (i + 1) % 3
                return my_pm[i], my_rpm[i]

            Zst = self.sb(st, "Zst" + sfx, [128, 64]); r_Z = Reg()
            lnw = self.sb(st, "lnw" + sfx, [128, 64]); r_lnw = Reg()
            lnb = self.sb(st, "lnb" + sfx, [128, 64]); r_lnb = Reg()
            raws = self.sb(st, "raws" + sfx, [64, 3, NT + 4]); r_raws = Reg()
            names = ["r_s", "k_s", "v_s", "sg", "al", "kk", "k2", "Lp", "t0", "t1", "t2", "rinv", "bon"]
            Tt = {n: self.sb(st, "T_" + n + sfx, [64, NT]) for n in names}
            R = {n: Reg() for n in names}
            AR = self.sb(st, "AR" + sfx, [128, 4, 2, 128]); r_AR = Reg()
            BK = self.sb(st, "BK" + sfx, [64, 4, 2, 128]); r_BK = Reg()
            BKh = self.sb(st, "BKh" + sfx, [64, 4, 2, 128]); r_BKh = Reg()
            WC = self.sb(st, "WC" + sfx, [64, 4]); r_WC = Reg()
            TM = self.sb(st, "TM" + sfx, [128, 4, 3, 64]); r_TM = Reg()
            SC = self.sb(st, "SC" + sfx, [128, 4, 512]); r_SC = Reg()
            PP = [self.sb(st, "PP%d" % i + sfx, [128, 512]) for i in range(2)]; r_PP = [Reg(), Reg()]
            PT = [self.sb(st, "PTt%d" % i + sfx, [128, 512]) for i in range(2)]; r_PT = [Reg(), Reg()]
            Tm = self.sb(st, "Tm" + sfx, [128, 512]); r_Tm = Reg()
            XTs = self.sb(st, "XTs" + sfx, [128, 64]); r_XTs = Reg()
            UTs = self.sb(st, "UTs" + sfx, [128, 64]); r_UTs = Reg()
            Yt = self.sb(st, "Yt" + sfx, [128, 4, 64]); r_Yt = Reg()
            yc = self.sb(st, "yc" + sfx, [128, 4, 64]); r_yc = Reg()
            ysq = self.sb(st, "ysq" + sfx, [128, 4, 64]); r_ysq = Reg()
            st4 = self.sb(st, "st4" + sfx, [128, 16]); r_st4 = Reg()
            zbs = [self.sb(st, "zb%d" % i + sfx, [128, 4, 64], BF16) for i in range(2)]; r_zbs = [Reg(), Reg()]
            mixo = [self.sb(st, "mixo%d" % i + sfx, [128, 4, 64], BF16) for i in range(2)]; r_mixo = [Reg(), Reg()]
            state = {"mi": 0}

            def run(h):
                P.dma("sp", lambda e: e.dma_start(out=lnw[:], in_=din["ln_w"].ap()[:, h * 64:(h + 1) * 64]), w=[r_lnw])
                P.dma("sp", lambda e: e.dma_start(out=lnb[:], in_=din["ln_b"].ap()[:, h * 64:(h + 1) * 64]), w=[r_lnb])
                P.op("dve", lambda e: e.tensor_scalar(out=fr(Zst[:]), in0=mt[:, 0:64], scalar1=0.0, scalar2=None, op0=ALU.mult), r=[r_mt], w=[r_Z])
                if not state.get("ar_init"):
                    state["ar_init"] = True
                    for a_ in range(2):
                        P.op("dve", lambda e, a_=a_: e.tensor_scalar(out=fr(AR[:, a_ * 2:a_ * 2 + 2, :, :].rearrange("p a b c -> p (a b c)")), in0=mt[:],
                                                                    scalar1=0.0, scalar2=None, op0=ALU.mult), r=[r_mt], w=[r_AR])
                for G in range(4):
                    t0g = G * NT
                    P.dma("sp", lambda e, G=G: e.dma_start(out=raws[:, :, 0:NT + 1],
                                                           in_=self.rawd.ap()[:, h * 64:(h + 1) * 64, G * NT:G * NT + NT + 1].rearrange("i c t -> c i t")),
                          r=self.rw_regs, w=[r_raws])
                    zb, r_zb = zbs[G % 2], r_zbs[G % 2]
                    P.dma("sp", lambda e, G=G, zb=zb: e.dma_start(out=zb[:], in_=self.zbd.ap()[G * NT:(G + 1) * NT, h * 64:(h + 1) * 64].rearrange("(t p) c -> p t c", p=128)),
                          r=self.rw_regs, w=[r_zb])
                    for i in range(3):
                        nm = ("r_s", "k_s", "v_s")[i]
                        P.op("dve", lambda e, i=i: e.tensor_tensor(out=Tt["t0"][:], in0=raws[:, i, 0:NT], in1=raws[:, i, 1:NT + 1], op=ALU.subtract),
                             r=[r_raws], w=[R["t0"]])
                        P.op("dve", lambda e, i=i, nm=nm: e.scalar_tensor_tensor(out=Tt[nm][:], in0=Tt["t0"][:], scalar=per[:, h, i:i + 1],
                                                                                in1=raws[:, i, 1:NT + 1], op0=ALU.mult, op1=ALU.add),
                             r=[R["t0"], r_raws, r_per], w=[R[nm]])
                        yield
                    for (wmat, src, r_src, dst, bcol) in ((w2, twd, r_twd, "sg", 3), (a2, ads, r_ads, "al", 4)):
                        pm, r_pm = next_pm()
                        P.op("pe", lambda e, pm=pm, wmat=wmat, src=src, h=h, t0g=t0g: e.matmul(
                            pm[0:64, :], lhsT=wmat[:, h * 64:(h + 1) * 64], rhs=src[:, t0g:t0g + NT], start=True, stop=True),
                            r=[r_w2, r_a2, r_src], w=[r_pm])
                        P.op("act", lambda e, pm=pm, dst=dst, bcol=bcol, h=h: e.activation(out=Tt[dst][:], in_=pm[0:64, :], func=AF.Sigmoid,
                                                                                           bias=per[:, h, bcol:bcol + 1]),
                             r=[r_pm, r_per], w=[R[dst]])
                    P.op("act", lambda e, h=h: e.activation(out=Tt["t1"][:], in_=Tt["k_s"][:], func=AF.Square, scale=per[:, h, 5:6]),
                         r=[R["k_s"], r_per], w=[R["t1"]])
                    pm, r_pm = next_pm()
                    P.op("pe", lambda e, pm=pm: e.matmul(pm[0:64, :], lhsT=ones[:], rhs=Tt["t1"][:], start=True, stop=True),
                         r=[r_ones, R["t1"]], w=[r_pm])
                    P.op("act", lambda e, pm=pm: e.activation(out=Tt["rinv"][:], in_=pm[0:64, :], func=AF.Sqrt), r=[r_pm], w=[R["rinv"]])
                    yield
                    P.op("dve", lambda e: e.tensor_scalar(out=Tt["rinv"][:], in0=Tt["rinv"][:], scalar1=1e-12, scalar2=None, op0=ALU.max),
                         r=[R["rinv"]], w=[R["rinv"]])
                    P.op("dve", lambda e: e.reciprocal(Tt["rinv"][:], Tt["rinv"][:]), r=[R["rinv"]], w=[R["rinv"]])
                    P.op("dve", lambda e, h=h: e.scalar_tensor_tensor(out=Tt["kk"][:], in0=Tt["k_s"][:], scalar=per[:, h, 5:6], in1=Tt["rinv"][:],
                                                                     op0=ALU.mult, op1=ALU.mult), r=[R["k_s"], R["rinv"], r_per], w=[R["kk"]])
                    yield
                    P.op("dve", lambda e, h=h: e.tensor_scalar(out=Tt["t1"][:], in0=Tt["al"][:], scalar1=per[:, h, 6:7], scalar2=omka[:, h:h + 1],
                                                              op0=ALU.mult, op1=ALU.add), r=[R["al"], r_per, r_omka], w=[R["t1"]])
                    P.op("dve", lambda e: e.tensor_tensor(out=Tt["k2"][:], in0=Tt["k_s"][:], in1=Tt["t1"][:], op=ALU.mult),
                         r=[R["k_s"], R["t1"]], w=[R["k2"]])
                    P.op("dve", lambda e, h=h: e.scalar_tensor_tensor(out=Tt["t1"][:], in0=Tt["r_s"][:], scalar=per[:, h, 7:8], in1=Tt["k2"][:],
                                                                     op0=ALU.mult, op1=ALU.mult), r=[R["r_s"], R["k2"], r_per], w=[R["t1"]])
                    yield
                    pm, r_pm = next_pm()
                    P.op("pe", lambda e, pm=pm: e.matmul(pm[0:64, :], lhsT=ones[:], rhs=Tt["t1"][:], start=True, stop=True),
                         r=[r_ones, R["t1"]], w=[r_pm])
                    P.op("dve", lambda e, pm=pm: e.tensor_tensor(out=Tt["bon"][:], in0=pm[0:64, :], in1=Tt["v_s"][:], op=ALU.mult),
                         r=[r_pm, R["v_s"]], w=[R["bon"]])
                    P.op("dve", lambda e: e.tensor_tensor_scan(out=Tt["Lp"][:], data0=cmask[:], data1=Tt["sg"][:], initial=0.0,
                                                              op0=ALU.mult, op1=ALU.add), r=[r_cmask, R["sg"]], w=[R["Lp"]])
                    P.op("dve", lambda e: e.tensor_tensor(out=Tt["t1"][:], in0=Tt["Lp"][:], in1=Tt["sg"][:], op=ALU.subtract),
                         r=[R["Lp"], R["sg"]], w=[R["t1"]])
                    P.op("act", lambda e: e.activation(out=Tt["t1"][:], in_=Tt["t1"][:], func=AF.Exp, scale=-C0), r=[R["t1"]], w=[R["t1"]])
                    yield
                    P.op("dve", lambda e: e.scalar_tensor_tensor(out=fr(AR[0:64, :, 0, :]), in0=v3(Tt["kk"][:]), scalar=-1.0, in1=v3(Tt["t1"][:]),
                                                                op0=ALU.mult, op1=ALU.mult), r=[R["kk"], R["t1"]], w=[r_AR])
                    P.op("act", lambda e: e.activation(out=Tt["rinv"][:], in_=Tt["Lp"][:], func=AF.Exp, scale=-C0), r=[R["Lp"]], w=[R["rinv"]])
                    P.op("dve", lambda e: e.tensor_tensor(out=fr(AR[0:64, :, 1, :]), in0=v3(Tt["r_s"][:]), in1=v3(Tt["rinv"][:]), op=ALU.mult),
                         r=[R["r_s"], R["rinv"]], w=[r_AR])
                    yield
                    P.op("act", lambda e: e.activation(out=Tt["t1"][:], in_=Tt["Lp"][:], func=AF.Exp, scale=C0), r=[R["Lp"]], w=[R["t1"]])
                    P.op("dve", lambda e: e.tensor_tensor(out=Tt["t2"][:], in0=Tt["kk"][:], in1=Tt["al"][:], op=ALU.mult),
                         r=[R["kk"], R["al"]], w=[R["t2"]])
                    yield
                    P.op("dve", lambda e: e.tensor_tensor(out=fr(BK[:, :, 0, :]), in0=v3(Tt["t2"][:]), in1=v3(Tt["t1"][:]), op=ALU.mult),
                         r=[R["t2"], R["t1"]], w=[r_BK])
                    P.op("dve", lambda e: e.tensor_tensor(out=fr(BK[:, :, 1, :]), in0=v3(Tt["k2"][:]), in1=v3(Tt["t1"][:]), op=ALU.mult),
                         r=[R["k2"], R["t1"]], w=[r_BK])
                    yield
                    P.op("dve", lambda e: e.tensor_tensor(out=BKh[:].rearrange("p a b c -> p a (b c)"), in0=BK[:].rearrange("p a b c -> p a (b c)"),
                                                          in1=Tt["rinv"][:, 127:NT:128].unsqueeze(2).to_broadcast([64, 4, 256]), op=ALU.mult), r=[r_BK, R["rinv"]], w=[r_BKh])
                    yield
                    for c2 in range(2):
                        pm, r_pm = next_pm()
                        for cc in range(2):
                            c = c2 * 2 + cc
                            srcs = (Tt["v_s"][:, c * 128:(c + 1) * 128], BKh[:, c, 0, :], BKh[:, c, 1, :])
                            for k3 in range(3):
                                P.op("pe", lambda e, pm=pm, cc=cc, k3=k3, src=srcs[k3]: e.transpose(
                                    pm[:, (cc * 3 + k3) * 64:(cc * 3 + k3 + 1) * 64], src, ident[0:64, 0:64]),
                                    r=[R["v_s"], r_BKh, r_ident], w=[r_pm], signal=(cc == 1 and k3 == 2))
                        P.op("act", lambda e, pm=pm, c2=c2: e.activation(out=fr(TM[:, c2 * 2:c2 * 2 + 2, :, :]),
                                                                         in_=pm[:, 0:384].rearrange("p (a b c) -> p a b c", a=2, b=3), func=AF.Copy),
                             r=[r_pm], w=[r_TM])
                        yield
                    for c in range(4):
                        pm, r_pm = next_pm()
                        for k3 in range(2):
                            P.op("pe", lambda e, pm=pm, c=c, k3=k3: e.matmul(pm[:, k3 * 256:(k3 + 1) * 256], lhsT=fr(BK[:, c, k3, :]),
                                                                            rhs=fr(AR[0:64, c, :, :]), start=True, stop=True),
                                 r=[r_BK, r_AR], w=[r_pm], signal=(k3 == 1))
                        P.op("dve", lambda e, pm=pm, c=c: e.tensor_tensor(out=fr(SC[:, c, :]), in0=pm[:, :], in1=msc[:], op=ALU.mult),
                             r=[r_pm, r_msc], w=[r_SC])
                        yield
                    pm, r_pm = next_pm()
                    for c in range(4):
                        P.op("pe", lambda e, pm=pm, c=c: e.matmul(pm[:, c * 128:(c + 1) * 128], lhsT=fr(AR[0:64, c, 0, :]), rhs=fr(BK[:, c, 0, :]),
                                                                 start=True, stop=True), r=[r_AR, r_BK], w=[r_pm], signal=(c == 3))
                    P.op("dve", lambda e, pm=pm: e.tensor_tensor(out=fr(PT[0][:]), in0=pm[:, :], in1=mt[:], op=ALU.mult),
                         r=[r_pm, r_mt], w=[r_PT[0]])
                    yield
                    P.op("dve", lambda e: e.tensor_tensor(out=fr(v3(Tm[:])), in0=SC[:, :, 0:128], in1=v3(idr[:]), op=ALU.add), r=[r_SC, r_idr], w=[r_Tm])
                    yield
                    cur = 0
                    for it in range(6):
                        nxt = 1 - cur
                        if it < 5:
                            pm, r_pm = next_pm()
                            for c in range(4):
                                Pv = SC[:, c, 0:128] if it == 0 else PP[cur][:, c * 128:(c + 1) * 128]
                                P.op("pe", lambda e, pm=pm, c=c, cur=cur, Pv=Pv: e.matmul(pm[:, c * 128:(c + 1) * 128], lhsT=fr(PT[cur][:, c * 128:(c + 1) * 128]),
                                                                                         rhs=fr(Pv), start=True, stop=True),
                                     r=[r_PT[cur], r_SC if it == 0 else r_PP[cur]], w=[r_pm], signal=(c == 3))
                            P.op("act", lambda e, pm=pm, nxt=nxt: e.activation(out=fr(PP[nxt][:]), in_=pm[:, :], func=AF.Copy),
                                 r=[r_pm], w=[r_PP[nxt]])
                            yield
                        pm, r_pm = next_pm()
                        for c in range(4):
                            Pv = SC[:, c, 0:128] if it == 0 else PP[cur][:, c * 128:(c + 1) * 128]
                            P.op("pe", lambda e, pm=pm, c=c, cur=cur, Pv=Pv: e.matmul(pm[:, c * 128:(c + 1) * 128], lhsT=fr(Pv),
                                                                                     rhs=fr(PT[cur][:, c * 128:(c + 1) * 128]), start=True, stop=True),
                                 r=[r_PT[cur], r_SC if it == 0 else r_PP[cur]], w=[r_pm], signal=(c == 3))
                        P.op("dve", lambda e, pm=pm, nxt=nxt: e.tensor_copy(fr(PT[nxt][:]), pm[:, :]), r=[r_pm], w=[r_PT[nxt]])
                        yield
                        pm, r_pm = next_pm()
                        for c in range(4):
                            P.op("pe", lambda e, pm=pm, c=c, nxt=nxt: e.matmul(pm[:, c * 128:(c + 1) * 128], lhsT=fr(PT[nxt][:, c * 128:(c + 1) * 128]),
                                                                              rhs=fr(Tm[:, c * 128:(c + 1) * 128]), start=True, stop=True),
                                 r=[r_PT[nxt], r_Tm], w=[r_pm], signal=(c == 3))
                        P.op("dve", lambda e, pm=pm: e.tensor_tensor(out=fr(Tm[:]), in0=pm[:, :], in1=Tm[:], op=ALU.add), r=[r_pm, r_Tm], w=[r_Tm])
                        yield
                        cur = nxt
                    for c in range(4):
                        pm, r_pm = next_pm()
                        P.op("pe", lambda e, pm=pm, c=c: e.matmul(pm[:, 0:64], lhsT=fr(AR[:, c, 0, :]), rhs=fr(Zst[:]), start=True, stop=False),
                             r=[r_AR, r_Z], w=[r_pm], signal=False)
                        P.op("pe", lambda e, pm=pm, c=c: e.matmul(pm[:, 0:64], lhsT=fr(SC[:, c, 256:384]), rhs=fr(TM[:, c, 0, :]), start=False, stop=True),
                             r=[r_SC, r_TM], w=[r_pm])
                        P.op("act", lambda e, pm=pm: e.activation(out=fr(XTs[:]), in_=pm[:, 0:64], func=AF.Copy), r=[r_pm], w=[r_XTs])
                        yield
                        pm, r_pm = next_pm()
                        P.op("pe", lambda e, pm=pm, c=c: e.matmul(pm[:, 0:64], lhsT=fr(Tm[:, c * 128:(c + 1) * 128]), rhs=fr(XTs[:]), start=True, stop=True),
                             r=[r_Tm, r_XTs], w=[r_pm])
                        P.op("act", lambda e, pm=pm: e.activation(out=fr(UTs[:]), in_=pm[:, 0:64], func=AF.Copy), r=[r_pm], w=[r_UTs])
                        yield
                        yo = pY[:, c * 64:(c + 1) * 64]
                        P.op("pe", lambda e, yo=yo, c=c: e.matmul(yo, lhsT=fr(AR[:, c, 1, :]), rhs=fr(Zst[:]), start=True, stop=False),
                             r=[r_AR, r_Z], w=[r_pY], signal=False)
                        P.op("pe", lambda e, yo=yo, c=c: e.matmul(yo, lhsT=fr(SC[:, c, 128:256]), rhs=fr(UTs[:]), start=False, stop=False),
                             r=[r_SC, r_UTs], w=[r_pY], signal=False)
                        P.op("pe", lambda e, yo=yo, c=c: e.matmul(yo, lhsT=fr(SC[:, c, 384:512]), rhs=fr(TM[:, c, 0, :]), start=False, stop=True),
                             r=[r_SC, r_TM], w=[r_pY])
                        yield
                        pm, r_pm = next_pm()
                        P.op("pe", lambda e, pm=pm, c=c: e.matmul(pm[0:64, 0:64], lhsT=fr(TM[:, c, 1, :]), rhs=fr(UTs[:]), start=True, stop=False),
                             r=[r_TM, r_UTs], w=[r_pm], signal=False)
                        P.op("pe", lambda e, pm=pm, c=c: e.matmul(pm[0:64, 0:64], lhsT=fr(TM[:, c, 2, :]), rhs=fr(TM[:, c, 0, :]), start=False, stop=True),
                             r=[r_TM], w=[r_pm])
                        P.op("dve", lambda e, pm=pm, c=c: e.scalar_tensor_tensor(out=fr(Zst[0:64, :]), in0=Zst[0:64, :], scalar=Tt["rinv"][:, c * 128 + 127:c * 128 + 128], in1=pm[0:64, 0:64],
                                                                                op0=ALU.mult, op1=ALU.add), r=[r_Z, R["rinv"], r_pm], w=[r_Z])
                        yield
                    P.op("act", lambda e: e.activation(out=Yt[:], in_=pY[:, 0:256].rearrange("p (a b) -> p a b", b=64), func=AF.Copy), r=[r_pY], w=[r_Yt])
                    if h == 0 and G == 0:
                        self.dbg_dump("yraw", Yt[:], [128, 4, 64], r_Yt)
                    P.op("dve", lambda e: e.reduce_sum(out=st4[:, 0:4], in_=Yt[:], axis=AX.X), r=[r_Yt], w=[r_st4])
                    yield
                    P.op("dve", lambda e: e.scalar_tensor_tensor(out=yc[:], in0=st4[:, 0:4].unsqueeze(2).to_broadcast([128, 4, 64]), scalar=-1.0 / 64,
                                                                in1=Yt[:], op0=ALU.mult, op1=ALU.add), r=[r_Yt, r_st4], w=[r_yc])
                    P.op("dve", lambda e: e.tensor_tensor(out=ysq[:], in0=yc[:], in1=yc[:], op=ALU.mult), r=[r_yc], w=[r_ysq])
                    P.op("dve", lambda e: e.reduce_sum(out=st4[:, 4:8], in_=ysq[:], axis=AX.X), r=[r_ysq], w=[r_st4])
                    yield
                    P.op("dve", lambda e: e.tensor_scalar(out=st4[:, 4:8], in0=st4[:, 4:8], scalar1=1.0 / 64, scalar2=64e-5, op0=ALU.mult, op1=ALU.add),
                         r=[r_st4], w=[r_st4])
                    P.op("act", lambda e: e.activation(out=st4[:, 4:8], in_=st4[:, 4:8], func=AF.Sqrt), r=[r_st4], w=[r_st4])
                    P.op("dve", lambda e: e.reciprocal(st4[:, 8:12], st4[:, 4:8]), r=[r_st4], w=[r_st4])
                    yield
                    pm, r_pm = next_pm()
                    for tq in range(4):
                        P.op("pe", lambda e, pm=pm, tq=tq: e.transpose(pm[:, tq * 64:(tq + 1) * 64], Tt["bon"][:, tq * 128:(tq + 1) * 128],
                                                                     ident[0:64, 0:64]), r=[R["bon"], r_ident], w=[r_pm], signal=(tq == 3))
                    P.op("dve", lambda e: e.tensor_tensor(out=yc[:], in0=yc[:], in1=st4[:, 8:12].unsqueeze(2).to_broadcast([128, 4, 64]), op=ALU.mult),
                         r=[r_yc, r_st4], w=[r_yc])
                    P.op("dve", lambda e: e.tensor_tensor(out=yc[:], in0=yc[:], in1=lnw[:, :].unsqueeze(1).to_broadcast([128, 4, 64]), op=ALU.mult),
                         r=[r_yc, r_lnw], w=[r_yc])
                    yield
                    P.op("dve", lambda e: e.tensor_tensor(out=yc[:], in0=yc[:], in1=lnb[:, :].unsqueeze(1).to_broadcast([128, 4, 64]), op=ALU.add),
                         r=[r_yc, r_lnb], w=[r_yc])
                    P.op("dve", lambda e, pm=pm: e.tensor_tensor(out=yc[:], in0=yc[:], in1=pm[:, 0:256].rearrange("p (a b) -> p a b", b=64), op=ALU.add),
                         r=[r_yc, r_pm], w=[r_yc])
                    yield
                    if h == 0 and G == 0:
                        self.dbg_dump("ob00", yc[:], [128, 4, 64], r_yc)

                    mo = mixo[state["mi"] % 2]; r_mo = r_mixo[state["mi"] % 2]; state["mi"] += 1
                    P.op("dve", lambda e, mo=mo, zb=zb: e.tensor_tensor(out=mo[:], in0=yc[:], in1=zb[:], op=ALU.mult), r=[r_yc, r_zb], w=[r_mo])
                    dst = self.mixd.ap()[G * NT:(G + 1) * NT, 1024 + h * 64:1024 + (h + 1) * 64].rearrange("(t p) c -> p t c", p=128)
                    rr = Reg()
                    self.rw_store_regs.append(rr)
                    P.dma("pool", lambda e, mo=mo, dst=dst: e.dma_start(out=dst, in_=mo[:]), r=[r_mo], w=[rr])
                    yield
            return run

        runs = [make_set(0, self.px, self.r_px), make_set(1, self.py2, self.r_py2)]
        for hp in range(8):
            gens = [runs[0](2 * hp), runs[1](2 * hp + 1)]
            for _ in range(RW_STAGGER):
                try:
                    next(gens[0])
                except StopIteration:
                    break
            while gens:
                for gq in list(gens):
                    try:
                        next(gq)
                    except StopIteration:
                        gens.remove(gq)
        self.dbg_dump("mixall", self.mixd.ap(), [S, 2048], list(self.r_mixd) + self.rw_store_regs, BF16)

    def phase_out(self, st):
        nc, P = self.nc, self.P
        din = self.din
        wo = self.sb(st, "wo", [128, NCH, D - 512], BF16); r_wo = [self.r_wo0] + [Reg() for _ in range(3)]
        for nb in range(1, 4):
            P.dma("pool", lambda e, nb=nb: e.dma_start(out=wo[:, :, (nb - 1) * 512:nb * 512],
                                                       in_=din["w_out"].ap()[:, nb * 512:(nb + 1) * 512].rearrange("(c p) n -> p c n", p=128)),
                  w=[r_wo[nb]])

        def wo_blk(c, nb):
            return self.wo0[:, c, :] if nb == 0 else wo[:, c, (nb - 1) * 512:nb * 512]
        gpo = self.sb(st, "gpo", [128, D]); r_gpo = Reg()
        P.dma("sp", lambda e: e.dma_start(out=gpo[:], in_=din["g_post"].ap()), w=[r_gpo])
        idb = self.sb(st, "idb2", [128, 128], BF16); r_idb = Reg()
        idf = self.sb(st, "idf2", [128, 128]); r_idf = Reg()
        P.dma("sp", lambda e: e.dma_start(out=idf[:], in_=din["c_ident"].ap()), w=[r_idf])
        P.op("dve", lambda e: e.tensor_copy(idb[:], idf[:]), r=[r_idf], w=[r_idb])
        mx = [self.sb(st, "mx%d" % i, [128, D], BF16) for i in range(2)]; r_mx = [Reg(), Reg()]
        mT = [self.sb(st, "mT%d" % i, [128, NCH, 128], BF16) for i in range(2)]; r_mT = [Reg(), Reg()]
        xr = [self.sb(st, "xr%d" % i, [128, D]) for i in range(2)]; r_xr = [Reg(), Reg()]
        ysbs = [self.sb(st, "ysb%d" % i, [128, D]) for i in range(2)]; r_ysbs = [Reg(), Reg()]
        jks = [self.sb(st, "jk%d" % i, [128, D]) for i in range(2)]; r_jks = [Reg(), Reg()]
        s1s = [self.sb(st, "s1_%d" % i, [128, 4]) for i in range(2)]; r_s1s = [Reg(), Reg()]
        ob = [self.sb(st, "ob%d" % i, [128, D]) for i in range(2)]; r_ob = [Reg(), Reg()]
        mix_regs = list(self.r_mixd) + self.rw_store_regs

        def stage_a(T):
            b = T % 2
            P.dma("sp", lambda e: e.dma_start(out=mx[b][:], in_=self.mixd.ap()[T * 128:(T + 1) * 128, :]), r=mix_regs, w=[r_mx[b]])
            P.dma("sp", lambda e: e.dma_start(out=xr[b][:], in_=din["x"].ap()[T * 128:(T + 1) * 128, :]), w=[r_xr[b]])
            for gq in range(4):
                for jq in range(4):
                    c = gq * 4 + jq
                    P.op("pe", lambda e, c=c, jq=jq: e.transpose(self.ptr[:, jq, :], mx[b][:, c * 128:(c + 1) * 128], idb[:]),
                         r=[r_mx[b], r_idb], w=[self.r_ptr], signal=(jq == 3))
                P.op("dve", lambda e, gq=gq: e.tensor_copy(mT[b][:, gq * 4:(gq + 1) * 4, :], self.ptr[:]),
                     r=[self.r_ptr], w=[r_mT[b]])

        def stage_b(T):
            b = T % 2
            ysb, r_ysb = ysbs[b], r_ysbs[b]
            for nb in range(4):
                pm, r_pm = self.next_pm()
                for c in range(NCH):
                    P.op("pe", lambda e, pm=pm, c=c, nb=nb: e.matmul(pm[:, :], lhsT=mT[b][:, c, :], rhs=wo_blk(c, nb),
                                                                   start=(c == 0), stop=(c == NCH - 1)),
                         r=[r_mT[b], r_wo[nb]], w=[r_pm], signal=(c == NCH - 1))
                P.op("act", lambda e, pm=pm, nb=nb: e.activation(out=ysb[:, nb * 512:(nb + 1) * 512], in_=pm[:, :], func=AF.Copy),
                     r=[r_pm], w=[r_ysb])

        def stage_c(T):
            b = T % 2
            ysb, r_ysb, jk, r_jk, s1, r_s1 = ysbs[b], r_ysbs[b], jks[b], r_jks[b], s1s[b], r_s1s[b]
            P.op("act", lambda e: e.activation(out=jk[:], in_=ysb[:], func=AF.Square), r=[r_ysb], w=[r_jk])
            P.op("dve", lambda e: e.reduce_sum(out=s1[:, 0:1], in_=jk[:], axis=AX.X), r=[r_jk], w=[r_s1])
            P.op("dve", lambda e: e.tensor_scalar(out=s1[:, 0:1], in0=s1[:, 0:1], scalar1=1.0 / D, scalar2=1e-6, op0=ALU.mult, op1=ALU.add),
                 r=[r_s1], w=[r_s1])
            P.op("act", lambda e: e.activation(out=s1[:, 0:1], in_=s1[:, 0:1], func=AF.Sqrt), r=[r_s1], w=[r_s1])
            P.op("dve", lambda e: e.reciprocal(s1[:, 1:2], s1[:, 0:1]), r=[r_s1], w=[r_s1])
            P.op("dve", lambda e: e.tensor_tensor(out=jk[:], in0=ysb[:], in1=gpo[:], op=ALU.mult), r=[r_ysb, r_gpo], w=[r_jk])
            P.op("dve", lambda e: e.scalar_tensor_tensor(out=ob[b][:], in0=jk[:], scalar=s1[:, 1:2], in1=xr[b][:], op0=ALU.mult, op1=ALU.add),
                 r=[r_jk, r_s1, r_xr[b]], w=[r_ob[b]])
            ro = Reg()
            self.out_regs.append(ro)
            P.dma("pool", lambda e: e.dma_start(out=self.out.ap()[T * 128:(T + 1) * 128, :], in_=ob[b][:]), r=[r_ob[b]], w=[ro])

        stage_a(0)
        for T in range(16):
            stage_b(T)
            if T + 1 < 16:
                stage_a(T + 1)
            stage_c(T)


def _build(in_shapes, dbg=()):
    b = B(in_shapes, dbg)
    nc = b.build()
    return nc, b


def kernel(**inputs):
    inputs = {k: np.asarray(v) for k, v in inputs.items()}
    consts = _consts()
    L = _layout_inputs(inputs)
    shared = dict(consts)
    shared.update(L)
    in_shapes = {"x": (S, D)}
    for k, v in shared.items():
        in_shapes[k] = v.shape
    nc, b = _build(in_shapes)
    active = [0, 1, 4, 5]
    zero_x = np.zeros((S, D), np.float32)
    in_maps = []
    for c in range(8):
        if c in active:
            m = {"x": np.ascontiguousarray(inputs["x"][active.index(c)])}
        else:
            m = {"x": zero_x}
        m.update(shared)
        in_maps.append(m)
    res = run_bass_kernel_spmd(nc, in_maps, core_ids=list(range(8)))
    out = np.stack([res.results[c]["out"] for c in active], 0)
    return out.astype(np.float32)
```
